# Optimizing a Trainium2 kernel written in Bass

```python
import jax, jax.numpy as jnp
from jax import lax
import numpy as np

D_MODEL = 1024
BATCH = 8
SEQ = 2048
DEPTH = 2
DEC_BATCH = 128
DEC_SEQ = 1
PAST_LEN = 16384
PAGE_SIZE = 128

H_RET = 4
H_M = 4
HEAD_DIM = 128
W_RET = H_RET * HEAD_DIM
W_M = H_M * HEAD_DIM
D_MIX = W_RET + W_M
CONV_W = 4
CHUNK = 128
ROPE_BASE = 10000.0
LN_EPS = 1e-5
GN_EPS = 1e-5
ALPHA = (2 * DEPTH) ** 0.25
BETA = (8 * DEPTH) ** -0.25
N_IN = 4 * W_RET + 5 * W_M + 2 * H_M
SPLITS = [W_RET, 2 * W_RET, 3 * W_RET, 4 * W_RET,
          4 * W_RET + 2 * W_M, 4 * W_RET + 3 * W_M, 4 * W_RET + 4 * W_M,
          4 * W_RET + 5 * W_M, 4 * W_RET + 5 * W_M + H_M]

kernel_name = 'hymba_retention_mlstm_deepnorm_step'

F32 = jnp.float32


def layer_norm(x, g, b):
    xf = x.astype(F32)
    mu = xf.mean(-1, keepdims=True)
    var = jnp.square(xf - mu).mean(-1, keepdims=True)
    return ((xf - mu) * lax.rsqrt(var + LN_EPS)).astype(x.dtype) * g + b


def head_norm(h, g):
    mu = h.mean(-1, keepdims=True)
    var = jnp.square(h - mu).mean(-1, keepdims=True)
    return (h - mu) * lax.rsqrt(var + GN_EPS) * g.reshape(h.shape[-2], h.shape[-1]).astype(F32)


def rotary(x, pos):
    half = HEAD_DIM // 2
    inv = ROPE_BASE ** (-jnp.arange(half, dtype=F32) / half)
    ang = pos.astype(F32)[:, None] * inv[None, :]
    cos = jnp.cos(ang)[None, :, None, :]
    sin = jnp.sin(ang)[None, :, None, :]
    x1 = x[..., :half].astype(F32)
    x2 = x[..., half:].astype(F32)
    return jnp.concatenate([x1 * cos - x2 * sin, x1 * sin + x2 * cos], axis=-1)


def causal_conv(u, buf, w, b):
    T = u.shape[1]
    full = jnp.concatenate([buf.astype(u.dtype), u], axis=1)
    out = b + sum(full[:, j:j + T] * w[j] for j in range(CONV_W))
    return jax.nn.silu(out), full[:, -(CONV_W - 1):]


def chunk_len(T):
    return CHUNK if T % CHUNK == 0 else T


def to_chunks(a, L):
    B, T = a.shape[0], a.shape[1]
    if a.ndim == 4:
        return a.reshape(B, T // L, L, a.shape[2], a.shape[3]).transpose(1, 0, 3, 2, 4).astype(F32)
    return a.reshape(B, T // L, L, a.shape[2]).transpose(1, 0, 3, 2).astype(F32)


def from_chunks(o):
    N, B, H, L, D = o.shape
    return o.transpose(1, 0, 3, 2, 4).reshape(B, N * L, H, D)


def retention(q, k, v, S0):
    T = q.shape[1]
    L = chunk_len(T)
    log_gamma = jnp.log(1.0 - 2.0 ** (-5.0 - jnp.arange(H_RET, dtype=F32)))
    idx = jnp.arange(L, dtype=F32)
    diff = idx[:, None] - idx[None, :]
    decay = jnp.exp(log_gamma[:, None, None] * jnp.maximum(diff, 0.0)) * (diff >= 0)
    q_decay = jnp.exp(log_gamma[:, None] * (idx + 1.0))[None, :, :, None]
    k_decay = jnp.exp(log_gamma[:, None] * (L - 1.0 - idx))[None, :, :, None]
    c_decay = jnp.exp(log_gamma * L)[None, :, None, None]

    def step(S, xs):
        qc, kc, vc = xs
        sc = jnp.einsum('bhld,bhmd->bhlm', qc, kc) * decay
        o = jnp.einsum('bhlm,bhmd->bhld', sc, vc) + jnp.einsum('bhld,bhde->bhle', qc, S) * q_decay
        S = S * c_decay + jnp.einsum('bhld,bhle->bhde', kc * k_decay, vc)
        return S, o

    S, o = lax.scan(step, S0.astype(F32), (to_chunks(q, L), to_chunks(k, L), to_chunks(v, L)))
    return from_chunks(o), S


def mlstm(q, k, v, i_pre, f_pre, C0, n0, m0):
    T = q.shape[1]
    L = chunk_len(T)
    causal = jnp.arange(L)[:, None] >= jnp.arange(L)[None, :]

    def step(carry, xs):
        C, n, m = carry
        qc, kc, vc, ic, fc = xs
        b = jnp.cumsum(jax.nn.log_sigmoid(fc), axis=-1)
        a = b + m[..., None]
        Dm = jnp.where(causal, b[..., :, None] - b[..., None, :] + ic[..., None, :], -jnp.inf)
        mt = jnp.maximum(a, Dm.max(-1))
        w_intra = jnp.exp(Dm - mt[..., None])
        w_inter = jnp.exp(a - mt)
        s = jnp.einsum('bhld,bhmd->bhlm', qc, kc) * w_intra
        num = jnp.einsum('bhlm,bhmd->bhld', s, vc) + jnp.einsum('bhld,bhde->bhle', qc, C) * w_inter[..., None]
        den = s.sum(-1) + jnp.einsum('bhld,bhd->bhl', qc, n) * w_inter
        h = num / jnp.maximum(jnp.abs(den), jnp.exp(-mt))[..., None]
        bL = b[..., -1]
        g = bL[..., None] - b + ic
        m_new = jnp.maximum(bL + m, g.max(-1))
        wk = jnp.exp(g - m_new[..., None])
        wc = jnp.exp(bL + m - m_new)
        C_new = C * wc[..., None, None] + jnp.einsum('bhld,bhle->bhde', kc * wk[..., None], vc)
        n_new = n * wc[..., None] + jnp.einsum('bhld,bhl->bhd', kc, wk)
        return (C_new, n_new, m_new), h

    (C, n, m), h = lax.scan(step, (C0.astype(F32), n0.astype(F32), m0.astype(F32)),
                            (to_chunks(q, L), to_chunks(k, L), to_chunks(v, L),
                             to_chunks(i_pre, L), to_chunks(f_pre, L)))
    return from_chunks(h), C, n, m


def mixer_layer(x, pos, S0, C0, n0, m0, buf0, w_in, conv_w, conv_b, b_i, b_f,
                g_ret, g_m, w_out, ln_g, ln_b):
    B, T, _ = x.shape
    proj = jnp.einsum('btd,dn->btn', x, w_in)
    rq, rk, rv, rz, mqk, mv, mo, mz, mi, mf = jnp.split(proj, SPLITS, axis=-1)
    q_r = rotary(rq.reshape(B, T, H_RET, HEAD_DIM), pos)
    k_r = rotary(rk.reshape(B, T, H_RET, HEAD_DIM), pos) * (HEAD_DIM ** -0.5)
    o_r, S_new = retention(q_r, k_r, rv.reshape(B, T, H_RET, HEAD_DIM), S0)
    o_r = head_norm(o_r, g_ret).astype(x.dtype).reshape(B, T, W_RET) * jax.nn.silu(rz)
    qk, buf_new = causal_conv(mqk, buf0, conv_w, conv_b)
    q_m = qk[..., :W_M].reshape(B, T, H_M, HEAD_DIM)
    k_m = qk[..., W_M:].reshape(B, T, H_M, HEAD_DIM) * (HEAD_DIM ** -0.5)
    i_pre = mi.astype(F32) + b_i.astype(F32)
    f_pre = mf.astype(F32) + b_f.astype(F32)
    h_m, C_new, n_new, m_new = mlstm(q_m, k_m, mv.reshape(B, T, H_M, HEAD_DIM), i_pre, f_pre, C0, n0, m0)
    h_m = head_norm(h_m, g_m).astype(x.dtype).reshape(B, T, W_M) * jax.nn.sigmoid(mo) * jax.nn.silu(mz)
    mix = jnp.einsum('btm,md->btd', jnp.concatenate([o_r, h_m], axis=-1), w_out)
    y = layer_norm(ALPHA * x + mix, ln_g, ln_b)
    return y, S_new, C_new, n_new, m_new, buf_new


def setup_inputs(seed: int = 0) -> dict:
    key = jax.random.key(seed)
    ks = jax.random.split(key, 20)
    nrm = jax.random.normal
    return {
        'x_prompt': nrm(ks[0], (BATCH, SEQ, D_MODEL), F32),
        'x_sample': nrm(ks[1], (DEC_BATCH, DEC_SEQ, D_MODEL), F32),
        'state_ret': 0.5 * nrm(ks[2], (DEPTH, DEC_BATCH, H_RET, HEAD_DIM, HEAD_DIM), F32),
        'state_mlstm_C': 0.5 * nrm(ks[3], (DEPTH, DEC_BATCH, H_M, HEAD_DIM, HEAD_DIM), F32),
        'state_mlstm_n': 0.5 * nrm(ks[4], (DEPTH, DEC_BATCH, H_M, HEAD_DIM), F32),
        'state_mlstm_m': nrm(ks[5], (DEPTH, DEC_BATCH, H_M), F32),
        'state_conv': nrm(ks[6], (DEPTH, DEC_BATCH, CONV_W - 1, 2 * W_M), F32),
        'w_in': nrm(ks[7], (DEPTH, D_MODEL, N_IN), F32) * D_MODEL ** -0.5,
        'conv_w': nrm(ks[8], (DEPTH, CONV_W, 2 * W_M), F32) * CONV_W ** -0.5,
        'conv_b': 0.02 * nrm(ks[9], (DEPTH, 2 * W_M), F32),
        'b_i': 0.1 * nrm(ks[10], (DEPTH, H_M), F32),
        'b_f': jnp.linspace(3.0, 6.0, H_M, dtype=F32)[None, :] + 0.1 * nrm(ks[11], (DEPTH, H_M), F32),
        'g_ret': 1.0 + 0.02 * nrm(ks[12], (DEPTH, W_RET), F32),
        'g_m': 1.0 + 0.02 * nrm(ks[13], (DEPTH, W_M), F32),
        'w_out': nrm(ks[14], (DEPTH, D_MIX, D_MODEL), F32) * (D_MIX ** -0.5) * BETA,
        'ln_g': 1.0 + 0.02 * nrm(ks[15], (DEPTH, D_MODEL), F32),
        'ln_b': 0.02 * nrm(ks[16], (DEPTH, D_MODEL), F32),
    }


def reference(x_prompt, x_sample, state_ret, state_mlstm_C, state_mlstm_n, state_mlstm_m,
              state_conv, w_in, conv_w, conv_b, b_i, b_f, g_ret, g_m, w_out, ln_g, ln_b):
    B, T = x_prompt.shape[0], x_prompt.shape[1]
    Bs, Ts = x_sample.shape[0], x_sample.shape[1]
    pos_p = jnp.arange(T, dtype=jnp.int32)
    pos_s = PAST_LEN + jnp.arange(Ts, dtype=jnp.int32)
    S0p = jnp.zeros((B, H_RET, HEAD_DIM, HEAD_DIM), F32)
    C0p = jnp.zeros((B, H_M, HEAD_DIM, HEAD_DIM), F32)
    n0p = jnp.zeros((B, H_M, HEAD_DIM), F32)
    m0p = jnp.zeros((B, H_M), F32)
    buf0p = jnp.zeros((B, CONV_W - 1, 2 * W_M), x_prompt.dtype)
    xp, xs = x_prompt, x_sample
    rp, cp, np_, mp, vp = [], [], [], [], []
    rs, cs, ns, ms, vs = [], [], [], [], []
    for l in range(DEPTH):
        w = (w_in[l], conv_w[l], conv_b[l], b_i[l], b_f[l], g_ret[l], g_m[l], w_out[l], ln_g[l], ln_b[l])
        xp, S, C, n, m, buf = mixer_layer(xp, pos_p, S0p, C0p, n0p, m0p, buf0p, *w)
        rp.append(S); cp.append(C); np_.append(n); mp.append(m); vp.append(buf)
        xs, S, C, n, m, buf = mixer_layer(xs, pos_s, state_ret[l], state_mlstm_C[l], state_mlstm_n[l],
                                          state_mlstm_m[l], state_conv[l], *w)
        rs.append(S); cs.append(C); ns.append(n); ms.append(m); vs.append(buf)
    return (xp, xs,
            jnp.stack(rp), jnp.stack(cp), jnp.stack(np_), jnp.stack(mp), jnp.stack(vp),
            jnp.stack(rs), jnp.stack(cs), jnp.stack(ns), jnp.stack(ms), jnp.stack(vs))
```

```python
from contextlib import ExitStack
import numpy as np
import concourse.bass as bass
import concourse.mybir as mybir
from concourse.bass_utils import run_bass_kernel_spmd

F32 = mybir.dt.float32
BF16 = mybir.dt.bfloat16
AF = mybir.ActivationFunctionType
ALU = mybir.AluOpType
AX = mybir.AxisListType

D = 1024
T = 2048
NT = 16
L = 128
NS = 16
H = 4
HD = 128
N_IN = 4616
PAST_LEN = 16384
ALPHA = (2 * 2) ** 0.25
GN_EPS = 1e-5
LN_EPS = 1e-5
ISQ = float(HD ** -0.5)
LNISQ = float(np.log(HD ** -0.5))
GAM = [float(np.float32(1.0) - np.float32(2.0) ** np.float32(-5.0 - h)) for h in range(H)]
LOGG = [float(np.log(np.float32(g))) for g in GAM]
GAML = [float(np.exp(np.float32(lg) * L)) for lg in LOGG]

C_RQ, C_RK, C_RV, C_RZ, C_MQK, C_MV, C_MO, C_MZ, C_G = 0, 512, 1024, 1536, 2048, 3072, 3584, 4096, 4608

K_ID = 0
K_TRI = 128
K_KDEC = 256
K_QDEC = 260
K_ONE = 772
K_GAM = 900
NCST = 904


class Buf:
    __slots__ = ("name", "w", "r", "psum")

    def __init__(self, name="", psum=False):
        self.name = name
        self.w = None
        self.r = []
        self.psum = psum


class Chan:
    def __init__(self, prog, name):
        self.sem = prog.new_sem(name)
        self.val = 0


class Prog:
    ENG = ("pe", "act", "dve", "pool", "sp")

    def __init__(self, nc, stack):
        self.nc = nc
        self.stack = stack
        self.items = {e: [] for e in self.ENG}
        self.cnt = {e: 0 for e in self.ENG}
        self.esem = {e: self.new_sem("prog_" + e) for e in self.ENG}
        self.waited = {e: {} for e in self.ENG}
        self.chans = []

    def new_sem(self, name):
        return self.stack.enter_context(self.nc.semaphore(name))

    def chan(self, name):
        c = Chan(self, name)
        self.chans.append(c)
        return c

    def _need(self, eng, reads, writes):
        need = {}

        def add(ev):
            if ev is None:
                return
            k = (ev[0], id(ev[1]) if ev[0] == 'c' else ev[1])
            if k not in need or need[k][2] < ev[2]:
                need[k] = ev

        for b in reads:
            add(b.w)
            if b.psum:
                for r in b.r:
                    if not (r[0] == 'e' and r[1] == eng):
                        add(r)
        for b in writes:
            if b.w is not None and not (b.w[0] == 'e' and b.w[1] == eng):
                add(b.w)
            for r in b.r:
                if not (r[0] == 'e' and r[1] == eng):
                    add(r)
        wd = self.waited[eng]
        for k, ev in need.items():
            if wd.get(k, 0) >= ev[2]:
                continue
            wd[k] = ev[2]
            if ev[0] == 'e':
                self.items[eng].append(('we', ev[1], ev[2]))
            else:
                self.items[eng].append(('wc', ev[1].sem, ev[2]))

    def op(self, eng, fn, reads=(), writes=()):
        self._need(eng, reads, writes)
        self.cnt[eng] += 1
        ev = ('e', eng, self.cnt[eng])
        self.items[eng].append(('op', fn, self.cnt[eng]))
        for b in reads:
            b.r.append(ev)
        for b in writes:
            b.w = ev
            b.r = []
        return ev

    def dma(self, eng, chan, out, in_, reads=(), writes=(), **kw):
        self._need(eng, reads, writes)
        chan.val += 16
        ev = ('c', chan, chan.val)
        self.items[eng].append(('dma', lambda e, out=out, in_=in_, kw=kw, sem=chan.sem:
                                e.dma_start(out=out, in_=in_, **kw).then_inc(sem, 16)))
        for b in reads:
            b.r.append(ev)
        for b in writes:
            b.w = ev
            b.r = []
        return ev

    def wait_chan(self, eng, chan):
        self.items[eng].append(('wc', chan.sem, chan.val))

    def finish(self, block):
        targets = {e: set() for e in self.ENG}
        for e in self.ENG:
            for it in self.items[e]:
                if it[0] == 'we':
                    targets[it[1]].add(it[2])
        rank = {}
        for e in self.ENG:
            rank[e] = {s_: i + 1 for i, s_ in enumerate(sorted(targets[e]))}
        self.n_signals = {e: len(rank[e]) for e in self.ENG}

        def run(eng, e):
            sem = self.esem[eng]
            rk = rank[eng]
            for it in self.items[eng]:
                if it[0] == 'we':
                    e.wait_ge(self.esem[it[1]], rank[it[1]][it[2]])
                elif it[0] == 'wc':
                    e.wait_ge(it[1], it[2])
                elif it[0] == 'op':
                    ins = it[1](e)
                    if it[2] in rk:
                        ins.then_inc(sem, 1)
                else:
                    it[1](e)

        @block.tensor
        def _(e):
            run("pe", e)

        @block.scalar
        def _(e):
            run("act", e)

        @block.vector
        def _(e):
            run("dve", e)

        @block.gpsimd
        def _(e):
            run("pool", e)

        @block.sync
        def _(e):
            run("sp", e)


class StopBuild(Exception):
    pass


def build_program(n_layers=2, n_tiles=NT, do_sample=True, dbg=None, stop_at=None):
    nc = bass.Bass("TRN2", target_bir_lowering=False)
    dt_in = lambda name, shape: nc.dram_tensor(name, shape, F32, kind="ExternalInput").ap()
    dt_out = lambda name, shape: nc.dram_tensor(name, shape, F32, kind="ExternalOutput").ap()
    xp = dt_in("xp", [T, D])
    xs = dt_in("xs", [NS, D])
    sret = dt_in("sret", [2, NS, H, HD, HD])
    sC = dt_in("sC", [2, NS, H, HD, HD])
    sn = dt_in("sn", [2, NS, H, HD])
    sm = dt_in("sm", [2, NS, H])
    sconv = dt_in("sconv", [2, NS, 3, D])
    w_in = dt_in("w_in", [2, D, N_IN])
    conv_w = dt_in("conv_w", [2, 4, D])
    conv_b = dt_in("conv_b", [2, D])
    b_i = dt_in("b_i", [2, H])
    b_f = dt_in("b_f", [2, H])
    g_ret = dt_in("g_ret", [2, 512])
    g_m = dt_in("g_m", [2, 512])
    w_out = dt_in("w_out", [2, D, D])
    ln_g = dt_in("ln_g", [2, D])
    ln_b = dt_in("ln_b", [2, D])
    cst_d = dt_in("cst", [128, NCST])
    rope_p = dt_in("rope_p", [T, 2, HD])
    rope_s = dt_in("rope_s", [2 * HD])

    y_p = dt_out("y_p", [T, D])
    y_s = dt_out("y_s", [NS, D])
    ret_p = dt_out("ret_p", [2, H, HD, HD])
    C_p = dt_out("C_p", [2, H, HD, HD])
    n_p = dt_out("n_p", [2, H, HD])
    m_p = dt_out("m_p", [2, H])
    conv_p = dt_out("conv_p", [2, 3, D])
    ret_s = dt_out("ret_s", [2, NS, H, HD, HD])
    C_s = dt_out("C_s", [2, NS, H, HD, HD])
    n_s = dt_out("n_s", [2, NS, H, HD])
    m_s = dt_out("m_s", [2, NS, H])
    conv_s = dt_out("conv_s", [2, NS, 3, D])
    y0 = nc.dram_tensor("y0_scratch", [T, D], F32, kind="Internal").ap()
    dbg_out = {}
    if dbg:
        for name, shape in dbg.items():
            dbg_out[name] = dt_out("dbg_" + name, shape)

    with ExitStack() as st:
        P = Prog(nc, st)
        sb = lambda name, shape, dt=F32: st.enter_context(nc.sbuf_tensor("s_" + name, shape, dt))
        out_chans = []

        def ochan(name):
            c = P.chan(name)
            out_chans.append(c)
            return c

        banks = []
        for i in range(8):
            t_ = st.enter_context(nc.psum_tensor("bank%d" % i, [128, 512], F32))
            banks.append((t_, Buf("bank%d" % i, psum=True)))
        bank_ctr = [0]

        def nb():
            i = bank_ctr[0] % 8
            bank_ctr[0] += 1
            return banks[i]

        def bfv(bank_t):
            return bank_t[:].bitcast(BF16)

        cst = sb("cst", [128, NCST]); CST = Buf("cst")
        ident_bf = sb("ident_bf", [128, 128], BF16); IDB = Buf()
        mask_bf = sb("mask_bf", [128, 128], BF16); MSK = Buf()
        c_ld = P.chan("c_ld")
        P.dma("sp", c_ld, cst[:], cst_d, writes=[CST])
        ident_f = cst[:, K_ID:K_ID + 128]
        tri_f = cst[:, K_TRI:K_TRI + 128]
        ones_f = cst[:, K_ONE:K_ONE + 128]
        P.op("dve", lambda e: e.tensor_copy(out=ident_bf[:], in_=ident_f), reads=[CST], writes=[IDB])
        P.op("dve", lambda e: e.tensor_copy(out=mask_bf[:], in_=tri_f), reads=[CST], writes=[MSK])
        kdec = lambda h: cst[:, K_KDEC + h:K_KDEC + h + 1]
        qdecT = cst[:, K_QDEC:K_QDEC + 512].rearrange("p (h l) -> p h l", h=4)

        w_in_sb = sb("w_in_sb", [128, 8, N_IN], BF16)
        WIN = [Buf("win%d" % i) for i in range(10)]
        w_out_sb = sb("w_out_sb", [128, 8, D], BF16)
        WOUT = [Buf("wout0"), Buf("wout1")]
        c_win = [P.chan("c_win%d" % i) for i in range(10)]
        c_wout = [P.chan("c_wout%d" % i) for i in range(2)]
        gbc = sb("gbc", [128, 1024]); GBC = Buf()
        lng = sb("lng", [128, 1024]); LNG = Buf()
        lnb = sb("lnb", [128, 1024]); LNB = Buf()
        bias8 = sb("bias8", [128, 8]); BIAS8 = Buf()
        cwb_in = sb("cwb_in", [40, 128]); CWBIN = Buf()
        cwT = sb("cwT", [128, 40]); CWT = Buf()
        diag = sb("diag", [128, 32, 128], BF16); DIAG = Buf()
        c_par = [P.chan("c_par%d" % i) for i in range(8)]

        def wblk(i):
            return (i * 512, min((i + 1) * 512, N_IN))

        def load_weights(l):
            wv = w_in[l].rearrange("(k p) n -> p k n", p=128)
            for i in range(10):
                a, b_ = wblk(i)
                P.dma("pool", c_win[i], w_in_sb[:, :, a:b_], wv[:, :, a:b_], writes=[WIN[i]])
            wo = w_out[l].rearrange("(k p) n -> p k n", p=128)
            for i in range(2):
                P.dma("pool", c_wout[i], w_out_sb[:, :, i * 512:(i + 1) * 512], wo[:, :, i * 512:(i + 1) * 512],
                      writes=[WOUT[i]])

        def load_params(l):
            P.dma("sp", c_par[0], gbc[:, 0:512], g_ret[l].partition_broadcast(128), writes=[GBC])
            P.dma("sp", c_par[1], gbc[:, 512:1024], g_m[l].partition_broadcast(128), writes=[GBC])
            P.dma("sp", c_par[2], lng[:], ln_g[l].partition_broadcast(128), writes=[LNG])
            P.dma("sp", c_par[3], lnb[:], ln_b[l].partition_broadcast(128), writes=[LNB])
            P.dma("sp", c_par[4], bias8[:, 0:4], b_i[l].partition_broadcast(128), writes=[BIAS8])
            P.dma("sp", c_par[5], bias8[:, 4:8], b_f[l].partition_broadcast(128), writes=[BIAS8])
            P.dma("sp", c_par[6], cwb_in[0:32, :], conv_w[l].rearrange("j (ch c) -> (j ch) c", c=128), writes=[CWBIN])
            P.dma("sp", c_par[7], cwb_in[32:40, :], conv_b[l].rearrange("(ch c) -> ch c", c=128), writes=[CWBIN])
            P.op("pool", lambda e: e.tensor_scalar(out=gbc[:, 512:1024], in0=gbc[:, 512:1024], scalar1=0.5, scalar2=None,
                                                   op0=ALU.mult), reads=[GBC], writes=[GBC])
            bt, BK = nb()
            P.op("pe", lambda e: e.transpose(bt[:, 0:40], cwb_in[0:40, :], ident_f[0:40, 0:40]), reads=[CWBIN, CST], writes=[BK])
            P.op("act", lambda e: e.copy(out=cwT[:], in_=bt[:, 0:40]), reads=[BK], writes=[CWT])
            for ch in range(8):
                for j in range(4):
                    idx = j * 8 + ch
                    P.op("pool", lambda e, ch=ch, j=j, idx=idx: e.tensor_scalar(
                        out=diag[:, ch * 4 + j, :], in0=ident_f, scalar1=cwT[:, idx:idx + 1], scalar2=None, op0=ALU.mult),
                        reads=[CST, CWT], writes=[DIAG])

        gates = sb("gates", [128, NT, 8]); GATES = Buf()
        lneg = sb("lneg", [128, NT, 4]); LNEG = Buf()
        bneg = sb("bneg", [128, NT, 4]); BNEG = Buf()
        u_sb = sb("u_sb", [128, NT, 4]); USB = Buf()
        uT_sb = sb("uT_sb", [64, 128]); UTS = Buf()
        umaxc = sb("umaxc", [64, 1]); UMX = Buf()
        row = sb("row", [1, 6, 64]); ROW = Buf()
        cw_b = sb("cw_b", [128, 2, NT + 1, 4]); CWB = Buf()
        pk = sb("pk", [128, NT, 4]); PK = Buf()
        thr = sb("thr", [128, NT, 4]); THR = Buf()
        tmp64 = sb("tmp64", [128, NT, 4]); TMP64 = Buf()

        x_sb = [sb("x_sb%d" % i, [128, D]) for i in range(2)]; XSB = [Buf(), Buf()]
        rope_sb = [sb("rope_sb%d" % i, [128, 2, HD]) for i in range(2)]; ROPE = [Buf(), Buf()]
        c_x = [P.chan("c_x0"), P.chan("c_x1")]
        c_rope = [P.chan("c_rope0"), P.chan("c_rope1")]
        x_bf = sb("x_bf", [128, D], BF16); XBF = Buf()
        xT = sb("xT", [128, 8, 128], BF16); XT = Buf()
        tmp_t = sb("tmp_t", [128, 8, HD]); TMPT = Buf()
        tmp_u = sb("tmp_u", [128, 8, HD]); TMPU = Buf()
        qk_rot = sb("qk_rot", [128, 8, HD], BF16); QKR = Buf()
        v2 = sb("v2", [128, 4, HD], BF16); V2 = Buf()
        sz = sb("sz", [128, 1024]); SZ = Buf()
        qT2 = sb("qT2", [128, 4, 128], BF16); QT2 = Buf()
        kT = sb("kT", [128, 4, 128], BF16); KT = Buf()
        s2 = sb("s2", [128, 4, 128], BF16); S2 = Buf()
        S_f = sb("S_f", [128, 4, HD]); SF = Buf()
        S_bf = sb("S_bf", [128, 4, HD], BF16); SBF = Buf()
        hist = sb("hist", [128, 8, 131], BF16); HIST = Buf()
        qkm = sb("qkm", [128, 8, 128], BF16); QKM = Buf()
        kp = sb("kp", [128, 4, 128], BF16); KP = Buf()
        v1 = sb("v1", [128, 4, 130], BF16); V1 = Buf()
        th = sb("th", [128, 512]); TH = Buf()
        s2m = sb("s2m", [128, 4, 128], BF16); S2M = Buf()
        C_f = sb("C_f", [128, 4, 130]); CF = Buf()
        C_bf = sb("C_bf", [128, 4, 130], BF16); CBF = Buf()
        dn = sb("dn", [128, 4]); DN = Buf()
        hm = sb("hm", [128, 4, HD]); HM = Buf()
        o_sb = sb("o_sb", [128, 4, HD]); OSB = Buf()
        stats = sb("stats", [128, 8, 6]); STATS = Buf()
        mv = sb("mv", [128, 8, 2]); MV = Buf()
        rstd = sb("rstd", [128, 8]); RSTD = Buf()
        nbias = sb("nbias", [128, 8]); NBIAS = Buf()
        mhalf = sb("mhalf", [128, 8]); MHALF = Buf()
        on = sb("on", [128, 8, HD]); ON = Buf()
        gated = sb("gated", [128, 1024], BF16); GATED = Buf()
        gT = sb("gT", [128, 8, 128], BF16); GT = Buf()
        ypre = sb("ypre", [128, D]); YPRE = Buf()
        lstats = sb("lstats", [128, 2, 6]); LSTATS = Buf()
        lmv = sb("lmv", [128, 2]); LMV = Buf()
        lrs = sb("lrs", [128, 2]); LRS = Buf()
        y_sb = [sb("y_sb%d" % i, [128, D]) for i in range(2)]; YSB = [Buf(), Buf()]
        c_y = [ochan("c_y0"), ochan("c_y1")]
        convp_sb = sb("convp_sb", [128, 8, 3]); CONVP = Buf()
        convp_T = sb("convp_T", [3, D]); CONVPT = Buf()
        c_fin = [ochan("c_fin%d" % i) for i in range(5)]

        P.op("pool", lambda e: e.memset(mhalf[:], -0.5), writes=[MHALF])

        def dbg_dump(name, src_ap, bufs):
            if name in dbg_out:
                c = ochan("c_dbg_" + name)
                P.dma("sp", c, dbg_out[name], src_ap, reads=bufs)

        def load_x(l, t, par):
            src = xp if l == 0 else y0
            P.dma("sp", c_x[par], x_sb[par][:], src[t * 128:(t + 1) * 128, :], writes=[XSB[par]],
                  reads=([Y0B[t]] if l > 0 else []))

        def load_main(l, t):
            par = t % 2
            load_x(l, t, par)
            P.dma("sp", c_rope[par], rope_sb[par][:], rope_p[t * 128:(t + 1) * 128], writes=[ROPE[par]])

        def make_xT(par):
            P.op("pool", lambda e: e.tensor_copy(out=x_bf[:], in_=x_sb[par][:]), reads=[XSB[par]], writes=[XBF])
            bt, BK = nb()
            v = bfv(bt)
            for k in range(8):
                P.op("pe", lambda e, k=k: e.transpose(v[:, k * 128:(k + 1) * 128], x_bf[:, k * 128:(k + 1) * 128], ident_bf[:]),
                     reads=[XBF, IDB], writes=[BK])
            P.op("act", lambda e: e.copy(out=xT[:].rearrange("p k c -> p (k c)"), in_=v), reads=[BK], writes=[XT])

        Y0B = [Buf("y0_%d" % t) for t in range(NT)]

        def prepass(l):
            for t in range(n_tiles):
                par = t % 2
                load_x(l, t, par)
                make_xT(par)
                bt, BK = nb()
                for k in range(8):
                    P.op("pe", lambda e, k=k: e.matmul(bt[:, 0:8], lhsT=xT[:, k, :], rhs=w_in_sb[:, k, C_G:C_G + 8],
                                                       start=(k == 0), stop=(k == 7)), reads=[XT, WIN[9]], writes=[BK])
                P.op("dve", lambda e, t=t: e.tensor_tensor(out=gates[:, t, :], in0=bt[:, 0:8], in1=bias8[:], op=ALU.add),
                     reads=[BK, BIAS8], writes=[GATES])
            nt = n_tiles
            stage('pp_loop')
            P.op("act", lambda e: e.activation(out=lneg[:, 0:nt, :], in_=gates[:, 0:nt, 4:8], func=AF.Exp, scale=-1.0),
                 reads=[GATES], writes=[LNEG])
            P.op("act", lambda e: e.activation(out=lneg[:, 0:nt, :], in_=lneg[:, 0:nt, :], func=AF.Ln, bias=1.0),
                 reads=[LNEG], writes=[LNEG])
            dbg_dump('gates', gates[:], [GATES])
            dbg_dump('lneg', lneg[:], [LNEG])
            stage('pp_a')
            bt, BK = nb()
            ln2 = lneg[:].rearrange("p t h -> p (t h)")
            P.op("pe", lambda e: e.matmul(bt[:, 0:nt * 4], lhsT=tri_f, rhs=ln2[:, 0:nt * 4], start=True, stop=True),
                 reads=[LNEG, CST], writes=[BK])
            stage('pp_b')
            bt2, BK2 = nb()
            P.op("pe", lambda e: e.matmul(bt2[0:1, 0:nt * 4], lhsT=ones_f[:, 0:1], rhs=ln2[:, 0:nt * 4], start=True, stop=True),
                 reads=[LNEG, CST], writes=[BK2])
            stage('pp_c')
            P.op("act", lambda e: e.copy(out=bneg[:].rearrange("p t h -> p (t h)")[:, 0:nt * 4], in_=bt[:, 0:nt * 4]),
                 reads=[BK], writes=[BNEG])
            P.op("dve", lambda e: e.tensor_tensor(out=u_sb[:, 0:nt, :], in0=gates[:, 0:nt, 0:4], in1=bneg[:, 0:nt, :], op=ALU.add),
                 reads=[GATES, BNEG], writes=[USB])
            P.op("act", lambda e: e.copy(out=row[0:1, 1, 0:nt * 4], in_=bt2[0:1, 0:nt * 4]), reads=[BK2], writes=[ROW])
            stage('pp_cum')
            bt3, BK3 = nb()
            u2 = u_sb[:].rearrange("p t h -> p (t h)")
            P.op("pe", lambda e: e.transpose(bt3[0:nt * 4, 0:128], u2[:, 0:nt * 4], ident_f), reads=[USB, CST], writes=[BK3])
            P.op("dve", lambda e: e.tensor_reduce(out=umaxc[0:nt * 4, :], in_=bt3[0:nt * 4, 0:128], axis=AX.X, op=ALU.max),
                 reads=[BK3], writes=[UMX])
            bt4, BK4 = nb()
            P.op("pe", lambda e: e.transpose(bt4[0:1, 0:nt * 4], umaxc[0:nt * 4, 0:1], ident_f[0:nt * 4, 0:nt * 4]),
                 reads=[UMX, CST], writes=[BK4])
            P.op("act", lambda e: e.copy(out=row[0:1, 0, 0:nt * 4], in_=bt4[0:1, 0:nt * 4]), reads=[BK4], writes=[ROW])
            stage('pp_umax')
            rv_ = lambda i: row[0:1, i, 0:nt * 4].rearrange("p (t h) -> p t h", h=4)
            P.op("dve", lambda e: e.tensor_scalar(out=row[0:1, 3, 0:nt * 4], in0=row[0:1, 1, 0:nt * 4], scalar1=-1.0, scalar2=None,
                                                  op0=ALU.mult), reads=[ROW], writes=[ROW])
            for h in range(4):
                P.op("dve", lambda e, h=h: e.tensor_tensor_scan(out=rv_(2)[:, :, h], data0=rv_(0)[:, :, h], data1=rv_(3)[:, :, h],
                                                                initial=0.0, op0=ALU.max, op1=ALU.add), reads=[ROW], writes=[ROW])
            P.op("dve", lambda e: e.tensor_tensor(out=row[0:1, 3, 0:nt * 4], in0=row[0:1, 2, 0:nt * 4], in1=row[0:1, 1, 0:nt * 4],
                                                  op=ALU.add), reads=[ROW], writes=[ROW])
            P.op("dve", lambda e: e.memset(row[0:1, 5, 0:4], 0.0), reads=[ROW], writes=[ROW])
            if nt > 1:
                P.op("dve", lambda e: e.tensor_copy(out=row[0:1, 5, 4:nt * 4], in_=row[0:1, 2, 0:(nt - 1) * 4]), reads=[ROW], writes=[ROW])
            P.op("dve", lambda e: e.tensor_tensor(out=row[0:1, 4, 0:nt * 4], in0=row[0:1, 5, 0:nt * 4], in1=row[0:1, 3, 0:nt * 4],
                                                  op=ALU.subtract), reads=[ROW], writes=[ROW])
            P.op("act", lambda e: e.activation(out=row[0:1, 4, 0:nt * 4], in_=row[0:1, 4, 0:nt * 4], func=AF.Exp),
                 reads=[ROW], writes=[ROW])
            stage('pp_scan')
            bt5, BK5 = nb()
            P.op("pe", lambda e: e.matmul(bt5[:, 0:128], lhsT=ones_f[0:1, :], rhs=row[0:1, 3:5, :].rearrange("p a b -> p (a b)"),
                                          start=True, stop=True), reads=[ROW, CST], writes=[BK5])
            P.op("act", lambda e: e.copy(out=cw_b[:, :, 0:NT, :], in_=bt5[:, 0:128].rearrange("p (a t h) -> p a t h", a=2, h=4)),
                 reads=[BK5], writes=[CWB])
            P.op("pool", lambda e: e.memset(cw_b[:, :, NT, :], 1.0), reads=[CWB], writes=[CWB])
            P.op("dve", lambda e: e.tensor_tensor(out=tmp64[:, 0:nt, :], in0=u_sb[:, 0:nt, :], in1=cw_b[:, 0, 0:nt, :], op=ALU.subtract),
                 reads=[USB, CWB], writes=[TMP64])
            P.op("dve", lambda e: e.tensor_scalar(out=tmp64[:, 0:nt, :], in0=tmp64[:, 0:nt, :], scalar1=LNISQ, scalar2=None, op0=ALU.add),
                 reads=[TMP64], writes=[TMP64])
            P.op("act", lambda e: e.activation(out=pk[:, 0:nt, :], in_=tmp64[:, 0:nt, :], func=AF.Exp),
                 reads=[TMP64], writes=[PK])
            P.op("dve", lambda e: e.tensor_tensor(out=tmp64[:, 0:nt, :], in0=bneg[:, 0:nt, :], in1=cw_b[:, 0, 0:nt, :], op=ALU.subtract),
                 reads=[BNEG, CWB, PK], writes=[TMP64])
            P.op("act", lambda e: e.activation(out=thr[:, 0:nt, :], in_=tmp64[:, 0:nt, :], func=AF.Exp), reads=[TMP64], writes=[THR])
            P.dma("sp", c_fin[0], m_p[l:l + 1, :], row[0:1, 2, (nt - 1) * 4:nt * 4], reads=[ROW])

        def main_tile(l, t):
            par = t % 2
            last = (t == n_tiles - 1)
            if t == 0:
                load_main(l, 0)
            if not last:
                load_main(l, t + 1)
            make_xT(par)

            def proj(bt, BK, c0, n, wb):
                for k in range(8):
                    P.op("pe", lambda e, k=k: e.matmul(bt[:, 0:n], lhsT=xT[:, k, :], rhs=w_in_sb[:, k, c0:c0 + n],
                                                       start=(k == 0), stop=(k == 7)), reads=[XT, WIN[wb]], writes=[BK])

            bq, BQ = nb(); proj(bq, BQ, C_RQ, 512, 0)
            bk_, BKK = nb(); proj(bk_, BKK, C_RK, 512, 1)
            cos2 = rope_sb[par][:, 0, :]
            sin2 = rope_sb[par][:, 1, :]
            for i, (bt, BK) in enumerate(((bq, BQ), (bk_, BKK))):
                src = bt[:].rearrange("p (h d) -> p h d", h=4)
                dst_t = tmp_t[:, i * 4:(i + 1) * 4, :]
                dst_u = tmp_u[:, i * 4:(i + 1) * 4, :]
                P.op("dve", lambda e, src=src, dst_t=dst_t: e.tensor_tensor(
                    out=dst_t, in0=src, in1=cos2.unsqueeze(1).broadcast_to([128, 4, HD]), op=ALU.mult),
                    reads=[BK, ROPE[par]], writes=[TMPT])
                P.op("dve", lambda e, src=src, dst_u=dst_u: e.tensor_tensor(
                    out=dst_u[:, :, 0:64], in0=src[:, :, 64:128], in1=sin2[:, 0:64].unsqueeze(1).broadcast_to([128, 4, 64]), op=ALU.mult),
                    reads=[BK, ROPE[par]], writes=[TMPU])
                P.op("dve", lambda e, src=src, dst_u=dst_u: e.tensor_tensor(
                    out=dst_u[:, :, 64:128], in0=src[:, :, 0:64], in1=sin2[:, 64:128].unsqueeze(1).broadcast_to([128, 4, 64]), op=ALU.mult),
                    reads=[BK, ROPE[par]], writes=[TMPU])
            P.op("pool", lambda e: e.tensor_tensor(out=qk_rot[:], in0=tmp_t[:], in1=tmp_u[:], op=ALU.add),
                 reads=[TMPT, TMPU], writes=[QKR])
            stage('rot')
            bv, BV = nb(); proj(bv, BV, C_RV, 512, 2)
            bz, BZ = nb(); proj(bz, BZ, C_RZ, 512, 3)
            for h in range(4):
                P.op("act", lambda e, h=h: e.activation(out=v2[:, h, :], in_=bv[:, h * 128:(h + 1) * 128], func=AF.Copy, scale=kdec(h)),
                     reads=[BV, CST], writes=[V2])
            P.op("act", lambda e: e.activation(out=sz[:, 0:512], in_=bz[:], func=AF.Silu), reads=[BZ], writes=[SZ])
            stage('vz')
            btq, BTQ = nb(); vtq = bfv(btq)
            btk, BTK = nb(); vtk = bfv(btk)
            for g in range(4):
                P.op("pe", lambda e, g=g: e.transpose(vtq[:, g * 128:(g + 1) * 128], qk_rot[:, g, :], ident_bf[:]),
                     reads=[QKR, IDB], writes=[BTQ])
            for g in range(4):
                P.op("pe", lambda e, g=g: e.transpose(vtk[:, g * 128:(g + 1) * 128], qk_rot[:, 4 + g, :], ident_bf[:]),
                     reads=[QKR, IDB], writes=[BTK])
            P.op("dve", lambda e: e.tensor_tensor(out=qT2[:], in0=vtq[:, 0:512].rearrange("p (h l) -> p h l", h=4), in1=qdecT, op=ALU.mult),
                 reads=[BTQ, CST], writes=[QT2])
            P.op("act", lambda e: e.copy(out=kT[:].rearrange("p h l -> p (h l)"), in_=vtk[:, 0:512]), reads=[BTK], writes=[KT])
            stage('qkT')
            bs, BS = nb()
            for h in range(4):
                P.op("pe", lambda e, h=h: e.matmul(bs[:, h * 128:(h + 1) * 128], lhsT=kT[:, h, :], rhs=qT2[:, h, :], start=True, stop=True),
                     reads=[KT, QT2], writes=[BS])
            P.op("dve", lambda e: e.tensor_tensor(out=s2[:], in0=bs[:].rearrange("p (h l) -> p h l", h=4),
                                                  in1=mask_bf[:].unsqueeze(1).broadcast_to([128, 4, 128]), op=ALU.mult),
                 reads=[BS, MSK], writes=[S2])
            stage('scores')
            bo, BO = nb()
            for h in range(4):
                first = (t == 0)
                P.op("pe", lambda e, h=h, first=first: e.matmul(bo[:, h * 128:(h + 1) * 128], lhsT=s2[:, h, :], rhs=v2[:, h, :],
                                                                start=True, stop=first), reads=[S2, V2], writes=[BO])
                if not first:
                    P.op("pe", lambda e, h=h: e.matmul(bo[:, h * 128:(h + 1) * 128], lhsT=qT2[:, h, :], rhs=S_bf[:, h, :],
                                                       start=False, stop=True), reads=[QT2, SBF], writes=[BO])
            bu, BU = nb()
            for h in range(4):
                P.op("pe", lambda e, h=h: e.matmul(bu[:, h * 128:(h + 1) * 128], lhsT=qk_rot[:, 4 + h, :], rhs=v2[:, h, :],
                                                   start=True, stop=True), reads=[QKR, V2], writes=[BU])
            for h in range(4):
                if t == 0:
                    P.op("dve", lambda e, h=h: e.tensor_copy(out=S_f[:, h, :], in_=bu[:, h * 128:(h + 1) * 128]), reads=[BU], writes=[SF])
                else:
                    P.op("dve", lambda e, h=h: e.scalar_tensor_tensor(out=S_f[:, h, :], in0=S_f[:, h, :], scalar=GAML[h],
                                                                     in1=bu[:, h * 128:(h + 1) * 128], op0=ALU.mult, op1=ALU.add),
                         reads=[BU, SF], writes=[SF])
            if not last:
                for h in range(4):
                    P.op("act", lambda e, h=h: e.activation(out=S_bf[:, h, :], in_=S_f[:, h, :], func=AF.Copy, scale=GAML[h]),
                         reads=[SF], writes=[SBF])
            P.op("act", lambda e: e.copy(out=o_sb[:].rearrange("p h d -> p (h d)"), in_=bo[:]), reads=[BO], writes=[OSB])
            for h in range(4):
                P.op("dve", lambda e, h=h: e.bn_stats(out=stats[:, h, :], in_=o_sb[:, h, :]), reads=[OSB], writes=[STATS])

            stage('ret')
            bc0, BC0 = nb(); bc1, BC1 = nb()
            for ch in range(8):
                bt, BK = (bc0, BC0) if ch < 4 else (bc1, BC1)
                c0 = C_MQK + ch * 128
                wb = 4 + ch // 4
                for k in range(8):
                    P.op("pe", lambda e, k=k, bt=bt, ch=ch, c0=c0: e.matmul(bt[:, (ch % 4) * 128:(ch % 4 + 1) * 128],
                                                                        lhsT=w_in_sb[:, k, c0:c0 + 128], rhs=xT[:, k, :],
                                                                        start=(k == 0), stop=(k == 7)),
                         reads=[XT, WIN[wb]], writes=[BK])
            if t == 0:
                P.op("pool", lambda e: e.memset(hist[:, :, 0:3], 0.0), writes=[HIST])
            else:
                P.op("pool", lambda e: e.tensor_copy(out=hist[:, :, 0:3], in_=hist[:, :, 128:131]), reads=[HIST], writes=[HIST])
            for i, (bt, BK) in enumerate(((bc0, BC0), (bc1, BC1))):
                P.op("act", lambda e, i=i, bt=bt: e.copy(out=hist[:, i * 4:(i + 1) * 4, 3:131], in_=bt[:].rearrange("p (c t) -> p c t", c=4)),
                     reads=[BK], writes=[HIST])
                if last:
                    P.op("dve", lambda e, i=i, bt=bt: e.tensor_copy(out=convp_sb[:, i * 4:(i + 1) * 4, :],
                                                                    in_=bt[:].rearrange("p (c t) -> p c t", c=4)[:, :, 125:128]),
                         reads=[BK], writes=[CONVP])
            bd0, BD0 = nb(); bd1, BD1 = nb()
            for ch in range(8):
                bt, BK = (bd0, BD0) if ch < 4 else (bd1, BD1)
                for j in range(4):
                    P.op("pe", lambda e, j=j, bt=bt, ch=ch: e.matmul(bt[:, (ch % 4) * 128:(ch % 4 + 1) * 128], lhsT=diag[:, ch * 4 + j, :],
                                                                 rhs=hist[:, ch, j:j + 128], start=(j == 0), stop=(j == 3)),
                         reads=[DIAG, HIST], writes=[BK])
            for ch in range(8):
                bt, BK = (bd0, BD0) if ch < 4 else (bd1, BD1)
                P.op("act", lambda e, bt=bt, ch=ch: e.activation(out=qkm[:, ch, :], in_=bt[:, (ch % 4) * 128:(ch % 4 + 1) * 128],
                                                             func=AF.Silu, bias=cwT[:, 32 + ch:33 + ch]),
                     reads=[BK, CWT], writes=[QKM])
            stage('conv')
            bkt, BKT = nb(); vkt = bfv(bkt)
            for h in range(4):
                P.op("pe", lambda e, h=h: e.transpose(vkt[:, h * 128:(h + 1) * 128], qkm[:, 4 + h, :], ident_bf[:]),
                     reads=[QKM, IDB], writes=[BKT])
            for h in range(4):
                P.op("act", lambda e, h=h: e.activation(out=kp[:, h, :], in_=vkt[:, h * 128:(h + 1) * 128], func=AF.Copy,
                                                        scale=pk[:, t, h:h + 1]), reads=[BKT, PK], writes=[KP])
            bmv, BMV = nb(); proj(bmv, BMV, C_MV, 512, 6)
            bmo, BMO = nb(); proj(bmo, BMO, C_MO, 512, 7)
            bmz, BMZ = nb(); proj(bmz, BMZ, C_MZ, 512, 8)
            if l == 0 and t == 0:
                P.op("pool", lambda e: e.memset(v1[:, :, 128:130], 1.0), writes=[V1])
            P.op("act", lambda e: e.copy(out=v1[:, :, 0:128], in_=bmv[:].rearrange("p (h d) -> p h d", h=4)), reads=[BMV], writes=[V1])
            P.op("act", lambda e: e.activation(out=th[:], in_=bmo[:], func=AF.Tanh, scale=0.5), reads=[BMO], writes=[TH])
            P.op("act", lambda e: e.activation(out=sz[:, 512:1024], in_=bmz[:], func=AF.Silu), reads=[BMZ], writes=[SZ])
            bsm, BSM = nb()
            for h in range(4):
                P.op("pe", lambda e, h=h: e.matmul(bsm[:, h * 128:(h + 1) * 128], lhsT=qkm[:, 4 + h, :], rhs=qkm[:, h, :], start=True, stop=True),
                     reads=[QKM], writes=[BSM])
            for h in range(4):
                P.op("dve", lambda e, h=h: e.scalar_tensor_tensor(out=s2m[:, h, :], in0=bsm[:, h * 128:(h + 1) * 128], scalar=pk[:, t, h:h + 1],
                                                                 in1=mask_bf[:], op0=ALU.mult, op1=ALU.mult),
                     reads=[BSM, PK, MSK], writes=[S2M])
            stage('mscores')
            bn0, BN0 = nb(); bn1, BN1 = nb()
            for h in range(4):
                bt, BK = (bn0, BN0) if h < 2 else (bn1, BN1)
                o_ = (h % 2) * 130
                first = (t == 0)
                P.op("pe", lambda e, h=h, bt=bt, o_=o_, first=first: e.matmul(bt[:, o_:o_ + 130], lhsT=s2m[:, h, :], rhs=v1[:, h, :],
                                                                            start=True, stop=first), reads=[S2M, V1], writes=[BK])
                if not first:
                    P.op("pe", lambda e, h=h, bt=bt, o_=o_: e.matmul(bt[:, o_:o_ + 130], lhsT=qkm[:, h, :], rhs=C_bf[:, h, :],
                                                                   start=False, stop=True), reads=[QKM, CBF], writes=[BK])
            bu0, BU0 = nb(); bu1, BU1 = nb()
            for h in range(4):
                bt, BK = (bu0, BU0) if h < 2 else (bu1, BU1)
                o_ = (h % 2) * 130
                P.op("pe", lambda e, h=h, bt=bt, o_=o_: e.matmul(bt[:, o_:o_ + 130], lhsT=kp[:, h, :], rhs=v1[:, h, :], start=True, stop=True),
                     reads=[KP, V1], writes=[BK])
            for h in range(4):
                bt, BK = (bu0, BU0) if h < 2 else (bu1, BU1)
                o_ = (h % 2) * 130
                if t == 0:
                    P.op("dve", lambda e, h=h, bt=bt, o_=o_: e.tensor_copy(out=C_f[:, h, :], in_=bt[:, o_:o_ + 130]), reads=[BK], writes=[CF])
                else:
                    P.op("dve", lambda e, h=h, bt=bt, o_=o_: e.scalar_tensor_tensor(out=C_f[:, h, :], in0=C_f[:, h, :], scalar=cw_b[:, 1, t, h:h + 1],
                                                                                  in1=bt[:, o_:o_ + 130], op0=ALU.mult, op1=ALU.add),
                         reads=[BK, CF, CWB], writes=[CF])
            if not last:
                for h in range(4):
                    P.op("act", lambda e, h=h: e.activation(out=C_bf[:, h, :], in_=C_f[:, h, :], func=AF.Copy, scale=cw_b[:, 1, t + 1, h:h + 1]),
                         reads=[CF, CWB], writes=[CBF])
            for i, (bt, BK) in enumerate(((bn0, BN0), (bn1, BN1))):
                P.op("act", lambda e, i=i, bt=bt: e.activation(out=dn[:, 2 * i:2 * i + 2], in_=bt[:, 128:259:130], func=AF.Abs),
                     reads=[BK], writes=[DN])
            P.op("dve", lambda e: e.tensor_tensor(out=dn[:], in0=dn[:], in1=thr[:, t, :], op=ALU.max), reads=[DN, THR], writes=[DN])
            P.op("dve", lambda e: e.reciprocal(out=dn[:], in_=dn[:]), reads=[DN], writes=[DN])
            for h in range(4):
                bt, BK = (bn0, BN0) if h < 2 else (bn1, BN1)
                o_ = (h % 2) * 130
                P.op("act", lambda e, h=h, bt=bt, o_=o_: e.activation(out=hm[:, h, :], in_=bt[:, o_:o_ + 128], func=AF.Copy, scale=dn[:, h:h + 1]),
                     reads=[BK, DN], writes=[HM])
            for h in range(4):
                P.op("dve", lambda e, h=h: e.bn_stats(out=stats[:, 4 + h, :], in_=hm[:, h, :]), reads=[HM], writes=[STATS])
            stage('mlstm')
            for g in range(8):
                P.op("dve", lambda e, g=g: e.bn_aggr(out=mv[:, g, :], in_=stats[:, g, :]), reads=[STATS], writes=[MV])
            P.op("pool", lambda e: e.tensor_scalar(out=rstd[:], in0=mv[:, :, 1], scalar1=GN_EPS, scalar2=None, op0=ALU.add),
                 reads=[MV], writes=[RSTD])
            P.op("pool", lambda e: e.tensor_tensor(out=rstd[:], in0=rstd[:], in1=mhalf[:], op=ALU.pow), reads=[RSTD, MHALF], writes=[RSTD])
            P.op("dve", lambda e: e.scalar_tensor_tensor(out=nbias[:], in0=mv[:, :, 0], scalar=-1.0, in1=rstd[:], op0=ALU.mult, op1=ALU.mult),
                 reads=[MV, RSTD], writes=[NBIAS])
            for g in range(8):
                if g < 4:
                    src = o_sb[:, g, :]; SB_ = OSB
                else:
                    src = hm[:, g - 4, :]; SB_ = HM
                P.op("act", lambda e, g=g, src=src: e.activation(out=on[:, g, :], in_=src, func=AF.Identity, scale=rstd[:, g:g + 1],
                                                               bias=nbias[:, g:g + 1]), reads=[SB_, RSTD, NBIAS], writes=[ON])
            P.op("dve", lambda e: e.scalar_tensor_tensor(out=sz[:, 512:1024], in0=th[:], scalar=1.0, in1=sz[:, 512:1024], op0=ALU.add, op1=ALU.mult),
                 reads=[TH, SZ], writes=[SZ])
            P.op("pool", lambda e: e.tensor_tensor(out=sz[:], in0=sz[:], in1=gbc[:], op=ALU.mult), reads=[SZ, GBC], writes=[SZ])
            P.op("pool", lambda e: e.tensor_tensor(out=gated[:], in0=on[:].rearrange("p g d -> p (g d)"), in1=sz[:], op=ALU.mult),
                 reads=[ON, SZ], writes=[GATED])
            stage('gate')
            bg, BG = nb(); vg = bfv(bg)
            for k in range(8):
                P.op("pe", lambda e, k=k: e.transpose(vg[:, k * 128:(k + 1) * 128], gated[:, k * 128:(k + 1) * 128], ident_bf[:]),
                     reads=[GATED, IDB], writes=[BG])
            P.op("act", lambda e: e.copy(out=gT[:].rearrange("p k c -> p (k c)"), in_=vg), reads=[BG], writes=[GT])
            for n in range(2):
                bt, BK = nb()
                for k in range(8):
                    P.op("pe", lambda e, k=k, n=n, bt=bt: e.matmul(bt[:], lhsT=gT[:, k, :], rhs=w_out_sb[:, k, n * 512:(n + 1) * 512],
                                                               start=(k == 0), stop=(k == 7)), reads=[GT, WOUT[n]], writes=[BK])
                P.op("dve", lambda e, n=n, bt=bt: e.scalar_tensor_tensor(out=ypre[:, n * 512:(n + 1) * 512], in0=x_sb[par][:, n * 512:(n + 1) * 512],
                                                                     scalar=ALPHA, in1=bt[:], op0=ALU.mult, op1=ALU.add),
                     reads=[BK, XSB[par]], writes=[YPRE])
                P.op("dve", lambda e, n=n: e.bn_stats(out=lstats[:, n, :], in_=ypre[:, n * 512:(n + 1) * 512]), reads=[YPRE], writes=[LSTATS])
            P.op("dve", lambda e: e.bn_aggr(out=lmv[:], in_=lstats[:].rearrange("p a b -> p (a b)")), reads=[LSTATS], writes=[LMV])
            P.op("pool", lambda e: e.tensor_scalar(out=lrs[:, 0:1], in0=lmv[:, 1:2], scalar1=LN_EPS, scalar2=None, op0=ALU.add),
                 reads=[LMV], writes=[LRS])
            P.op("pool", lambda e: e.tensor_tensor(out=lrs[:, 0:1], in0=lrs[:, 0:1], in1=mhalf[:, 0:1], op=ALU.pow), reads=[LRS, MHALF], writes=[LRS])
            P.op("dve", lambda e: e.scalar_tensor_tensor(out=lrs[:, 1:2], in0=lmv[:, 0:1], scalar=-1.0, in1=lrs[:, 0:1], op0=ALU.mult, op1=ALU.mult),
                 reads=[LMV, LRS], writes=[LRS])
            ys = y_sb[par]
            P.op("act", lambda e: e.activation(out=ys[:], in_=ypre[:], func=AF.Identity, scale=lrs[:, 0:1], bias=lrs[:, 1:2]),
                 reads=[YPRE, LRS], writes=[YSB[par]])
            P.op("pool", lambda e: e.tensor_tensor(out=ys[:], in0=ys[:], in1=lng[:], op=ALU.mult), reads=[YSB[par], LNG], writes=[YSB[par]])
            P.op("pool", lambda e: e.tensor_tensor(out=ys[:], in0=ys[:], in1=lnb[:], op=ALU.add), reads=[YSB[par], LNB], writes=[YSB[par]])
            if l == n_layers - 1:
                P.dma("sp", c_y[par], y_p[t * 128:(t + 1) * 128, :], ys[:], reads=[YSB[par]])
            else:
                P.dma("sp", c_y[par], y0[t * 128:(t + 1) * 128, :], ys[:], reads=[YSB[par]], writes=[Y0B[t]])


        xs_sb = sb("xs_sb", [NS, D]); XSS = Buf("xs")
        ropes = sb("ropes", [NS, 2, HD]); ROPES = Buf()
        sg = sb("sg", [NS, 16, 4]); SG = Buf()
        wexp = sb("wexp", [NS, NS, 4]); WEXP = Buf()
        wcbs = sb("wcbs", [128, NS * 4]); WCBS = Buf()
        qTs = sb("qTs", [128, 8, NS]); QTS = Buf()
        oTs = sb("oTs", [128, 2, 64]); OTS = Buf()
        sstats = sb("sstats", [NS, 8, 6]); SSTATS = Buf()
        smv = sb("smv", [NS, 8, 2]); SMV = Buf()
        srs = sb("srs", [NS, 8]); SRS = Buf()
        snb = sb("snb", [NS, 8]); SNB = Buf()
        slst = sb("slst", [NS, 2, 6]); SLST = Buf()
        slmv = sb("slmv", [NS, 2]); SLMV = Buf()
        slrs = sb("slrs", [NS, 2]); SLRS = Buf()
        c_s = [P.chan("c_s%d" % i) for i in range(8)]
        c_sg = [P.chan("c_sg0"), P.chan("c_sg1")]
        c_cg = [P.chan("c_cg0"), P.chan("c_cg1")]
        c_so = [ochan("c_so%d" % i) for i in range(8)]
        c_sgo = [ochan("c_sgo0"), ochan("c_sgo1")]
        c_cgo = [ochan("c_cgo0"), ochan("c_cgo1")]
        id16 = ident_f[0:NS, 0:NS]
        gamb = cst[0:NS, K_GAM:K_GAM + 4]
        sbank_ctr = [0]

        def snb_():
            i = sbank_ctr[0] % 6
            sbank_ctr[0] += 1
            return banks[i]

        def sample_path(l):
            S16 = slice(0, NS)
            if l == 0:
                P.dma("sp", c_s[0], xs_sb[:], xs, writes=[XSS])
                P.dma("sp", c_s[7], ropes[:].rearrange("p a d -> p (a d)"), rope_s.partition_broadcast(NS), writes=[ROPES])
            sc = [x_sb[0][S16, :], x_sb[1][S16, :], y_sb[0][S16, :]]
            SCB = [XSB[0], XSB[1], YSB[0]]
            for j in range(3):
                P.dma("sp", c_s[1 + j], sc[j], sconv[l][:, j, :], writes=[SCB[j]])
            P.dma("sp", c_s[4], sg[:, 0, :], sm[l], writes=[SG])
            n0v = o_sb[S16, :, :]
            P.dma("sp", c_s[5], n0v, sn[l], writes=[OSB])
            P.op("pool", lambda e: e.tensor_copy(out=x_bf[S16, :], in_=xs_sb[:]), reads=[XSS], writes=[XBF])
            bt, BK = nb(); v = bfv(bt)
            for k in range(8):
                P.op("pe", lambda e, k=k: e.transpose(v[:, k * NS:(k + 1) * NS], x_bf[S16, k * 128:(k + 1) * 128], ident_bf[S16, 0:NS]),
                     reads=[XBF, IDB], writes=[BK])
            P.op("act", lambda e: e.copy(out=xT[:, :, 0:NS], in_=v[:, 0:8 * NS].rearrange("p (k s) -> p k s", k=8)), reads=[BK], writes=[XT])

            def sproj(c0, n, wb):
                bt, BK = nb()
                for k in range(8):
                    P.op("pe", lambda e, k=k: e.matmul(bt[S16, 0:n], lhsT=xT[:, k, 0:NS], rhs=w_in_sb[:, k, c0:c0 + n],
                                                       start=(k == 0), stop=(k == 7)), reads=[XT, WIN[wb]], writes=[BK])
                return bt, BK

            cos2 = ropes[:, 0, :]
            sin2 = ropes[:, 1, :]
            for i, c0 in enumerate((C_RQ, C_RK)):
                bt, BK = sproj(c0, 512, i)
                src = bt[S16, :].rearrange("p (h d) -> p h d", h=4)
                dst_t = tmp_t[S16, i * 4:(i + 1) * 4, :]
                dst_u = tmp_u[S16, i * 4:(i + 1) * 4, :]
                P.op("dve", lambda e, src=src, dst_t=dst_t: e.tensor_tensor(out=dst_t, in0=src, in1=cos2.unsqueeze(1).broadcast_to([NS, 4, HD]), op=ALU.mult),
                     reads=[BK, ROPES], writes=[TMPT])
                P.op("dve", lambda e, src=src, dst_u=dst_u: e.tensor_tensor(out=dst_u[:, :, 0:64], in0=src[:, :, 64:128],
                                                                          in1=sin2[:, 0:64].unsqueeze(1).broadcast_to([NS, 4, 64]), op=ALU.mult),
                     reads=[BK, ROPES], writes=[TMPU])
                P.op("dve", lambda e, src=src, dst_u=dst_u: e.tensor_tensor(out=dst_u[:, :, 64:128], in0=src[:, :, 0:64],
                                                                          in1=sin2[:, 64:128].unsqueeze(1).broadcast_to([NS, 4, 64]), op=ALU.mult),
                     reads=[BK, ROPES], writes=[TMPU])
            qk_s = on[S16, :, :]
            P.op("pool", lambda e: e.tensor_tensor(out=qk_s, in0=tmp_t[S16, :, :], in1=tmp_u[S16, :, :], op=ALU.add), reads=[TMPT, TMPU], writes=[ON])
            v2s = hm[S16, :, :]
            bt, BK = sproj(C_RV, 512, 2)
            P.op("act", lambda e, bt=bt: e.mul(out=v2s.rearrange("p h d -> p (h d)"), in_=bt[S16, :], mul=ISQ), reads=[BK], writes=[HM])
            bt, BK = sproj(C_RZ, 512, 3)
            P.op("act", lambda e, bt=bt: e.activation(out=sz[S16, 0:512], in_=bt[S16, :], func=AF.Silu), reads=[BK], writes=[SZ])
            mqk_s = ypre[S16, :]
            for i in range(2):
                bt, BK = sproj(C_MQK + i * 512, 512, 4 + i)
                P.op("act", lambda e, bt=bt, i=i: e.copy(out=mqk_s[:, i * 512:(i + 1) * 512], in_=bt[S16, :]), reads=[BK], writes=[YPRE])
            P.dma("sp", c_so[0], conv_s[l][:, 0, :], sc[1], reads=[SCB[1]])
            P.dma("sp", c_so[1], conv_s[l][:, 1, :], sc[2], reads=[SCB[2]])
            P.dma("sp", c_so[2], conv_s[l][:, 2, :], mqk_s, reads=[YPRE])
            Wb = y_sb[1][S16, :]; WB = YSB[1]
            acc = tmp_t[S16, :, :].rearrange("p g d -> p (g d)")
            tmpc = tmp_u[S16, :, :].rearrange("p g d -> p (g d)")
            full = [sc[0], sc[1], sc[2], mqk_s]
            FB = [SCB[0], SCB[1], SCB[2], YPRE]
            for n_, j in enumerate((3, 0, 1, 2)):
                P.dma("sp", c_s[6], Wb, conv_w[l][j].partition_broadcast(NS), writes=[WB])
                if n_ == 0:
                    P.op("dve", lambda e, j=j: e.tensor_tensor(out=acc, in0=full[j], in1=Wb, op=ALU.mult), reads=[FB[j], WB, ON], writes=[TMPT])
                else:
                    P.op("dve", lambda e, j=j: e.tensor_tensor(out=tmpc, in0=full[j], in1=Wb, op=ALU.mult), reads=[FB[j], WB, ON], writes=[TMPU])
                    P.op("pool", lambda e: e.tensor_tensor(out=acc, in0=acc, in1=tmpc, op=ALU.add), reads=[TMPU, TMPT], writes=[TMPT])
            P.dma("sp", c_s[6], Wb, conv_b[l].partition_broadcast(NS), writes=[WB])
            P.op("pool", lambda e: e.tensor_tensor(out=acc, in0=acc, in1=Wb, op=ALU.add), reads=[WB, TMPT], writes=[TMPT])
            P.op("act", lambda e: e.activation(out=acc, in_=acc, func=AF.Silu), reads=[TMPT], writes=[TMPT])
            qkm_s = tmp_t[S16, :, :]
            v_s = gated[S16, :].bitcast(F32).rearrange("p (h d) -> p h d", h=4)
            bt, BK = sproj(C_MV, 512, 6)
            P.op("act", lambda e, bt=bt: e.copy(out=v_s.rearrange("p h d -> p (h d)"), in_=bt[S16, :]), reads=[BK], writes=[GATED])
            bt, BK = sproj(C_MO, 512, 7)
            P.op("act", lambda e, bt=bt: e.activation(out=th[S16, :], in_=bt[S16, :], func=AF.Tanh, scale=0.5), reads=[BK], writes=[TH])
            bt, BK = sproj(C_MZ, 512, 8)
            P.op("act", lambda e, bt=bt: e.activation(out=sz[S16, 512:1024], in_=bt[S16, :], func=AF.Silu), reads=[BK], writes=[SZ])
            bt, BK = sproj(C_G, 8, 9)
            G_ = lambda i: sg[:, i, :]
            P.op("dve", lambda e, bt=bt: e.tensor_tensor(out=sg[:, 1:3, :].rearrange("p a h -> p (a h)"), in0=bt[S16, 0:8], in1=bias8[S16, :], op=ALU.add),
                 reads=[BK, BIAS8], writes=[SG])
            P.op("act", lambda e: e.activation(out=G_(3), in_=G_(2), func=AF.Exp, scale=-1.0), reads=[SG], writes=[SG])
            P.op("act", lambda e: e.activation(out=G_(3), in_=G_(3), func=AF.Ln, bias=1.0), reads=[SG], writes=[SG])
            P.op("dve", lambda e: e.tensor_tensor(out=G_(4), in0=G_(1), in1=G_(3), op=ALU.add), reads=[SG], writes=[SG])
            P.op("dve", lambda e: e.tensor_tensor(out=G_(5), in0=G_(4), in1=G_(0), op=ALU.max), reads=[SG], writes=[SG])
            P.op("dve", lambda e: e.tensor_tensor(out=G_(6), in0=G_(5), in1=G_(3), op=ALU.subtract), reads=[SG], writes=[SG])
            P.dma("sp", c_so[3], m_s[l], G_(6), reads=[SG])
            P.op("dve", lambda e: e.tensor_tensor(out=G_(14), in0=G_(4), in1=G_(5), op=ALU.subtract), reads=[SG], writes=[SG])
            P.op("dve", lambda e: e.tensor_scalar(out=G_(14), in0=G_(14), scalar1=LNISQ, scalar2=None, op0=ALU.add), reads=[SG], writes=[SG])
            P.op("act", lambda e: e.activation(out=G_(7), in_=G_(14), func=AF.Exp), reads=[SG], writes=[SG])
            P.op("dve", lambda e: e.tensor_tensor(out=G_(14), in0=G_(3), in1=G_(5), op=ALU.subtract), reads=[SG], writes=[SG])
            P.op("act", lambda e: e.activation(out=G_(8), in_=G_(14), func=AF.Exp), reads=[SG], writes=[SG])
            P.op("dve", lambda e: e.tensor_tensor(out=G_(14), in0=G_(0), in1=G_(5), op=ALU.subtract), reads=[SG], writes=[SG])
            P.op("act", lambda e: e.activation(out=G_(9), in_=G_(14), func=AF.Exp), reads=[SG], writes=[SG])
            bt, BK = nb()
            for g in range(4):
                P.op("pe", lambda e, g=g, bt=bt: e.transpose(bt[:, g * NS:(g + 1) * NS], qk_s[:, g, :], id16), reads=[ON, CST], writes=[BK])
            for g in range(4):
                P.op("pe", lambda e, g=g, bt=bt: e.transpose(bt[:, (4 + g) * NS:(5 + g) * NS], qkm_s[:, g, :], id16), reads=[TMPT, CST], writes=[BK])
            P.op("act", lambda e, bt=bt: e.copy(out=qTs[:].rearrange("p g s -> p (g s)"), in_=bt[:, 0:8 * NS]), reads=[BK], writes=[QTS])
            bc4 = lambda i: sg[:, i, :].unsqueeze(2).broadcast_to([NS, 4, HD])
            P.op("dve", lambda e: e.tensor_tensor(out=qkm_s[:, 4:8, :], in0=qkm_s[:, 4:8, :], in1=bc4(7), op=ALU.mult), reads=[TMPT, SG], writes=[TMPT])
            prod = tmp_u[S16, :, :]
            P.op("dve", lambda e: e.tensor_tensor(out=prod[:, 0:4, :], in0=qk_s[:, 0:4, :], in1=qk_s[:, 4:8, :], op=ALU.mult), reads=[ON], writes=[TMPU])
            P.op("dve", lambda e: e.tensor_reduce(out=G_(10), in_=prod[:, 0:4, :], axis=AX.X, op=ALU.add), reads=[TMPU], writes=[SG])
            P.op("dve", lambda e: e.tensor_tensor(out=prod[:, 4:8, :], in0=qkm_s[:, 0:4, :], in1=qkm_s[:, 4:8, :], op=ALU.mult), reads=[TMPT], writes=[TMPU])
            P.op("dve", lambda e: e.tensor_reduce(out=G_(11), in_=prod[:, 4:8, :], axis=AX.X, op=ALU.add), reads=[TMPU], writes=[SG])
            P.op("dve", lambda e: e.tensor_tensor(out=prod[:, 0:4, :], in0=qkm_s[:, 0:4, :], in1=n0v, op=ALU.mult), reads=[TMPT, OSB, SG], writes=[TMPU])
            P.op("dve", lambda e: e.tensor_reduce(out=G_(13), in_=prod[:, 0:4, :], axis=AX.X, op=ALU.add), reads=[TMPU], writes=[SG])
            P.op("dve", lambda e: e.tensor_tensor(out=n0v, in0=n0v, in1=bc4(9), op=ALU.mult), reads=[OSB, SG], writes=[OSB])
            P.op("pool", lambda e: e.tensor_tensor(out=n0v, in0=n0v, in1=qkm_s[:, 4:8, :], op=ALU.add), reads=[OSB, TMPT], writes=[OSB])
            P.dma("sp", c_so[4], n_s[l], n0v, reads=[OSB])
            P.op("dve", lambda e: e.tensor_tensor(out=wexp[:], in0=sg[:, 9, :].unsqueeze(1).broadcast_to([NS, NS, 4]),
                                                  in1=id16.unsqueeze(2).broadcast_to([NS, NS, 4]), op=ALU.mult), reads=[SG, CST], writes=[WEXP])
            bt, BK = nb()
            P.op("pe", lambda e, bt=bt: e.matmul(bt[:, 0:NS * 4], lhsT=ones_f[0:NS, :], rhs=wexp[:].rearrange("p s h -> p (s h)"), start=True, stop=True),
                 reads=[WEXP, CST], writes=[BK])
            P.op("act", lambda e, bt=bt: e.copy(out=wcbs[:], in_=bt[:, 0:NS * 4]), reads=[BK], writes=[WCBS])
            GS = [x_sb[0][:].rearrange("p (s h e) -> p s h e", s=2, h=4), x_sb[1][:].rearrange("p (s h e) -> p s h e", s=2, h=4)]
            GC = [y_sb[0][:].rearrange("p (s h e) -> p s h e", s=2, h=4), y_sb[1][:].rearrange("p (s h e) -> p s h e", s=2, h=4)]
            vexp = ypre[S16, :].rearrange("p (s h e) -> p s h e", s=2, h=4)
            bor, BOR = banks[6]
            bom, BOM = banks[7]
            for g in range(NS // 2):
                par = g % 2
                s0 = g * 2
                P.dma("sp", c_sg[par], GS[par], sret[l][s0:s0 + 2].rearrange("s h d e -> d s h e"), writes=[XSB[par]])
                P.dma("sp", c_cg[par], GC[par], sC[l][s0:s0 + 2].rearrange("s h d e -> d s h e"), writes=[YSB[par]])
                for typ in range(2):
                    Gt = GS[par] if typ == 0 else GC[par]
                    GB = XSB[par] if typ == 0 else YSB[par]
                    bo_, BO_ = (bor, BOR) if typ == 0 else (bom, BOM)
                    for sl in range(2):
                        for h in range(4):
                            col = h * NS + s0 + sl
                            P.op("pe", lambda e, Gt=Gt, bo_=bo_, sl=sl, h=h, col=col, typ=typ, s0=s0: e.matmul(
                                bo_[:, typ * 0 + col:col + 1], lhsT=Gt[:, sl, h, :], rhs=qTs[:, typ * 4 + h, s0 + sl:s0 + sl + 1], start=True, stop=True),
                                reads=[GB, QTS], writes=[BO_])
                    vsrc = v2s if typ == 0 else v_s
                    VB = HM if typ == 0 else GATED
                    P.op("dve", lambda e, vsrc=vsrc, s0=s0: e.tensor_tensor(
                        out=vexp, in0=vsrc.unsqueeze(1).broadcast_to([NS, 2, 4, HD]),
                        in1=id16[:, s0:s0 + 2].unsqueeze(2).unsqueeze(3).broadcast_to([NS, 2, 4, HD]), op=ALU.mult),
                        reads=[VB, CST], writes=[YPRE])
                    ksrc = qk_s[:, 4:8, :] if typ == 0 else qkm_s[:, 4:8, :]
                    KB = ON if typ == 0 else TMPT
                    for sl in range(2):
                        bu_, BU_ = snb_()
                        for h in range(4):
                            P.op("pe", lambda e, bu_=bu_, h=h, sl=sl, ksrc=ksrc: e.matmul(bu_[:, h * 128:(h + 1) * 128], lhsT=ksrc[:, h, :], rhs=vexp[:, sl, h, :],
                                                                                    start=True, stop=True), reads=[KB, YPRE], writes=[BU_])
                        for h in range(4):
                            if typ == 0:
                                P.op("dve", lambda e, bu_=bu_, h=h, sl=sl, Gt=Gt: e.scalar_tensor_tensor(
                                    out=Gt[:, sl, h, :], in0=Gt[:, sl, h, :], scalar=GAM[h], in1=bu_[:, h * 128:(h + 1) * 128], op0=ALU.mult, op1=ALU.add),
                                    reads=[BU_, GB], writes=[GB])
                            else:
                                ci = (s0 + sl) * 4 + h
                                P.op("dve", lambda e, bu_=bu_, h=h, sl=sl, Gt=Gt, ci=ci: e.scalar_tensor_tensor(
                                    out=Gt[:, sl, h, :], in0=Gt[:, sl, h, :], scalar=wcbs[:, ci:ci + 1], in1=bu_[:, h * 128:(h + 1) * 128], op0=ALU.mult, op1=ALU.add),
                                    reads=[BU_, GB, WCBS], writes=[GB])
                P.dma("sp", c_sgo[par], ret_s[l][s0:s0 + 2].rearrange("s h d e -> d s h e"), GS[par], reads=[XSB[par]])
                P.dma("sp", c_cgo[par], C_s[l][s0:s0 + 2].rearrange("s h d e -> d s h e"), GC[par], reads=[YSB[par]])
            P.op("act", lambda e: e.copy(out=oTs[:, 0, :], in_=bor[:, 0:64]), reads=[BOR], writes=[OTS])
            P.op("act", lambda e: e.copy(out=oTs[:, 1, :], in_=bom[:, 0:64]), reads=[BOM], writes=[OTS])
            btr_, BTR_ = nb()
            btm_, BTM_ = nb()
            for typ, (bt, BK) in enumerate(((btr_, BTR_), (btm_, BTM_))):
                for h in range(4):
                    P.op("pe", lambda e, bt=bt, typ=typ, h=h: e.transpose(bt[S16, h * 128:(h + 1) * 128], oTs[:, typ, h * NS:(h + 1) * NS], ident_f),
                         reads=[OTS, CST], writes=[BK])
            hs = tmp_u[S16, :, :]
            t2 = ypre[S16, :].rearrange("p (g d) -> p g d", g=8)
            inter_r = btr_[S16, :].rearrange("p (h d) -> p h d", h=4)
            inter_m = btm_[S16, :].rearrange("p (h d) -> p h d", h=4)
            P.op("dve", lambda e: e.tensor_tensor(out=hs[:, 0:4, :], in0=inter_r, in1=gamb.unsqueeze(2).broadcast_to([NS, 4, HD]), op=ALU.mult),
                 reads=[BTR_, CST, SG], writes=[TMPU])
            P.op("dve", lambda e: e.tensor_tensor(out=t2[:, 0:4, :], in0=v2s, in1=bc4(10), op=ALU.mult), reads=[HM, SG], writes=[YPRE])
            P.op("pool", lambda e: e.tensor_tensor(out=hs[:, 0:4, :], in0=hs[:, 0:4, :], in1=t2[:, 0:4, :], op=ALU.add), reads=[TMPU, YPRE], writes=[TMPU])
            P.op("dve", lambda e: e.tensor_tensor(out=hs[:, 4:8, :], in0=inter_m, in1=bc4(9), op=ALU.mult), reads=[BTM_, SG], writes=[TMPU])
            P.op("dve", lambda e: e.tensor_tensor(out=t2[:, 4:8, :], in0=v_s, in1=bc4(11), op=ALU.mult), reads=[GATED, SG], writes=[YPRE])
            P.op("pool", lambda e: e.tensor_tensor(out=hs[:, 4:8, :], in0=hs[:, 4:8, :], in1=t2[:, 4:8, :], op=ALU.add), reads=[TMPU, YPRE], writes=[TMPU])
            P.op("dve", lambda e: e.tensor_tensor(out=G_(12), in0=G_(13), in1=G_(9), op=ALU.mult), reads=[SG], writes=[SG])
            P.op("dve", lambda e: e.tensor_tensor(out=G_(12), in0=G_(12), in1=G_(11), op=ALU.add), reads=[SG], writes=[SG])
            P.op("act", lambda e: e.activation(out=G_(12), in_=G_(12), func=AF.Abs), reads=[SG], writes=[SG])
            P.op("dve", lambda e: e.tensor_tensor(out=G_(12), in0=G_(12), in1=G_(8), op=ALU.max), reads=[SG], writes=[SG])
            P.op("dve", lambda e: e.reciprocal(out=G_(12), in_=G_(12)), reads=[SG], writes=[SG])
            P.op("dve", lambda e: e.tensor_tensor(out=hs[:, 4:8, :], in0=hs[:, 4:8, :], in1=bc4(12), op=ALU.mult), reads=[TMPU, SG], writes=[TMPU])
            for g in range(8):
                P.op("dve", lambda e, g=g: e.bn_stats(out=sstats[:, g, :], in_=hs[:, g, :]), reads=[TMPU], writes=[SSTATS])
            for g in range(8):
                P.op("dve", lambda e, g=g: e.bn_aggr(out=smv[:, g, :], in_=sstats[:, g, :]), reads=[SSTATS], writes=[SMV])
            P.op("pool", lambda e: e.tensor_scalar(out=srs[:], in0=smv[:, :, 1], scalar1=GN_EPS, scalar2=None, op0=ALU.add), reads=[SMV], writes=[SRS])
            P.op("pool", lambda e: e.tensor_tensor(out=srs[:], in0=srs[:], in1=mhalf[S16, :], op=ALU.pow), reads=[SRS, MHALF], writes=[SRS])
            P.op("dve", lambda e: e.scalar_tensor_tensor(out=snb[:], in0=smv[:, :, 0], scalar=-1.0, in1=srs[:], op0=ALU.mult, op1=ALU.mult),
                 reads=[SMV, SRS], writes=[SNB])
            on2 = on[S16, :, :]
            for g in range(8):
                P.op("act", lambda e, g=g: e.activation(out=on2[:, g, :], in_=hs[:, g, :], func=AF.Identity, scale=srs[:, g:g + 1], bias=snb[:, g:g + 1]),
                     reads=[TMPU, SRS, SNB], writes=[ON])
            P.op("dve", lambda e: e.scalar_tensor_tensor(out=sz[S16, 512:1024], in0=th[S16, :], scalar=1.0, in1=sz[S16, 512:1024], op0=ALU.add, op1=ALU.mult),
                 reads=[TH, SZ], writes=[SZ])
            P.op("pool", lambda e: e.tensor_tensor(out=sz[S16, :], in0=sz[S16, :], in1=gbc[S16, :], op=ALU.mult), reads=[SZ, GBC], writes=[SZ])
            P.op("pool", lambda e: e.tensor_tensor(out=gated[S16, :], in0=on2.rearrange("p g d -> p (g d)"), in1=sz[S16, :], op=ALU.mult),
                 reads=[ON, SZ], writes=[GATED])
            bt, BK = nb(); vg = bfv(bt)
            for k in range(8):
                P.op("pe", lambda e, k=k: e.transpose(vg[:, k * NS:(k + 1) * NS], gated[S16, k * 128:(k + 1) * 128], ident_bf[S16, 0:NS]),
                     reads=[GATED, IDB], writes=[BK])
            P.op("act", lambda e: e.copy(out=gT[:, :, 0:NS], in_=vg[:, 0:8 * NS].rearrange("p (k s) -> p k s", k=8)), reads=[BK], writes=[GT])
            yps = ypre[S16, :]
            for n in range(2):
                bt, BK = nb()
                for k in range(8):
                    P.op("pe", lambda e, k=k, n=n, bt=bt: e.matmul(bt[S16, :], lhsT=gT[:, k, 0:NS], rhs=w_out_sb[:, k, n * 512:(n + 1) * 512],
                                                               start=(k == 0), stop=(k == 7)), reads=[GT, WOUT[n]], writes=[BK])
                P.op("dve", lambda e, n=n, bt=bt: e.scalar_tensor_tensor(out=yps[:, n * 512:(n + 1) * 512], in0=xs_sb[:, n * 512:(n + 1) * 512], scalar=ALPHA,
                                                                     in1=bt[S16, :], op0=ALU.mult, op1=ALU.add), reads=[BK, XSS], writes=[YPRE])
                P.op("dve", lambda e, n=n: e.bn_stats(out=slst[:, n, :], in_=yps[:, n * 512:(n + 1) * 512]), reads=[YPRE], writes=[SLST])
            P.op("dve", lambda e: e.bn_aggr(out=slmv[:], in_=slst[:].rearrange("p a b -> p (a b)")), reads=[SLST], writes=[SLMV])
            P.op("pool", lambda e: e.tensor_scalar(out=slrs[:, 0:1], in0=slmv[:, 1:2], scalar1=LN_EPS, scalar2=None, op0=ALU.add), reads=[SLMV], writes=[SLRS])
            P.op("pool", lambda e: e.tensor_tensor(out=slrs[:, 0:1], in0=slrs[:, 0:1], in1=mhalf[S16, 0:1], op=ALU.pow), reads=[SLRS, MHALF], writes=[SLRS])
            P.op("dve", lambda e: e.scalar_tensor_tensor(out=slrs[:, 1:2], in0=slmv[:, 0:1], scalar=-1.0, in1=slrs[:, 0:1], op0=ALU.mult, op1=ALU.mult),
                 reads=[SLMV, SLRS], writes=[SLRS])
            P.op("act", lambda e: e.activation(out=xs_sb[:], in_=yps, func=AF.Identity, scale=slrs[:, 0:1], bias=slrs[:, 1:2]),
                 reads=[YPRE, SLRS], writes=[XSS])
            P.op("pool", lambda e: e.tensor_tensor(out=xs_sb[:], in0=xs_sb[:], in1=lng[S16, :], op=ALU.mult), reads=[XSS, LNG], writes=[XSS])
            P.op("pool", lambda e: e.tensor_tensor(out=xs_sb[:], in0=xs_sb[:], in1=lnb[S16, :], op=ALU.add), reads=[XSS, LNB], writes=[XSS])
            if l == n_layers - 1:
                P.dma("sp", c_so[5], y_s, xs_sb[:], reads=[XSS])

        def finalize_prompt(l):
            P.dma("sp", c_fin[1], ret_p[l].rearrange("h d e -> d h e"), S_f[:], reads=[SF])
            P.dma("sp", c_fin[2], C_p[l].rearrange("h d e -> d h e"), C_f[:, :, 0:128], reads=[CF])
            P.dma("sp", c_fin[3], n_p[l].rearrange("h d -> d h"), C_f[:, :, 128], reads=[CF], allow_slow_non_contiguous=True)
            for half in range(2):
                bt, BK = nb()
                for c4 in range(4):
                    ch = half * 4 + c4
                    P.op("pe", lambda e, ch=ch, c4=c4, bt=bt: e.transpose(bt[0:3, c4 * 128:(c4 + 1) * 128], convp_sb[:, ch, :], ident_f),
                         reads=[CONVP, CST], writes=[BK])
                P.op("act", lambda e, half=half, bt=bt: e.copy(out=convp_T[:, half * 512:(half + 1) * 512], in_=bt[0:3, :]),
                     reads=[BK], writes=[CONVPT])
            P.dma("sp", c_fin[4], conv_p[l], convp_T[:], reads=[CONVPT])

        def stage(name):
            if stop_at == name:
                raise StopBuild()

        try:
            for l in range(n_layers):
                load_weights(l)
                stage("weights")
                load_params(l)
                stage("params")
                prepass(l)
                stage("prepass")
                for t in range(n_tiles):
                    main_tile(l, t)
                stage("tiles")
                finalize_prompt(l)
                if do_sample:
                    sample_path(l)
                stage('sample')
        except StopBuild:
            pass

        for c in P.chans:
            if c.val:
                P.wait_chan("sp", c)
        with nc.Block() as block:
            P.finish(block)
    return nc


def make_consts():
    cst = np.zeros((128, NCST), np.float32)
    cst[:, K_ID:K_ID + 128] = np.eye(128, dtype=np.float32)
    idx = np.arange(128)
    cst[:, K_TRI:K_TRI + 128] = (idx[:, None] <= idx[None, :]).astype(np.float32)
    for h in range(H):
        lg = np.float32(LOGG[h])
        cst[:, K_KDEC + h] = np.exp(lg * (np.float32(L - 1) - idx.astype(np.float32))).astype(np.float32) * np.float32(ISQ)
        cst[:, K_QDEC + h * 128:K_QDEC + (h + 1) * 128] = np.exp(lg * (idx.astype(np.float32) + 1.0 - L)).astype(np.float32)[None, :]
    cst[:, K_ONE:K_ONE + 128] = 1.0
    for h in range(H):
        cst[:, K_GAM + h] = np.float32(GAM[h])
    half = HD // 2
    inv = (np.float32(10000.0) ** (-np.arange(half, dtype=np.float32) / np.float32(half))).astype(np.float32)

    def rope(pos):
        ang = (pos.astype(np.float32)[:, None] * inv[None, :]).astype(np.float32)
        c = np.cos(ang.astype(np.float64)).astype(np.float32)
        s = np.sin(ang.astype(np.float64)).astype(np.float32)
        out = np.zeros((len(pos), 2, HD), np.float32)
        out[:, 0, :half] = c
        out[:, 0, half:] = c
        out[:, 1, :half] = -s
        out[:, 1, half:] = s
        return out

    rope_p = rope(np.arange(T))
    rope_s = rope(np.array([PAST_LEN])).reshape(2 * HD)
    return cst, rope_p, rope_s


_CACHE = {}


def kernel(x_prompt, x_sample, state_ret, state_mlstm_C, state_mlstm_n, state_mlstm_m, state_conv,
           w_in, conv_w, conv_b, b_i, b_f, g_ret, g_m, w_out, ln_g, ln_b):
    n = 8
    if "nc" not in _CACHE:
        _CACHE["nc"] = build_program()
    nc = _CACHE["nc"]
    cst, rope_p, rope_s = make_consts()
    f = lambda a: np.ascontiguousarray(np.asarray(a, dtype=np.float32))
    shared = dict(w_in=f(w_in), conv_w=f(conv_w), conv_b=f(conv_b), b_i=f(b_i), b_f=f(b_f), g_ret=f(g_ret), g_m=f(g_m),
                  w_out=f(w_out), ln_g=f(ln_g), ln_b=f(ln_b), cst=cst, rope_p=rope_p, rope_s=rope_s)
    in_maps = []
    for c in range(n):
        s0, s1 = c * NS, (c + 1) * NS
        m = dict(shared)
        m["xp"] = f(x_prompt[c])
        m["xs"] = f(x_sample[s0:s1, 0, :])
        m["sret"] = f(state_ret[:, s0:s1])
        m["sC"] = f(state_mlstm_C[:, s0:s1])
        m["sn"] = f(state_mlstm_n[:, s0:s1])
        m["sm"] = f(state_mlstm_m[:, s0:s1])
        m["sconv"] = f(state_conv[:, s0:s1])
        in_maps.append(m)
    res = run_bass_kernel_spmd(nc, in_maps, core_ids=list(range(n)))
    R = res.results
    y_p = np.stack([R[c]["y_p"] for c in range(n)], 0)
    y_s = np.concatenate([R[c]["y_s"] for c in range(n)], 0)[:, None, :]
    ret_p = np.stack([R[c]["ret_p"] for c in range(n)], 1)
    C_p = np.stack([R[c]["C_p"] for c in range(n)], 1)
    n_p = np.stack([R[c]["n_p"] for c in range(n)], 1)
    m_p = np.stack([R[c]["m_p"] for c in range(n)], 1)
    conv_p = np.stack([R[c]["conv_p"] for c in range(n)], 1)
    ret_s = np.concatenate([R[c]["ret_s"] for c in range(n)], 1)
    C_s = np.concatenate([R[c]["C_s"] for c in range(n)], 1)
    n_s = np.concatenate([R[c]["n_s"] for c in range(n)], 1)
    m_s = np.concatenate([R[c]["m_s"] for c in range(n)], 1)
    conv_s = np.concatenate([R[c]["conv_s"] for c in range(n)], 1)
    return (y_p, y_s, ret_p, C_p, n_p, m_p, conv_p, ret_s, C_s, n_s, m_s, conv_s)
```

```python
from contextlib import ExitStack
import numpy as np
import concourse.bass as bass
import concourse.mybir as mybir
from concourse.bass_utils import run_bass_kernel_spmd

F32 = mybir.dt.float32
BF16 = mybir.dt.bfloat16
AF = mybir.ActivationFunctionType
ALU = mybir.AluOpType
AX = mybir.AxisListType

D = 1024
T = 2048
NT = 16
L = 128
NS = 16
H = 4
HD = 128
N_IN = 4616
PAST_LEN = 16384
ALPHA = (2 * 2) ** 0.25
GN_EPS = 1e-5
LN_EPS = 1e-5
ISQ = float(HD ** -0.5)
LNISQ = float(np.log(HD ** -0.5))
GAM = [float(np.float32(1.0) - np.float32(2.0) ** np.float32(-5.0 - h)) for h in range(H)]
LOGG = [float(np.log(np.float32(g))) for g in GAM]
GAML = [float(np.exp(np.float32(lg) * L)) for lg in LOGG]

C_RQ, C_RK, C_RV, C_RZ, C_MQK, C_MV, C_MO, C_MZ, C_G = 0, 512, 1024, 1536, 2048, 3072, 3584, 4096, 4608

K_ID = 0
K_TRI = 128
K_KDEC = 256
K_QDEC = 260
K_ONE = 772
K_GAM = 900
NCST = 904


class Buf:
    __slots__ = ("name", "w", "r", "psum")

    def __init__(self, name="", psum=False):
        self.name = name
        self.w = None
        self.r = []
        self.psum = psum


class Chan:
    def __init__(self, prog, name):
        self.sem = prog.new_sem(name)
        self.val = 0


class Prog:
    ENG = ("pe", "act", "dve", "pool", "sp")

    def __init__(self, nc, stack):
        self.nc = nc
        self.stack = stack
        self.items = {e: [] for e in self.ENG}
        self.cnt = {e: 0 for e in self.ENG}
        self.esem = {e: self.new_sem("prog_" + e) for e in self.ENG}
        self.waited = {e: {} for e in self.ENG}
        self.chans = []

    def new_sem(self, name):
        return self.stack.enter_context(self.nc.semaphore(name))

    def chan(self, name):
        c = Chan(self, name)
        self.chans.append(c)
        return c

    def _need(self, eng, reads, writes):
        need = {}

        def add(ev):
            if ev is None:
                return
            k = (ev[0], id(ev[1]) if ev[0] == 'c' else ev[1])
            if k not in need or need[k][2] < ev[2]:
                need[k] = ev

        for b in reads:
            add(b.w)
            if b.psum:
                for r in b.r:
                    if not (r[0] == 'e' and r[1] == eng):
                        add(r)
        for b in writes:
            if b.w is not None and not (b.w[0] == 'e' and b.w[1] == eng):
                add(b.w)
            for r in b.r:
                if not (r[0] == 'e' and r[1] == eng):
                    add(r)
        wd = self.waited[eng]
        for k, ev in need.items():
            if wd.get(k, 0) >= ev[2]:
                continue
            wd[k] = ev[2]
            if ev[0] == 'e':
                self.items[eng].append(('we', ev[1], ev[2]))
            else:
                self.items[eng].append(('wc', ev[1].sem, ev[2]))

    def op(self, eng, fn, reads=(), writes=()):
        self._need(eng, reads, writes)
        self.cnt[eng] += 1
        ev = ('e', eng, self.cnt[eng])
        self.items[eng].append(('op', fn, self.cnt[eng]))
        for b in reads:
            b.r.append(ev)
        for b in writes:
            b.w = ev
            b.r = []
        return ev

    def dma(self, eng, chan, out, in_, reads=(), writes=(), **kw):
        self._need(eng, reads, writes)
        chan.val += 16
        ev = ('c', chan, chan.val)
        self.items[eng].append(('dma', lambda e, out=out, in_=in_, kw=kw, sem=chan.sem:
                                e.dma_start(out=out, in_=in_, **kw).then_inc(sem, 16)))
        for b in reads:
            b.r.append(ev)
        for b in writes:
            b.w = ev
            b.r = []
        return ev

    def wait_chan(self, eng, chan):
        self.items[eng].append(('wc', chan.sem, chan.val))

    def finish(self, block):
        targets = {e: set() for e in self.ENG}
        for e in self.ENG:
            for it in self.items[e]:
                if it[0] == 'we':
                    targets[it[1]].add(it[2])
        rank = {}
        for e in self.ENG:
            rank[e] = {s_: i + 1 for i, s_ in enumerate(sorted(targets[e]))}
        self.n_signals = {e: len(rank[e]) for e in self.ENG}

        def run(eng, e):
            sem = self.esem[eng]
            rk = rank[eng]
            for it in self.items[eng]:
                if it[0] == 'we':
                    e.wait_ge(self.esem[it[1]], rank[it[1]][it[2]])
                elif it[0] == 'wc':
                    e.wait_ge(it[1], it[2])
                elif it[0] == 'op':
                    ins = it[1](e)
                    if it[2] in rk:
                        ins.then_inc(sem, 1)
                else:
                    it[1](e)

        @block.tensor
        def _(e):
            run("pe", e)

        @block.scalar
        def _(e):
            run("act", e)

        @block.vector
        def _(e):
            run("dve", e)

        @block.gpsimd
        def _(e):
            run("pool", e)

        @block.sync
        def _(e):
            run("sp", e)


class StopBuild(Exception):
    pass


def build_program(n_layers=2, n_tiles=NT, do_sample=True, dbg=None, stop_at=None):
    nc = bass.Bass("TRN2", target_bir_lowering=False)
    dt_in = lambda name, shape: nc.dram_tensor(name, shape, F32, kind="ExternalInput").ap()
    dt_out = lambda name, shape: nc.dram_tensor(name, shape, F32, kind="ExternalOutput").ap()
    xp = dt_in("xp", [T, D])
    xs = dt_in("xs", [NS, D])
    sret = dt_in("sret", [2, NS, H, HD, HD])
    sC = dt_in("sC", [2, NS, H, HD, HD])
    sn = dt_in("sn", [2, NS, H, HD])
    sm = dt_in("sm", [2, NS, H])
    sconv = dt_in("sconv", [2, NS, 3, D])
    w_in = dt_in("w_in", [2, D, N_IN])
    conv_w = dt_in("conv_w", [2, 4, D])
    conv_b = dt_in("conv_b", [2, D])
    b_i = dt_in("b_i", [2, H])
    b_f = dt_in("b_f", [2, H])
    g_ret = dt_in("g_ret", [2, 512])
    g_m = dt_in("g_m", [2, 512])
    w_out = dt_in("w_out", [2, D, D])
    ln_g = dt_in("ln_g", [2, D])
    ln_b = dt_in("ln_b", [2, D])
    cst_d = dt_in("cst", [128, NCST])
    rope_p = dt_in("rope_p", [T, 2, HD])
    rope_s = dt_in("rope_s", [2 * HD])

    y_p = dt_out("y_p", [T, D])
    y_s = dt_out("y_s", [NS, D])
    ret_p = dt_out("ret_p", [2, H, HD, HD])
    C_p = dt_out("C_p", [2, H, HD, HD])
    n_p = dt_out("n_p", [2, H, HD])
    m_p = dt_out("m_p", [2, H])
    conv_p = dt_out("conv_p", [2, 3, D])
    ret_s = dt_out("ret_s", [2, NS, H, HD, HD])
    C_s = dt_out("C_s", [2, NS, H, HD, HD])
    n_s = dt_out("n_s", [2, NS, H, HD])
    m_s = dt_out("m_s", [2, NS, H])
    conv_s = dt_out("conv_s", [2, NS, 3, D])
    y0 = nc.dram_tensor("y0_scratch", [T, D], F32, kind="Internal").ap()
    dbg_out = {}
    if dbg:
        for name, shape in dbg.items():
            dbg_out[name] = dt_out("dbg_" + name, shape)

    with ExitStack() as st:
        P = Prog(nc, st)
        sb = lambda name, shape, dt=F32: st.enter_context(nc.sbuf_tensor("s_" + name, shape, dt))
        out_chans = []

        def ochan(name):
            c = P.chan(name)
            out_chans.append(c)
            return c

        banks = []
        for i in range(8):
            t_ = st.enter_context(nc.psum_tensor("bank%d" % i, [128, 512], F32))
            banks.append((t_, Buf("bank%d" % i, psum=True)))
        bank_ctr = [0]

        def nb():
            i = bank_ctr[0] % 8
            bank_ctr[0] += 1
            return banks[i]

        def bfv(bank_t):
            return bank_t[:].bitcast(BF16)

        cst = sb("cst", [128, NCST]); CST = Buf("cst")
        ident_bf = sb("ident_bf", [128, 128], BF16); IDB = Buf()
        mask_bf = sb("mask_bf", [128, 128], BF16); MSK = Buf()
        c_ld = P.chan("c_ld")
        P.dma("sp", c_ld, cst[:], cst_d, writes=[CST])
        ident_f = cst[:, K_ID:K_ID + 128]
        tri_f = cst[:, K_TRI:K_TRI + 128]
        ones_f = cst[:, K_ONE:K_ONE + 128]
        P.op("dve", lambda e: e.tensor_copy(out=ident_bf[:], in_=ident_f), reads=[CST], writes=[IDB])
        P.op("dve", lambda e: e.tensor_copy(out=mask_bf[:], in_=tri_f), reads=[CST], writes=[MSK])
        kdec = lambda h: cst[:, K_KDEC + h:K_KDEC + h + 1]
        qdecT = cst[:, K_QDEC:K_QDEC + 512].rearrange("p (h l) -> p h l", h=4)

        w_in_sb = sb("w_in_sb", [128, 8, N_IN], BF16)
        WIN = [Buf("win%d" % i) for i in range(10)]
        w_out_sb = sb("w_out_sb", [128, 8, D], BF16)
        WOUT = [Buf("wout0"), Buf("wout1")]
        c_win = [P.chan("c_win%d" % i) for i in range(10)]
        c_wout = [P.chan("c_wout%d" % i) for i in range(2)]
        gbc = sb("gbc", [128, 1024]); GBC = Buf()
        lng = sb("lng", [128, 1024]); LNG = Buf()
        lnb = sb("lnb", [128, 1024]); LNB = Buf()
        bias8 = sb("bias8", [128, 8]); BIAS8 = Buf()
        cwb_in = sb("cwb_in", [40, 128]); CWBIN = Buf()
        cwT = sb("cwT", [128, 40]); CWT = Buf()
        diag = sb("diag", [128, 32, 128], BF16); DIAG = Buf()
        c_par = [P.chan("c_par%d" % i) for i in range(8)]

        def wblk(i):
            return (i * 512, min((i + 1) * 512, N_IN))

        def load_weights(l):
            wv = w_in[l].rearrange("(k p) n -> p k n", p=128)
            for i in range(10):
                a, b_ = wblk(i)
                P.dma("pool", c_win[i], w_in_sb[:, :, a:b_], wv[:, :, a:b_], writes=[WIN[i]])
            wo = w_out[l].rearrange("(k p) n -> p k n", p=128)
            for i in range(2):
                P.dma("pool", c_wout[i], w_out_sb[:, :, i * 512:(i + 1) * 512], wo[:, :, i * 512:(i + 1) * 512],
                      writes=[WOUT[i]])

        def load_params(l):
            P.dma("sp", c_par[0], gbc[:, 0:512], g_ret[l].partition_broadcast(128), writes=[GBC])
            P.dma("sp", c_par[1], gbc[:, 512:1024], g_m[l].partition_broadcast(128), writes=[GBC])
            P.dma("sp", c_par[2], lng[:], ln_g[l].partition_broadcast(128), writes=[LNG])
            P.dma("sp", c_par[3], lnb[:], ln_b[l].partition_broadcast(128), writes=[LNB])
            P.dma("sp", c_par[4], bias8[:, 0:4], b_i[l].partition_broadcast(128), writes=[BIAS8])
            P.dma("sp", c_par[5], bias8[:, 4:8], b_f[l].partition_broadcast(128), writes=[BIAS8])
            P.dma("sp", c_par[6], cwb_in[0:32, :], conv_w[l].rearrange("j (ch c) -> (j ch) c", c=128), writes=[CWBIN])
            P.dma("sp", c_par[7], cwb_in[32:40, :], conv_b[l].rearrange("(ch c) -> ch c", c=128), writes=[CWBIN])
            P.op("pool", lambda e: e.tensor_scalar(out=gbc[:, 512:1024], in0=gbc[:, 512:1024], scalar1=0.5, scalar2=None,
                                                   op0=ALU.mult), reads=[GBC], writes=[GBC])
            bt, BK = nb()
            P.op("pe", lambda e: e.transpose(bt[:, 0:40], cwb_in[0:40, :], ident_f[0:40, 0:40]), reads=[CWBIN, CST], writes=[BK])
            P.op("act", lambda e: e.copy(out=cwT[:], in_=bt[:, 0:40]), reads=[BK], writes=[CWT])
            for ch in range(8):
                for j in range(4):
                    idx = j * 8 + ch
                    P.op("pool", lambda e, ch=ch, j=j, idx=idx: e.tensor_scalar(
                        out=diag[:, ch * 4 + j, :], in0=ident_f, scalar1=cwT[:, idx:idx + 1], scalar2=None, op0=ALU.mult),
                        reads=[CST, CWT], writes=[DIAG])

        gates = sb("gates", [128, NT, 8]); GATES = Buf()
        lneg = sb("lneg", [128, NT, 4]); LNEG = Buf()
        bneg = sb("bneg", [128, NT, 4]); BNEG = Buf()
        u_sb = sb("u_sb", [128, NT, 4]); USB = Buf()
        uT_sb = sb("uT_sb", [64, 128]); UTS = Buf()
        umaxc = sb("umaxc", [64, 1]); UMX = Buf()
        row = sb("row", [1, 6, 64]); ROW = Buf()
        cw_b = sb("cw_b", [128, 2, NT + 1, 4]); CWB = Buf()
        pk = sb("pk", [128, NT, 4]); PK = Buf()
        thr = sb("thr", [128, NT, 4]); THR = Buf()
        tmp64 = sb("tmp64", [128, NT, 4]); TMP64 = Buf()

        x_sb = [sb("x_sb%d" % i, [128, D]) for i in range(3)]; XSB = [Buf(), Buf(), Buf()]
        rope_sb = [sb("rope_sb%d" % i, [128, 2, HD]) for i in range(2)]; ROPE = [Buf(), Buf()]
        c_x = [P.chan("c_x0"), P.chan("c_x1"), P.chan("c_x2")]
        c_rope = [P.chan("c_rope0"), P.chan("c_rope1")]
        x_bf = sb("x_bf", [128, D], BF16); XBF = Buf()
        xT2 = [sb("xT%d" % i, [128, 8, 128], BF16) for i in range(2)]; XT2 = [Buf(), Buf()]
        xT = xT2[0]; XT = XT2[0]
        tmp_t = sb("tmp_t", [128, 8, HD]); TMPT = Buf()
        tmp_u = sb("tmp_u", [128, 8, HD]); TMPU = Buf()
        qk_rot = sb("qk_rot", [128, 8, HD], BF16); QKR = Buf()
        v2 = sb("v2", [128, 4, HD], BF16); V2 = Buf()
        sz2 = [sb("sz%d" % i, [128, 1024]) for i in range(2)]; SZ2 = [Buf(), Buf()]
        sz = sz2[0]; SZ = SZ2[0]
        qT2 = sb("qT2", [128, 4, 128], BF16); QT2 = Buf()
        kT = sb("kT", [128, 4, 128], BF16); KT = Buf()
        s2 = sb("s2", [128, 4, 128], BF16); S2 = Buf()
        S_f = sb("S_f", [128, 4, HD]); SF = Buf()
        S_bf = sb("S_bf", [128, 4, HD], BF16); SBF = Buf()
        hist = sb("hist", [128, 8, 131], BF16); HIST = Buf()
        qkm = sb("qkm", [128, 8, 128], BF16); QKM = Buf()
        kp = sb("kp", [128, 4, 128], BF16); KP = Buf()
        v1 = sb("v1", [128, 4, 130], BF16); V1 = Buf()
        th = sb("th", [128, 512]); TH = Buf()
        s2m = sb("s2m", [128, 4, 128], BF16); S2M = Buf()
        C_f = sb("C_f", [128, 4, 130]); CF = Buf()
        C_bf = sb("C_bf", [128, 4, 130], BF16); CBF = Buf()
        dn = sb("dn", [128, 4]); DN = Buf()
        hm = sb("hm", [128, 4, HD]); HM = Buf()
        o_sb = sb("o_sb", [128, 4, HD]); OSB = Buf()
        stats = sb("stats", [128, 8, 6]); STATS = Buf()
        mv = sb("mv", [128, 8, 2]); MV = Buf()
        rstd = sb("rstd", [128, 8]); RSTD = Buf()
        nbias = sb("nbias", [128, 8]); NBIAS = Buf()
        mhalf = sb("mhalf", [128, 8]); MHALF = Buf()
        on = sb("on", [128, 8, HD]); ON = Buf()
        gated = sb("gated", [128, 1024], BF16); GATED = Buf()
        gT = sb("gT", [128, 8, 128], BF16); GT = Buf()
        ypre = sb("ypre", [128, D]); YPRE = Buf()
        lstats = sb("lstats", [128, 2, 6]); LSTATS = Buf()
        lmv = sb("lmv", [128, 2]); LMV = Buf()
        lrs = sb("lrs", [128, 2]); LRS = Buf()
        c_y = [ochan("c_y0"), ochan("c_y1")]
        convp_sb = sb("convp_sb", [128, 8, 3]); CONVP = Buf()
        convp_T = sb("convp_T", [3, D]); CONVPT = Buf()
        c_fin = [ochan("c_fin%d" % i) for i in range(5)]

        P.op("pool", lambda e: e.memset(mhalf[:], -0.5), writes=[MHALF])

        def dbg_dump(name, src_ap, bufs):
            if name in dbg_out:
                c = ochan("c_dbg_" + name)
                P.dma("sp", c, dbg_out[name], src_ap, reads=bufs)

        def load_x(l, t, par):
            src = xp if l == 0 else y0
            P.dma("sp", c_x[par], x_sb[par][:], src[t * 128:(t + 1) * 128, :], writes=[XSB[par]],
                  reads=([Y0B[t]] if l > 0 else []))

        def load_main(l, t):
            par = t % 2
            load_x(l, t, t % 3)
            P.dma("sp", c_rope[par], rope_sb[par][:], rope_p[t * 128:(t + 1) * 128], writes=[ROPE[par]])

        def make_xT(par, xT=None, XT=None):
            if xT is None:
                xT, XT = xT2[0], XT2[0]
            P.op("pool", lambda e: e.tensor_copy(out=x_bf[:], in_=x_sb[par][:]), reads=[XSB[par]], writes=[XBF])
            bt, BK = nb()
            v = bfv(bt)
            for k in range(8):
                P.op("pe", lambda e, k=k: e.transpose(v[:, k * 128:(k + 1) * 128], x_bf[:, k * 128:(k + 1) * 128], ident_bf[:]),
                     reads=[XBF, IDB], writes=[BK])
            P.op("act", lambda e: e.copy(out=xT[:].rearrange("p k c -> p (k c)"), in_=v), reads=[BK], writes=[XT])

        Y0B = [Buf("y0_%d" % t) for t in range(NT)]

        def prepass(l):
            for t in range(n_tiles):
                par = t % 2
                load_x(l, t, par)
                make_xT(par)
                bt, BK = nb()
                for k in range(8):
                    P.op("pe", lambda e, k=k: e.matmul(bt[:, 0:8], lhsT=xT[:, k, :], rhs=w_in_sb[:, k, C_G:C_G + 8],
                                                       start=(k == 0), stop=(k == 7)), reads=[XT, WIN[9]], writes=[BK])
                P.op("dve", lambda e, t=t: e.tensor_tensor(out=gates[:, t, :], in0=bt[:, 0:8], in1=bias8[:], op=ALU.add),
                     reads=[BK, BIAS8], writes=[GATES])
            nt = n_tiles
            stage('pp_loop')
            P.op("act", lambda e: e.activation(out=lneg[:, 0:nt, :], in_=gates[:, 0:nt, 4:8], func=AF.Exp, scale=-1.0),
                 reads=[GATES], writes=[LNEG])
            P.op("act", lambda e: e.activation(out=lneg[:, 0:nt, :], in_=lneg[:, 0:nt, :], func=AF.Ln, bias=1.0),
                 reads=[LNEG], writes=[LNEG])
            dbg_dump('gates', gates[:], [GATES])
            dbg_dump('lneg', lneg[:], [LNEG])
            stage('pp_a')
            bt, BK = nb()
            ln2 = lneg[:].rearrange("p t h -> p (t h)")
            P.op("pe", lambda e: e.matmul(bt[:, 0:nt * 4], lhsT=tri_f, rhs=ln2[:, 0:nt * 4], start=True, stop=True),
                 reads=[LNEG, CST], writes=[BK])
            stage('pp_b')
            bt2, BK2 = nb()
            P.op("pe", lambda e: e.matmul(bt2[0:1, 0:nt * 4], lhsT=ones_f[:, 0:1], rhs=ln2[:, 0:nt * 4], start=True, stop=True),
                 reads=[LNEG, CST], writes=[BK2])
            stage('pp_c')
            P.op("act", lambda e: e.copy(out=bneg[:].rearrange("p t h -> p (t h)")[:, 0:nt * 4], in_=bt[:, 0:nt * 4]),
                 reads=[BK], writes=[BNEG])
            P.op("dve", lambda e: e.tensor_tensor(out=u_sb[:, 0:nt, :], in0=gates[:, 0:nt, 0:4], in1=bneg[:, 0:nt, :], op=ALU.add),
                 reads=[GATES, BNEG], writes=[USB])
            P.op("act", lambda e: e.copy(out=row[0:1, 1, 0:nt * 4], in_=bt2[0:1, 0:nt * 4]), reads=[BK2], writes=[ROW])
            stage('pp_cum')
            bt3, BK3 = nb()
            u2 = u_sb[:].rearrange("p t h -> p (t h)")
            P.op("pe", lambda e: e.transpose(bt3[0:nt * 4, 0:128], u2[:, 0:nt * 4], ident_f), reads=[USB, CST], writes=[BK3])
            P.op("dve", lambda e: e.tensor_reduce(out=umaxc[0:nt * 4, :], in_=bt3[0:nt * 4, 0:128], axis=AX.X, op=ALU.max),
                 reads=[BK3], writes=[UMX])
            bt4, BK4 = nb()
            P.op("pe", lambda e: e.transpose(bt4[0:1, 0:nt * 4], umaxc[0:nt * 4, 0:1], ident_f[0:nt * 4, 0:nt * 4]),
                 reads=[UMX, CST], writes=[BK4])
            P.op("act", lambda e: e.copy(out=row[0:1, 0, 0:nt * 4], in_=bt4[0:1, 0:nt * 4]), reads=[BK4], writes=[ROW])
            stage('pp_umax')
            rv_ = lambda i: row[0:1, i, 0:nt * 4].rearrange("p (t h) -> p t h", h=4)
            P.op("dve", lambda e: e.tensor_scalar(out=row[0:1, 3, 0:nt * 4], in0=row[0:1, 1, 0:nt * 4], scalar1=-1.0, scalar2=None,
                                                  op0=ALU.mult), reads=[ROW], writes=[ROW])
            for h in range(4):
                P.op("dve", lambda e, h=h: e.tensor_tensor_scan(out=rv_(2)[:, :, h], data0=rv_(0)[:, :, h], data1=rv_(3)[:, :, h],
                                                                initial=0.0, op0=ALU.max, op1=ALU.add), reads=[ROW], writes=[ROW])
            P.op("dve", lambda e: e.tensor_tensor(out=row[0:1, 3, 0:nt * 4], in0=row[0:1, 2, 0:nt * 4], in1=row[0:1, 1, 0:nt * 4],
                                                  op=ALU.add), reads=[ROW], writes=[ROW])
            P.op("dve", lambda e: e.memset(row[0:1, 5, 0:4], 0.0), reads=[ROW], writes=[ROW])
            if nt > 1:
                P.op("dve", lambda e: e.tensor_copy(out=row[0:1, 5, 4:nt * 4], in_=row[0:1, 2, 0:(nt - 1) * 4]), reads=[ROW], writes=[ROW])
            P.op("dve", lambda e: e.tensor_tensor(out=row[0:1, 4, 0:nt * 4], in0=row[0:1, 5, 0:nt * 4], in1=row[0:1, 3, 0:nt * 4],
                                                  op=ALU.subtract), reads=[ROW], writes=[ROW])
            P.op("act", lambda e: e.activation(out=row[0:1, 4, 0:nt * 4], in_=row[0:1, 4, 0:nt * 4], func=AF.Exp),
                 reads=[ROW], writes=[ROW])
            stage('pp_scan')
            bt5, BK5 = nb()
            P.op("pe", lambda e: e.matmul(bt5[:, 0:128], lhsT=ones_f[0:1, :], rhs=row[0:1, 3:5, :].rearrange("p a b -> p (a b)"),
                                          start=True, stop=True), reads=[ROW, CST], writes=[BK5])
            P.op("act", lambda e: e.copy(out=cw_b[:, :, 0:NT, :], in_=bt5[:, 0:128].rearrange("p (a t h) -> p a t h", a=2, h=4)),
                 reads=[BK5], writes=[CWB])
            P.op("pool", lambda e: e.memset(cw_b[:, :, NT, :], 1.0), reads=[CWB], writes=[CWB])
            P.op("dve", lambda e: e.tensor_tensor(out=tmp64[:, 0:nt, :], in0=u_sb[:, 0:nt, :], in1=cw_b[:, 0, 0:nt, :], op=ALU.subtract),
                 reads=[USB, CWB], writes=[TMP64])
            P.op("dve", lambda e: e.tensor_scalar(out=tmp64[:, 0:nt, :], in0=tmp64[:, 0:nt, :], scalar1=LNISQ, scalar2=None, op0=ALU.add),
                 reads=[TMP64], writes=[TMP64])
            P.op("act", lambda e: e.activation(out=pk[:, 0:nt, :], in_=tmp64[:, 0:nt, :], func=AF.Exp),
                 reads=[TMP64], writes=[PK])
            P.op("dve", lambda e: e.tensor_tensor(out=tmp64[:, 0:nt, :], in0=bneg[:, 0:nt, :], in1=cw_b[:, 0, 0:nt, :], op=ALU.subtract),
                 reads=[BNEG, CWB, PK], writes=[TMP64])
            P.op("act", lambda e: e.activation(out=thr[:, 0:nt, :], in_=tmp64[:, 0:nt, :], func=AF.Exp), reads=[TMP64], writes=[THR])
            P.dma("sp", c_fin[0], m_p[l:l + 1, :], row[0:1, 2, (nt - 1) * 4:nt * 4], reads=[ROW])

        def main_tile(l, t):
            par = t % 2
            last = (t == n_tiles - 1)
            xT = xT2[par]; XT = XT2[par]
            xi = t % 3
            sz = sz2[par]; SZ = SZ2[par]
            if t == 0:
                load_main(l, 0)
            if not last:
                load_main(l, t + 1)
            make_xT(xi, xT, XT)

            def proj(bt, BK, c0, n, wb):
                for k in range(8):
                    P.op("pe", lambda e, k=k: e.matmul(bt[:, 0:n], lhsT=xT[:, k, :], rhs=w_in_sb[:, k, c0:c0 + n],
                                                       start=(k == 0), stop=(k == 7)), reads=[XT, WIN[wb]], writes=[BK])

            bq, BQ = nb(); proj(bq, BQ, C_RQ, 512, 0)
            bk_, BKK = nb(); proj(bk_, BKK, C_RK, 512, 1)
            cos2 = rope_sb[par][:, 0, :]
            sin2 = rope_sb[par][:, 1, :]
            for i, (bt, BK) in enumerate(((bq, BQ), (bk_, BKK))):
                src = bt[:].rearrange("p (h d) -> p h d", h=4)
                dst_t = tmp_t[:, i * 4:(i + 1) * 4, :]
                dst_u = tmp_u[:, i * 4:(i + 1) * 4, :]
                P.op("dve", lambda e, src=src, dst_t=dst_t: e.tensor_tensor(
                    out=dst_t, in0=src, in1=cos2.unsqueeze(1).broadcast_to([128, 4, HD]), op=ALU.mult),
                    reads=[BK, ROPE[par]], writes=[TMPT])
                P.op("dve", lambda e, src=src, dst_u=dst_u: e.tensor_tensor(
                    out=dst_u[:, :, 0:64], in0=src[:, :, 64:128], in1=sin2[:, 0:64].unsqueeze(1).broadcast_to([128, 4, 64]), op=ALU.mult),
                    reads=[BK, ROPE[par]], writes=[TMPU])
                P.op("dve", lambda e, src=src, dst_u=dst_u: e.tensor_tensor(
                    out=dst_u[:, :, 64:128], in0=src[:, :, 0:64], in1=sin2[:, 64:128].unsqueeze(1).broadcast_to([128, 4, 64]), op=ALU.mult),
                    reads=[BK, ROPE[par]], writes=[TMPU])
            P.op("pool", lambda e: e.tensor_tensor(out=qk_rot[:], in0=tmp_t[:], in1=tmp_u[:], op=ALU.add),
                 reads=[TMPT, TMPU], writes=[QKR])
            yield
            bv, BV = nb(); proj(bv, BV, C_RV, 512, 2)
            bz, BZ = nb(); proj(bz, BZ, C_RZ, 512, 3)
            for h in range(4):
                P.op("act", lambda e, h=h: e.activation(out=v2[:, h, :], in_=bv[:, h * 128:(h + 1) * 128], func=AF.Copy, scale=kdec(h)),
                     reads=[BV, CST], writes=[V2])
            P.op("act", lambda e: e.activation(out=sz[:, 0:512], in_=bz[:], func=AF.Silu), reads=[BZ], writes=[SZ])
            yield
            btq, BTQ = nb(); vtq = bfv(btq)
            btk, BTK = nb(); vtk = bfv(btk)
            for g in range(4):
                P.op("pe", lambda e, g=g: e.transpose(vtq[:, g * 128:(g + 1) * 128], qk_rot[:, g, :], ident_bf[:]),
                     reads=[QKR, IDB], writes=[BTQ])
            for g in range(4):
                P.op("pe", lambda e, g=g: e.transpose(vtk[:, g * 128:(g + 1) * 128], qk_rot[:, 4 + g, :], ident_bf[:]),
                     reads=[QKR, IDB], writes=[BTK])
            P.op("dve", lambda e: e.tensor_tensor(out=qT2[:], in0=vtq[:, 0:512].rearrange("p (h l) -> p h l", h=4), in1=qdecT, op=ALU.mult),
                 reads=[BTQ, CST], writes=[QT2])
            P.op("act", lambda e: e.copy(out=kT[:].rearrange("p h l -> p (h l)"), in_=vtk[:, 0:512]), reads=[BTK], writes=[KT])
            yield
            bs, BS = nb()
            for h in range(4):
                P.op("pe", lambda e, h=h: e.matmul(bs[:, h * 128:(h + 1) * 128], lhsT=kT[:, h, :], rhs=qT2[:, h, :], start=True, stop=True),
                     reads=[KT, QT2], writes=[BS])
            P.op("dve", lambda e: e.tensor_tensor(out=s2[:], in0=bs[:].rearrange("p (h l) -> p h l", h=4),
                                                  in1=mask_bf[:].unsqueeze(1).broadcast_to([128, 4, 128]), op=ALU.mult),
                 reads=[BS, MSK], writes=[S2])
            yield
            bo, BO = nb()
            for h in range(4):
                first = (t == 0)
                P.op("pe", lambda e, h=h, first=first: e.matmul(bo[:, h * 128:(h + 1) * 128], lhsT=s2[:, h, :], rhs=v2[:, h, :],
                                                                start=True, stop=first), reads=[S2, V2], writes=[BO])
                if not first:
                    P.op("pe", lambda e, h=h: e.matmul(bo[:, h * 128:(h + 1) * 128], lhsT=qT2[:, h, :], rhs=S_bf[:, h, :],
                                                       start=False, stop=True), reads=[QT2, SBF], writes=[BO])
            bu, BU = nb()
            for h in range(4):
                P.op("pe", lambda e, h=h: e.matmul(bu[:, h * 128:(h + 1) * 128], lhsT=qk_rot[:, 4 + h, :], rhs=v2[:, h, :],
                                                   start=True, stop=True), reads=[QKR, V2], writes=[BU])
            for h in range(4):
                if t == 0:
                    P.op("dve", lambda e, h=h: e.tensor_copy(out=S_f[:, h, :], in_=bu[:, h * 128:(h + 1) * 128]), reads=[BU], writes=[SF])
                else:
                    P.op("dve", lambda e, h=h: e.scalar_tensor_tensor(out=S_f[:, h, :], in0=S_f[:, h, :], scalar=GAML[h],
                                                                     in1=bu[:, h * 128:(h + 1) * 128], op0=ALU.mult, op1=ALU.add),
                         reads=[BU, SF], writes=[SF])
            if not last:
                for h in range(4):
                    P.op("act", lambda e, h=h: e.activation(out=S_bf[:, h, :], in_=S_f[:, h, :], func=AF.Copy, scale=GAML[h]),
                         reads=[SF], writes=[SBF])
            P.op("act", lambda e: e.copy(out=o_sb[:].rearrange("p h d -> p (h d)"), in_=bo[:]), reads=[BO], writes=[OSB])
            for h in range(4):
                P.op("dve", lambda e, h=h: e.bn_stats(out=stats[:, h, :], in_=o_sb[:, h, :]), reads=[OSB], writes=[STATS])

            yield
            bc0, BC0 = nb(); bc1, BC1 = nb()
            for ch in range(8):
                bt, BK = (bc0, BC0) if ch < 4 else (bc1, BC1)
                c0 = C_MQK + ch * 128
                wb = 4 + ch // 4
                for k in range(8):
                    P.op("pe", lambda e, k=k, bt=bt, ch=ch, c0=c0: e.matmul(bt[:, (ch % 4) * 128:(ch % 4 + 1) * 128],
                                                                        lhsT=w_in_sb[:, k, c0:c0 + 128], rhs=xT[:, k, :],
                                                                        start=(k == 0), stop=(k == 7)),
                         reads=[XT, WIN[wb]], writes=[BK])
            if t == 0:
                P.op("pool", lambda e: e.memset(hist[:, :, 0:3], 0.0), writes=[HIST])
            else:
                P.op("pool", lambda e: e.tensor_copy(out=hist[:, :, 0:3], in_=hist[:, :, 128:131]), reads=[HIST], writes=[HIST])
            for i, (bt, BK) in enumerate(((bc0, BC0), (bc1, BC1))):
                P.op("act", lambda e, i=i, bt=bt: e.copy(out=hist[:, i * 4:(i + 1) * 4, 3:131], in_=bt[:].rearrange("p (c t) -> p c t", c=4)),
                     reads=[BK], writes=[HIST])
                if last:
                    P.op("dve", lambda e, i=i, bt=bt: e.tensor_copy(out=convp_sb[:, i * 4:(i + 1) * 4, :],
                                                                    in_=bt[:].rearrange("p (c t) -> p c t", c=4)[:, :, 125:128]),
                         reads=[BK], writes=[CONVP])
            bd0, BD0 = nb(); bd1, BD1 = nb()
            for ch in range(8):
                bt, BK = (bd0, BD0) if ch < 4 else (bd1, BD1)
                for j in range(4):
                    P.op("pe", lambda e, j=j, bt=bt, ch=ch: e.matmul(bt[:, (ch % 4) * 128:(ch % 4 + 1) * 128], lhsT=diag[:, ch * 4 + j, :],
                                                                 rhs=hist[:, ch, j:j + 128], start=(j == 0), stop=(j == 3)),
                         reads=[DIAG, HIST], writes=[BK])
            for ch in range(8):
                bt, BK = (bd0, BD0) if ch < 4 else (bd1, BD1)
                P.op("act", lambda e, bt=bt, ch=ch: e.activation(out=qkm[:, ch, :], in_=bt[:, (ch % 4) * 128:(ch % 4 + 1) * 128],
                                                             func=AF.Silu, bias=cwT[:, 32 + ch:33 + ch]),
                     reads=[BK, CWT], writes=[QKM])
            yield
            bkt, BKT = nb(); vkt = bfv(bkt)
            for h in range(4):
                P.op("pe", lambda e, h=h: e.transpose(vkt[:, h * 128:(h + 1) * 128], qkm[:, 4 + h, :], ident_bf[:]),
                     reads=[QKM, IDB], writes=[BKT])
            for h in range(4):
                P.op("act", lambda e, h=h: e.activation(out=kp[:, h, :], in_=vkt[:, h * 128:(h + 1) * 128], func=AF.Copy,
                                                        scale=pk[:, t, h:h + 1]), reads=[BKT, PK], writes=[KP])
            bmv, BMV = nb(); proj(bmv, BMV, C_MV, 512, 6)
            bmo, BMO = nb(); proj(bmo, BMO, C_MO, 512, 7)
            bmz, BMZ = nb(); proj(bmz, BMZ, C_MZ, 512, 8)
            if l == 0 and t == 0:
                P.op("pool", lambda e: e.memset(v1[:, :, 128:130], 1.0), writes=[V1])
            P.op("act", lambda e: e.copy(out=v1[:, :, 0:128], in_=bmv[:].rearrange("p (h d) -> p h d", h=4)), reads=[BMV], writes=[V1])
            P.op("act", lambda e: e.activation(out=th[:], in_=bmo[:], func=AF.Tanh, scale=0.5), reads=[BMO], writes=[TH])
            P.op("act", lambda e: e.activation(out=sz[:, 512:1024], in_=bmz[:], func=AF.Silu), reads=[BMZ], writes=[SZ])
            bsm, BSM = nb()
            for h in range(4):
                P.op("pe", lambda e, h=h: e.matmul(bsm[:, h * 128:(h + 1) * 128], lhsT=qkm[:, 4 + h, :], rhs=qkm[:, h, :], start=True, stop=True),
                     reads=[QKM], writes=[BSM])
            for h in range(4):
                P.op("dve", lambda e, h=h: e.scalar_tensor_tensor(out=s2m[:, h, :], in0=bsm[:, h * 128:(h + 1) * 128], scalar=pk[:, t, h:h + 1],
                                                                 in1=mask_bf[:], op0=ALU.mult, op1=ALU.mult),
                     reads=[BSM, PK, MSK], writes=[S2M])
            yield
            bn0, BN0 = nb(); bn1, BN1 = nb()
            for h in range(4):
                bt, BK = (bn0, BN0) if h < 2 else (bn1, BN1)
                o_ = (h % 2) * 130
                first = (t == 0)
                P.op("pe", lambda e, h=h, bt=bt, o_=o_, first=first: e.matmul(bt[:, o_:o_ + 130], lhsT=s2m[:, h, :], rhs=v1[:, h, :],
                                                                            start=True, stop=first), reads=[S2M, V1], writes=[BK])
                if not first:
                    P.op("pe", lambda e, h=h, bt=bt, o_=o_: e.matmul(bt[:, o_:o_ + 130], lhsT=qkm[:, h, :], rhs=C_bf[:, h, :],
                                                                   start=False, stop=True), reads=[QKM, CBF], writes=[BK])
            bu0, BU0 = nb(); bu1, BU1 = nb()
            for h in range(4):
                bt, BK = (bu0, BU0) if h < 2 else (bu1, BU1)
                o_ = (h % 2) * 130
                P.op("pe", lambda e, h=h, bt=bt, o_=o_: e.matmul(bt[:, o_:o_ + 130], lhsT=kp[:, h, :], rhs=v1[:, h, :], start=True, stop=True),
                     reads=[KP, V1], writes=[BK])
            for h in range(4):
                bt, BK = (bu0, BU0) if h < 2 else (bu1, BU1)
                o_ = (h % 2) * 130
                if t == 0:
                    P.op("dve", lambda e, h=h, bt=bt, o_=o_: e.tensor_copy(out=C_f[:, h, :], in_=bt[:, o_:o_ + 130]), reads=[BK], writes=[CF])
                else:
                    P.op("dve", lambda e, h=h, bt=bt, o_=o_: e.scalar_tensor_tensor(out=C_f[:, h, :], in0=C_f[:, h, :], scalar=cw_b[:, 1, t, h:h + 1],
                                                                                  in1=bt[:, o_:o_ + 130], op0=ALU.mult, op1=ALU.add),
                         reads=[BK, CF, CWB], writes=[CF])
            if not last:
                for h in range(4):
                    P.op("act", lambda e, h=h: e.activation(out=C_bf[:, h, :], in_=C_f[:, h, :], func=AF.Copy, scale=cw_b[:, 1, t + 1, h:h + 1]),
                         reads=[CF, CWB], writes=[CBF])
            for i, (bt, BK) in enumerate(((bn0, BN0), (bn1, BN1))):
                P.op("act", lambda e, i=i, bt=bt: e.activation(out=dn[:, 2 * i:2 * i + 2], in_=bt[:, 128:259:130], func=AF.Abs),
                     reads=[BK], writes=[DN])
            P.op("dve", lambda e: e.tensor_tensor(out=dn[:], in0=dn[:], in1=thr[:, t, :], op=ALU.max), reads=[DN, THR], writes=[DN])
            P.op("dve", lambda e: e.reciprocal(out=dn[:], in_=dn[:]), reads=[DN], writes=[DN])
            for h in range(4):
                bt, BK = (bn0, BN0) if h < 2 else (bn1, BN1)
                o_ = (h % 2) * 130
                P.op("act", lambda e, h=h, bt=bt, o_=o_: e.activation(out=hm[:, h, :], in_=bt[:, o_:o_ + 128], func=AF.Copy, scale=dn[:, h:h + 1]),
                     reads=[BK, DN], writes=[HM])
            for h in range(4):
                P.op("dve", lambda e, h=h: e.bn_stats(out=stats[:, 4 + h, :], in_=hm[:, h, :]), reads=[HM], writes=[STATS])
            yield
            for g in range(8):
                P.op("dve", lambda e, g=g: e.bn_aggr(out=mv[:, g, :], in_=stats[:, g, :]), reads=[STATS], writes=[MV])
            P.op("pool", lambda e: e.tensor_scalar(out=rstd[:], in0=mv[:, :, 1], scalar1=GN_EPS, scalar2=None, op0=ALU.add),
                 reads=[MV], writes=[RSTD])
            P.op("pool", lambda e: e.tensor_tensor(out=rstd[:], in0=rstd[:], in1=mhalf[:], op=ALU.pow), reads=[RSTD, MHALF], writes=[RSTD])
            P.op("dve", lambda e: e.scalar_tensor_tensor(out=nbias[:], in0=mv[:, :, 0], scalar=-1.0, in1=rstd[:], op0=ALU.mult, op1=ALU.mult),
                 reads=[MV, RSTD], writes=[NBIAS])
            for g in range(8):
                if g < 4:
                    src = o_sb[:, g, :]; SB_ = OSB
                else:
                    src = hm[:, g - 4, :]; SB_ = HM
                P.op("act", lambda e, g=g, src=src: e.activation(out=on[:, g, :], in_=src, func=AF.Identity, scale=rstd[:, g:g + 1],
                                                               bias=nbias[:, g:g + 1]), reads=[SB_, RSTD, NBIAS], writes=[ON])
            P.op("dve", lambda e: e.scalar_tensor_tensor(out=sz[:, 512:1024], in0=th[:], scalar=1.0, in1=sz[:, 512:1024], op0=ALU.add, op1=ALU.mult),
                 reads=[TH, SZ], writes=[SZ])
            P.op("pool", lambda e: e.tensor_tensor(out=sz[:], in0=sz[:], in1=gbc[:], op=ALU.mult), reads=[SZ, GBC], writes=[SZ])
            P.op("pool", lambda e: e.tensor_tensor(out=gated[:], in0=on[:].rearrange("p g d -> p (g d)"), in1=sz[:], op=ALU.mult),
                 reads=[ON, SZ], writes=[GATED])
            yield
            bg, BG = nb(); vg = bfv(bg)
            for k in range(8):
                P.op("pe", lambda e, k=k: e.transpose(vg[:, k * 128:(k + 1) * 128], gated[:, k * 128:(k + 1) * 128], ident_bf[:]),
                     reads=[GATED, IDB], writes=[BG])
            P.op("act", lambda e: e.copy(out=gT[:].rearrange("p k c -> p (k c)"), in_=vg), reads=[BG], writes=[GT])
            for n in range(2):
                bt, BK = nb()
                for k in range(8):
                    P.op("pe", lambda e, k=k, n=n, bt=bt: e.matmul(bt[:], lhsT=gT[:, k, :], rhs=w_out_sb[:, k, n * 512:(n + 1) * 512],
                                                               start=(k == 0), stop=(k == 7)), reads=[GT, WOUT[n]], writes=[BK])
                P.op("dve", lambda e, n=n, bt=bt: e.scalar_tensor_tensor(out=ypre[:, n * 512:(n + 1) * 512], in0=x_sb[xi][:, n * 512:(n + 1) * 512],
                                                                     scalar=ALPHA, in1=bt[:], op0=ALU.mult, op1=ALU.add),
                     reads=[BK, XSB[xi]], writes=[YPRE])
                P.op("dve", lambda e, n=n: e.bn_stats(out=lstats[:, n, :], in_=ypre[:, n * 512:(n + 1) * 512]), reads=[YPRE], writes=[LSTATS])
            P.op("dve", lambda e: e.bn_aggr(out=lmv[:], in_=lstats[:].rearrange("p a b -> p (a b)")), reads=[LSTATS], writes=[LMV])
            P.op("pool", lambda e: e.tensor_scalar(out=lrs[:, 0:1], in0=lmv[:, 1:2], scalar1=LN_EPS, scalar2=None, op0=ALU.add),
                 reads=[LMV], writes=[LRS])
            P.op("pool", lambda e: e.tensor_tensor(out=lrs[:, 0:1], in0=lrs[:, 0:1], in1=mhalf[:, 0:1], op=ALU.pow), reads=[LRS, MHALF], writes=[LRS])
            P.op("dve", lambda e: e.scalar_tensor_tensor(out=lrs[:, 1:2], in0=lmv[:, 0:1], scalar=-1.0, in1=lrs[:, 0:1], op0=ALU.mult, op1=ALU.mult),
                 reads=[LMV, LRS], writes=[LRS])
            P.op("act", lambda e: e.activation(out=ypre[:], in_=ypre[:], func=AF.Identity, scale=lrs[:, 0:1], bias=lrs[:, 1:2]),
                 reads=[YPRE, LRS], writes=[YPRE])
            P.op("pool", lambda e: e.tensor_tensor(out=ypre[:], in0=ypre[:], in1=lng[:], op=ALU.mult), reads=[YPRE, LNG], writes=[YPRE])
            P.op("pool", lambda e: e.tensor_tensor(out=ypre[:], in0=ypre[:], in1=lnb[:], op=ALU.add), reads=[YPRE, LNB], writes=[YPRE])
            if l == n_layers - 1:
                P.dma("sp", c_y[par], y_p[t * 128:(t + 1) * 128, :], ypre[:], reads=[YPRE])
            else:
                P.dma("sp", c_y[par], y0[t * 128:(t + 1) * 128, :], ypre[:], reads=[YPRE], writes=[Y0B[t]])


        xs_sb = sb("xs_sb", [NS, D]); XSS = Buf("xs")
        ropes = sb("ropes", [NS, 2, HD]); ROPES = Buf()
        sg = sb("sg", [NS, 16, 4]); SG = Buf()
        wexp = sb("wexp", [NS, NS, 4]); WEXP = Buf()
        wcbs = sb("wcbs", [128, NS * 4]); WCBS = Buf()
        qTs = sb("qTs", [128, 8, NS]); QTS = Buf()
        oTs = sb("oTs", [128, 2, 64]); OTS = Buf()
        sstats = sb("sstats", [NS, 8, 6]); SSTATS = Buf()
        smv = sb("smv", [NS, 8, 2]); SMV = Buf()
        srs = sb("srs", [NS, 8]); SRS = Buf()
        snb = sb("snb", [NS, 8]); SNB = Buf()
        slst = sb("slst", [NS, 2, 6]); SLST = Buf()
        slmv = sb("slmv", [NS, 2]); SLMV = Buf()
        slrs = sb("slrs", [NS, 2]); SLRS = Buf()
        c_s = [P.chan("c_s%d" % i) for i in range(8)]
        c_sg = [P.chan("c_sg0"), P.chan("c_sg1")]
        c_cg = [P.chan("c_cg0"), P.chan("c_cg1")]
        c_so = [ochan("c_so%d" % i) for i in range(8)]
        c_sgo = [ochan("c_sgo0"), ochan("c_sgo1")]
        c_cgo = [ochan("c_cgo0"), ochan("c_cgo1")]
        id16 = ident_f[0:NS, 0:NS]
        gamb = cst[0:NS, K_GAM:K_GAM + 4]
        sbank_ctr = [0]

        def snb_():
            i = sbank_ctr[0] % 6
            sbank_ctr[0] += 1
            return banks[i]

        def sample_path(l):
            S16 = slice(0, NS)
            if l == 0:
                P.dma("sp", c_s[0], xs_sb[:], xs, writes=[XSS])
                P.dma("sp", c_s[7], ropes[:].rearrange("p a d -> p (a d)"), rope_s.partition_broadcast(NS), writes=[ROPES])
            sc = [x_sb[0][S16, :], x_sb[1][S16, :], x_sb[2][S16, :]]
            SCB = [XSB[0], XSB[1], XSB[2]]
            szs = sz2[0]; SZ = SZ2[0]; sz = sz2[0]
            for j in range(3):
                P.dma("sp", c_s[1 + j], sc[j], sconv[l][:, j, :], writes=[SCB[j]])
            P.dma("sp", c_s[4], sg[:, 0, :], sm[l], writes=[SG])
            n0v = o_sb[S16, :, :]
            P.dma("sp", c_s[5], n0v, sn[l], writes=[OSB])
            P.op("pool", lambda e: e.tensor_copy(out=x_bf[S16, :], in_=xs_sb[:]), reads=[XSS], writes=[XBF])
            bt, BK = nb(); v = bfv(bt)
            for k in range(8):
                P.op("pe", lambda e, k=k: e.transpose(v[:, k * NS:(k + 1) * NS], x_bf[S16, k * 128:(k + 1) * 128], ident_bf[S16, 0:NS]),
                     reads=[XBF, IDB], writes=[BK])
            P.op("act", lambda e: e.copy(out=xT[:, :, 0:NS], in_=v[:, 0:8 * NS].rearrange("p (k s) -> p k s", k=8)), reads=[BK], writes=[XT])

            def sproj(c0, n, wb):
                bt, BK = nb()
                for k in range(8):
                    P.op("pe", lambda e, k=k: e.matmul(bt[S16, 0:n], lhsT=xT[:, k, 0:NS], rhs=w_in_sb[:, k, c0:c0 + n],
                                                       start=(k == 0), stop=(k == 7)), reads=[XT, WIN[wb]], writes=[BK])
                return bt, BK

            cos2 = ropes[:, 0, :]
            sin2 = ropes[:, 1, :]
            for i, c0 in enumerate((C_RQ, C_RK)):
                bt, BK = sproj(c0, 512, i)
                src = bt[S16, :].rearrange("p (h d) -> p h d", h=4)
                dst_t = tmp_t[S16, i * 4:(i + 1) * 4, :]
                dst_u = tmp_u[S16, i * 4:(i + 1) * 4, :]
                P.op("dve", lambda e, src=src, dst_t=dst_t: e.tensor_tensor(out=dst_t, in0=src, in1=cos2.unsqueeze(1).broadcast_to([NS, 4, HD]), op=ALU.mult),
                     reads=[BK, ROPES], writes=[TMPT])
                P.op("dve", lambda e, src=src, dst_u=dst_u: e.tensor_tensor(out=dst_u[:, :, 0:64], in0=src[:, :, 64:128],
                                                                          in1=sin2[:, 0:64].unsqueeze(1).broadcast_to([NS, 4, 64]), op=ALU.mult),
                     reads=[BK, ROPES], writes=[TMPU])
                P.op("dve", lambda e, src=src, dst_u=dst_u: e.tensor_tensor(out=dst_u[:, :, 64:128], in0=src[:, :, 0:64],
                                                                          in1=sin2[:, 64:128].unsqueeze(1).broadcast_to([NS, 4, 64]), op=ALU.mult),
                     reads=[BK, ROPES], writes=[TMPU])
            qk_s = on[S16, :, :]
            P.op("pool", lambda e: e.tensor_tensor(out=qk_s, in0=tmp_t[S16, :, :], in1=tmp_u[S16, :, :], op=ALU.add), reads=[TMPT, TMPU], writes=[ON])
            v2s = hm[S16, :, :]
            bt, BK = sproj(C_RV, 512, 2)
            P.op("act", lambda e, bt=bt: e.mul(out=v2s.rearrange("p h d -> p (h d)"), in_=bt[S16, :], mul=ISQ), reads=[BK], writes=[HM])
            bt, BK = sproj(C_RZ, 512, 3)
            P.op("act", lambda e, bt=bt: e.activation(out=sz[S16, 0:512], in_=bt[S16, :], func=AF.Silu), reads=[BK], writes=[SZ])
            mqk_s = ypre[S16, :]
            for i in range(2):
                bt, BK = sproj(C_MQK + i * 512, 512, 4 + i)
                P.op("act", lambda e, bt=bt, i=i: e.copy(out=mqk_s[:, i * 512:(i + 1) * 512], in_=bt[S16, :]), reads=[BK], writes=[YPRE])
            P.dma("sp", c_so[0], conv_s[l][:, 0, :], sc[1], reads=[SCB[1]])
            P.dma("sp", c_so[1], conv_s[l][:, 1, :], sc[2], reads=[SCB[2]])
            P.dma("sp", c_so[2], conv_s[l][:, 2, :], mqk_s, reads=[YPRE])
            Wb = sz2[1][S16, :]; WB = SZ2[1]
            acc = tmp_t[S16, :, :].rearrange("p g d -> p (g d)")
            tmpc = tmp_u[S16, :, :].rearrange("p g d -> p (g d)")
            full = [sc[0], sc[1], sc[2], mqk_s]
            FB = [SCB[0], SCB[1], SCB[2], YPRE]
            for n_, j in enumerate((3, 0, 1, 2)):
                P.dma("sp", c_s[6], Wb, conv_w[l][j].partition_broadcast(NS), writes=[WB])
                if n_ == 0:
                    P.op("dve", lambda e, j=j: e.tensor_tensor(out=acc, in0=full[j], in1=Wb, op=ALU.mult), reads=[FB[j], WB, ON], writes=[TMPT])
                else:
                    P.op("dve", lambda e, j=j: e.tensor_tensor(out=tmpc, in0=full[j], in1=Wb, op=ALU.mult), reads=[FB[j], WB, ON], writes=[TMPU])
                    P.op("pool", lambda e: e.tensor_tensor(out=acc, in0=acc, in1=tmpc, op=ALU.add), reads=[TMPU, TMPT], writes=[TMPT])
            P.dma("sp", c_s[6], Wb, conv_b[l].partition_broadcast(NS), writes=[WB])
            P.op("pool", lambda e: e.tensor_tensor(out=acc, in0=acc, in1=Wb, op=ALU.add), reads=[WB, TMPT], writes=[TMPT])
            P.op("act", lambda e: e.activation(out=acc, in_=acc, func=AF.Silu), reads=[TMPT], writes=[TMPT])
            qkm_s = tmp_t[S16, :, :]
            v_s = gated[S16, :].bitcast(F32).rearrange("p (h d) -> p h d", h=4)
            bt, BK = sproj(C_MV, 512, 6)
            P.op("act", lambda e, bt=bt: e.copy(out=v_s.rearrange("p h d -> p (h d)"), in_=bt[S16, :]), reads=[BK], writes=[GATED])
            bt, BK = sproj(C_MO, 512, 7)
            P.op("act", lambda e, bt=bt: e.activation(out=th[S16, :], in_=bt[S16, :], func=AF.Tanh, scale=0.5), reads=[BK], writes=[TH])
            bt, BK = sproj(C_MZ, 512, 8)
            P.op("act", lambda e, bt=bt: e.activation(out=sz[S16, 512:1024], in_=bt[S16, :], func=AF.Silu), reads=[BK], writes=[SZ])
            bt, BK = sproj(C_G, 8, 9)
            G_ = lambda i: sg[:, i, :]
            P.op("dve", lambda e, bt=bt: e.tensor_tensor(out=sg[:, 1:3, :].rearrange("p a h -> p (a h)"), in0=bt[S16, 0:8], in1=bias8[S16, :], op=ALU.add),
                 reads=[BK, BIAS8], writes=[SG])
            P.op("act", lambda e: e.activation(out=G_(3), in_=G_(2), func=AF.Exp, scale=-1.0), reads=[SG], writes=[SG])
            P.op("act", lambda e: e.activation(out=G_(3), in_=G_(3), func=AF.Ln, bias=1.0), reads=[SG], writes=[SG])
            P.op("dve", lambda e: e.tensor_tensor(out=G_(4), in0=G_(1), in1=G_(3), op=ALU.add), reads=[SG], writes=[SG])
            P.op("dve", lambda e: e.tensor_tensor(out=G_(5), in0=G_(4), in1=G_(0), op=ALU.max), reads=[SG], writes=[SG])
            P.op("dve", lambda e: e.tensor_tensor(out=G_(6), in0=G_(5), in1=G_(3), op=ALU.subtract), reads=[SG], writes=[SG])
            P.dma("sp", c_so[3], m_s[l], G_(6), reads=[SG])
            P.op("dve", lambda e: e.tensor_tensor(out=G_(14), in0=G_(4), in1=G_(5), op=ALU.subtract), reads=[SG], writes=[SG])
            P.op("dve", lambda e: e.tensor_scalar(out=G_(14), in0=G_(14), scalar1=LNISQ, scalar2=None, op0=ALU.add), reads=[SG], writes=[SG])
            P.op("act", lambda e: e.activation(out=G_(7), in_=G_(14), func=AF.Exp), reads=[SG], writes=[SG])
            P.op("dve", lambda e: e.tensor_tensor(out=G_(14), in0=G_(3), in1=G_(5), op=ALU.subtract), reads=[SG], writes=[SG])
            P.op("act", lambda e: e.activation(out=G_(8), in_=G_(14), func=AF.Exp), reads=[SG], writes=[SG])
            P.op("dve", lambda e: e.tensor_tensor(out=G_(14), in0=G_(0), in1=G_(5), op=ALU.subtract), reads=[SG], writes=[SG])
            P.op("act", lambda e: e.activation(out=G_(9), in_=G_(14), func=AF.Exp), reads=[SG], writes=[SG])
            bt, BK = nb()
            for g in range(4):
                P.op("pe", lambda e, g=g, bt=bt: e.transpose(bt[:, g * NS:(g + 1) * NS], qk_s[:, g, :], id16), reads=[ON, CST], writes=[BK])
            for g in range(4):
                P.op("pe", lambda e, g=g, bt=bt: e.transpose(bt[:, (4 + g) * NS:(5 + g) * NS], qkm_s[:, g, :], id16), reads=[TMPT, CST], writes=[BK])
            P.op("act", lambda e, bt=bt: e.copy(out=qTs[:].rearrange("p g s -> p (g s)"), in_=bt[:, 0:8 * NS]), reads=[BK], writes=[QTS])
            bc4 = lambda i: sg[:, i, :].unsqueeze(2).broadcast_to([NS, 4, HD])
            P.op("dve", lambda e: e.tensor_tensor(out=qkm_s[:, 4:8, :], in0=qkm_s[:, 4:8, :], in1=bc4(7), op=ALU.mult), reads=[TMPT, SG], writes=[TMPT])
            prod = tmp_u[S16, :, :]
            P.op("dve", lambda e: e.tensor_tensor(out=prod[:, 0:4, :], in0=qk_s[:, 0:4, :], in1=qk_s[:, 4:8, :], op=ALU.mult), reads=[ON], writes=[TMPU])
            P.op("dve", lambda e: e.tensor_reduce(out=G_(10), in_=prod[:, 0:4, :], axis=AX.X, op=ALU.add), reads=[TMPU], writes=[SG])
            P.op("dve", lambda e: e.tensor_tensor(out=prod[:, 4:8, :], in0=qkm_s[:, 0:4, :], in1=qkm_s[:, 4:8, :], op=ALU.mult), reads=[TMPT], writes=[TMPU])
            P.op("dve", lambda e: e.tensor_reduce(out=G_(11), in_=prod[:, 4:8, :], axis=AX.X, op=ALU.add), reads=[TMPU], writes=[SG])
            P.op("dve", lambda e: e.tensor_tensor(out=prod[:, 0:4, :], in0=qkm_s[:, 0:4, :], in1=n0v, op=ALU.mult), reads=[TMPT, OSB, SG], writes=[TMPU])
            P.op("dve", lambda e: e.tensor_reduce(out=G_(13), in_=prod[:, 0:4, :], axis=AX.X, op=ALU.add), reads=[TMPU], writes=[SG])
            P.op("dve", lambda e: e.tensor_tensor(out=n0v, in0=n0v, in1=bc4(9), op=ALU.mult), reads=[OSB, SG], writes=[OSB])
            P.op("pool", lambda e: e.tensor_tensor(out=n0v, in0=n0v, in1=qkm_s[:, 4:8, :], op=ALU.add), reads=[OSB, TMPT], writes=[OSB])
            P.dma("sp", c_so[4], n_s[l], n0v, reads=[OSB])
            P.op("dve", lambda e: e.tensor_tensor(out=wexp[:], in0=sg[:, 9, :].unsqueeze(1).broadcast_to([NS, NS, 4]),
                                                  in1=id16.unsqueeze(2).broadcast_to([NS, NS, 4]), op=ALU.mult), reads=[SG, CST], writes=[WEXP])
            bt, BK = nb()
            P.op("pe", lambda e, bt=bt: e.matmul(bt[:, 0:NS * 4], lhsT=ones_f[0:NS, :], rhs=wexp[:].rearrange("p s h -> p (s h)"), start=True, stop=True),
                 reads=[WEXP, CST], writes=[BK])
            P.op("act", lambda e, bt=bt: e.copy(out=wcbs[:], in_=bt[:, 0:NS * 4]), reads=[BK], writes=[WCBS])
            GS = [x_sb[0][:].rearrange("p (s h e) -> p s h e", s=2, h=4), x_sb[1][:].rearrange("p (s h e) -> p s h e", s=2, h=4)]
            GC = [x_sb[2][:].rearrange("p (s h e) -> p s h e", s=2, h=4), sz2[1][:].rearrange("p (s h e) -> p s h e", s=2, h=4)]
            GCB = [XSB[2], SZ2[1]]
            vexp = ypre[S16, :].rearrange("p (s h e) -> p s h e", s=2, h=4)
            bor, BOR = banks[6]
            bom, BOM = banks[7]
            for g in range(NS // 2):
                par = g % 2
                s0 = g * 2
                P.dma("sp", c_sg[par], GS[par], sret[l][s0:s0 + 2].rearrange("s h d e -> d s h e"), writes=[XSB[par]])
                P.dma("sp", c_cg[par], GC[par], sC[l][s0:s0 + 2].rearrange("s h d e -> d s h e"), writes=[GCB[par]])
                for typ in range(2):
                    Gt = GS[par] if typ == 0 else GC[par]
                    GB = XSB[par] if typ == 0 else GCB[par]
                    bo_, BO_ = (bor, BOR) if typ == 0 else (bom, BOM)
                    for sl in range(2):
                        for h in range(4):
                            col = h * NS + s0 + sl
                            P.op("pe", lambda e, Gt=Gt, bo_=bo_, sl=sl, h=h, col=col, typ=typ, s0=s0: e.matmul(
                                bo_[:, typ * 0 + col:col + 1], lhsT=Gt[:, sl, h, :], rhs=qTs[:, typ * 4 + h, s0 + sl:s0 + sl + 1], start=True, stop=True),
                                reads=[GB, QTS], writes=[BO_])
                    vsrc = v2s if typ == 0 else v_s
                    VB = HM if typ == 0 else GATED
                    P.op("dve", lambda e, vsrc=vsrc, s0=s0: e.tensor_tensor(
                        out=vexp, in0=vsrc.unsqueeze(1).broadcast_to([NS, 2, 4, HD]),
                        in1=id16[:, s0:s0 + 2].unsqueeze(2).unsqueeze(3).broadcast_to([NS, 2, 4, HD]), op=ALU.mult),
                        reads=[VB, CST], writes=[YPRE])
                    ksrc = qk_s[:, 4:8, :] if typ == 0 else qkm_s[:, 4:8, :]
                    KB = ON if typ == 0 else TMPT
                    for sl in range(2):
                        bu_, BU_ = snb_()
                        for h in range(4):
                            P.op("pe", lambda e, bu_=bu_, h=h, sl=sl, ksrc=ksrc: e.matmul(bu_[:, h * 128:(h + 1) * 128], lhsT=ksrc[:, h, :], rhs=vexp[:, sl, h, :],
                                                                                    start=True, stop=True), reads=[KB, YPRE], writes=[BU_])
                        for h in range(4):
                            if typ == 0:
                                P.op("dve", lambda e, bu_=bu_, h=h, sl=sl, Gt=Gt: e.scalar_tensor_tensor(
                                    out=Gt[:, sl, h, :], in0=Gt[:, sl, h, :], scalar=GAM[h], in1=bu_[:, h * 128:(h + 1) * 128], op0=ALU.mult, op1=ALU.add),
                                    reads=[BU_, GB], writes=[GB])
                            else:
                                ci = (s0 + sl) * 4 + h
                                P.op("dve", lambda e, bu_=bu_, h=h, sl=sl, Gt=Gt, ci=ci: e.scalar_tensor_tensor(
                                    out=Gt[:, sl, h, :], in0=Gt[:, sl, h, :], scalar=wcbs[:, ci:ci + 1], in1=bu_[:, h * 128:(h + 1) * 128], op0=ALU.mult, op1=ALU.add),
                                    reads=[BU_, GB, WCBS], writes=[GB])
                P.dma("sp", c_sgo[par], ret_s[l][s0:s0 + 2].rearrange("s h d e -> d s h e"), GS[par], reads=[XSB[par]])
                P.dma("sp", c_cgo[par], C_s[l][s0:s0 + 2].rearrange("s h d e -> d s h e"), GC[par], reads=[GCB[par]])
            P.op("act", lambda e: e.copy(out=oTs[:, 0, :], in_=bor[:, 0:64]), reads=[BOR], writes=[OTS])
            P.op("act", lambda e: e.copy(out=oTs[:, 1, :], in_=bom[:, 0:64]), reads=[BOM], writes=[OTS])
            btr_, BTR_ = nb()
            btm_, BTM_ = nb()
            for typ, (bt, BK) in enumerate(((btr_, BTR_), (btm_, BTM_))):
                for h in range(4):
                    P.op("pe", lambda e, bt=bt, typ=typ, h=h: e.transpose(bt[S16, h * 128:(h + 1) * 128], oTs[:, typ, h * NS:(h + 1) * NS], ident_f),
                         reads=[OTS, CST], writes=[BK])
            hs = tmp_u[S16, :, :]
            t2 = ypre[S16, :].rearrange("p (g d) -> p g d", g=8)
            inter_r = btr_[S16, :].rearrange("p (h d) -> p h d", h=4)
            inter_m = btm_[S16, :].rearrange("p (h d) -> p h d", h=4)
            P.op("dve", lambda e: e.tensor_tensor(out=hs[:, 0:4, :], in0=inter_r, in1=gamb.unsqueeze(2).broadcast_to([NS, 4, HD]), op=ALU.mult),
                 reads=[BTR_, CST, SG], writes=[TMPU])
            P.op("dve", lambda e: e.tensor_tensor(out=t2[:, 0:4, :], in0=v2s, in1=bc4(10), op=ALU.mult), reads=[HM, SG], writes=[YPRE])
            P.op("pool", lambda e: e.tensor_tensor(out=hs[:, 0:4, :], in0=hs[:, 0:4, :], in1=t2[:, 0:4, :], op=ALU.add), reads=[TMPU, YPRE], writes=[TMPU])
            P.op("dve", lambda e: e.tensor_tensor(out=hs[:, 4:8, :], in0=inter_m, in1=bc4(9), op=ALU.mult), reads=[BTM_, SG], writes=[TMPU])
            P.op("dve", lambda e: e.tensor_tensor(out=t2[:, 4:8, :], in0=v_s, in1=bc4(11), op=ALU.mult), reads=[GATED, SG], writes=[YPRE])
            P.op("pool", lambda e: e.tensor_tensor(out=hs[:, 4:8, :], in0=hs[:, 4:8, :], in1=t2[:, 4:8, :], op=ALU.add), reads=[TMPU, YPRE], writes=[TMPU])
            P.op("dve", lambda e: e.tensor_tensor(out=G_(12), in0=G_(13), in1=G_(9), op=ALU.mult), reads=[SG], writes=[SG])
            P.op("dve", lambda e: e.tensor_tensor(out=G_(12), in0=G_(12), in1=G_(11), op=ALU.add), reads=[SG], writes=[SG])
            P.op("act", lambda e: e.activation(out=G_(12), in_=G_(12), func=AF.Abs), reads=[SG], writes=[SG])
            P.op("dve", lambda e: e.tensor_tensor(out=G_(12), in0=G_(12), in1=G_(8), op=ALU.max), reads=[SG], writes=[SG])
            P.op("dve", lambda e: e.reciprocal(out=G_(12), in_=G_(12)), reads=[SG], writes=[SG])
            P.op("dve", lambda e: e.tensor_tensor(out=hs[:, 4:8, :], in0=hs[:, 4:8, :], in1=bc4(12), op=ALU.mult), reads=[TMPU, SG], writes=[TMPU])
            for g in range(8):
                P.op("dve", lambda e, g=g: e.bn_stats(out=sstats[:, g, :], in_=hs[:, g, :]), reads=[TMPU], writes=[SSTATS])
            for g in range(8):
                P.op("dve", lambda e, g=g: e.bn_aggr(out=smv[:, g, :], in_=sstats[:, g, :]), reads=[SSTATS], writes=[SMV])
            P.op("pool", lambda e: e.tensor_scalar(out=srs[:], in0=smv[:, :, 1], scalar1=GN_EPS, scalar2=None, op0=ALU.add), reads=[SMV], writes=[SRS])
            P.op("pool", lambda e: e.tensor_tensor(out=srs[:], in0=srs[:], in1=mhalf[S16, :], op=ALU.pow), reads=[SRS, MHALF], writes=[SRS])
            P.op("dve", lambda e: e.scalar_tensor_tensor(out=snb[:], in0=smv[:, :, 0], scalar=-1.0, in1=srs[:], op0=ALU.mult, op1=ALU.mult),
                 reads=[SMV, SRS], writes=[SNB])
            on2 = on[S16, :, :]
            for g in range(8):
                P.op("act", lambda e, g=g: e.activation(out=on2[:, g, :], in_=hs[:, g, :], func=AF.Identity, scale=srs[:, g:g + 1], bias=snb[:, g:g + 1]),
                     reads=[TMPU, SRS, SNB], writes=[ON])
            P.op("dve", lambda e: e.scalar_tensor_tensor(out=sz[S16, 512:1024], in0=th[S16, :], scalar=1.0, in1=sz[S16, 512:1024], op0=ALU.add, op1=ALU.mult),
                 reads=[TH, SZ], writes=[SZ])
            P.op("pool", lambda e: e.tensor_tensor(out=sz[S16, :], in0=sz[S16, :], in1=gbc[S16, :], op=ALU.mult), reads=[SZ, GBC], writes=[SZ])
            P.op("pool", lambda e: e.tensor_tensor(out=gated[S16, :], in0=on2.rearrange("p g d -> p (g d)"), in1=sz[S16, :], op=ALU.mult),
                 reads=[ON, SZ], writes=[GATED])
            bt, BK = nb(); vg = bfv(bt)
            for k in range(8):
                P.op("pe", lambda e, k=k: e.transpose(vg[:, k * NS:(k + 1) * NS], gated[S16, k * 128:(k + 1) * 128], ident_bf[S16, 0:NS]),
                     reads=[GATED, IDB], writes=[BK])
            P.op("act", lambda e: e.copy(out=gT[:, :, 0:NS], in_=vg[:, 0:8 * NS].rearrange("p (k s) -> p k s", k=8)), reads=[BK], writes=[GT])
            yps = ypre[S16, :]
            for n in range(2):
                bt, BK = nb()
                for k in range(8):
                    P.op("pe", lambda e, k=k, n=n, bt=bt: e.matmul(bt[S16, :], lhsT=gT[:, k, 0:NS], rhs=w_out_sb[:, k, n * 512:(n + 1) * 512],
                                                               start=(k == 0), stop=(k == 7)), reads=[GT, WOUT[n]], writes=[BK])
                P.op("dve", lambda e, n=n, bt=bt: e.scalar_tensor_tensor(out=yps[:, n * 512:(n + 1) * 512], in0=xs_sb[:, n * 512:(n + 1) * 512], scalar=ALPHA,
                                                                     in1=bt[S16, :], op0=ALU.mult, op1=ALU.add), reads=[BK, XSS], writes=[YPRE])
                P.op("dve", lambda e, n=n: e.bn_stats(out=slst[:, n, :], in_=yps[:, n * 512:(n + 1) * 512]), reads=[YPRE], writes=[SLST])
            P.op("dve", lambda e: e.bn_aggr(out=slmv[:], in_=slst[:].rearrange("p a b -> p (a b)")), reads=[SLST], writes=[SLMV])
            P.op("pool", lambda e: e.tensor_scalar(out=slrs[:, 0:1], in0=slmv[:, 1:2], scalar1=LN_EPS, scalar2=None, op0=ALU.add), reads=[SLMV], writes=[SLRS])
            P.op("pool", lambda e: e.tensor_tensor(out=slrs[:, 0:1], in0=slrs[:, 0:1], in1=mhalf[S16, 0:1], op=ALU.pow), reads=[SLRS, MHALF], writes=[SLRS])
            P.op("dve", lambda e: e.scalar_tensor_tensor(out=slrs[:, 1:2], in0=slmv[:, 0:1], scalar=-1.0, in1=slrs[:, 0:1], op0=ALU.mult, op1=ALU.mult),
                 reads=[SLMV, SLRS], writes=[SLRS])
            P.op("act", lambda e: e.activation(out=xs_sb[:], in_=yps, func=AF.Identity, scale=slrs[:, 0:1], bias=slrs[:, 1:2]),
                 reads=[YPRE, SLRS], writes=[XSS])
            P.op("pool", lambda e: e.tensor_tensor(out=xs_sb[:], in0=xs_sb[:], in1=lng[S16, :], op=ALU.mult), reads=[XSS, LNG], writes=[XSS])
            P.op("pool", lambda e: e.tensor_tensor(out=xs_sb[:], in0=xs_sb[:], in1=lnb[S16, :], op=ALU.add), reads=[XSS, LNB], writes=[XSS])
            if l == n_layers - 1:
                P.dma("sp", c_so[5], y_s, xs_sb[:], reads=[XSS])

        def finalize_prompt(l):
            P.dma("sp", c_fin[1], ret_p[l].rearrange("h d e -> d h e"), S_f[:], reads=[SF])
            P.dma("sp", c_fin[2], C_p[l].rearrange("h d e -> d h e"), C_f[:, :, 0:128], reads=[CF])
            P.dma("sp", c_fin[3], n_p[l].rearrange("h d -> d h"), C_f[:, :, 128], reads=[CF], allow_slow_non_contiguous=True)
            for half in range(2):
                bt, BK = nb()
                for c4 in range(4):
                    ch = half * 4 + c4
                    P.op("pe", lambda e, ch=ch, c4=c4, bt=bt: e.transpose(bt[0:3, c4 * 128:(c4 + 1) * 128], convp_sb[:, ch, :], ident_f),
                         reads=[CONVP, CST], writes=[BK])
                P.op("act", lambda e, half=half, bt=bt: e.copy(out=convp_T[:, half * 512:(half + 1) * 512], in_=bt[0:3, :]),
                     reads=[BK], writes=[CONVPT])
            P.dma("sp", c_fin[4], conv_p[l], convp_T[:], reads=[CONVPT])

        def stage(name):
            if stop_at == name:
                raise StopBuild()

        try:
            for l in range(n_layers):
                load_weights(l)
                stage("weights")
                load_params(l)
                stage("params")
                prepass(l)
                stage("prepass")
                OFF = 5
                gens = [main_tile(l, t) for t in range(n_tiles)]
                NSEG = 10
                for step in range(n_tiles * NSEG + OFF):
                    pass
                order = []
                for t in range(n_tiles):
                    for k in range(NSEG):
                        order.append((t * NSEG + k + (0 if True else 0), t, k))
                slots = {}
                for t in range(n_tiles):
                    for k in range(NSEG):
                        slots.setdefault(t * OFF + k, []).append((t, k))
                for sl_ in sorted(slots):
                    for (t, k) in sorted(slots[sl_]):
                        try:
                            next(gens[t])
                        except StopIteration:
                            pass
                stage("tiles")
                finalize_prompt(l)
                if do_sample:
                    sample_path(l)
                stage('sample')
        except StopBuild:
            pass

        for c in P.chans:
            if c.val:
                P.wait_chan("sp", c)
        with nc.Block() as block:
            P.finish(block)
    return nc


def make_consts():
    cst = np.zeros((128, NCST), np.float32)
    cst[:, K_ID:K_ID + 128] = np.eye(128, dtype=np.float32)
    idx = np.arange(128)
    cst[:, K_TRI:K_TRI + 128] = (idx[:, None] <= idx[None, :]).astype(np.float32)
    for h in range(H):
        lg = np.float32(LOGG[h])
        cst[:, K_KDEC + h] = np.exp(lg * (np.float32(L - 1) - idx.astype(np.float32))).astype(np.float32) * np.float32(ISQ)
        cst[:, K_QDEC + h * 128:K_QDEC + (h + 1) * 128] = np.exp(lg * (idx.astype(np.float32) + 1.0 - L)).astype(np.float32)[None, :]
    cst[:, K_ONE:K_ONE + 128] = 1.0
    for h in range(H):
        cst[:, K_GAM + h] = np.float32(GAM[h])
    half = HD // 2
    inv = (np.float32(10000.0) ** (-np.arange(half, dtype=np.float32) / np.float32(half))).astype(np.float32)

    def rope(pos):
        ang = (pos.astype(np.float32)[:, None] * inv[None, :]).astype(np.float32)
        c = np.cos(ang.astype(np.float64)).astype(np.float32)
        s = np.sin(ang.astype(np.float64)).astype(np.float32)
        out = np.zeros((len(pos), 2, HD), np.float32)
        out[:, 0, :half] = c
        out[:, 0, half:] = c
        out[:, 1, :half] = -s
        out[:, 1, half:] = s
        return out

    rope_p = rope(np.arange(T))
    rope_s = rope(np.array([PAST_LEN])).reshape(2 * HD)
    return cst, rope_p, rope_s


_CACHE = {}


def kernel(x_prompt, x_sample, state_ret, state_mlstm_C, state_mlstm_n, state_mlstm_m, state_conv,
           w_in, conv_w, conv_b, b_i, b_f, g_ret, g_m, w_out, ln_g, ln_b):
    n = 8
    if "nc" not in _CACHE:
        _CACHE["nc"] = build_program()
    nc = _CACHE["nc"]
    cst, rope_p, rope_s = make_consts()
    f = lambda a: np.ascontiguousarray(np.asarray(a, dtype=np.float32))
    shared = dict(w_in=f(w_in), conv_w=f(conv_w), conv_b=f(conv_b), b_i=f(b_i), b_f=f(b_f), g_ret=f(g_ret), g_m=f(g_m),
                  w_out=f(w_out), ln_g=f(ln_g), ln_b=f(ln_b), cst=cst, rope_p=rope_p, rope_s=rope_s)
    in_maps = []
    for c in range(n):
        s0, s1 = c * NS, (c + 1) * NS
        m = dict(shared)
        m["xp"] = f(x_prompt[c])
        m["xs"] = f(x_sample[s0:s1, 0, :])
        m["sret"] = f(state_ret[:, s0:s1])
        m["sC"] = f(state_mlstm_C[:, s0:s1])
        m["sn"] = f(state_mlstm_n[:, s0:s1])
        m["sm"] = f(state_mlstm_m[:, s0:s1])
        m["sconv"] = f(state_conv[:, s0:s1])
        in_maps.append(m)
    res = run_bass_kernel_spmd(nc, in_maps, core_ids=list(range(n)))
    R = res.results
    y_p = np.stack([R[c]["y_p"] for c in range(n)], 0)
    y_s = np.concatenate([R[c]["y_s"] for c in range(n)], 0)[:, None, :]
    ret_p = np.stack([R[c]["ret_p"] for c in range(n)], 1)
    C_p = np.stack([R[c]["C_p"] for c in range(n)], 1)
    n_p = np.stack([R[c]["n_p"] for c in range(n)], 1)
    m_p = np.stack([R[c]["m_p"] for c in range(n)], 1)
    conv_p = np.stack([R[c]["conv_p"] for c in range(n)], 1)
    ret_s = np.concatenate([R[c]["ret_s"] for c in range(n)], 1)
    C_s = np.concatenate([R[c]["C_s"] for c in range(n)], 1)
    n_s = np.concatenate([R[c]["n_s"] for c in range(n)], 1)
    m_s = np.concatenate([R[c]["m_s"] for c in range(n)], 1)
    conv_s = np.concatenate([R[c]["conv_s"] for c in range(n)], 1)
    return (y_p, y_s, ret_p, C_p, n_p, m_p, conv_p, ret_s, C_s, n_s, m_s, conv_s)
```

```python
from contextlib import ExitStack
import numpy as np
import concourse.bass as bass
import concourse.mybir as mybir
from concourse.bass_utils import run_bass_kernel_spmd

F32 = mybir.dt.float32
BF16 = mybir.dt.bfloat16
AF = mybir.ActivationFunctionType
ALU = mybir.AluOpType
AX = mybir.AxisListType

D = 1024
T = 2048
NT = 16
L = 128
NS = 16
H = 4
HD = 128
N_IN = 4616
PAST_LEN = 16384
ALPHA = (2 * 2) ** 0.25
GN_EPS = 1e-5
LN_EPS = 1e-5
ISQ = float(HD ** -0.5)
LNISQ = float(np.log(HD ** -0.5))
GAM = [float(np.float32(1.0) - np.float32(2.0) ** np.float32(-5.0 - h)) for h in range(H)]
LOGG = [float(np.log(np.float32(g))) for g in GAM]
GAML = [float(np.exp(np.float32(lg) * L)) for lg in LOGG]

C_RQ, C_RK, C_RV, C_RZ, C_MQK, C_MV, C_MO, C_MZ, C_G = 0, 512, 1024, 1536, 2048, 3072, 3584, 4096, 4608

K_ID = 0
K_TRI = 128
K_KDEC = 256
K_QDEC = 260
K_ONE = 772
K_GAM = 900
NCST = 904


class Buf:
    __slots__ = ("name", "w", "r", "psum")

    def __init__(self, name="", psum=False):
        self.name = name
        self.w = None
        self.r = []
        self.psum = psum


class Chan:
    def __init__(self, prog, name):
        self.sem = prog.new_sem(name)
        self.val = 0


class Prog:
    ENG = ("pe", "act", "dve", "pool", "sp")

    def __init__(self, nc, stack):
        self.nc = nc
        self.stack = stack
        self.items = {e: [] for e in self.ENG}
        self.cnt = {e: 0 for e in self.ENG}
        self.esem = {e: self.new_sem("prog_" + e) for e in self.ENG}
        self.waited = {e: {} for e in self.ENG}
        self.chans = []

    def new_sem(self, name):
        return self.stack.enter_context(self.nc.semaphore(name))

    def chan(self, name):
        c = Chan(self, name)
        self.chans.append(c)
        return c

    def _need(self, eng, reads, writes):
        need = {}

        def add(ev):
            if ev is None:
                return
            k = (ev[0], id(ev[1]) if ev[0] == 'c' else ev[1])
            if k not in need or need[k][2] < ev[2]:
                need[k] = ev

        for b in reads:
            add(b.w)
            if b.psum:
                for r in b.r:
                    if not (r[0] == 'e' and r[1] == eng):
                        add(r)
        for b in writes:
            if b.w is not None and not (b.w[0] == 'e' and b.w[1] == eng):
                add(b.w)
            for r in b.r:
                if not (r[0] == 'e' and r[1] == eng):
                    add(r)
        wd = self.waited[eng]
        for k, ev in need.items():
            if wd.get(k, 0) >= ev[2]:
                continue
            wd[k] = ev[2]
            if ev[0] == 'e':
                self.items[eng].append(('we', ev[1], ev[2]))
            else:
                self.items[eng].append(('wc', ev[1].sem, ev[2]))

    def op(self, eng, fn, reads=(), writes=()):
        self._need(eng, reads, writes)
        self.cnt[eng] += 1
        ev = ('e', eng, self.cnt[eng])
        self.items[eng].append(('op', fn, self.cnt[eng]))
        for b in reads:
            b.r.append(ev)
        for b in writes:
            b.w = ev
            b.r = []
        return ev

    def dma(self, eng, chan, out, in_, reads=(), writes=(), **kw):
        self._need(eng, reads, writes)
        chan.val += 16
        ev = ('c', chan, chan.val)
        self.items[eng].append(('dma', lambda e, out=out, in_=in_, kw=kw, sem=chan.sem:
                                e.dma_start(out=out, in_=in_, **kw).then_inc(sem, 16)))
        for b in reads:
            b.r.append(ev)
        for b in writes:
            b.w = ev
            b.r = []
        return ev

    def barrier(self):
        for eng in self.ENG:
            wd = self.waited[eng]
            for e2 in self.ENG:
                if e2 == eng or self.cnt[e2] == 0:
                    continue
                k = ('e', e2)
                if wd.get(k, 0) >= self.cnt[e2]:
                    continue
                wd[k] = self.cnt[e2]
                self.items[eng].append(('we', e2, self.cnt[e2]))
            for c in self.chans:
                if c.val == 0:
                    continue
                k = ('c', id(c))
                if wd.get(k, 0) >= c.val:
                    continue
                wd[k] = c.val
                self.items[eng].append(('wc', c.sem, c.val))

    def wait_chan(self, eng, chan):
        self.items[eng].append(('wc', chan.sem, chan.val))

    def finish(self, block):
        targets = {e: set() for e in self.ENG}
        for e in self.ENG:
            for it in self.items[e]:
                if it[0] == 'we':
                    targets[it[1]].add(it[2])
        rank = {}
        for e in self.ENG:
            rank[e] = {s_: i + 1 for i, s_ in enumerate(sorted(targets[e]))}
        self.n_signals = {e: len(rank[e]) for e in self.ENG}

        def run(eng, e):
            sem = self.esem[eng]
            rk = rank[eng]
            for it in self.items[eng]:
                if it[0] == 'we':
                    e.wait_ge(self.esem[it[1]], rank[it[1]][it[2]])
                elif it[0] == 'wc':
                    e.wait_ge(it[1], it[2])
                elif it[0] == 'op':
                    ins = it[1](e)
                    if it[2] in rk:
                        ins.then_inc(sem, 1)
                else:
                    it[1](e)

        @block.tensor
        def _(e):
            run("pe", e)

        @block.scalar
        def _(e):
            run("act", e)

        @block.vector
        def _(e):
            run("dve", e)

        @block.gpsimd
        def _(e):
            run("pool", e)

        @block.sync
        def _(e):
            run("sp", e)


class StopBuild(Exception):
    pass


def build_program(n_layers=2, n_tiles=NT, do_sample=True, dbg=None, stop_at=None):
    nc = bass.Bass("TRN2", target_bir_lowering=False)
    dt_in = lambda name, shape: nc.dram_tensor(name, shape, F32, kind="ExternalInput").ap()
    dt_out = lambda name, shape: nc.dram_tensor(name, shape, F32, kind="ExternalOutput").ap()
    xp = dt_in("xp", [T, D])
    xs = dt_in("xs", [NS, D])
    sret = dt_in("sret", [2, NS, H, HD, HD])
    sC = dt_in("sC", [2, NS, H, HD, HD])
    sn = dt_in("sn", [2, NS, H, HD])
    sm = dt_in("sm", [2, NS, H])
    sconv = dt_in("sconv", [2, NS, 3, D])
    w_in = dt_in("w_in", [2, D, N_IN])
    conv_w = dt_in("conv_w", [2, 4, D])
    conv_b = dt_in("conv_b", [2, D])
    b_i = dt_in("b_i", [2, H])
    b_f = dt_in("b_f", [2, H])
    g_ret = dt_in("g_ret", [2, 512])
    g_m = dt_in("g_m", [2, 512])
    w_out = dt_in("w_out", [2, D, D])
    ln_g = dt_in("ln_g", [2, D])
    ln_b = dt_in("ln_b", [2, D])
    cst_d = dt_in("cst", [128, NCST])
    rope_p = dt_in("rope_p", [T, 2, HD])
    rope_s = dt_in("rope_s", [2 * HD])

    y_p = dt_out("y_p", [T, D])
    y_s = dt_out("y_s", [NS, D])
    ret_p = dt_out("ret_p", [2, H, HD, HD])
    C_p = dt_out("C_p", [2, H, HD, HD])
    n_p = dt_out("n_p", [2, H, HD])
    m_p = dt_out("m_p", [2, H])
    conv_p = dt_out("conv_p", [2, 3, D])
    ret_s = dt_out("ret_s", [2, NS, H, HD, HD])
    C_s = dt_out("C_s", [2, NS, H, HD, HD])
    n_s = dt_out("n_s", [2, NS, H, HD])
    m_s = dt_out("m_s", [2, NS, H])
    conv_s = dt_out("conv_s", [2, NS, 3, D])
    y0 = nc.dram_tensor("y0_scratch", [T, D], F32, kind="Internal").ap()
    dbg_out = {}
    if dbg:
        for name, shape in dbg.items():
            dbg_out[name] = dt_out("dbg_" + name, shape)

    with ExitStack() as st:
        P = Prog(nc, st)
        sb = lambda name, shape, dt=F32: st.enter_context(nc.sbuf_tensor("s_" + name, shape, dt))
        out_chans = []

        def ochan(name):
            c = P.chan(name)
            out_chans.append(c)
            return c

        banks = []
        for i in range(8):
            t_ = st.enter_context(nc.psum_tensor("bank%d" % i, [128, 512], F32))
            banks.append((t_, Buf("bank%d" % i, psum=True)))
        bank_ctr = [0]

        def nb():
            i = bank_ctr[0] % 8
            bank_ctr[0] += 1
            return banks[i]

        def bfv(bank_t):
            return bank_t[:].bitcast(BF16)

        cst = sb("cst", [128, NCST]); CST = Buf("cst")
        ident_bf = sb("ident_bf", [128, 128], BF16); IDB = Buf()
        mask_bf = sb("mask_bf", [128, 128], BF16); MSK = Buf()
        c_ld = P.chan("c_ld")
        P.dma("sp", c_ld, cst[:], cst_d, writes=[CST])
        ident_f = cst[:, K_ID:K_ID + 128]
        tri_f = cst[:, K_TRI:K_TRI + 128]
        ones_f = cst[:, K_ONE:K_ONE + 128]
        P.op("dve", lambda e: e.tensor_copy(out=ident_bf[:], in_=ident_f), reads=[CST], writes=[IDB])
        P.op("dve", lambda e: e.tensor_copy(out=mask_bf[:], in_=tri_f), reads=[CST], writes=[MSK])
        kdec = lambda h: cst[:, K_KDEC + h:K_KDEC + h + 1]
        qdecT = cst[:, K_QDEC:K_QDEC + 512].rearrange("p (h l) -> p h l", h=4)

        w_in_sb = sb("w_in_sb", [128, 8, N_IN], BF16)
        WIN = [Buf("win%d" % i) for i in range(10)]
        w_out_sb = sb("w_out_sb", [128, 8, D], BF16)
        WOUT = [Buf("wout0"), Buf("wout1")]
        c_win = [P.chan("c_win%d" % i) for i in range(10)]
        c_wout = [P.chan("c_wout%d" % i) for i in range(2)]
        gbc = sb("gbc", [128, 1024]); GBC = Buf()
        lng = sb("lng", [128, 1024]); LNG = Buf()
        lnb = sb("lnb", [128, 1024]); LNB = Buf()
        bias8 = sb("bias8", [128, 8]); BIAS8 = Buf()
        cwb_in = sb("cwb_in", [40, 128]); CWBIN = Buf()
        cwT = sb("cwT", [128, 40]); CWT = Buf()
        diag = sb("diag", [128, 32, 128], BF16); DIAG = Buf()
        c_par = [P.chan("c_par%d" % i) for i in range(8)]

        def wblk(i):
            return (i * 512, min((i + 1) * 512, N_IN))

        def load_weights(l):
            wv = w_in[l].rearrange("(k p) n -> p k n", p=128)
            for i in range(10):
                a, b_ = wblk(i)
                P.dma("pool", c_win[i], w_in_sb[:, :, a:b_], wv[:, :, a:b_], writes=[WIN[i]])
            wo = w_out[l].rearrange("(k p) n -> p k n", p=128)
            for i in range(2):
                P.dma("pool", c_wout[i], w_out_sb[:, :, i * 512:(i + 1) * 512], wo[:, :, i * 512:(i + 1) * 512],
                      writes=[WOUT[i]])

        def load_params(l):
            P.dma("sp", c_par[0], gbc[:, 0:512], g_ret[l].partition_broadcast(128), writes=[GBC])
            P.dma("sp", c_par[1], gbc[:, 512:1024], g_m[l].partition_broadcast(128), writes=[GBC])
            P.dma("sp", c_par[2], lng[:], ln_g[l].partition_broadcast(128), writes=[LNG])
            P.dma("sp", c_par[3], lnb[:], ln_b[l].partition_broadcast(128), writes=[LNB])
            P.dma("sp", c_par[4], bias8[:, 0:4], b_i[l].partition_broadcast(128), writes=[BIAS8])
            P.dma("sp", c_par[5], bias8[:, 4:8], b_f[l].partition_broadcast(128), writes=[BIAS8])
            P.dma("sp", c_par[6], cwb_in[0:32, :], conv_w[l].rearrange("j (ch c) -> (j ch) c", c=128), writes=[CWBIN])
            P.dma("sp", c_par[7], cwb_in[32:40, :], conv_b[l].rearrange("(ch c) -> ch c", c=128), writes=[CWBIN])
            P.op("pool", lambda e: e.tensor_scalar(out=gbc[:, 512:1024], in0=gbc[:, 512:1024], scalar1=0.5, scalar2=None,
                                                   op0=ALU.mult), reads=[GBC], writes=[GBC])
            bt, BK = nb()
            P.op("pe", lambda e: e.transpose(bt[:, 0:40], cwb_in[0:40, :], ident_f[0:40, 0:40]), reads=[CWBIN, CST], writes=[BK])
            P.op("act", lambda e: e.copy(out=cwT[:], in_=bt[:, 0:40]), reads=[BK], writes=[CWT])
            for ch in range(8):
                for j in range(4):
                    idx = j * 8 + ch
                    P.op("pool", lambda e, ch=ch, j=j, idx=idx: e.tensor_scalar(
                        out=diag[:, ch * 4 + j, :], in0=ident_f, scalar1=cwT[:, idx:idx + 1], scalar2=None, op0=ALU.mult),
                        reads=[CST, CWT], writes=[DIAG])

        gates = sb("gates", [128, NT, 8]); GATES = Buf()
        lneg = sb("lneg", [128, NT, 4]); LNEG = Buf()
        bneg = sb("bneg", [128, NT, 4]); BNEG = Buf()
        u_sb = sb("u_sb", [128, NT, 4]); USB = Buf()
        uT_sb = sb("uT_sb", [64, 128]); UTS = Buf()
        umaxc = sb("umaxc", [64, 1]); UMX = Buf()
        row = sb("row", [1, 6, 64]); ROW = Buf()
        cw_b = sb("cw_b", [128, 2, NT + 1, 4]); CWB = Buf()
        pk = sb("pk", [128, NT, 4]); PK = Buf()
        thr = sb("thr", [128, NT, 4]); THR = Buf()
        tmp64 = sb("tmp64", [128, NT, 4]); TMP64 = Buf()

        x_sb = [sb("x_sb%d" % i, [128, D]) for i in range(3)]; XSB = [Buf(), Buf(), Buf()]
        rope_sb = [sb("rope_sb%d" % i, [128, 2, HD]) for i in range(2)]; ROPE = [Buf(), Buf()]
        c_x = [P.chan("c_x0"), P.chan("c_x1"), P.chan("c_x2")]
        c_rope = [P.chan("c_rope0"), P.chan("c_rope1")]
        x_bf2 = [sb("x_bf%d" % i, [128, D], BF16) for i in range(2)]; XBF2 = [Buf(), Buf()]
        x_bf = x_bf2[0]; XBF = XBF2[0]
        c_xb = [P.chan("c_xb0"), P.chan("c_xb1")]
        xT2 = [sb("xT%d" % i, [128, 8, 128], BF16) for i in range(2)]; XT2 = [Buf(), Buf()]
        xT = xT2[0]; XT = XT2[0]
        tmp_t = sb("tmp_t", [128, 8, HD]); TMPT = Buf(); TMPT_K = Buf()
        tmp_u = sb("tmp_u", [128, 8, HD]); TMPU = Buf(); TMPU_K = Buf()
        qk_rot = sb("qk_rot", [128, 8, HD], BF16); QKR = Buf(); QKR_K = Buf()
        v2 = sb("v2", [128, 4, HD], BF16); V2 = Buf()
        sz2 = [sb("sz%d" % i, [128, 1024]) for i in range(2)]; SZ2 = [Buf(), Buf()]
        sz = sz2[0]; SZ = SZ2[0]
        SZA2 = [Buf(), Buf()]; SZA = SZA2[0]
        qT2 = sb("qT2", [128, 4, 128], BF16); QT2 = Buf()
        kT = sb("kT", [128, 4, 128], BF16); KT = Buf()
        s2 = sb("s2", [128, 4, 128], BF16); S2 = Buf()
        S_f = sb("S_f", [128, 4, HD]); SF = Buf()
        S_bf = sb("S_bf", [128, 4, HD], BF16); SBF = Buf()
        hist = sb("hist", [128, 8, 131], BF16); HIST = Buf()
        qkm = sb("qkm", [128, 8, 128], BF16); QKM = Buf()
        kp = sb("kp", [128, 4, 128], BF16); KP = Buf()
        v1 = sb("v1", [128, 4, 130], BF16); V1 = Buf()
        th = sb("th", [128, 512]); TH = Buf()
        s2m = sb("s2m", [128, 4, 128], BF16); S2M = Buf()
        C_f = sb("C_f", [128, 4, 130]); CF = Buf()
        C_bf = sb("C_bf", [128, 4, 130], BF16); CBF = Buf()
        dn = sb("dn", [128, 4]); DN = Buf()
        hm = sb("hm", [128, 4, HD]); HM = Buf()
        o_sb = sb("o_sb", [128, 4, HD]); OSB = Buf()
        stats = sb("stats", [128, 8, 6]); STATS = Buf()
        mv = sb("mv", [128, 8, 2]); MV = Buf()
        rstd = sb("rstd", [128, 8]); RSTD = Buf()
        nbias = sb("nbias", [128, 8]); NBIAS = Buf()
        mhalf = sb("mhalf", [128, 8]); MHALF = Buf()
        on = sb("on", [128, 8, HD]); ON = Buf(); ON_M = Buf()
        gated = sb("gated", [128, 1024], BF16); GATED = Buf(); GATED_M = Buf()
        gT = sb("gT", [128, 8, 128], BF16); GT = Buf()
        ypre = sb("ypre", [128, D]); YPRE = Buf(); YPRE_B = Buf()
        lstats = sb("lstats", [128, 2, 6]); LSTATS = Buf()
        lmv = sb("lmv", [128, 2]); LMV = Buf()
        lrs = sb("lrs", [128, 2]); LRS = Buf()
        c_y = [ochan("c_y0"), ochan("c_y1")]
        convp_sb = sb("convp_sb", [128, 8, 3]); CONVP = Buf()
        convp_T = sb("convp_T", [3, D]); CONVPT = Buf()
        c_fin = [ochan("c_fin%d" % i) for i in range(5)]

        P.op("pool", lambda e: e.memset(mhalf[:], -0.5), writes=[MHALF])

        def dbg_dump(name, src_ap, bufs):
            if name in dbg_out:
                c = ochan("c_dbg_" + name)
                P.dma("sp", c, dbg_out[name], src_ap, reads=bufs)

        def load_x(l, t, par):
            src = xp if l == 0 else y0
            P.dma("sp", c_x[par], x_sb[par][:], src[t * 128:(t + 1) * 128, :], writes=[XSB[par]],
                  reads=([Y0B[t]] if l > 0 else []))

        def load_xbf(l, t, par):
            src = xp if l == 0 else y0
            P.dma("pool", c_xb[par], x_bf2[par][:], src[t * 128:(t + 1) * 128, :], writes=[XBF2[par]],
                  reads=([Y0B[t]] if l > 0 else []))

        def load_main(l, t):
            par = t % 2
            load_x(l, t, t % 3)
            P.dma("sp", c_rope[par], rope_sb[par][:], rope_p[t * 128:(t + 1) * 128], writes=[ROPE[par]])

        def make_xT(par, xT=None, XT=None, xi=None):
            if xT is None:
                xT, XT = xT2[0], XT2[0]
            if xi is None:
                xi = par
            xb = x_bf2[par]; XB = XBF2[par]
            P.op("dve", lambda e: e.tensor_copy(out=xb[:], in_=x_sb[xi][:]), reads=[XSB[xi]], writes=[XB])
            bt, BK = nb()
            v = bfv(bt)
            for k in range(8):
                P.op("pe", lambda e, k=k: e.transpose(v[:, k * 128:(k + 1) * 128], xb[:, k * 128:(k + 1) * 128], ident_bf[:]),
                     reads=[XB, IDB], writes=[BK])
            P.op("act", lambda e: e.copy(out=xT[:].rearrange("p k c -> p (k c)"), in_=v), reads=[BK], writes=[XT])

        Y0B = [Buf("y0_%d" % t) for t in range(NT)]

        def prepass(l):
            load_x(l, 0, 0)
            for t in range(n_tiles):
                par = t % 2
                if t + 1 < n_tiles:
                    load_x(l, t + 1, (t + 1) % 2)
                make_xT(par)
                bt, BK = nb()
                for k in range(8):
                    P.op("pe", lambda e, k=k: e.matmul(bt[:, 0:8], lhsT=xT[:, k, :], rhs=w_in_sb[:, k, C_G:C_G + 8],
                                                       start=(k == 0), stop=(k == 7)), reads=[XT, WIN[9]], writes=[BK])
                P.op("dve", lambda e, t=t: e.tensor_tensor(out=gates[:, t, :], in0=bt[:, 0:8], in1=bias8[:], op=ALU.add),
                     reads=[BK, BIAS8], writes=[GATES])
            nt = n_tiles
            stage('pp_loop')
            P.op("act", lambda e: e.activation(out=lneg[:, 0:nt, :], in_=gates[:, 0:nt, 4:8], func=AF.Exp, scale=-1.0),
                 reads=[GATES], writes=[LNEG])
            P.op("act", lambda e: e.activation(out=lneg[:, 0:nt, :], in_=lneg[:, 0:nt, :], func=AF.Ln, bias=1.0),
                 reads=[LNEG], writes=[LNEG])
            dbg_dump('gates', gates[:], [GATES])
            dbg_dump('lneg', lneg[:], [LNEG])
            stage('pp_a')
            bt, BK = nb()
            ln2 = lneg[:].rearrange("p t h -> p (t h)")
            P.op("pe", lambda e: e.matmul(bt[:, 0:nt * 4], lhsT=tri_f, rhs=ln2[:, 0:nt * 4], start=True, stop=True),
                 reads=[LNEG, CST], writes=[BK])
            stage('pp_b')
            bt2, BK2 = nb()
            P.op("pe", lambda e: e.matmul(bt2[0:1, 0:nt * 4], lhsT=ones_f[:, 0:1], rhs=ln2[:, 0:nt * 4], start=True, stop=True),
                 reads=[LNEG, CST], writes=[BK2])
            stage('pp_c')
            P.op("act", lambda e: e.copy(out=bneg[:].rearrange("p t h -> p (t h)")[:, 0:nt * 4], in_=bt[:, 0:nt * 4]),
                 reads=[BK], writes=[BNEG])
            P.op("dve", lambda e: e.tensor_tensor(out=u_sb[:, 0:nt, :], in0=gates[:, 0:nt, 0:4], in1=bneg[:, 0:nt, :], op=ALU.add),
                 reads=[GATES, BNEG], writes=[USB])
            P.op("act", lambda e: e.copy(out=row[0:1, 1, 0:nt * 4], in_=bt2[0:1, 0:nt * 4]), reads=[BK2], writes=[ROW])
            stage('pp_cum')
            bt3, BK3 = nb()
            u2 = u_sb[:].rearrange("p t h -> p (t h)")
            P.op("pe", lambda e: e.transpose(bt3[0:nt * 4, 0:128], u2[:, 0:nt * 4], ident_f), reads=[USB, CST], writes=[BK3])
            P.op("dve", lambda e: e.tensor_reduce(out=umaxc[0:nt * 4, :], in_=bt3[0:nt * 4, 0:128], axis=AX.X, op=ALU.max),
                 reads=[BK3], writes=[UMX])
            bt4, BK4 = nb()
            P.op("pe", lambda e: e.transpose(bt4[0:1, 0:nt * 4], umaxc[0:nt * 4, 0:1], ident_f[0:nt * 4, 0:nt * 4]),
                 reads=[UMX, CST], writes=[BK4])
            P.op("act", lambda e: e.copy(out=row[0:1, 0, 0:nt * 4], in_=bt4[0:1, 0:nt * 4]), reads=[BK4], writes=[ROW])
            stage('pp_umax')
            rv_ = lambda i: row[0:1, i, 0:nt * 4].rearrange("p (t h) -> p t h", h=4)
            P.op("dve", lambda e: e.tensor_scalar(out=row[0:1, 3, 0:nt * 4], in0=row[0:1, 1, 0:nt * 4], scalar1=-1.0, scalar2=None,
                                                  op0=ALU.mult), reads=[ROW], writes=[ROW])
            for h in range(4):
                P.op("dve", lambda e, h=h: e.tensor_tensor_scan(out=rv_(2)[:, :, h], data0=rv_(0)[:, :, h], data1=rv_(3)[:, :, h],
                                                                initial=0.0, op0=ALU.max, op1=ALU.add), reads=[ROW], writes=[ROW])
            P.op("dve", lambda e: e.tensor_tensor(out=row[0:1, 3, 0:nt * 4], in0=row[0:1, 2, 0:nt * 4], in1=row[0:1, 1, 0:nt * 4],
                                                  op=ALU.add), reads=[ROW], writes=[ROW])
            P.op("dve", lambda e: e.memset(row[0:1, 5, 0:4], 0.0), reads=[ROW], writes=[ROW])
            if nt > 1:
                P.op("dve", lambda e: e.tensor_copy(out=row[0:1, 5, 4:nt * 4], in_=row[0:1, 2, 0:(nt - 1) * 4]), reads=[ROW], writes=[ROW])
            P.op("dve", lambda e: e.tensor_tensor(out=row[0:1, 4, 0:nt * 4], in0=row[0:1, 5, 0:nt * 4], in1=row[0:1, 3, 0:nt * 4],
                                                  op=ALU.subtract), reads=[ROW], writes=[ROW])
            P.op("act", lambda e: e.activation(out=row[0:1, 4, 0:nt * 4], in_=row[0:1, 4, 0:nt * 4], func=AF.Exp),
                 reads=[ROW], writes=[ROW])
            stage('pp_scan')
            bt5, BK5 = nb()
            P.op("pe", lambda e: e.matmul(bt5[:, 0:128], lhsT=ones_f[0:1, :], rhs=row[0:1, 3:5, :].rearrange("p a b -> p (a b)"),
                                          start=True, stop=True), reads=[ROW, CST], writes=[BK5])
            P.op("act", lambda e: e.copy(out=cw_b[:, :, 0:NT, :], in_=bt5[:, 0:128].rearrange("p (a t h) -> p a t h", a=2, h=4)),
                 reads=[BK5], writes=[CWB])
            P.op("pool", lambda e: e.memset(cw_b[:, :, NT, :], 1.0), reads=[CWB], writes=[CWB])
            P.op("dve", lambda e: e.tensor_tensor(out=tmp64[:, 0:nt, :], in0=u_sb[:, 0:nt, :], in1=cw_b[:, 0, 0:nt, :], op=ALU.subtract),
                 reads=[USB, CWB], writes=[TMP64])
            P.op("dve", lambda e: e.tensor_scalar(out=tmp64[:, 0:nt, :], in0=tmp64[:, 0:nt, :], scalar1=LNISQ, scalar2=None, op0=ALU.add),
                 reads=[TMP64], writes=[TMP64])
            P.op("act", lambda e: e.activation(out=pk[:, 0:nt, :], in_=tmp64[:, 0:nt, :], func=AF.Exp),
                 reads=[TMP64], writes=[PK])
            P.op("dve", lambda e: e.tensor_tensor(out=tmp64[:, 0:nt, :], in0=bneg[:, 0:nt, :], in1=cw_b[:, 0, 0:nt, :], op=ALU.subtract),
                 reads=[BNEG, CWB, PK], writes=[TMP64])
            P.op("act", lambda e: e.activation(out=thr[:, 0:nt, :], in_=tmp64[:, 0:nt, :], func=AF.Exp), reads=[TMP64], writes=[THR])
            P.dma("sp", c_fin[0], m_p[l:l + 1, :], row[0:1, 2, (nt - 1) * 4:nt * 4], reads=[ROW])

        def main_tile(l, t):
            par = t % 2
            last = (t == n_tiles - 1)
            xT = xT2[par]; XT = XT2[par]
            xi = t % 3
            sz = sz2[par]; SZ = SZ2[par]; SZA = SZA2[par]
            if t == 0:
                load_main(l, 0)
            if not last:
                load_main(l, t + 1)
            make_xT(par, xT, XT, xi)

            def proj(bt, BK, c0, n, wb):
                for k in range(8):
                    P.op("pe", lambda e, k=k: e.matmul(bt[:, 0:n], lhsT=xT[:, k, :], rhs=w_in_sb[:, k, c0:c0 + n],
                                                       start=(k == 0), stop=(k == 7)), reads=[XT, WIN[wb]], writes=[BK])

            bq, BQ = nb(); proj(bq, BQ, C_RQ, 512, 0)
            bk_, BKK = nb(); proj(bk_, BKK, C_RK, 512, 1)
            cos2 = rope_sb[par][:, 0, :]
            sin2 = rope_sb[par][:, 1, :]
            for i, (bt, BK) in enumerate(((bq, BQ), (bk_, BKK))):
                src = bt[:].rearrange("p (h d) -> p h d", h=4)
                dst_t = tmp_t[:, i * 4:(i + 1) * 4, :]
                dst_u = tmp_u[:, i * 4:(i + 1) * 4, :]
                TT_ = TMPT if i == 0 else TMPT_K
                TU_ = TMPU if i == 0 else TMPU_K
                P.op("dve", lambda e, src=src, dst_t=dst_t: e.tensor_tensor(
                    out=dst_t, in0=src, in1=cos2.unsqueeze(1).broadcast_to([128, 4, HD]), op=ALU.mult),
                    reads=[BK, ROPE[par]], writes=[TT_])
                P.op("dve", lambda e, src=src, dst_u=dst_u: e.tensor_tensor(
                    out=dst_u[:, :, 0:64], in0=src[:, :, 64:128], in1=sin2[:, 0:64].unsqueeze(1).broadcast_to([128, 4, 64]), op=ALU.mult),
                    reads=[BK, ROPE[par]], writes=[TU_])
                P.op("dve", lambda e, src=src, dst_u=dst_u: e.tensor_tensor(
                    out=dst_u[:, :, 64:128], in0=src[:, :, 0:64], in1=sin2[:, 64:128].unsqueeze(1).broadcast_to([128, 4, 64]), op=ALU.mult),
                    reads=[BK, ROPE[par]], writes=[TU_])
                if i == 0:
                    P.op("pool", lambda e: e.tensor_tensor(out=qk_rot[:, 0:4, :], in0=tmp_t[:, 0:4, :], in1=tmp_u[:, 0:4, :], op=ALU.add),
                         reads=[TMPT, TMPU], writes=[QKR])
            P.op("dve", lambda e: e.tensor_tensor(out=qk_rot[:, 4:8, :], in0=tmp_t[:, 4:8, :], in1=tmp_u[:, 4:8, :], op=ALU.add),
                 reads=[TMPT_K, TMPU_K], writes=[QKR_K])
            yield
            bv, BV = nb(); proj(bv, BV, C_RV, 512, 2)
            bz, BZ = nb(); proj(bz, BZ, C_RZ, 512, 3)
            for h in range(4):
                P.op("act", lambda e, h=h: e.activation(out=v2[:, h, :], in_=bv[:, h * 128:(h + 1) * 128], func=AF.Copy, scale=kdec(h)),
                     reads=[BV, CST], writes=[V2])
            P.op("act", lambda e: e.activation(out=sz[:, 0:512], in_=bz[:], func=AF.Silu), reads=[BZ], writes=[SZA])
            P.op("pool", lambda e: e.tensor_tensor(out=sz[:, 0:512], in0=sz[:, 0:512], in1=gbc[:, 0:512], op=ALU.mult), reads=[SZA, GBC], writes=[SZA])
            yield
            btq, BTQ = nb(); vtq = bfv(btq)
            btk, BTK = nb(); vtk = bfv(btk)
            for g in range(4):
                P.op("pe", lambda e, g=g: e.transpose(vtq[:, g * 128:(g + 1) * 128], qk_rot[:, g, :], ident_bf[:]),
                     reads=[QKR, IDB], writes=[BTQ])
            for g in range(4):
                P.op("pe", lambda e, g=g: e.transpose(vtk[:, g * 128:(g + 1) * 128], qk_rot[:, 4 + g, :], ident_bf[:]),
                     reads=[QKR_K, IDB], writes=[BTK])
            P.op("dve", lambda e: e.tensor_tensor(out=qT2[:], in0=vtq[:, 0:512].rearrange("p (h l) -> p h l", h=4), in1=qdecT, op=ALU.mult),
                 reads=[BTQ, CST], writes=[QT2])
            P.op("act", lambda e: e.copy(out=kT[:].rearrange("p h l -> p (h l)"), in_=vtk[:, 0:512]), reads=[BTK], writes=[KT])
            yield
            bs, BS = nb()
            for h in range(4):
                P.op("pe", lambda e, h=h: e.matmul(bs[:, h * 128:(h + 1) * 128], lhsT=kT[:, h, :], rhs=qT2[:, h, :], start=True, stop=True),
                     reads=[KT, QT2], writes=[BS])
            P.op("dve", lambda e: e.tensor_tensor(out=s2[:], in0=bs[:].rearrange("p (h l) -> p h l", h=4),
                                                  in1=mask_bf[:].unsqueeze(1).broadcast_to([128, 4, 128]), op=ALU.mult),
                 reads=[BS, MSK], writes=[S2])
            yield
            bo, BO = nb()
            for h in range(4):
                first = (t == 0)
                P.op("pe", lambda e, h=h, first=first: e.matmul(bo[:, h * 128:(h + 1) * 128], lhsT=s2[:, h, :], rhs=v2[:, h, :],
                                                                start=True, stop=first), reads=[S2, V2], writes=[BO])
                if not first:
                    P.op("pe", lambda e, h=h: e.matmul(bo[:, h * 128:(h + 1) * 128], lhsT=qT2[:, h, :], rhs=S_bf[:, h, :],
                                                       start=False, stop=True), reads=[QT2, SBF], writes=[BO])
            bu, BU = nb()
            for h in range(4):
                P.op("pe", lambda e, h=h: e.matmul(bu[:, h * 128:(h + 1) * 128], lhsT=qk_rot[:, 4 + h, :], rhs=v2[:, h, :],
                                                   start=True, stop=True), reads=[QKR_K, V2], writes=[BU])
            for h in range(4):
                if t == 0:
                    P.op("dve", lambda e, h=h: e.tensor_copy(out=S_f[:, h, :], in_=bu[:, h * 128:(h + 1) * 128]), reads=[BU], writes=[SF])
                else:
                    P.op("dve", lambda e, h=h: e.scalar_tensor_tensor(out=S_f[:, h, :], in0=S_f[:, h, :], scalar=GAML[h],
                                                                     in1=bu[:, h * 128:(h + 1) * 128], op0=ALU.mult, op1=ALU.add),
                         reads=[BU, SF], writes=[SF])
            if not last:
                for h in range(4):
                    P.op("act", lambda e, h=h: e.activation(out=S_bf[:, h, :], in_=S_f[:, h, :], func=AF.Copy, scale=GAML[h]),
                         reads=[SF], writes=[SBF])
            P.op("act", lambda e: e.copy(out=o_sb[:].rearrange("p h d -> p (h d)"), in_=bo[:]), reads=[BO], writes=[OSB])
            for h in range(4):
                P.op("dve", lambda e, h=h: e.bn_stats(out=stats[:, h, :], in_=o_sb[:, h, :]), reads=[OSB], writes=[STATS])

            yield
            bc0, BC0 = nb(); bc1, BC1 = nb()
            for ch in range(8):
                bt, BK = (bc0, BC0) if ch < 4 else (bc1, BC1)
                c0 = C_MQK + ch * 128
                wb = 4 + ch // 4
                for k in range(8):
                    P.op("pe", lambda e, k=k, bt=bt, ch=ch, c0=c0: e.matmul(bt[:, (ch % 4) * 128:(ch % 4 + 1) * 128],
                                                                        lhsT=w_in_sb[:, k, c0:c0 + 128], rhs=xT[:, k, :],
                                                                        start=(k == 0), stop=(k == 7)),
                         reads=[XT, WIN[wb]], writes=[BK])
            if t == 0:
                P.op("pool", lambda e: e.memset(hist[:, :, 0:3], 0.0), writes=[HIST])
            else:
                P.op("pool", lambda e: e.tensor_copy(out=hist[:, :, 0:3], in_=hist[:, :, 128:131]), reads=[HIST], writes=[HIST])
            for i, (bt, BK) in enumerate(((bc0, BC0), (bc1, BC1))):
                P.op("act", lambda e, i=i, bt=bt: e.copy(out=hist[:, i * 4:(i + 1) * 4, 3:131], in_=bt[:].rearrange("p (c t) -> p c t", c=4)),
                     reads=[BK], writes=[HIST])
                if last:
                    P.op("dve", lambda e, i=i, bt=bt: e.tensor_copy(out=convp_sb[:, i * 4:(i + 1) * 4, :],
                                                                    in_=bt[:].rearrange("p (c t) -> p c t", c=4)[:, :, 125:128]),
                         reads=[BK], writes=[CONVP])
            bd0, BD0 = nb(); bd1, BD1 = nb()
            for ch in range(8):
                bt, BK = (bd0, BD0) if ch < 4 else (bd1, BD1)
                for j in range(4):
                    P.op("pe", lambda e, j=j, bt=bt, ch=ch: e.matmul(bt[:, (ch % 4) * 128:(ch % 4 + 1) * 128], lhsT=diag[:, ch * 4 + j, :],
                                                                 rhs=hist[:, ch, j:j + 128], start=(j == 0), stop=(j == 3)),
                         reads=[DIAG, HIST], writes=[BK])
            for ch in range(8):
                bt, BK = (bd0, BD0) if ch < 4 else (bd1, BD1)
                P.op("act", lambda e, bt=bt, ch=ch: e.activation(out=qkm[:, ch, :], in_=bt[:, (ch % 4) * 128:(ch % 4 + 1) * 128],
                                                             func=AF.Silu, bias=cwT[:, 32 + ch:33 + ch]),
                     reads=[BK, CWT], writes=[QKM])
            yield
            bkt, BKT = nb(); vkt = bfv(bkt)
            for h in range(4):
                P.op("pe", lambda e, h=h: e.transpose(vkt[:, h * 128:(h + 1) * 128], qkm[:, 4 + h, :], ident_bf[:]),
                     reads=[QKM, IDB], writes=[BKT])
            for h in range(4):
                P.op("act", lambda e, h=h: e.activation(out=kp[:, h, :], in_=vkt[:, h * 128:(h + 1) * 128], func=AF.Copy,
                                                        scale=pk[:, t, h:h + 1]), reads=[BKT, PK], writes=[KP])
            bmv, BMV = nb(); proj(bmv, BMV, C_MV, 512, 6)
            bmo, BMO = nb(); proj(bmo, BMO, C_MO, 512, 7)
            bmz, BMZ = nb(); proj(bmz, BMZ, C_MZ, 512, 8)
            if l == 0 and t == 0:
                P.op("pool", lambda e: e.memset(v1[:, :, 128:130], 1.0), writes=[V1])
            P.op("act", lambda e: e.copy(out=v1[:, :, 0:128], in_=bmv[:].rearrange("p (h d) -> p h d", h=4)), reads=[BMV], writes=[V1])
            P.op("act", lambda e: e.activation(out=th[:], in_=bmo[:], func=AF.Tanh, scale=0.5), reads=[BMO], writes=[TH])
            P.op("act", lambda e: e.activation(out=sz[:, 512:1024], in_=bmz[:], func=AF.Silu), reads=[BMZ], writes=[SZ])
            P.op("dve", lambda e: e.scalar_tensor_tensor(out=sz[:, 512:1024], in0=th[:], scalar=1.0, in1=sz[:, 512:1024], op0=ALU.add, op1=ALU.mult),
                 reads=[TH, SZ], writes=[SZ])
            P.op("pool", lambda e: e.tensor_tensor(out=sz[:, 512:1024], in0=sz[:, 512:1024], in1=gbc[:, 512:1024], op=ALU.mult), reads=[SZ, GBC], writes=[SZ])
            bsm, BSM = nb()
            for h in range(4):
                P.op("pe", lambda e, h=h: e.matmul(bsm[:, h * 128:(h + 1) * 128], lhsT=qkm[:, 4 + h, :], rhs=qkm[:, h, :], start=True, stop=True),
                     reads=[QKM], writes=[BSM])
            for h in range(4):
                P.op("dve", lambda e, h=h: e.scalar_tensor_tensor(out=s2m[:, h, :], in0=bsm[:, h * 128:(h + 1) * 128], scalar=pk[:, t, h:h + 1],
                                                                 in1=mask_bf[:], op0=ALU.mult, op1=ALU.mult),
                     reads=[BSM, PK, MSK], writes=[S2M])
            yield
            bn0, BN0 = nb(); bn1, BN1 = nb()
            for h in range(4):
                bt, BK = (bn0, BN0) if h < 2 else (bn1, BN1)
                o_ = (h % 2) * 130
                first = (t == 0)
                P.op("pe", lambda e, h=h, bt=bt, o_=o_, first=first: e.matmul(bt[:, o_:o_ + 130], lhsT=s2m[:, h, :], rhs=v1[:, h, :],
                                                                            start=True, stop=first), reads=[S2M, V1], writes=[BK])
                if not first:
                    P.op("pe", lambda e, h=h, bt=bt, o_=o_: e.matmul(bt[:, o_:o_ + 130], lhsT=qkm[:, h, :], rhs=C_bf[:, h, :],
                                                                   start=False, stop=True), reads=[QKM, CBF], writes=[BK])
            bu0, BU0 = nb(); bu1, BU1 = nb()
            for h in range(4):
                bt, BK = (bu0, BU0) if h < 2 else (bu1, BU1)
                o_ = (h % 2) * 130
                P.op("pe", lambda e, h=h, bt=bt, o_=o_: e.matmul(bt[:, o_:o_ + 130], lhsT=kp[:, h, :], rhs=v1[:, h, :], start=True, stop=True),
                     reads=[KP, V1], writes=[BK])
            for h in range(4):
                bt, BK = (bu0, BU0) if h < 2 else (bu1, BU1)
                o_ = (h % 2) * 130
                if t == 0:
                    P.op("dve", lambda e, h=h, bt=bt, o_=o_: e.tensor_copy(out=C_f[:, h, :], in_=bt[:, o_:o_ + 130]), reads=[BK], writes=[CF])
                else:
                    P.op("dve", lambda e, h=h, bt=bt, o_=o_: e.scalar_tensor_tensor(out=C_f[:, h, :], in0=C_f[:, h, :], scalar=cw_b[:, 1, t, h:h + 1],
                                                                                  in1=bt[:, o_:o_ + 130], op0=ALU.mult, op1=ALU.add),
                         reads=[BK, CF, CWB], writes=[CF])
            if not last:
                for h in range(4):
                    P.op("act", lambda e, h=h: e.activation(out=C_bf[:, h, :], in_=C_f[:, h, :], func=AF.Copy, scale=cw_b[:, 1, t + 1, h:h + 1]),
                         reads=[CF, CWB], writes=[CBF])
            for i, (bt, BK) in enumerate(((bn0, BN0), (bn1, BN1))):
                P.op("act", lambda e, i=i, bt=bt: e.activation(out=dn[:, 2 * i:2 * i + 2], in_=bt[:, 128:259:130], func=AF.Abs),
                     reads=[BK], writes=[DN])
            P.op("dve", lambda e: e.tensor_tensor(out=dn[:], in0=dn[:], in1=thr[:, t, :], op=ALU.max), reads=[DN, THR], writes=[DN])
            P.op("dve", lambda e: e.reciprocal(out=dn[:], in_=dn[:]), reads=[DN], writes=[DN])
            for h in range(4):
                bt, BK = (bn0, BN0) if h < 2 else (bn1, BN1)
                o_ = (h % 2) * 130
                P.op("act", lambda e, h=h, bt=bt, o_=o_: e.activation(out=hm[:, h, :], in_=bt[:, o_:o_ + 128], func=AF.Copy, scale=dn[:, h:h + 1]),
                     reads=[BK, DN], writes=[HM])
            for h in range(4):
                P.op("dve", lambda e, h=h: e.bn_stats(out=stats[:, 4 + h, :], in_=hm[:, h, :]), reads=[HM], writes=[STATS])
            yield
            for g in range(8):
                P.op("dve", lambda e, g=g: e.bn_aggr(out=mv[:, g, :], in_=stats[:, g, :]), reads=[STATS], writes=[MV])
            P.op("pool", lambda e: e.tensor_scalar(out=rstd[:], in0=mv[:, :, 1], scalar1=GN_EPS, scalar2=None, op0=ALU.add),
                 reads=[MV], writes=[RSTD])
            P.op("pool", lambda e: e.tensor_tensor(out=rstd[:], in0=rstd[:], in1=mhalf[:], op=ALU.pow), reads=[RSTD, MHALF], writes=[RSTD])
            P.op("dve", lambda e: e.scalar_tensor_tensor(out=nbias[:], in0=mv[:, :, 0], scalar=-1.0, in1=rstd[:], op0=ALU.mult, op1=ALU.mult),
                 reads=[MV, RSTD], writes=[NBIAS])
            for g in range(8):
                if g < 4:
                    src = o_sb[:, g, :]; SB_ = OSB
                else:
                    src = hm[:, g - 4, :]; SB_ = HM
                P.op("act", lambda e, g=g, src=src: e.activation(out=on[:, g, :], in_=src, func=AF.Identity, scale=rstd[:, g:g + 1],
                                                               bias=nbias[:, g:g + 1]), reads=[SB_, RSTD, NBIAS], writes=[ON if g < 4 else ON_M])
            onf = on[:].rearrange("p g d -> p (g d)")
            P.op("pool", lambda e: e.tensor_tensor(out=gated[:, 0:512], in0=onf[:, 0:512], in1=sz[:, 0:512], op=ALU.mult),
                 reads=[ON, SZA], writes=[GATED])
            P.op("dve", lambda e: e.tensor_tensor(out=gated[:, 512:1024], in0=onf[:, 512:1024], in1=sz[:, 512:1024], op=ALU.mult),
                 reads=[ON_M, SZ], writes=[GATED_M])
            yield
            bg, BG = nb(); vg = bfv(bg)
            for k in range(8):
                P.op("pe", lambda e, k=k: e.transpose(vg[:, k * 128:(k + 1) * 128], gated[:, k * 128:(k + 1) * 128], ident_bf[:]),
                     reads=[GATED if k < 4 else GATED_M, IDB], writes=[BG])
            P.op("act", lambda e: e.copy(out=gT[:].rearrange("p k c -> p (k c)"), in_=vg), reads=[BG], writes=[GT])
            for n in range(2):
                bt, BK = nb()
                for k in range(8):
                    P.op("pe", lambda e, k=k, n=n, bt=bt: e.matmul(bt[:], lhsT=gT[:, k, :], rhs=w_out_sb[:, k, n * 512:(n + 1) * 512],
                                                               start=(k == 0), stop=(k == 7)), reads=[GT, WOUT[n]], writes=[BK])
                P.op("dve", lambda e, n=n, bt=bt: e.scalar_tensor_tensor(out=ypre[:, n * 512:(n + 1) * 512], in0=x_sb[xi][:, n * 512:(n + 1) * 512],
                                                                     scalar=ALPHA, in1=bt[:], op0=ALU.mult, op1=ALU.add),
                     reads=[BK, XSB[xi]], writes=[YPRE if n == 0 else YPRE_B])
                P.op("dve", lambda e, n=n: e.bn_stats(out=lstats[:, n, :], in_=ypre[:, n * 512:(n + 1) * 512]), reads=[YPRE if n == 0 else YPRE_B], writes=[LSTATS])
            P.op("dve", lambda e: e.bn_aggr(out=lmv[:], in_=lstats[:].rearrange("p a b -> p (a b)")), reads=[LSTATS], writes=[LMV])
            P.op("pool", lambda e: e.tensor_scalar(out=lrs[:, 0:1], in0=lmv[:, 1:2], scalar1=LN_EPS, scalar2=None, op0=ALU.add),
                 reads=[LMV], writes=[LRS])
            P.op("pool", lambda e: e.tensor_tensor(out=lrs[:, 0:1], in0=lrs[:, 0:1], in1=mhalf[:, 0:1], op=ALU.pow), reads=[LRS, MHALF], writes=[LRS])
            P.op("dve", lambda e: e.scalar_tensor_tensor(out=lrs[:, 1:2], in0=lmv[:, 0:1], scalar=-1.0, in1=lrs[:, 0:1], op0=ALU.mult, op1=ALU.mult),
                 reads=[LMV, LRS], writes=[LRS])
            P.op("act", lambda e: e.activation(out=ypre[:], in_=ypre[:], func=AF.Identity, scale=lrs[:, 0:1], bias=lrs[:, 1:2]),
                 reads=[YPRE, YPRE_B, LRS], writes=[YPRE, YPRE_B])
            P.op("pool", lambda e: e.tensor_tensor(out=ypre[:, 0:512], in0=ypre[:, 0:512], in1=lng[:, 0:512], op=ALU.mult), reads=[YPRE, LNG], writes=[YPRE])
            P.op("dve", lambda e: e.tensor_tensor(out=ypre[:, 512:1024], in0=ypre[:, 512:1024], in1=lng[:, 512:1024], op=ALU.mult), reads=[YPRE_B, LNG], writes=[YPRE_B])
            P.op("pool", lambda e: e.tensor_tensor(out=ypre[:, 0:512], in0=ypre[:, 0:512], in1=lnb[:, 0:512], op=ALU.add), reads=[YPRE, LNB], writes=[YPRE])
            P.op("dve", lambda e: e.tensor_tensor(out=ypre[:, 512:1024], in0=ypre[:, 512:1024], in1=lnb[:, 512:1024], op=ALU.add), reads=[YPRE_B, LNB], writes=[YPRE_B])
            if l == n_layers - 1:
                P.dma("sp", c_y[par], y_p[t * 128:(t + 1) * 128, :], ypre[:], reads=[YPRE, YPRE_B])
            else:
                P.dma("sp", c_y[par], y0[t * 128:(t + 1) * 128, :], ypre[:], reads=[YPRE, YPRE_B], writes=[Y0B[t]])


        xs_sb = sb("xs_sb", [NS, D]); XSS = Buf("xs")
        ropes = sb("ropes", [NS, 2, HD]); ROPES = Buf()
        sg = sb("sg", [NS, 16, 4]); SG = Buf()
        wexp = sb("wexp", [NS, NS, 4]); WEXP = Buf()
        wcbs = sb("wcbs", [128, NS * 4]); WCBS = Buf()
        qTs = sb("qTs", [128, 8, NS]); QTS = Buf()
        oTs = sb("oTs", [128, 2, 64]); OTS = Buf()
        sstats = sb("sstats", [NS, 8, 6]); SSTATS = Buf()
        smv = sb("smv", [NS, 8, 2]); SMV = Buf()
        srs = sb("srs", [NS, 8]); SRS = Buf()
        snb = sb("snb", [NS, 8]); SNB = Buf()
        slst = sb("slst", [NS, 2, 6]); SLST = Buf()
        slmv = sb("slmv", [NS, 2]); SLMV = Buf()
        slrs = sb("slrs", [NS, 2]); SLRS = Buf()
        c_s = [P.chan("c_s%d" % i) for i in range(8)]
        c_sg = [P.chan("c_sg0"), P.chan("c_sg1")]
        c_cg = [P.chan("c_cg0"), P.chan("c_cg1")]
        c_so = [ochan("c_so%d" % i) for i in range(8)]
        c_sgo = [ochan("c_sgo0"), ochan("c_sgo1")]
        c_cgo = [ochan("c_cgo0"), ochan("c_cgo1")]
        id16 = ident_f[0:NS, 0:NS]
        gamb = cst[0:NS, K_GAM:K_GAM + 4]
        sbank_ctr = [0]

        def snb_():
            i = sbank_ctr[0] % 6
            sbank_ctr[0] += 1
            return banks[i]

        def sample_path(l):
            S16 = slice(0, NS)
            if l == 0:
                P.dma("sp", c_s[0], xs_sb[:], xs, writes=[XSS])
                P.dma("sp", c_s[7], ropes[:].rearrange("p a d -> p (a d)"), rope_s.partition_broadcast(NS), writes=[ROPES])
            sc = [x_sb[0][S16, :], x_sb[1][S16, :], x_sb[2][S16, :]]
            SCB = [XSB[0], XSB[1], XSB[2]]
            szs = sz2[0]; SZ = SZ2[0]; sz = sz2[0]
            for j in range(3):
                P.dma("sp", c_s[1 + j], sc[j], sconv[l][:, j, :], writes=[SCB[j]])
            P.dma("sp", c_s[4], sg[:, 0, :], sm[l], writes=[SG])
            n0v = o_sb[S16, :, :]
            P.dma("sp", c_s[5], n0v, sn[l], writes=[OSB])
            P.op("pool", lambda e: e.tensor_copy(out=x_bf[S16, :], in_=xs_sb[:]), reads=[XSS], writes=[XBF])
            bt, BK = nb(); v = bfv(bt)
            for k in range(8):
                P.op("pe", lambda e, k=k: e.transpose(v[:, k * NS:(k + 1) * NS], x_bf[S16, k * 128:(k + 1) * 128], ident_bf[S16, 0:NS]),
                     reads=[XBF, IDB], writes=[BK])
            P.op("act", lambda e: e.copy(out=xT[:, :, 0:NS], in_=v[:, 0:8 * NS].rearrange("p (k s) -> p k s", k=8)), reads=[BK], writes=[XT])

            def sproj(c0, n, wb):
                bt, BK = nb()
                for k in range(8):
                    P.op("pe", lambda e, k=k: e.matmul(bt[S16, 0:n], lhsT=xT[:, k, 0:NS], rhs=w_in_sb[:, k, c0:c0 + n],
                                                       start=(k == 0), stop=(k == 7)), reads=[XT, WIN[wb]], writes=[BK])
                return bt, BK

            cos2 = ropes[:, 0, :]
            sin2 = ropes[:, 1, :]
            for i, c0 in enumerate((C_RQ, C_RK)):
                bt, BK = sproj(c0, 512, i)
                src = bt[S16, :].rearrange("p (h d) -> p h d", h=4)
                dst_t = tmp_t[S16, i * 4:(i + 1) * 4, :]
                dst_u = tmp_u[S16, i * 4:(i + 1) * 4, :]
                P.op("dve", lambda e, src=src, dst_t=dst_t: e.tensor_tensor(out=dst_t, in0=src, in1=cos2.unsqueeze(1).broadcast_to([NS, 4, HD]), op=ALU.mult),
                     reads=[BK, ROPES], writes=[TMPT])
                P.op("dve", lambda e, src=src, dst_u=dst_u: e.tensor_tensor(out=dst_u[:, :, 0:64], in0=src[:, :, 64:128],
                                                                          in1=sin2[:, 0:64].unsqueeze(1).broadcast_to([NS, 4, 64]), op=ALU.mult),
                     reads=[BK, ROPES], writes=[TMPU])
                P.op("dve", lambda e, src=src, dst_u=dst_u: e.tensor_tensor(out=dst_u[:, :, 64:128], in0=src[:, :, 0:64],
                                                                          in1=sin2[:, 64:128].unsqueeze(1).broadcast_to([NS, 4, 64]), op=ALU.mult),
                     reads=[BK, ROPES], writes=[TMPU])
            qk_s = on[S16, :, :]
            P.op("pool", lambda e: e.tensor_tensor(out=qk_s, in0=tmp_t[S16, :, :], in1=tmp_u[S16, :, :], op=ALU.add), reads=[TMPT, TMPU], writes=[ON])
            v2s = hm[S16, :, :]
            bt, BK = sproj(C_RV, 512, 2)
            P.op("act", lambda e, bt=bt: e.mul(out=v2s.rearrange("p h d -> p (h d)"), in_=bt[S16, :], mul=ISQ), reads=[BK], writes=[HM])
            bt, BK = sproj(C_RZ, 512, 3)
            P.op("act", lambda e, bt=bt: e.activation(out=sz[S16, 0:512], in_=bt[S16, :], func=AF.Silu), reads=[BK], writes=[SZ])
            mqk_s = ypre[S16, :]
            for i in range(2):
                bt, BK = sproj(C_MQK + i * 512, 512, 4 + i)
                P.op("act", lambda e, bt=bt, i=i: e.copy(out=mqk_s[:, i * 512:(i + 1) * 512], in_=bt[S16, :]), reads=[BK], writes=[YPRE])
            P.dma("sp", c_so[0], conv_s[l][:, 0, :], sc[1], reads=[SCB[1]])
            P.dma("sp", c_so[1], conv_s[l][:, 1, :], sc[2], reads=[SCB[2]])
            P.dma("sp", c_so[2], conv_s[l][:, 2, :], mqk_s, reads=[YPRE])
            Wb = sz2[1][S16, :]; WB = SZ2[1]
            acc = tmp_t[S16, :, :].rearrange("p g d -> p (g d)")
            tmpc = tmp_u[S16, :, :].rearrange("p g d -> p (g d)")
            full = [sc[0], sc[1], sc[2], mqk_s]
            FB = [SCB[0], SCB[1], SCB[2], YPRE]
            for n_, j in enumerate((3, 0, 1, 2)):
                P.dma("sp", c_s[6], Wb, conv_w[l][j].partition_broadcast(NS), writes=[WB])
                if n_ == 0:
                    P.op("dve", lambda e, j=j: e.tensor_tensor(out=acc, in0=full[j], in1=Wb, op=ALU.mult), reads=[FB[j], WB, ON], writes=[TMPT])
                else:
                    P.op("dve", lambda e, j=j: e.tensor_tensor(out=tmpc, in0=full[j], in1=Wb, op=ALU.mult), reads=[FB[j], WB, ON], writes=[TMPU])
                    P.op("pool", lambda e: e.tensor_tensor(out=acc, in0=acc, in1=tmpc, op=ALU.add), reads=[TMPU, TMPT], writes=[TMPT])
            P.dma("sp", c_s[6], Wb, conv_b[l].partition_broadcast(NS), writes=[WB])
            P.op("pool", lambda e: e.tensor_tensor(out=acc, in0=acc, in1=Wb, op=ALU.add), reads=[WB, TMPT], writes=[TMPT])
            P.op("act", lambda e: e.activation(out=acc, in_=acc, func=AF.Silu), reads=[TMPT], writes=[TMPT])
            qkm_s = tmp_t[S16, :, :]
            v_s = gated[S16, :].bitcast(F32).rearrange("p (h d) -> p h d", h=4)
            bt, BK = sproj(C_MV, 512, 6)
            P.op("act", lambda e, bt=bt: e.copy(out=v_s.rearrange("p h d -> p (h d)"), in_=bt[S16, :]), reads=[BK], writes=[GATED])
            bt, BK = sproj(C_MO, 512, 7)
            P.op("act", lambda e, bt=bt: e.activation(out=th[S16, :], in_=bt[S16, :], func=AF.Tanh, scale=0.5), reads=[BK], writes=[TH])
            bt, BK = sproj(C_MZ, 512, 8)
            P.op("act", lambda e, bt=bt: e.activation(out=sz[S16, 512:1024], in_=bt[S16, :], func=AF.Silu), reads=[BK], writes=[SZ])
            bt, BK = sproj(C_G, 8, 9)
            G_ = lambda i: sg[:, i, :]
            P.op("dve", lambda e, bt=bt: e.tensor_tensor(out=sg[:, 1:3, :].rearrange("p a h -> p (a h)"), in0=bt[S16, 0:8], in1=bias8[S16, :], op=ALU.add),
                 reads=[BK, BIAS8], writes=[SG])
            P.op("act", lambda e: e.activation(out=G_(3), in_=G_(2), func=AF.Exp, scale=-1.0), reads=[SG], writes=[SG])
            P.op("act", lambda e: e.activation(out=G_(3), in_=G_(3), func=AF.Ln, bias=1.0), reads=[SG], writes=[SG])
            P.op("dve", lambda e: e.tensor_tensor(out=G_(4), in0=G_(1), in1=G_(3), op=ALU.add), reads=[SG], writes=[SG])
            P.op("dve", lambda e: e.tensor_tensor(out=G_(5), in0=G_(4), in1=G_(0), op=ALU.max), reads=[SG], writes=[SG])
            P.op("dve", lambda e: e.tensor_tensor(out=G_(6), in0=G_(5), in1=G_(3), op=ALU.subtract), reads=[SG], writes=[SG])
            P.dma("sp", c_so[3], m_s[l], G_(6), reads=[SG])
            P.op("dve", lambda e: e.tensor_tensor(out=G_(14), in0=G_(4), in1=G_(5), op=ALU.subtract), reads=[SG], writes=[SG])
            P.op("dve", lambda e: e.tensor_scalar(out=G_(14), in0=G_(14), scalar1=LNISQ, scalar2=None, op0=ALU.add), reads=[SG], writes=[SG])
            P.op("act", lambda e: e.activation(out=G_(7), in_=G_(14), func=AF.Exp), reads=[SG], writes=[SG])
            P.op("dve", lambda e: e.tensor_tensor(out=G_(14), in0=G_(3), in1=G_(5), op=ALU.subtract), reads=[SG], writes=[SG])
            P.op("act", lambda e: e.activation(out=G_(8), in_=G_(14), func=AF.Exp), reads=[SG], writes=[SG])
            P.op("dve", lambda e: e.tensor_tensor(out=G_(14), in0=G_(0), in1=G_(5), op=ALU.subtract), reads=[SG], writes=[SG])
            P.op("act", lambda e: e.activation(out=G_(9), in_=G_(14), func=AF.Exp), reads=[SG], writes=[SG])
            bt, BK = nb()
            for g in range(4):
                P.op("pe", lambda e, g=g, bt=bt: e.transpose(bt[:, g * NS:(g + 1) * NS], qk_s[:, g, :], id16), reads=[ON, CST], writes=[BK])
            for g in range(4):
                P.op("pe", lambda e, g=g, bt=bt: e.transpose(bt[:, (4 + g) * NS:(5 + g) * NS], qkm_s[:, g, :], id16), reads=[TMPT, CST], writes=[BK])
            P.op("act", lambda e, bt=bt: e.copy(out=qTs[:].rearrange("p g s -> p (g s)"), in_=bt[:, 0:8 * NS]), reads=[BK], writes=[QTS])
            bc4 = lambda i: sg[:, i, :].unsqueeze(2).broadcast_to([NS, 4, HD])
            P.op("dve", lambda e: e.tensor_tensor(out=qkm_s[:, 4:8, :], in0=qkm_s[:, 4:8, :], in1=bc4(7), op=ALU.mult), reads=[TMPT, SG], writes=[TMPT])
            prod = tmp_u[S16, :, :]
            P.op("dve", lambda e: e.tensor_tensor(out=prod[:, 0:4, :], in0=qk_s[:, 0:4, :], in1=qk_s[:, 4:8, :], op=ALU.mult), reads=[ON], writes=[TMPU])
            P.op("dve", lambda e: e.tensor_reduce(out=G_(10), in_=prod[:, 0:4, :], axis=AX.X, op=ALU.add), reads=[TMPU], writes=[SG])
            P.op("dve", lambda e: e.tensor_tensor(out=prod[:, 4:8, :], in0=qkm_s[:, 0:4, :], in1=qkm_s[:, 4:8, :], op=ALU.mult), reads=[TMPT], writes=[TMPU])
            P.op("dve", lambda e: e.tensor_reduce(out=G_(11), in_=prod[:, 4:8, :], axis=AX.X, op=ALU.add), reads=[TMPU], writes=[SG])
            P.op("dve", lambda e: e.tensor_tensor(out=prod[:, 0:4, :], in0=qkm_s[:, 0:4, :], in1=n0v, op=ALU.mult), reads=[TMPT, OSB, SG], writes=[TMPU])
            P.op("dve", lambda e: e.tensor_reduce(out=G_(13), in_=prod[:, 0:4, :], axis=AX.X, op=ALU.add), reads=[TMPU], writes=[SG])
            P.op("dve", lambda e: e.tensor_tensor(out=n0v, in0=n0v, in1=bc4(9), op=ALU.mult), reads=[OSB, SG], writes=[OSB])
            P.op("pool", lambda e: e.tensor_tensor(out=n0v, in0=n0v, in1=qkm_s[:, 4:8, :], op=ALU.add), reads=[OSB, TMPT], writes=[OSB])
            P.dma("sp", c_so[4], n_s[l], n0v, reads=[OSB])
            P.op("dve", lambda e: e.tensor_tensor(out=wexp[:], in0=sg[:, 9, :].unsqueeze(1).broadcast_to([NS, NS, 4]),
                                                  in1=id16.unsqueeze(2).broadcast_to([NS, NS, 4]), op=ALU.mult), reads=[SG, CST], writes=[WEXP])
            bt, BK = nb()
            P.op("pe", lambda e, bt=bt: e.matmul(bt[:, 0:NS * 4], lhsT=ones_f[0:NS, :], rhs=wexp[:].rearrange("p s h -> p (s h)"), start=True, stop=True),
                 reads=[WEXP, CST], writes=[BK])
            P.op("act", lambda e, bt=bt: e.copy(out=wcbs[:], in_=bt[:, 0:NS * 4]), reads=[BK], writes=[WCBS])
            GS = [x_sb[0][:].rearrange("p (s h e) -> p s h e", s=2, h=4), x_sb[1][:].rearrange("p (s h e) -> p s h e", s=2, h=4)]
            GC = [x_sb[2][:].rearrange("p (s h e) -> p s h e", s=2, h=4), sz2[1][:].rearrange("p (s h e) -> p s h e", s=2, h=4)]
            GCB = [XSB[2], SZ2[1]]
            vexp = ypre[S16, :].rearrange("p (s h e) -> p s h e", s=2, h=4)
            bor, BOR = banks[6]
            bom, BOM = banks[7]
            for g in range(NS // 2):
                par = g % 2
                s0 = g * 2
                P.dma("sp", c_sg[par], GS[par], sret[l][s0:s0 + 2].rearrange("s h d e -> d s h e"), writes=[XSB[par]])
                P.dma("sp", c_cg[par], GC[par], sC[l][s0:s0 + 2].rearrange("s h d e -> d s h e"), writes=[GCB[par]])
                for typ in range(2):
                    Gt = GS[par] if typ == 0 else GC[par]
                    GB = XSB[par] if typ == 0 else GCB[par]
                    bo_, BO_ = (bor, BOR) if typ == 0 else (bom, BOM)
                    for sl in range(2):
                        for h in range(4):
                            col = h * NS + s0 + sl
                            P.op("pe", lambda e, Gt=Gt, bo_=bo_, sl=sl, h=h, col=col, typ=typ, s0=s0: e.matmul(
                                bo_[:, typ * 0 + col:col + 1], lhsT=Gt[:, sl, h, :], rhs=qTs[:, typ * 4 + h, s0 + sl:s0 + sl + 1], start=True, stop=True),
                                reads=[GB, QTS], writes=[BO_])
                    vsrc = v2s if typ == 0 else v_s
                    VB = HM if typ == 0 else GATED
                    P.op("dve", lambda e, vsrc=vsrc, s0=s0: e.tensor_tensor(
                        out=vexp, in0=vsrc.unsqueeze(1).broadcast_to([NS, 2, 4, HD]),
                        in1=id16[:, s0:s0 + 2].unsqueeze(2).unsqueeze(3).broadcast_to([NS, 2, 4, HD]), op=ALU.mult),
                        reads=[VB, CST], writes=[YPRE])
                    ksrc = qk_s[:, 4:8, :] if typ == 0 else qkm_s[:, 4:8, :]
                    KB = ON if typ == 0 else TMPT
                    for sl in range(2):
                        bu_, BU_ = snb_()
                        for h in range(4):
                            P.op("pe", lambda e, bu_=bu_, h=h, sl=sl, ksrc=ksrc: e.matmul(bu_[:, h * 128:(h + 1) * 128], lhsT=ksrc[:, h, :], rhs=vexp[:, sl, h, :],
                                                                                    start=True, stop=True), reads=[KB, YPRE], writes=[BU_])
                        for h in range(4):
                            if typ == 0:
                                P.op("dve", lambda e, bu_=bu_, h=h, sl=sl, Gt=Gt: e.scalar_tensor_tensor(
                                    out=Gt[:, sl, h, :], in0=Gt[:, sl, h, :], scalar=GAM[h], in1=bu_[:, h * 128:(h + 1) * 128], op0=ALU.mult, op1=ALU.add),
                                    reads=[BU_, GB], writes=[GB])
                            else:
                                ci = (s0 + sl) * 4 + h
                                P.op("dve", lambda e, bu_=bu_, h=h, sl=sl, Gt=Gt, ci=ci: e.scalar_tensor_tensor(
                                    out=Gt[:, sl, h, :], in0=Gt[:, sl, h, :], scalar=wcbs[:, ci:ci + 1], in1=bu_[:, h * 128:(h + 1) * 128], op0=ALU.mult, op1=ALU.add),
                                    reads=[BU_, GB, WCBS], writes=[GB])
                P.dma("sp", c_sgo[par], ret_s[l][s0:s0 + 2].rearrange("s h d e -> d s h e"), GS[par], reads=[XSB[par]])
                P.dma("sp", c_cgo[par], C_s[l][s0:s0 + 2].rearrange("s h d e -> d s h e"), GC[par], reads=[GCB[par]])
            P.op("act", lambda e: e.copy(out=oTs[:, 0, :], in_=bor[:, 0:64]), reads=[BOR], writes=[OTS])
            P.op("act", lambda e: e.copy(out=oTs[:, 1, :], in_=bom[:, 0:64]), reads=[BOM], writes=[OTS])
            btr_, BTR_ = nb()
            btm_, BTM_ = nb()
            for typ, (bt, BK) in enumerate(((btr_, BTR_), (btm_, BTM_))):
                for h in range(4):
                    P.op("pe", lambda e, bt=bt, typ=typ, h=h: e.transpose(bt[S16, h * 128:(h + 1) * 128], oTs[:, typ, h * NS:(h + 1) * NS], ident_f),
                         reads=[OTS, CST], writes=[BK])
            hs = tmp_u[S16, :, :]
            t2 = ypre[S16, :].rearrange("p (g d) -> p g d", g=8)
            inter_r = btr_[S16, :].rearrange("p (h d) -> p h d", h=4)
            inter_m = btm_[S16, :].rearrange("p (h d) -> p h d", h=4)
            P.op("dve", lambda e: e.tensor_tensor(out=hs[:, 0:4, :], in0=inter_r, in1=gamb.unsqueeze(2).broadcast_to([NS, 4, HD]), op=ALU.mult),
                 reads=[BTR_, CST, SG], writes=[TMPU])
            P.op("dve", lambda e: e.tensor_tensor(out=t2[:, 0:4, :], in0=v2s, in1=bc4(10), op=ALU.mult), reads=[HM, SG], writes=[YPRE])
            P.op("pool", lambda e: e.tensor_tensor(out=hs[:, 0:4, :], in0=hs[:, 0:4, :], in1=t2[:, 0:4, :], op=ALU.add), reads=[TMPU, YPRE], writes=[TMPU])
            P.op("dve", lambda e: e.tensor_tensor(out=hs[:, 4:8, :], in0=inter_m, in1=bc4(9), op=ALU.mult), reads=[BTM_, SG], writes=[TMPU])
            P.op("dve", lambda e: e.tensor_tensor(out=t2[:, 4:8, :], in0=v_s, in1=bc4(11), op=ALU.mult), reads=[GATED, SG], writes=[YPRE])
            P.op("pool", lambda e: e.tensor_tensor(out=hs[:, 4:8, :], in0=hs[:, 4:8, :], in1=t2[:, 4:8, :], op=ALU.add), reads=[TMPU, YPRE], writes=[TMPU])
            P.op("dve", lambda e: e.tensor_tensor(out=G_(12), in0=G_(13), in1=G_(9), op=ALU.mult), reads=[SG], writes=[SG])
            P.op("dve", lambda e: e.tensor_tensor(out=G_(12), in0=G_(12), in1=G_(11), op=ALU.add), reads=[SG], writes=[SG])
            P.op("act", lambda e: e.activation(out=G_(12), in_=G_(12), func=AF.Abs), reads=[SG], writes=[SG])
            P.op("dve", lambda e: e.tensor_tensor(out=G_(12), in0=G_(12), in1=G_(8), op=ALU.max), reads=[SG], writes=[SG])
            P.op("dve", lambda e: e.reciprocal(out=G_(12), in_=G_(12)), reads=[SG], writes=[SG])
            P.op("dve", lambda e: e.tensor_tensor(out=hs[:, 4:8, :], in0=hs[:, 4:8, :], in1=bc4(12), op=ALU.mult), reads=[TMPU, SG], writes=[TMPU])
            for g in range(8):
                P.op("dve", lambda e, g=g: e.bn_stats(out=sstats[:, g, :], in_=hs[:, g, :]), reads=[TMPU], writes=[SSTATS])
            for g in range(8):
                P.op("dve", lambda e, g=g: e.bn_aggr(out=smv[:, g, :], in_=sstats[:, g, :]), reads=[SSTATS], writes=[SMV])
            P.op("pool", lambda e: e.tensor_scalar(out=srs[:], in0=smv[:, :, 1], scalar1=GN_EPS, scalar2=None, op0=ALU.add), reads=[SMV], writes=[SRS])
            P.op("pool", lambda e: e.tensor_tensor(out=srs[:], in0=srs[:], in1=mhalf[S16, :], op=ALU.pow), reads=[SRS, MHALF], writes=[SRS])
            P.op("dve", lambda e: e.scalar_tensor_tensor(out=snb[:], in0=smv[:, :, 0], scalar=-1.0, in1=srs[:], op0=ALU.mult, op1=ALU.mult),
                 reads=[SMV, SRS], writes=[SNB])
            on2 = on[S16, :, :]
            for g in range(8):
                P.op("act", lambda e, g=g: e.activation(out=on2[:, g, :], in_=hs[:, g, :], func=AF.Identity, scale=srs[:, g:g + 1], bias=snb[:, g:g + 1]),
                     reads=[TMPU, SRS, SNB], writes=[ON])
            P.op("dve", lambda e: e.scalar_tensor_tensor(out=sz[S16, 512:1024], in0=th[S16, :], scalar=1.0, in1=sz[S16, 512:1024], op0=ALU.add, op1=ALU.mult),
                 reads=[TH, SZ], writes=[SZ])
            P.op("pool", lambda e: e.tensor_tensor(out=sz[S16, :], in0=sz[S16, :], in1=gbc[S16, :], op=ALU.mult), reads=[SZ, GBC], writes=[SZ])
            P.op("pool", lambda e: e.tensor_tensor(out=gated[S16, :], in0=on2.rearrange("p g d -> p (g d)"), in1=sz[S16, :], op=ALU.mult),
                 reads=[ON, SZ], writes=[GATED])
            bt, BK = nb(); vg = bfv(bt)
            for k in range(8):
                P.op("pe", lambda e, k=k: e.transpose(vg[:, k * NS:(k + 1) * NS], gated[S16, k * 128:(k + 1) * 128], ident_bf[S16, 0:NS]),
                     reads=[GATED, IDB], writes=[BK])
            P.op("act", lambda e: e.copy(out=gT[:, :, 0:NS], in_=vg[:, 0:8 * NS].rearrange("p (k s) -> p k s", k=8)), reads=[BK], writes=[GT])
            yps = ypre[S16, :]
            for n in range(2):
                bt, BK = nb()
                for k in range(8):
                    P.op("pe", lambda e, k=k, n=n, bt=bt: e.matmul(bt[S16, :], lhsT=gT[:, k, 0:NS], rhs=w_out_sb[:, k, n * 512:(n + 1) * 512],
                                                               start=(k == 0), stop=(k == 7)), reads=[GT, WOUT[n]], writes=[BK])
                P.op("dve", lambda e, n=n, bt=bt: e.scalar_tensor_tensor(out=yps[:, n * 512:(n + 1) * 512], in0=xs_sb[:, n * 512:(n + 1) * 512], scalar=ALPHA,
                                                                     in1=bt[S16, :], op0=ALU.mult, op1=ALU.add), reads=[BK, XSS], writes=[YPRE])
                P.op("dve", lambda e, n=n: e.bn_stats(out=slst[:, n, :], in_=yps[:, n * 512:(n + 1) * 512]), reads=[YPRE], writes=[SLST])
            P.op("dve", lambda e: e.bn_aggr(out=slmv[:], in_=slst[:].rearrange("p a b -> p (a b)")), reads=[SLST], writes=[SLMV])
            P.op("pool", lambda e: e.tensor_scalar(out=slrs[:, 0:1], in0=slmv[:, 1:2], scalar1=LN_EPS, scalar2=None, op0=ALU.add), reads=[SLMV], writes=[SLRS])
            P.op("pool", lambda e: e.tensor_tensor(out=slrs[:, 0:1], in0=slrs[:, 0:1], in1=mhalf[S16, 0:1], op=ALU.pow), reads=[SLRS, MHALF], writes=[SLRS])
            P.op("dve", lambda e: e.scalar_tensor_tensor(out=slrs[:, 1:2], in0=slmv[:, 0:1], scalar=-1.0, in1=slrs[:, 0:1], op0=ALU.mult, op1=ALU.mult),
                 reads=[SLMV, SLRS], writes=[SLRS])
            P.op("act", lambda e: e.activation(out=xs_sb[:], in_=yps, func=AF.Identity, scale=slrs[:, 0:1], bias=slrs[:, 1:2]),
                 reads=[YPRE, SLRS], writes=[XSS])
            P.op("pool", lambda e: e.tensor_tensor(out=xs_sb[:], in0=xs_sb[:], in1=lng[S16, :], op=ALU.mult), reads=[XSS, LNG], writes=[XSS])
            P.op("pool", lambda e: e.tensor_tensor(out=xs_sb[:], in0=xs_sb[:], in1=lnb[S16, :], op=ALU.add), reads=[XSS, LNB], writes=[XSS])
            if l == n_layers - 1:
                P.dma("sp", c_so[5], y_s, xs_sb[:], reads=[XSS])

        def finalize_prompt(l):
            P.dma("sp", c_fin[1], ret_p[l].rearrange("h d e -> d h e"), S_f[:], reads=[SF])
            P.dma("sp", c_fin[2], C_p[l].rearrange("h d e -> d h e"), C_f[:, :, 0:128], reads=[CF])
            P.dma("sp", c_fin[3], n_p[l].rearrange("h d -> d h"), C_f[:, :, 128], reads=[CF], allow_slow_non_contiguous=True)
            for half in range(2):
                bt, BK = nb()
                for c4 in range(4):
                    ch = half * 4 + c4
                    P.op("pe", lambda e, ch=ch, c4=c4, bt=bt: e.transpose(bt[0:3, c4 * 128:(c4 + 1) * 128], convp_sb[:, ch, :], ident_f),
                         reads=[CONVP, CST], writes=[BK])
                P.op("act", lambda e, half=half, bt=bt: e.copy(out=convp_T[:, half * 512:(half + 1) * 512], in_=bt[0:3, :]),
                     reads=[BK], writes=[CONVPT])
            P.dma("sp", c_fin[4], conv_p[l], convp_T[:], reads=[CONVPT])

        def stage(name):
            if stop_at == name:
                raise StopBuild()

        try:
            for l in range(n_layers):
                load_weights(l)
                stage("weights")
                load_params(l)
                stage("params")
                prepass(l)
                stage("prepass")
                OFF = 5
                gens = [main_tile(l, t) for t in range(n_tiles)]
                NSEG = 10
                for step in range(n_tiles * NSEG + OFF):
                    pass
                order = []
                for t in range(n_tiles):
                    for k in range(NSEG):
                        order.append((t * NSEG + k + (0 if True else 0), t, k))
                slots = {}
                for t in range(n_tiles):
                    for k in range(NSEG):
                        slots.setdefault(t * OFF + k, []).append((t, k))
                for sl_ in sorted(slots):
                    for (t, k) in sorted(slots[sl_]):
                        try:
                            next(gens[t])
                        except StopIteration:
                            pass
                stage("tiles")
                finalize_prompt(l)
                if do_sample:
                    P.barrier()
                    sample_path(l)
                    P.barrier()
                stage('sample')
        except StopBuild:
            pass

        for c in P.chans:
            if c.val:
                P.wait_chan("sp", c)
        with nc.Block() as block:
            P.finish(block)
    return nc


def make_consts():
    cst = np.zeros((128, NCST), np.float32)
    cst[:, K_ID:K_ID + 128] = np.eye(128, dtype=np.float32)
    idx = np.arange(128)
    cst[:, K_TRI:K_TRI + 128] = (idx[:, None] <= idx[None, :]).astype(np.float32)
    for h in range(H):
        lg = np.float32(LOGG[h])
        cst[:, K_KDEC + h] = np.exp(lg * (np.float32(L - 1) - idx.astype(np.float32))).astype(np.float32) * np.float32(ISQ)
        cst[:, K_QDEC + h * 128:K_QDEC + (h + 1) * 128] = np.exp(lg * (idx.astype(np.float32) + 1.0 - L)).astype(np.float32)[None, :]
    cst[:, K_ONE:K_ONE + 128] = 1.0
    for h in range(H):
        cst[:, K_GAM + h] = np.float32(GAM[h])
    half = HD // 2
    inv = (np.float32(10000.0) ** (-np.arange(half, dtype=np.float32) / np.float32(half))).astype(np.float32)

    def rope(pos):
        ang = (pos.astype(np.float32)[:, None] * inv[None, :]).astype(np.float32)
        c = np.cos(ang.astype(np.float64)).astype(np.float32)
        s = np.sin(ang.astype(np.float64)).astype(np.float32)
        out = np.zeros((len(pos), 2, HD), np.float32)
        out[:, 0, :half] = c
        out[:, 0, half:] = c
        out[:, 1, :half] = -s
        out[:, 1, half:] = s
        return out

    rope_p = rope(np.arange(T))
    rope_s = rope(np.array([PAST_LEN])).reshape(2 * HD)
    return cst, rope_p, rope_s


_CACHE = {}


def kernel(x_prompt, x_sample, state_ret, state_mlstm_C, state_mlstm_n, state_mlstm_m, state_conv,
           w_in, conv_w, conv_b, b_i, b_f, g_ret, g_m, w_out, ln_g, ln_b):
    n = 8
    if "nc" not in _CACHE:
        _CACHE["nc"] = build_program()
    nc = _CACHE["nc"]
    cst, rope_p, rope_s = make_consts()
    f = lambda a: np.ascontiguousarray(np.asarray(a, dtype=np.float32))
    shared = dict(w_in=f(w_in), conv_w=f(conv_w), conv_b=f(conv_b), b_i=f(b_i), b_f=f(b_f), g_ret=f(g_ret), g_m=f(g_m),
                  w_out=f(w_out), ln_g=f(ln_g), ln_b=f(ln_b), cst=cst, rope_p=rope_p, rope_s=rope_s)
    in_maps = []
    for c in range(n):
        s0, s1 = c * NS, (c + 1) * NS
        m = dict(shared)
        m["xp"] = f(x_prompt[c])
        m["xs"] = f(x_sample[s0:s1, 0, :])
        m["sret"] = f(state_ret[:, s0:s1])
        m["sC"] = f(state_mlstm_C[:, s0:s1])
        m["sn"] = f(state_mlstm_n[:, s0:s1])
        m["sm"] = f(state_mlstm_m[:, s0:s1])
        m["sconv"] = f(state_conv[:, s0:s1])
        in_maps.append(m)
    res = run_bass_kernel_spmd(nc, in_maps, core_ids=list(range(n)))
    R = res.results
    y_p = np.stack([R[c]["y_p"] for c in range(n)], 0)
    y_s = np.concatenate([R[c]["y_s"] for c in range(n)], 0)[:, None, :]
    ret_p = np.stack([R[c]["ret_p"] for c in range(n)], 1)
    C_p = np.stack([R[c]["C_p"] for c in range(n)], 1)
    n_p = np.stack([R[c]["n_p"] for c in range(n)], 1)
    m_p = np.stack([R[c]["m_p"] for c in range(n)], 1)
    conv_p = np.stack([R[c]["conv_p"] for c in range(n)], 1)
    ret_s = np.concatenate([R[c]["ret_s"] for c in range(n)], 1)
    C_s = np.concatenate([R[c]["C_s"] for c in range(n)], 1)
    n_s = np.concatenate([R[c]["n_s"] for c in range(n)], 1)
    m_s = np.concatenate([R[c]["m_s"] for c in range(n)], 1)
    conv_s = np.concatenate([R[c]["conv_s"] for c in range(n)], 1)
    return (y_p, y_s, ret_p, C_p, n_p, m_p, conv_p, ret_s, C_s, n_s, m_s, conv_s)
```

```python
from contextlib import ExitStack
import numpy as np
import concourse.bass as bass
import concourse.mybir as mybir
from concourse.bass_utils import run_bass_kernel_spmd

F32 = mybir.dt.float32
BF16 = mybir.dt.bfloat16
AF = mybir.ActivationFunctionType
ALU = mybir.AluOpType
AX = mybir.AxisListType

D = 1024
T = 2048
NT = 16
L = 128
NS = 16
H = 4
HD = 128
N_IN = 4616
PAST_LEN = 16384
ALPHA = (2 * 2) ** 0.25
GN_EPS = 1e-5
LN_EPS = 1e-5
ISQ = float(HD ** -0.5)
LNISQ = float(np.log(HD ** -0.5))
GAM = [float(np.float32(1.0) - np.float32(2.0) ** np.float32(-5.0 - h)) for h in range(H)]
LOGG = [float(np.log(np.float32(g))) for g in GAM]
GAML = [float(np.exp(np.float32(lg) * L)) for lg in LOGG]

C_RQ, C_RK, C_RV, C_RZ, C_MQK, C_MV, C_MO, C_MZ, C_G = 0, 512, 1024, 1536, 2048, 3072, 3584, 4096, 4608

K_ID = 0
K_TRI = 128
K_KDEC = 256
K_QDEC = 260
K_ONE = 772
K_GAM = 900
NCST = 904


class Buf:
    __slots__ = ("name", "w", "r", "psum")

    def __init__(self, name="", psum=False):
        self.name = name
        self.w = None
        self.r = []
        self.psum = psum


class Chan:
    def __init__(self, prog, name):
        self.sem = prog.new_sem(name)
        self.val = 0


class Prog:
    ENG = ("pe", "act", "dve", "pool", "sp")

    def __init__(self, nc, stack):
        self.nc = nc
        self.stack = stack
        self.items = {e: [] for e in self.ENG}
        self.cnt = {e: 0 for e in self.ENG}
        self.esem = {e: self.new_sem("prog_" + e) for e in self.ENG}
        self.waited = {e: {} for e in self.ENG}
        self.chans = []

    def new_sem(self, name):
        return self.stack.enter_context(self.nc.semaphore(name))

    def chan(self, name):
        c = Chan(self, name)
        self.chans.append(c)
        return c

    def _need(self, eng, reads, writes):
        need = {}

        def add(ev):
            if ev is None:
                return
            k = (ev[0], id(ev[1]) if ev[0] == 'c' else ev[1])
            if k not in need or need[k][2] < ev[2]:
                need[k] = ev

        for b in reads:
            add(b.w)
            if b.psum:
                for r in b.r:
                    if not (r[0] == 'e' and r[1] == eng):
                        add(r)
        for b in writes:
            if b.w is not None and not (b.w[0] == 'e' and b.w[1] == eng):
                add(b.w)
            for r in b.r:
                if not (r[0] == 'e' and r[1] == eng):
                    add(r)
        wd = self.waited[eng]
        for k, ev in need.items():
            if wd.get(k, 0) >= ev[2]:
                continue
            wd[k] = ev[2]
            if ev[0] == 'e':
                self.items[eng].append(('we', ev[1], ev[2]))
            else:
                self.items[eng].append(('wc', ev[1].sem, ev[2]))

    def op(self, eng, fn, reads=(), writes=()):
        self._need(eng, reads, writes)
        self.cnt[eng] += 1
        ev = ('e', eng, self.cnt[eng])
        self.items[eng].append(('op', fn, self.cnt[eng]))
        for b in reads:
            b.r.append(ev)
        for b in writes:
            b.w = ev
            b.r = []
        return ev

    def dma(self, eng, chan, out, in_, reads=(), writes=(), **kw):
        self._need(eng, reads, writes)
        chan.val += 16
        ev = ('c', chan, chan.val)
        self.items[eng].append(('dma', lambda e, out=out, in_=in_, kw=kw, sem=chan.sem:
                                e.dma_start(out=out, in_=in_, **kw).then_inc(sem, 16)))
        for b in reads:
            b.r.append(ev)
        for b in writes:
            b.w = ev
            b.r = []
        return ev

    def barrier(self):
        for eng in self.ENG:
            wd = self.waited[eng]
            for e2 in self.ENG:
                if e2 == eng or self.cnt[e2] == 0:
                    continue
                k = ('e', e2)
                if wd.get(k, 0) >= self.cnt[e2]:
                    continue
                wd[k] = self.cnt[e2]
                self.items[eng].append(('we', e2, self.cnt[e2]))
            for c in self.chans:
                if c.val == 0:
                    continue
                k = ('c', id(c))
                if wd.get(k, 0) >= c.val:
                    continue
                wd[k] = c.val
                self.items[eng].append(('wc', c.sem, c.val))

    def wait_chan(self, eng, chan):
        self.items[eng].append(('wc', chan.sem, chan.val))

    def finish(self, block):
        targets = {e: set() for e in self.ENG}
        for e in self.ENG:
            for it in self.items[e]:
                if it[0] == 'we':
                    targets[it[1]].add(it[2])
        rank = {}
        for e in self.ENG:
            rank[e] = {s_: i + 1 for i, s_ in enumerate(sorted(targets[e]))}
        self.n_signals = {e: len(rank[e]) for e in self.ENG}

        def run(eng, e):
            sem = self.esem[eng]
            rk = rank[eng]
            for it in self.items[eng]:
                if it[0] == 'we':
                    e.wait_ge(self.esem[it[1]], rank[it[1]][it[2]])
                elif it[0] == 'wc':
                    e.wait_ge(it[1], it[2])
                elif it[0] == 'op':
                    ins = it[1](e)
                    if it[2] in rk:
                        ins.then_inc(sem, 1)
                else:
                    it[1](e)

        @block.tensor
        def _(e):
            run("pe", e)

        @block.scalar
        def _(e):
            run("act", e)

        @block.vector
        def _(e):
            run("dve", e)

        @block.gpsimd
        def _(e):
            run("pool", e)

        @block.sync
        def _(e):
            run("sp", e)


class StopBuild(Exception):
    pass


def build_program(n_layers=2, n_tiles=NT, do_sample=True, dbg=None, stop_at=None):
    nc = bass.Bass("TRN2", target_bir_lowering=False)
    dt_in = lambda name, shape: nc.dram_tensor(name, shape, F32, kind="ExternalInput").ap()
    dt_out = lambda name, shape: nc.dram_tensor(name, shape, F32, kind="ExternalOutput").ap()
    xp = dt_in("xp", [T, D])
    xs = dt_in("xs", [NS, D])
    sret = dt_in("sret", [2, NS, H, HD, HD])
    sC = dt_in("sC", [2, NS, H, HD, HD])
    sn = dt_in("sn", [2, NS, H, HD])
    sm = dt_in("sm", [2, NS, H])
    sconv = dt_in("sconv", [2, NS, 3, D])
    w_in = dt_in("w_in", [2, D, N_IN])
    conv_w = dt_in("conv_w", [2, 4, D])
    conv_b = dt_in("conv_b", [2, D])
    b_i = dt_in("b_i", [2, H])
    b_f = dt_in("b_f", [2, H])
    g_ret = dt_in("g_ret", [2, 512])
    g_m = dt_in("g_m", [2, 512])
    w_out = dt_in("w_out", [2, D, D])
    ln_g = dt_in("ln_g", [2, D])
    ln_b = dt_in("ln_b", [2, D])
    cst_d = dt_in("cst", [128, NCST])
    rope_p = dt_in("rope_p", [T, 2, HD])
    rope_s = dt_in("rope_s", [2 * HD])

    y_p = dt_out("y_p", [T, D])
    y_s = dt_out("y_s", [NS, D])
    ret_p = dt_out("ret_p", [2, H, HD, HD])
    C_p = dt_out("C_p", [2, H, HD, HD])
    n_p = dt_out("n_p", [2, H, HD])
    m_p = dt_out("m_p", [2, H])
    conv_p = dt_out("conv_p", [2, 3, D])
    ret_s = dt_out("ret_s", [2, NS, H, HD, HD])
    C_s = dt_out("C_s", [2, NS, H, HD, HD])
    n_s = dt_out("n_s", [2, NS, H, HD])
    m_s = dt_out("m_s", [2, NS, H])
    conv_s = dt_out("conv_s", [2, NS, 3, D])
    y0 = nc.dram_tensor("y0_scratch", [T, D], F32, kind="Internal").ap()
    dbg_out = {}
    if dbg:
        for name, shape in dbg.items():
            dbg_out[name] = dt_out("dbg_" + name, shape)

    with ExitStack() as st:
        P = Prog(nc, st)
        sb = lambda name, shape, dt=F32: st.enter_context(nc.sbuf_tensor("s_" + name, shape, dt))
        out_chans = []

        def ochan(name):
            c = P.chan(name)
            out_chans.append(c)
            return c

        banks = []
        for i in range(8):
            t_ = st.enter_context(nc.psum_tensor("bank%d" % i, [128, 512], F32))
            banks.append((t_, Buf("bank%d" % i, psum=True)))
        bank_ctr = [0]

        def nb():
            i = bank_ctr[0] % 8
            bank_ctr[0] += 1
            return banks[i]

        def bfv(bank_t):
            return bank_t[:].bitcast(BF16)

        cst = sb("cst", [128, NCST]); CST = Buf("cst")
        ident_bf = sb("ident_bf", [128, 128], BF16); IDB = Buf()
        mask_bf = sb("mask_bf", [128, 128], BF16); MSK = Buf()
        c_ld = P.chan("c_ld")
        P.dma("sp", c_ld, cst[:], cst_d, writes=[CST])
        ident_f = cst[:, K_ID:K_ID + 128]
        tri_f = cst[:, K_TRI:K_TRI + 128]
        ones_f = cst[:, K_ONE:K_ONE + 128]
        P.op("dve", lambda e: e.tensor_copy(out=ident_bf[:], in_=ident_f), reads=[CST], writes=[IDB])
        P.op("dve", lambda e: e.tensor_copy(out=mask_bf[:], in_=tri_f), reads=[CST], writes=[MSK])
        kdec = lambda h: cst[:, K_KDEC + h:K_KDEC + h + 1]
        qdecT = cst[:, K_QDEC:K_QDEC + 512].rearrange("p (h l) -> p h l", h=4)

        w_in_sb = sb("w_in_sb", [128, 8, N_IN], BF16)
        WIN = [Buf("win%d" % i) for i in range(10)]
        w_out_sb = sb("w_out_sb", [128, 8, D], BF16)
        WOUT = [Buf("wout0"), Buf("wout1")]
        c_win = [P.chan("c_win%d" % i) for i in range(10)]
        c_wout = [P.chan("c_wout%d" % i) for i in range(2)]
        gbc = sb("gbc", [128, 1024]); GBC = Buf()
        lng = sb("lng", [128, 1024]); LNG = Buf()
        lnb = sb("lnb", [128, 1024]); LNB = Buf()
        bias8 = sb("bias8", [128, 8]); BIAS8 = Buf()
        cwb_in = sb("cwb_in", [40, 128]); CWBIN = Buf()
        cwT = sb("cwT", [128, 40]); CWT = Buf()
        diag = sb("diag", [128, 32, 128], BF16); DIAG = Buf()
        c_par = [P.chan("c_par%d" % i) for i in range(8)]

        def wblk(i):
            return (i * 512, min((i + 1) * 512, N_IN))

        def load_weights(l):
            wv = w_in[l].rearrange("(k p) n -> p k n", p=128)
            for i in range(10):
                a, b_ = wblk(i)
                P.dma("pool", c_win[i], w_in_sb[:, :, a:b_], wv[:, :, a:b_], writes=[WIN[i]])
            wo = w_out[l].rearrange("(k p) n -> p k n", p=128)
            for i in range(2):
                P.dma("pool", c_wout[i], w_out_sb[:, :, i * 512:(i + 1) * 512], wo[:, :, i * 512:(i + 1) * 512],
                      writes=[WOUT[i]])

        def load_params(l):
            P.dma("sp", c_par[0], gbc[:, 0:512], g_ret[l].partition_broadcast(128), writes=[GBC])
            P.dma("sp", c_par[1], gbc[:, 512:1024], g_m[l].partition_broadcast(128), writes=[GBC])
            P.dma("sp", c_par[2], lng[:], ln_g[l].partition_broadcast(128), writes=[LNG])
            P.dma("sp", c_par[3], lnb[:], ln_b[l].partition_broadcast(128), writes=[LNB])
            P.dma("sp", c_par[4], bias8[:, 0:4], b_i[l].partition_broadcast(128), writes=[BIAS8])
            P.dma("sp", c_par[5], bias8[:, 4:8], b_f[l].partition_broadcast(128), writes=[BIAS8])
            P.dma("sp", c_par[6], cwb_in[0:32, :], conv_w[l].rearrange("j (ch c) -> (j ch) c", c=128), writes=[CWBIN])
            P.dma("sp", c_par[7], cwb_in[32:40, :], conv_b[l].rearrange("(ch c) -> ch c", c=128), writes=[CWBIN])
            P.op("pool", lambda e: e.tensor_scalar(out=gbc[:, 512:1024], in0=gbc[:, 512:1024], scalar1=0.5, scalar2=None,
                                                   op0=ALU.mult), reads=[GBC], writes=[GBC])
            bt, BK = nb()
            P.op("pe", lambda e: e.transpose(bt[:, 0:40], cwb_in[0:40, :], ident_f[0:40, 0:40]), reads=[CWBIN, CST], writes=[BK])
            P.op("act", lambda e: e.copy(out=cwT[:], in_=bt[:, 0:40]), reads=[BK], writes=[CWT])
            for ch in range(8):
                for j in range(4):
                    idx = j * 8 + ch
                    P.op("pool", lambda e, ch=ch, j=j, idx=idx: e.tensor_scalar(
                        out=diag[:, ch * 4 + j, :], in0=ident_f, scalar1=cwT[:, idx:idx + 1], scalar2=None, op0=ALU.mult),
                        reads=[CST, CWT], writes=[DIAG])

        gates = sb("gates", [128, NT, 8]); GATES = Buf()
        lneg = sb("lneg", [128, NT, 4]); LNEG = Buf()
        bneg = sb("bneg", [128, NT, 4]); BNEG = Buf()
        u_sb = sb("u_sb", [128, NT, 4]); USB = Buf()
        uT_sb = sb("uT_sb", [64, 128]); UTS = Buf()
        umaxc = sb("umaxc", [64, 1]); UMX = Buf()
        row = sb("row", [1, 6, 64]); ROW = Buf()
        cw_b = sb("cw_b", [128, 2, NT + 1, 4]); CWB = Buf()
        pk = sb("pk", [128, NT, 4]); PK = Buf()
        thr = sb("thr", [128, NT, 4]); THR = Buf()
        tmp64 = sb("tmp64", [128, NT, 4]); TMP64 = Buf()

        x_sb = [sb("x_sb%d" % i, [128, D]) for i in range(3)]; XSB = [Buf(), Buf(), Buf()]
        rope_sb = [sb("rope_sb%d" % i, [128, 2, HD]) for i in range(2)]; ROPE = [Buf(), Buf()]
        c_x = [P.chan("c_x0"), P.chan("c_x1"), P.chan("c_x2")]
        c_rope = [P.chan("c_rope0"), P.chan("c_rope1")]
        x_bf2 = [sb("x_bf%d" % i, [128, D], BF16) for i in range(2)]; XBF2 = [Buf(), Buf()]
        x_bf = x_bf2[0]; XBF = XBF2[0]
        c_xb = [P.chan("c_xb0"), P.chan("c_xb1")]
        xT2 = [sb("xT%d" % i, [128, 8, 128], BF16) for i in range(2)]; XT2 = [Buf(), Buf()]
        xT = xT2[0]; XT = XT2[0]
        tmp_t = sb("tmp_t", [128, 8, HD]); TMPT = Buf(); TMPT_K = Buf()
        tmp_u = sb("tmp_u", [128, 8, HD]); TMPU = Buf(); TMPU_K = Buf()
        qk_rot = sb("qk_rot", [128, 8, HD], BF16); QKR = Buf(); QKR_K = Buf()
        v2 = sb("v2", [128, 4, HD], BF16); V2 = Buf()
        sz2 = [sb("sz%d" % i, [128, 1024]) for i in range(2)]; SZ2 = [Buf(), Buf()]
        sz = sz2[0]; SZ = SZ2[0]
        SZA2 = [Buf(), Buf()]; SZA = SZA2[0]
        qT2 = sb("qT2", [128, 4, 128], BF16); QT2 = Buf()
        kT = sb("kT", [128, 4, 128], BF16); KT = Buf()
        s2 = sb("s2", [128, 4, 128], BF16); S2 = Buf()
        S_f = sb("S_f", [128, 4, HD]); SF = Buf()
        S_bf = sb("S_bf", [128, 4, HD], BF16); SBF = Buf()
        hist = sb("hist", [128, 8, 131], BF16); HIST = Buf()
        qkm = sb("qkm", [128, 8, 128], BF16); QKM = Buf()
        kp = sb("kp", [128, 4, 128], BF16); KP = Buf()
        v1 = sb("v1", [128, 4, 130], BF16); V1 = Buf()
        th = sb("th", [128, 512]); TH = Buf()
        s2m = sb("s2m", [128, 4, 128], BF16); S2M = Buf()
        C_f = sb("C_f", [128, 4, 130]); CF = Buf()
        C_bf = sb("C_bf", [128, 4, 130], BF16); CBF = Buf()
        dn = sb("dn", [128, 4]); DN = Buf()
        hm = sb("hm", [128, 4, HD]); HM = Buf()
        o_sb = sb("o_sb", [128, 4, HD]); OSB = Buf()
        stats = sb("stats", [128, 8, 6]); STATS = Buf()
        mv = sb("mv", [128, 8, 2]); MV = Buf()
        rstd = sb("rstd", [128, 8]); RSTD = Buf()
        nbias = sb("nbias", [128, 8]); NBIAS = Buf()
        mhalf = sb("mhalf", [128, 8]); MHALF = Buf()
        on = sb("on", [128, 8, HD]); ON = Buf(); ON_M = Buf()
        gated = sb("gated", [128, 1024], BF16); GATED = Buf(); GATED_M = Buf()
        gT = sb("gT", [128, 8, 128], BF16); GT = Buf()
        ypre = sb("ypre", [128, D]); YPRE = Buf(); YPRE_B = Buf()
        lstats = sb("lstats", [128, 2, 6]); LSTATS = Buf()
        lmv = sb("lmv", [128, 2]); LMV = Buf()
        lrs = sb("lrs", [128, 2]); LRS = Buf()
        c_y = [ochan("c_y0"), ochan("c_y1")]
        convp_sb = sb("convp_sb", [128, 8, 3]); CONVP = Buf()
        convp_T = sb("convp_T", [3, D]); CONVPT = Buf()
        c_fin = [ochan("c_fin%d" % i) for i in range(5)]

        P.op("pool", lambda e: e.memset(mhalf[:], -0.5), writes=[MHALF])

        def dbg_dump(name, src_ap, bufs):
            if name in dbg_out:
                c = ochan("c_dbg_" + name)
                P.dma("sp", c, dbg_out[name], src_ap, reads=bufs)

        def load_x(l, t, par):
            src = xp if l == 0 else y0
            P.dma("sp", c_x[par], x_sb[par][:], src[t * 128:(t + 1) * 128, :], writes=[XSB[par]],
                  reads=([Y0B[t]] if l > 0 else []))

        def load_xbf(l, t, par):
            src = xp if l == 0 else y0
            P.dma("pool", c_xb[par], x_bf2[par][:], src[t * 128:(t + 1) * 128, :], writes=[XBF2[par]],
                  reads=([Y0B[t]] if l > 0 else []))

        def load_main(l, t):
            par = t % 2
            load_x(l, t, t % 3)
            P.dma("sp", c_rope[par], rope_sb[par][:], rope_p[t * 128:(t + 1) * 128], writes=[ROPE[par]])

        def make_xT(par, xT=None, XT=None, xi=None):
            if xT is None:
                xT, XT = xT2[0], XT2[0]
            if xi is None:
                xi = par
            xb = x_bf2[par]; XB = XBF2[par]
            P.op("dve", lambda e: e.tensor_copy(out=xb[:], in_=x_sb[xi][:]), reads=[XSB[xi]], writes=[XB])
            bt, BK = nb()
            v = bfv(bt)
            for k in range(8):
                P.op("pe", lambda e, k=k: e.transpose(v[:, k * 128:(k + 1) * 128], xb[:, k * 128:(k + 1) * 128], ident_bf[:]),
                     reads=[XB, IDB], writes=[BK])
            P.op("act", lambda e: e.copy(out=xT[:].rearrange("p k c -> p (k c)"), in_=v), reads=[BK], writes=[XT])

        Y0B = [Buf("y0_%d" % t) for t in range(NT)]

        def prepass(l):
            load_x(l, 0, 0)
            for t in range(n_tiles):
                par = t % 2
                if t + 1 < n_tiles:
                    load_x(l, t + 1, (t + 1) % 2)
                make_xT(par)
                bt, BK = nb()
                for k in range(8):
                    P.op("pe", lambda e, k=k: e.matmul(bt[:, 0:8], lhsT=xT[:, k, :], rhs=w_in_sb[:, k, C_G:C_G + 8],
                                                       start=(k == 0), stop=(k == 7)), reads=[XT, WIN[9]], writes=[BK])
                P.op("dve", lambda e, t=t: e.tensor_tensor(out=gates[:, t, :], in0=bt[:, 0:8], in1=bias8[:], op=ALU.add),
                     reads=[BK, BIAS8], writes=[GATES])
            nt = n_tiles
            stage('pp_loop')
            P.op("act", lambda e: e.activation(out=lneg[:, 0:nt, :], in_=gates[:, 0:nt, 4:8], func=AF.Exp, scale=-1.0),
                 reads=[GATES], writes=[LNEG])
            P.op("act", lambda e: e.activation(out=lneg[:, 0:nt, :], in_=lneg[:, 0:nt, :], func=AF.Ln, bias=1.0),
                 reads=[LNEG], writes=[LNEG])
            dbg_dump('gates', gates[:], [GATES])
            dbg_dump('lneg', lneg[:], [LNEG])
            stage('pp_a')
            bt, BK = nb()
            ln2 = lneg[:].rearrange("p t h -> p (t h)")
            P.op("pe", lambda e: e.matmul(bt[:, 0:nt * 4], lhsT=tri_f, rhs=ln2[:, 0:nt * 4], start=True, stop=True),
                 reads=[LNEG, CST], writes=[BK])
            stage('pp_b')
            bt2, BK2 = nb()
            P.op("pe", lambda e: e.matmul(bt2[0:1, 0:nt * 4], lhsT=ones_f[:, 0:1], rhs=ln2[:, 0:nt * 4], start=True, stop=True),
                 reads=[LNEG, CST], writes=[BK2])
            stage('pp_c')
            P.op("act", lambda e: e.copy(out=bneg[:].rearrange("p t h -> p (t h)")[:, 0:nt * 4], in_=bt[:, 0:nt * 4]),
                 reads=[BK], writes=[BNEG])
            P.op("dve", lambda e: e.tensor_tensor(out=u_sb[:, 0:nt, :], in0=gates[:, 0:nt, 0:4], in1=bneg[:, 0:nt, :], op=ALU.add),
                 reads=[GATES, BNEG], writes=[USB])
            P.op("act", lambda e: e.copy(out=row[0:1, 1, 0:nt * 4], in_=bt2[0:1, 0:nt * 4]), reads=[BK2], writes=[ROW])
            stage('pp_cum')
            bt3, BK3 = nb()
            u2 = u_sb[:].rearrange("p t h -> p (t h)")
            P.op("pe", lambda e: e.transpose(bt3[0:nt * 4, 0:128], u2[:, 0:nt * 4], ident_f), reads=[USB, CST], writes=[BK3])
            P.op("dve", lambda e: e.tensor_reduce(out=umaxc[0:nt * 4, :], in_=bt3[0:nt * 4, 0:128], axis=AX.X, op=ALU.max),
                 reads=[BK3], writes=[UMX])
            bt4, BK4 = nb()
            P.op("pe", lambda e: e.transpose(bt4[0:1, 0:nt * 4], umaxc[0:nt * 4, 0:1], ident_f[0:nt * 4, 0:nt * 4]),
                 reads=[UMX, CST], writes=[BK4])
            P.op("act", lambda e: e.copy(out=row[0:1, 0, 0:nt * 4], in_=bt4[0:1, 0:nt * 4]), reads=[BK4], writes=[ROW])
            stage('pp_umax')
            rv_ = lambda i: row[0:1, i, 0:nt * 4].rearrange("p (t h) -> p t h", h=4)
            P.op("dve", lambda e: e.tensor_scalar(out=row[0:1, 3, 0:nt * 4], in0=row[0:1, 1, 0:nt * 4], scalar1=-1.0, scalar2=None,
                                                  op0=ALU.mult), reads=[ROW], writes=[ROW])
            for h in range(4):
                P.op("dve", lambda e, h=h: e.tensor_tensor_scan(out=rv_(2)[:, :, h], data0=rv_(0)[:, :, h], data1=rv_(3)[:, :, h],
                                                                initial=0.0, op0=ALU.max, op1=ALU.add), reads=[ROW], writes=[ROW])
            P.op("dve", lambda e: e.tensor_tensor(out=row[0:1, 3, 0:nt * 4], in0=row[0:1, 2, 0:nt * 4], in1=row[0:1, 1, 0:nt * 4],
                                                  op=ALU.add), reads=[ROW], writes=[ROW])
            P.op("dve", lambda e: e.memset(row[0:1, 5, 0:4], 0.0), reads=[ROW], writes=[ROW])
            if nt > 1:
                P.op("dve", lambda e: e.tensor_copy(out=row[0:1, 5, 4:nt * 4], in_=row[0:1, 2, 0:(nt - 1) * 4]), reads=[ROW], writes=[ROW])
            P.op("dve", lambda e: e.tensor_tensor(out=row[0:1, 4, 0:nt * 4], in0=row[0:1, 5, 0:nt * 4], in1=row[0:1, 3, 0:nt * 4],
                                                  op=ALU.subtract), reads=[ROW], writes=[ROW])
            P.op("act", lambda e: e.activation(out=row[0:1, 4, 0:nt * 4], in_=row[0:1, 4, 0:nt * 4], func=AF.Exp),
                 reads=[ROW], writes=[ROW])
            stage('pp_scan')
            bt5, BK5 = nb()
            P.op("pe", lambda e: e.matmul(bt5[:, 0:128], lhsT=ones_f[0:1, :], rhs=row[0:1, 3:5, :].rearrange("p a b -> p (a b)"),
                                          start=True, stop=True), reads=[ROW, CST], writes=[BK5])
            P.op("act", lambda e: e.copy(out=cw_b[:, :, 0:NT, :], in_=bt5[:, 0:128].rearrange("p (a t h) -> p a t h", a=2, h=4)),
                 reads=[BK5], writes=[CWB])
            P.op("pool", lambda e: e.memset(cw_b[:, :, NT, :], 1.0), reads=[CWB], writes=[CWB])
            P.op("dve", lambda e: e.tensor_tensor(out=tmp64[:, 0:nt, :], in0=u_sb[:, 0:nt, :], in1=cw_b[:, 0, 0:nt, :], op=ALU.subtract),
                 reads=[USB, CWB], writes=[TMP64])
            P.op("dve", lambda e: e.tensor_scalar(out=tmp64[:, 0:nt, :], in0=tmp64[:, 0:nt, :], scalar1=LNISQ, scalar2=None, op0=ALU.add),
                 reads=[TMP64], writes=[TMP64])
            P.op("act", lambda e: e.activation(out=pk[:, 0:nt, :], in_=tmp64[:, 0:nt, :], func=AF.Exp),
                 reads=[TMP64], writes=[PK])
            P.op("dve", lambda e: e.tensor_tensor(out=tmp64[:, 0:nt, :], in0=bneg[:, 0:nt, :], in1=cw_b[:, 0, 0:nt, :], op=ALU.subtract),
                 reads=[BNEG, CWB, PK], writes=[TMP64])
            P.op("act", lambda e: e.activation(out=thr[:, 0:nt, :], in_=tmp64[:, 0:nt, :], func=AF.Exp), reads=[TMP64], writes=[THR])
            P.dma("sp", c_fin[0], m_p[l:l + 1, :], row[0:1, 2, (nt - 1) * 4:nt * 4], reads=[ROW])

        def main_tile(l, t):
            par = t % 2
            last = (t == n_tiles - 1)
            xT = xT2[par]; XT = XT2[par]
            xi = t % 3
            sz = sz2[par]; SZ = SZ2[par]; SZA = SZA2[par]
            if t == 0:
                load_main(l, 0)
            if not last:
                load_main(l, t + 1)
            make_xT(par, xT, XT, xi)
            yield

            def proj(bt, BK, c0, n, wb):
                for k in range(8):
                    P.op("pe", lambda e, k=k: e.matmul(bt[:, 0:n], lhsT=xT[:, k, :], rhs=w_in_sb[:, k, c0:c0 + n],
                                                       start=(k == 0), stop=(k == 7)), reads=[XT, WIN[wb]], writes=[BK])

            cos2 = rope_sb[par][:, 0, :]
            sin2 = rope_sb[par][:, 1, :]
            for i in range(2):
                bt, BK = nb(); proj(bt, BK, C_RQ if i == 0 else C_RK, 512, i)
                src = bt[:].rearrange("p (h d) -> p h d", h=4)
                dst_t = tmp_t[:, i * 4:(i + 1) * 4, :]
                dst_u = tmp_u[:, i * 4:(i + 1) * 4, :]
                TT_ = TMPT if i == 0 else TMPT_K
                TU_ = TMPU if i == 0 else TMPU_K
                P.op("dve", lambda e, src=src, dst_t=dst_t: e.tensor_tensor(
                    out=dst_t, in0=src, in1=cos2.unsqueeze(1).broadcast_to([128, 4, HD]), op=ALU.mult),
                    reads=[BK, ROPE[par]], writes=[TT_])
                P.op("dve", lambda e, src=src, dst_u=dst_u: e.tensor_tensor(
                    out=dst_u[:, :, 0:64], in0=src[:, :, 64:128], in1=sin2[:, 0:64].unsqueeze(1).broadcast_to([128, 4, 64]), op=ALU.mult),
                    reads=[BK, ROPE[par]], writes=[TU_])
                P.op("dve", lambda e, src=src, dst_u=dst_u: e.tensor_tensor(
                    out=dst_u[:, :, 64:128], in0=src[:, :, 0:64], in1=sin2[:, 64:128].unsqueeze(1).broadcast_to([128, 4, 64]), op=ALU.mult),
                    reads=[BK, ROPE[par]], writes=[TU_])
                if i == 0:
                    P.op("pool", lambda e: e.tensor_tensor(out=qk_rot[:, 0:4, :], in0=tmp_t[:, 0:4, :], in1=tmp_u[:, 0:4, :], op=ALU.add),
                         reads=[TMPT, TMPU], writes=[QKR])
                    yield
            P.op("dve", lambda e: e.tensor_tensor(out=qk_rot[:, 4:8, :], in0=tmp_t[:, 4:8, :], in1=tmp_u[:, 4:8, :], op=ALU.add),
                 reads=[TMPT_K, TMPU_K], writes=[QKR_K])
            yield
            bv, BV = nb(); proj(bv, BV, C_RV, 512, 2)
            for h in range(4):
                P.op("act", lambda e, h=h: e.activation(out=v2[:, h, :], in_=bv[:, h * 128:(h + 1) * 128], func=AF.Copy, scale=kdec(h)),
                     reads=[BV, CST], writes=[V2])
            yield
            bz, BZ = nb(); proj(bz, BZ, C_RZ, 512, 3)
            P.op("act", lambda e: e.activation(out=sz[:, 0:512], in_=bz[:], func=AF.Silu), reads=[BZ], writes=[SZA])
            P.op("pool", lambda e: e.tensor_tensor(out=sz[:, 0:512], in0=sz[:, 0:512], in1=gbc[:, 0:512], op=ALU.mult), reads=[SZA, GBC], writes=[SZA])
            yield
            btq, BTQ = nb(); vtq = bfv(btq)
            for g in range(4):
                P.op("pe", lambda e, g=g: e.transpose(vtq[:, g * 128:(g + 1) * 128], qk_rot[:, g, :], ident_bf[:]),
                     reads=[QKR, IDB], writes=[BTQ])
            P.op("dve", lambda e: e.tensor_tensor(out=qT2[:], in0=vtq[:, 0:512].rearrange("p (h l) -> p h l", h=4), in1=qdecT, op=ALU.mult),
                 reads=[BTQ, CST], writes=[QT2])
            yield
            btk, BTK = nb(); vtk = bfv(btk)
            for g in range(4):
                P.op("pe", lambda e, g=g: e.transpose(vtk[:, g * 128:(g + 1) * 128], qk_rot[:, 4 + g, :], ident_bf[:]),
                     reads=[QKR_K, IDB], writes=[BTK])
            P.op("act", lambda e: e.copy(out=kT[:].rearrange("p h l -> p (h l)"), in_=vtk[:, 0:512]), reads=[BTK], writes=[KT])
            yield
            bs, BS = nb()
            for h in range(4):
                P.op("pe", lambda e, h=h: e.matmul(bs[:, h * 128:(h + 1) * 128], lhsT=kT[:, h, :], rhs=qT2[:, h, :], start=True, stop=True),
                     reads=[KT, QT2], writes=[BS])
            P.op("dve", lambda e: e.tensor_tensor(out=s2[:], in0=bs[:].rearrange("p (h l) -> p h l", h=4),
                                                  in1=mask_bf[:].unsqueeze(1).broadcast_to([128, 4, 128]), op=ALU.mult),
                 reads=[BS, MSK], writes=[S2])
            yield
            bo, BO = nb()
            for h in range(4):
                first = (t == 0)
                P.op("pe", lambda e, h=h, first=first: e.matmul(bo[:, h * 128:(h + 1) * 128], lhsT=s2[:, h, :], rhs=v2[:, h, :],
                                                                start=True, stop=first), reads=[S2, V2], writes=[BO])
                if not first:
                    P.op("pe", lambda e, h=h: e.matmul(bo[:, h * 128:(h + 1) * 128], lhsT=qT2[:, h, :], rhs=S_bf[:, h, :],
                                                       start=False, stop=True), reads=[QT2, SBF], writes=[BO])
            P.op("act", lambda e: e.copy(out=o_sb[:].rearrange("p h d -> p (h d)"), in_=bo[:]), reads=[BO], writes=[OSB])
            for h in range(4):
                P.op("dve", lambda e, h=h: e.bn_stats(out=stats[:, h, :], in_=o_sb[:, h, :]), reads=[OSB], writes=[STATS])
            yield
            bu, BU = nb()
            for h in range(4):
                P.op("pe", lambda e, h=h: e.matmul(bu[:, h * 128:(h + 1) * 128], lhsT=qk_rot[:, 4 + h, :], rhs=v2[:, h, :],
                                                   start=True, stop=True), reads=[QKR_K, V2], writes=[BU])
            for h in range(4):
                if t == 0:
                    P.op("dve", lambda e, h=h: e.tensor_copy(out=S_f[:, h, :], in_=bu[:, h * 128:(h + 1) * 128]), reads=[BU], writes=[SF])
                else:
                    P.op("dve", lambda e, h=h: e.scalar_tensor_tensor(out=S_f[:, h, :], in0=S_f[:, h, :], scalar=GAML[h],
                                                                     in1=bu[:, h * 128:(h + 1) * 128], op0=ALU.mult, op1=ALU.add),
                         reads=[BU, SF], writes=[SF])
            if not last:
                for h in range(4):
                    P.op("act", lambda e, h=h: e.activation(out=S_bf[:, h, :], in_=S_f[:, h, :], func=AF.Copy, scale=GAML[h]),
                         reads=[SF], writes=[SBF])
            yield
            if t == 0:
                P.op("pool", lambda e: e.memset(hist[:, :, 0:3], 0.0), writes=[HIST])
            else:
                P.op("pool", lambda e: e.tensor_copy(out=hist[:, :, 0:3], in_=hist[:, :, 128:131]), reads=[HIST], writes=[HIST])
            for i in range(2):
                bt, BK = nb()
                for c4 in range(4):
                    ch = i * 4 + c4
                    c0 = C_MQK + ch * 128
                    wb = 4 + ch // 4
                    for k in range(8):
                        P.op("pe", lambda e, k=k, bt=bt, c4=c4, c0=c0: e.matmul(bt[:, c4 * 128:(c4 + 1) * 128],
                                                                             lhsT=w_in_sb[:, k, c0:c0 + 128], rhs=xT[:, k, :],
                                                                             start=(k == 0), stop=(k == 7)),
                             reads=[XT, WIN[wb]], writes=[BK])
                P.op("act", lambda e, i=i, bt=bt: e.copy(out=hist[:, i * 4:(i + 1) * 4, 3:131], in_=bt[:].rearrange("p (c t) -> p c t", c=4)),
                     reads=[BK], writes=[HIST])
                if last:
                    P.op("dve", lambda e, i=i, bt=bt: e.tensor_copy(out=convp_sb[:, i * 4:(i + 1) * 4, :],
                                                                    in_=bt[:].rearrange("p (c t) -> p c t", c=4)[:, :, 125:128]),
                         reads=[BK], writes=[CONVP])
                yield
            for i in range(2):
                bt, BK = nb()
                for c4 in range(4):
                    ch = i * 4 + c4
                    for j in range(4):
                        P.op("pe", lambda e, j=j, bt=bt, ch=ch, c4=c4: e.matmul(bt[:, c4 * 128:(c4 + 1) * 128], lhsT=diag[:, ch * 4 + j, :],
                                                                            rhs=hist[:, ch, j:j + 128], start=(j == 0), stop=(j == 3)),
                             reads=[DIAG, HIST], writes=[BK])
                for c4 in range(4):
                    ch = i * 4 + c4
                    P.op("act", lambda e, bt=bt, ch=ch, c4=c4: e.activation(out=qkm[:, ch, :], in_=bt[:, c4 * 128:(c4 + 1) * 128],
                                                                        func=AF.Silu, bias=cwT[:, 32 + ch:33 + ch]),
                         reads=[BK, CWT], writes=[QKM])
                yield
            bkt, BKT = nb(); vkt = bfv(bkt)
            for h in range(4):
                P.op("pe", lambda e, h=h: e.transpose(vkt[:, h * 128:(h + 1) * 128], qkm[:, 4 + h, :], ident_bf[:]),
                     reads=[QKM, IDB], writes=[BKT])
            for h in range(4):
                P.op("act", lambda e, h=h: e.activation(out=kp[:, h, :], in_=vkt[:, h * 128:(h + 1) * 128], func=AF.Copy,
                                                        scale=pk[:, t, h:h + 1]), reads=[BKT, PK], writes=[KP])
            yield
            bmv, BMV = nb(); proj(bmv, BMV, C_MV, 512, 6)
            if l == 0 and t == 0:
                P.op("pool", lambda e: e.memset(v1[:, :, 128:130], 1.0), writes=[V1])
            P.op("act", lambda e: e.copy(out=v1[:, :, 0:128], in_=bmv[:].rearrange("p (h d) -> p h d", h=4)), reads=[BMV], writes=[V1])
            yield
            bmo, BMO = nb(); proj(bmo, BMO, C_MO, 512, 7)
            P.op("act", lambda e: e.activation(out=th[:], in_=bmo[:], func=AF.Tanh, scale=0.5), reads=[BMO], writes=[TH])
            yield
            bmz, BMZ = nb(); proj(bmz, BMZ, C_MZ, 512, 8)
            P.op("act", lambda e: e.activation(out=sz[:, 512:1024], in_=bmz[:], func=AF.Silu), reads=[BMZ], writes=[SZ])
            P.op("dve", lambda e: e.scalar_tensor_tensor(out=sz[:, 512:1024], in0=th[:], scalar=1.0, in1=sz[:, 512:1024], op0=ALU.add, op1=ALU.mult),
                 reads=[TH, SZ], writes=[SZ])
            P.op("pool", lambda e: e.tensor_tensor(out=sz[:, 512:1024], in0=sz[:, 512:1024], in1=gbc[:, 512:1024], op=ALU.mult), reads=[SZ, GBC], writes=[SZ])
            yield
            bsm, BSM = nb()
            for h in range(4):
                P.op("pe", lambda e, h=h: e.matmul(bsm[:, h * 128:(h + 1) * 128], lhsT=qkm[:, 4 + h, :], rhs=qkm[:, h, :], start=True, stop=True),
                     reads=[QKM], writes=[BSM])
            for h in range(4):
                P.op("dve", lambda e, h=h: e.scalar_tensor_tensor(out=s2m[:, h, :], in0=bsm[:, h * 128:(h + 1) * 128], scalar=pk[:, t, h:h + 1],
                                                                 in1=mask_bf[:], op0=ALU.mult, op1=ALU.mult),
                     reads=[BSM, PK, MSK], writes=[S2M])
            yield
            bn0, BN0 = nb(); bn1, BN1 = nb()
            for h in range(4):
                bt, BK = (bn0, BN0) if h < 2 else (bn1, BN1)
                o_ = (h % 2) * 130
                first = (t == 0)
                P.op("pe", lambda e, h=h, bt=bt, o_=o_, first=first: e.matmul(bt[:, o_:o_ + 130], lhsT=s2m[:, h, :], rhs=v1[:, h, :],
                                                                            start=True, stop=first), reads=[S2M, V1], writes=[BK])
                if not first:
                    P.op("pe", lambda e, h=h, bt=bt, o_=o_: e.matmul(bt[:, o_:o_ + 130], lhsT=qkm[:, h, :], rhs=C_bf[:, h, :],
                                                                   start=False, stop=True), reads=[QKM, CBF], writes=[BK])
            for i, (bt, BK) in enumerate(((bn0, BN0), (bn1, BN1))):
                P.op("act", lambda e, i=i, bt=bt: e.activation(out=dn[:, 2 * i:2 * i + 2], in_=bt[:, 128:259:130], func=AF.Abs),
                     reads=[BK], writes=[DN])
            P.op("dve", lambda e: e.tensor_tensor(out=dn[:], in0=dn[:], in1=thr[:, t, :], op=ALU.max), reads=[DN, THR], writes=[DN])
            P.op("dve", lambda e: e.reciprocal(out=dn[:], in_=dn[:]), reads=[DN], writes=[DN])
            for h in range(4):
                bt, BK = (bn0, BN0) if h < 2 else (bn1, BN1)
                o_ = (h % 2) * 130
                P.op("act", lambda e, h=h, bt=bt, o_=o_: e.activation(out=hm[:, h, :], in_=bt[:, o_:o_ + 128], func=AF.Copy, scale=dn[:, h:h + 1]),
                     reads=[BK, DN], writes=[HM])
            for h in range(4):
                P.op("dve", lambda e, h=h: e.bn_stats(out=stats[:, 4 + h, :], in_=hm[:, h, :]), reads=[HM], writes=[STATS])
            yield
            bu0, BU0 = nb(); bu1, BU1 = nb()
            for h in range(4):
                bt, BK = (bu0, BU0) if h < 2 else (bu1, BU1)
                o_ = (h % 2) * 130
                P.op("pe", lambda e, h=h, bt=bt, o_=o_: e.matmul(bt[:, o_:o_ + 130], lhsT=kp[:, h, :], rhs=v1[:, h, :], start=True, stop=True),
                     reads=[KP, V1], writes=[BK])
            for h in range(4):
                bt, BK = (bu0, BU0) if h < 2 else (bu1, BU1)
                o_ = (h % 2) * 130
                if t == 0:
                    P.op("dve", lambda e, h=h, bt=bt, o_=o_: e.tensor_copy(out=C_f[:, h, :], in_=bt[:, o_:o_ + 130]), reads=[BK], writes=[CF])
                else:
                    P.op("dve", lambda e, h=h, bt=bt, o_=o_: e.scalar_tensor_tensor(out=C_f[:, h, :], in0=C_f[:, h, :], scalar=cw_b[:, 1, t, h:h + 1],
                                                                                  in1=bt[:, o_:o_ + 130], op0=ALU.mult, op1=ALU.add),
                         reads=[BK, CF, CWB], writes=[CF])
            if not last:
                for h in range(4):
                    P.op("act", lambda e, h=h: e.activation(out=C_bf[:, h, :], in_=C_f[:, h, :], func=AF.Copy, scale=cw_b[:, 1, t + 1, h:h + 1]),
                         reads=[CF, CWB], writes=[CBF])
            yield
            for g in range(8):
                P.op("dve", lambda e, g=g: e.bn_aggr(out=mv[:, g, :], in_=stats[:, g, :]), reads=[STATS], writes=[MV])
            P.op("pool", lambda e: e.tensor_scalar(out=rstd[:], in0=mv[:, :, 1], scalar1=GN_EPS, scalar2=None, op0=ALU.add),
                 reads=[MV], writes=[RSTD])
            P.op("pool", lambda e: e.tensor_tensor(out=rstd[:], in0=rstd[:], in1=mhalf[:], op=ALU.pow), reads=[RSTD, MHALF], writes=[RSTD])
            P.op("dve", lambda e: e.scalar_tensor_tensor(out=nbias[:], in0=mv[:, :, 0], scalar=-1.0, in1=rstd[:], op0=ALU.mult, op1=ALU.mult),
                 reads=[MV, RSTD], writes=[NBIAS])
            for g in range(8):
                if g < 4:
                    src = o_sb[:, g, :]; SB_ = OSB
                else:
                    src = hm[:, g - 4, :]; SB_ = HM
                P.op("act", lambda e, g=g, src=src: e.activation(out=on[:, g, :], in_=src, func=AF.Identity, scale=rstd[:, g:g + 1],
                                                               bias=nbias[:, g:g + 1]), reads=[SB_, RSTD, NBIAS], writes=[ON if g < 4 else ON_M])
            onf = on[:].rearrange("p g d -> p (g d)")
            P.op("pool", lambda e: e.tensor_tensor(out=gated[:, 0:512], in0=onf[:, 0:512], in1=sz[:, 0:512], op=ALU.mult),
                 reads=[ON, SZA], writes=[GATED])
            P.op("dve", lambda e: e.tensor_tensor(out=gated[:, 512:1024], in0=onf[:, 512:1024], in1=sz[:, 512:1024], op=ALU.mult),
                 reads=[ON_M, SZ], writes=[GATED_M])
            yield
            bg, BG = nb(); vg = bfv(bg)
            for k in range(8):
                P.op("pe", lambda e, k=k: e.transpose(vg[:, k * 128:(k + 1) * 128], gated[:, k * 128:(k + 1) * 128], ident_bf[:]),
                     reads=[GATED if k < 4 else GATED_M, IDB], writes=[BG])
            P.op("act", lambda e: e.copy(out=gT[:].rearrange("p k c -> p (k c)"), in_=vg), reads=[BG], writes=[GT])
            for n in range(2):
                yield
                bt, BK = nb()
                for k in range(8):
                    P.op("pe", lambda e, k=k, n=n, bt=bt: e.matmul(bt[:], lhsT=gT[:, k, :], rhs=w_out_sb[:, k, n * 512:(n + 1) * 512],
                                                               start=(k == 0), stop=(k == 7)), reads=[GT, WOUT[n]], writes=[BK])
                P.op("dve", lambda e, n=n, bt=bt: e.scalar_tensor_tensor(out=ypre[:, n * 512:(n + 1) * 512], in0=x_sb[xi][:, n * 512:(n + 1) * 512],
                                                                     scalar=ALPHA, in1=bt[:], op0=ALU.mult, op1=ALU.add),
                     reads=[BK, XSB[xi]], writes=[YPRE if n == 0 else YPRE_B])
                P.op("dve", lambda e, n=n: e.bn_stats(out=lstats[:, n, :], in_=ypre[:, n * 512:(n + 1) * 512]), reads=[YPRE if n == 0 else YPRE_B], writes=[LSTATS])
            P.op("dve", lambda e: e.bn_aggr(out=lmv[:], in_=lstats[:].rearrange("p a b -> p (a b)")), reads=[LSTATS], writes=[LMV])
            P.op("pool", lambda e: e.tensor_scalar(out=lrs[:, 0:1], in0=lmv[:, 1:2], scalar1=LN_EPS, scalar2=None, op0=ALU.add),
                 reads=[LMV], writes=[LRS])
            P.op("pool", lambda e: e.tensor_tensor(out=lrs[:, 0:1], in0=lrs[:, 0:1], in1=mhalf[:, 0:1], op=ALU.pow), reads=[LRS, MHALF], writes=[LRS])
            P.op("dve", lambda e: e.scalar_tensor_tensor(out=lrs[:, 1:2], in0=lmv[:, 0:1], scalar=-1.0, in1=lrs[:, 0:1], op0=ALU.mult, op1=ALU.mult),
                 reads=[LMV, LRS], writes=[LRS])
            P.op("act", lambda e: e.activation(out=ypre[:], in_=ypre[:], func=AF.Identity, scale=lrs[:, 0:1], bias=lrs[:, 1:2]),
                 reads=[YPRE, YPRE_B, LRS], writes=[YPRE, YPRE_B])
            P.op("pool", lambda e: e.tensor_tensor(out=ypre[:, 0:512], in0=ypre[:, 0:512], in1=lng[:, 0:512], op=ALU.mult), reads=[YPRE, LNG], writes=[YPRE])
            P.op("dve", lambda e: e.tensor_tensor(out=ypre[:, 512:1024], in0=ypre[:, 512:1024], in1=lng[:, 512:1024], op=ALU.mult), reads=[YPRE_B, LNG], writes=[YPRE_B])
            P.op("pool", lambda e: e.tensor_tensor(out=ypre[:, 0:512], in0=ypre[:, 0:512], in1=lnb[:, 0:512], op=ALU.add), reads=[YPRE, LNB], writes=[YPRE])
            P.op("dve", lambda e: e.tensor_tensor(out=ypre[:, 512:1024], in0=ypre[:, 512:1024], in1=lnb[:, 512:1024], op=ALU.add), reads=[YPRE_B, LNB], writes=[YPRE_B])
            if l == n_layers - 1:
                P.dma("sp", c_y[par], y_p[t * 128:(t + 1) * 128, :], ypre[:], reads=[YPRE, YPRE_B])
            else:
                P.dma("sp", c_y[par], y0[t * 128:(t + 1) * 128, :], ypre[:], reads=[YPRE, YPRE_B], writes=[Y0B[t]])


        xs_sb = sb("xs_sb", [NS, D]); XSS = Buf("xs")
        ropes = sb("ropes", [NS, 2, HD]); ROPES = Buf()
        sg = sb("sg", [NS, 16, 4]); SG = Buf()
        wexp = sb("wexp", [NS, NS, 4]); WEXP = Buf()
        wcbs = sb("wcbs", [128, NS * 4]); WCBS = Buf()
        qTs = sb("qTs", [128, 8, NS]); QTS = Buf()
        oTs = sb("oTs", [128, 2, 64]); OTS = Buf()
        sstats = sb("sstats", [NS, 8, 6]); SSTATS = Buf()
        smv = sb("smv", [NS, 8, 2]); SMV = Buf()
        srs = sb("srs", [NS, 8]); SRS = Buf()
        snb = sb("snb", [NS, 8]); SNB = Buf()
        slst = sb("slst", [NS, 2, 6]); SLST = Buf()
        slmv = sb("slmv", [NS, 2]); SLMV = Buf()
        slrs = sb("slrs", [NS, 2]); SLRS = Buf()
        c_s = [P.chan("c_s%d" % i) for i in range(8)]
        c_sg = [P.chan("c_sg0"), P.chan("c_sg1")]
        c_cg = [P.chan("c_cg0"), P.chan("c_cg1")]
        c_so = [ochan("c_so%d" % i) for i in range(8)]
        c_sgo = [ochan("c_sgo0"), ochan("c_sgo1")]
        c_cgo = [ochan("c_cgo0"), ochan("c_cgo1")]
        id16 = ident_f[0:NS, 0:NS]
        gamb = cst[0:NS, K_GAM:K_GAM + 4]
        sbank_ctr = [0]

        def snb_():
            i = sbank_ctr[0] % 6
            sbank_ctr[0] += 1
            return banks[i]

        def sample_path(l):
            S16 = slice(0, NS)
            if l == 0:
                P.dma("sp", c_s[0], xs_sb[:], xs, writes=[XSS])
                P.dma("sp", c_s[7], ropes[:].rearrange("p a d -> p (a d)"), rope_s.partition_broadcast(NS), writes=[ROPES])
            sc = [x_sb[0][S16, :], x_sb[1][S16, :], x_sb[2][S16, :]]
            SCB = [XSB[0], XSB[1], XSB[2]]
            szs = sz2[0]; SZ = SZ2[0]; sz = sz2[0]
            for j in range(3):
                P.dma("sp", c_s[1 + j], sc[j], sconv[l][:, j, :], writes=[SCB[j]])
            P.dma("sp", c_s[4], sg[:, 0, :], sm[l], writes=[SG])
            n0v = o_sb[S16, :, :]
            P.dma("sp", c_s[5], n0v, sn[l], writes=[OSB])
            P.op("pool", lambda e: e.tensor_copy(out=x_bf[S16, :], in_=xs_sb[:]), reads=[XSS], writes=[XBF])
            bt, BK = nb(); v = bfv(bt)
            for k in range(8):
                P.op("pe", lambda e, k=k: e.transpose(v[:, k * NS:(k + 1) * NS], x_bf[S16, k * 128:(k + 1) * 128], ident_bf[S16, 0:NS]),
                     reads=[XBF, IDB], writes=[BK])
            P.op("act", lambda e: e.copy(out=xT[:, :, 0:NS], in_=v[:, 0:8 * NS].rearrange("p (k s) -> p k s", k=8)), reads=[BK], writes=[XT])

            def sproj(c0, n, wb):
                bt, BK = nb()
                for k in range(8):
                    P.op("pe", lambda e, k=k: e.matmul(bt[S16, 0:n], lhsT=xT[:, k, 0:NS], rhs=w_in_sb[:, k, c0:c0 + n],
                                                       start=(k == 0), stop=(k == 7)), reads=[XT, WIN[wb]], writes=[BK])
                return bt, BK

            cos2 = ropes[:, 0, :]
            sin2 = ropes[:, 1, :]
            for i, c0 in enumerate((C_RQ, C_RK)):
                bt, BK = sproj(c0, 512, i)
                src = bt[S16, :].rearrange("p (h d) -> p h d", h=4)
                dst_t = tmp_t[S16, i * 4:(i + 1) * 4, :]
                dst_u = tmp_u[S16, i * 4:(i + 1) * 4, :]
                P.op("dve", lambda e, src=src, dst_t=dst_t: e.tensor_tensor(out=dst_t, in0=src, in1=cos2.unsqueeze(1).broadcast_to([NS, 4, HD]), op=ALU.mult),
                     reads=[BK, ROPES], writes=[TMPT])
                P.op("dve", lambda e, src=src, dst_u=dst_u: e.tensor_tensor(out=dst_u[:, :, 0:64], in0=src[:, :, 64:128],
                                                                          in1=sin2[:, 0:64].unsqueeze(1).broadcast_to([NS, 4, 64]), op=ALU.mult),
                     reads=[BK, ROPES], writes=[TMPU])
                P.op("dve", lambda e, src=src, dst_u=dst_u: e.tensor_tensor(out=dst_u[:, :, 64:128], in0=src[:, :, 0:64],
                                                                          in1=sin2[:, 64:128].unsqueeze(1).broadcast_to([NS, 4, 64]), op=ALU.mult),
                     reads=[BK, ROPES], writes=[TMPU])
            qk_s = on[S16, :, :]
            P.op("pool", lambda e: e.tensor_tensor(out=qk_s, in0=tmp_t[S16, :, :], in1=tmp_u[S16, :, :], op=ALU.add), reads=[TMPT, TMPU], writes=[ON])
            v2s = hm[S16, :, :]
            bt, BK = sproj(C_RV, 512, 2)
            P.op("act", lambda e, bt=bt: e.mul(out=v2s.rearrange("p h d -> p (h d)"), in_=bt[S16, :], mul=ISQ), reads=[BK], writes=[HM])
            bt, BK = sproj(C_RZ, 512, 3)
            P.op("act", lambda e, bt=bt: e.activation(out=sz[S16, 0:512], in_=bt[S16, :], func=AF.Silu), reads=[BK], writes=[SZ])
            mqk_s = ypre[S16, :]
            for i in range(2):
                bt, BK = sproj(C_MQK + i * 512, 512, 4 + i)
                P.op("act", lambda e, bt=bt, i=i: e.copy(out=mqk_s[:, i * 512:(i + 1) * 512], in_=bt[S16, :]), reads=[BK], writes=[YPRE])
            P.dma("sp", c_so[0], conv_s[l][:, 0, :], sc[1], reads=[SCB[1]])
            P.dma("sp", c_so[1], conv_s[l][:, 1, :], sc[2], reads=[SCB[2]])
            P.dma("sp", c_so[2], conv_s[l][:, 2, :], mqk_s, reads=[YPRE])
            Wb = sz2[1][S16, :]; WB = SZ2[1]
            acc = tmp_t[S16, :, :].rearrange("p g d -> p (g d)")
            tmpc = tmp_u[S16, :, :].rearrange("p g d -> p (g d)")
            full = [sc[0], sc[1], sc[2], mqk_s]
            FB = [SCB[0], SCB[1], SCB[2], YPRE]
            for n_, j in enumerate((3, 0, 1, 2)):
                P.dma("sp", c_s[6], Wb, conv_w[l][j].partition_broadcast(NS), writes=[WB])
                if n_ == 0:
                    P.op("dve", lambda e, j=j: e.tensor_tensor(out=acc, in0=full[j], in1=Wb, op=ALU.mult), reads=[FB[j], WB, ON], writes=[TMPT])
                else:
                    P.op("dve", lambda e, j=j: e.tensor_tensor(out=tmpc, in0=full[j], in1=Wb, op=ALU.mult), reads=[FB[j], WB, ON], writes=[TMPU])
                    P.op("pool", lambda e: e.tensor_tensor(out=acc, in0=acc, in1=tmpc, op=ALU.add), reads=[TMPU, TMPT], writes=[TMPT])
            P.dma("sp", c_s[6], Wb, conv_b[l].partition_broadcast(NS), writes=[WB])
            P.op("pool", lambda e: e.tensor_tensor(out=acc, in0=acc, in1=Wb, op=ALU.add), reads=[WB, TMPT], writes=[TMPT])
            P.op("act", lambda e: e.activation(out=acc, in_=acc, func=AF.Silu), reads=[TMPT], writes=[TMPT])
            qkm_s = tmp_t[S16, :, :]
            v_s = gated[S16, :].bitcast(F32).rearrange("p (h d) -> p h d", h=4)
            bt, BK = sproj(C_MV, 512, 6)
            P.op("act", lambda e, bt=bt: e.copy(out=v_s.rearrange("p h d -> p (h d)"), in_=bt[S16, :]), reads=[BK], writes=[GATED])
            bt, BK = sproj(C_MO, 512, 7)
            P.op("act", lambda e, bt=bt: e.activation(out=th[S16, :], in_=bt[S16, :], func=AF.Tanh, scale=0.5), reads=[BK], writes=[TH])
            bt, BK = sproj(C_MZ, 512, 8)
            P.op("act", lambda e, bt=bt: e.activation(out=sz[S16, 512:1024], in_=bt[S16, :], func=AF.Silu), reads=[BK], writes=[SZ])
            bt, BK = sproj(C_G, 8, 9)
            G_ = lambda i: sg[:, i, :]
            P.op("dve", lambda e, bt=bt: e.tensor_tensor(out=sg[:, 1:3, :].rearrange("p a h -> p (a h)"), in0=bt[S16, 0:8], in1=bias8[S16, :], op=ALU.add),
                 reads=[BK, BIAS8], writes=[SG])
            P.op("act", lambda e: e.activation(out=G_(3), in_=G_(2), func=AF.Exp, scale=-1.0), reads=[SG], writes=[SG])
            P.op("act", lambda e: e.activation(out=G_(3), in_=G_(3), func=AF.Ln, bias=1.0), reads=[SG], writes=[SG])
            P.op("dve", lambda e: e.tensor_tensor(out=G_(4), in0=G_(1), in1=G_(3), op=ALU.add), reads=[SG], writes=[SG])
            P.op("dve", lambda e: e.tensor_tensor(out=G_(5), in0=G_(4), in1=G_(0), op=ALU.max), reads=[SG], writes=[SG])
            P.op("dve", lambda e: e.tensor_tensor(out=G_(6), in0=G_(5), in1=G_(3), op=ALU.subtract), reads=[SG], writes=[SG])
            P.dma("sp", c_so[3], m_s[l], G_(6), reads=[SG])
            P.op("dve", lambda e: e.tensor_tensor(out=G_(14), in0=G_(4), in1=G_(5), op=ALU.subtract), reads=[SG], writes=[SG])
            P.op("dve", lambda e: e.tensor_scalar(out=G_(14), in0=G_(14), scalar1=LNISQ, scalar2=None, op0=ALU.add), reads=[SG], writes=[SG])
            P.op("act", lambda e: e.activation(out=G_(7), in_=G_(14), func=AF.Exp), reads=[SG], writes=[SG])
            P.op("dve", lambda e: e.tensor_tensor(out=G_(14), in0=G_(3), in1=G_(5), op=ALU.subtract), reads=[SG], writes=[SG])
            P.op("act", lambda e: e.activation(out=G_(8), in_=G_(14), func=AF.Exp), reads=[SG], writes=[SG])
            P.op("dve", lambda e: e.tensor_tensor(out=G_(14), in0=G_(0), in1=G_(5), op=ALU.subtract), reads=[SG], writes=[SG])
            P.op("act", lambda e: e.activation(out=G_(9), in_=G_(14), func=AF.Exp), reads=[SG], writes=[SG])
            bt, BK = nb()
            for g in range(4):
                P.op("pe", lambda e, g=g, bt=bt: e.transpose(bt[:, g * NS:(g + 1) * NS], qk_s[:, g, :], id16), reads=[ON, CST], writes=[BK])
            for g in range(4):
                P.op("pe", lambda e, g=g, bt=bt: e.transpose(bt[:, (4 + g) * NS:(5 + g) * NS], qkm_s[:, g, :], id16), reads=[TMPT, CST], writes=[BK])
            P.op("act", lambda e, bt=bt: e.copy(out=qTs[:].rearrange("p g s -> p (g s)"), in_=bt[:, 0:8 * NS]), reads=[BK], writes=[QTS])
            bc4 = lambda i: sg[:, i, :].unsqueeze(2).broadcast_to([NS, 4, HD])
            P.op("dve", lambda e: e.tensor_tensor(out=qkm_s[:, 4:8, :], in0=qkm_s[:, 4:8, :], in1=bc4(7), op=ALU.mult), reads=[TMPT, SG], writes=[TMPT])
            prod = tmp_u[S16, :, :]
            P.op("dve", lambda e: e.tensor_tensor(out=prod[:, 0:4, :], in0=qk_s[:, 0:4, :], in1=qk_s[:, 4:8, :], op=ALU.mult), reads=[ON], writes=[TMPU])
            P.op("dve", lambda e: e.tensor_reduce(out=G_(10), in_=prod[:, 0:4, :], axis=AX.X, op=ALU.add), reads=[TMPU], writes=[SG])
            P.op("dve", lambda e: e.tensor_tensor(out=prod[:, 4:8, :], in0=qkm_s[:, 0:4, :], in1=qkm_s[:, 4:8, :], op=ALU.mult), reads=[TMPT], writes=[TMPU])
            P.op("dve", lambda e: e.tensor_reduce(out=G_(11), in_=prod[:, 4:8, :], axis=AX.X, op=ALU.add), reads=[TMPU], writes=[SG])
            P.op("dve", lambda e: e.tensor_tensor(out=prod[:, 0:4, :], in0=qkm_s[:, 0:4, :], in1=n0v, op=ALU.mult), reads=[TMPT, OSB, SG], writes=[TMPU])
            P.op("dve", lambda e: e.tensor_reduce(out=G_(13), in_=prod[:, 0:4, :], axis=AX.X, op=ALU.add), reads=[TMPU], writes=[SG])
            P.op("dve", lambda e: e.tensor_tensor(out=n0v, in0=n0v, in1=bc4(9), op=ALU.mult), reads=[OSB, SG], writes=[OSB])
            P.op("pool", lambda e: e.tensor_tensor(out=n0v, in0=n0v, in1=qkm_s[:, 4:8, :], op=ALU.add), reads=[OSB, TMPT], writes=[OSB])
            P.dma("sp", c_so[4], n_s[l], n0v, reads=[OSB])
            P.op("dve", lambda e: e.tensor_tensor(out=wexp[:], in0=sg[:, 9, :].unsqueeze(1).broadcast_to([NS, NS, 4]),
                                                  in1=id16.unsqueeze(2).broadcast_to([NS, NS, 4]), op=ALU.mult), reads=[SG, CST], writes=[WEXP])
            bt, BK = nb()
            P.op("pe", lambda e, bt=bt: e.matmul(bt[:, 0:NS * 4], lhsT=ones_f[0:NS, :], rhs=wexp[:].rearrange("p s h -> p (s h)"), start=True, stop=True),
                 reads=[WEXP, CST], writes=[BK])
            P.op("act", lambda e, bt=bt: e.copy(out=wcbs[:], in_=bt[:, 0:NS * 4]), reads=[BK], writes=[WCBS])
            GS = [x_sb[0][:].rearrange("p (s h e) -> p s h e", s=2, h=4), x_sb[1][:].rearrange("p (s h e) -> p s h e", s=2, h=4)]
            GC = [x_sb[2][:].rearrange("p (s h e) -> p s h e", s=2, h=4), sz2[1][:].rearrange("p (s h e) -> p s h e", s=2, h=4)]
            GCB = [XSB[2], SZ2[1]]
            vexp = ypre[S16, :].rearrange("p (s h e) -> p s h e", s=2, h=4)
            bor, BOR = banks[6]
            bom, BOM = banks[7]
            for g in range(NS // 2):
                par = g % 2
                s0 = g * 2
                P.dma("sp", c_sg[par], GS[par], sret[l][s0:s0 + 2].rearrange("s h d e -> d s h e"), writes=[XSB[par]])
                P.dma("sp", c_cg[par], GC[par], sC[l][s0:s0 + 2].rearrange("s h d e -> d s h e"), writes=[GCB[par]])
                for typ in range(2):
                    Gt = GS[par] if typ == 0 else GC[par]
                    GB = XSB[par] if typ == 0 else GCB[par]
                    bo_, BO_ = (bor, BOR) if typ == 0 else (bom, BOM)
                    for sl in range(2):
                        for h in range(4):
                            col = h * NS + s0 + sl
                            P.op("pe", lambda e, Gt=Gt, bo_=bo_, sl=sl, h=h, col=col, typ=typ, s0=s0: e.matmul(
                                bo_[:, typ * 0 + col:col + 1], lhsT=Gt[:, sl, h, :], rhs=qTs[:, typ * 4 + h, s0 + sl:s0 + sl + 1], start=True, stop=True),
                                reads=[GB, QTS], writes=[BO_])
                    vsrc = v2s if typ == 0 else v_s
                    VB = HM if typ == 0 else GATED
                    P.op("dve", lambda e, vsrc=vsrc, s0=s0: e.tensor_tensor(
                        out=vexp, in0=vsrc.unsqueeze(1).broadcast_to([NS, 2, 4, HD]),
                        in1=id16[:, s0:s0 + 2].unsqueeze(2).unsqueeze(3).broadcast_to([NS, 2, 4, HD]), op=ALU.mult),
                        reads=[VB, CST], writes=[YPRE])
                    ksrc = qk_s[:, 4:8, :] if typ == 0 else qkm_s[:, 4:8, :]
                    KB = ON if typ == 0 else TMPT
                    for sl in range(2):
                        bu_, BU_ = snb_()
                        for h in range(4):
                            P.op("pe", lambda e, bu_=bu_, h=h, sl=sl, ksrc=ksrc: e.matmul(bu_[:, h * 128:(h + 1) * 128], lhsT=ksrc[:, h, :], rhs=vexp[:, sl, h, :],
                                                                                    start=True, stop=True), reads=[KB, YPRE], writes=[BU_])
                        for h in range(4):
                            if typ == 0:
                                P.op("dve", lambda e, bu_=bu_, h=h, sl=sl, Gt=Gt: e.scalar_tensor_tensor(
                                    out=Gt[:, sl, h, :], in0=Gt[:, sl, h, :], scalar=GAM[h], in1=bu_[:, h * 128:(h + 1) * 128], op0=ALU.mult, op1=ALU.add),
                                    reads=[BU_, GB], writes=[GB])
                            else:
                                ci = (s0 + sl) * 4 + h
                                P.op("dve", lambda e, bu_=bu_, h=h, sl=sl, Gt=Gt, ci=ci: e.scalar_tensor_tensor(
                                    out=Gt[:, sl, h, :], in0=Gt[:, sl, h, :], scalar=wcbs[:, ci:ci + 1], in1=bu_[:, h * 128:(h + 1) * 128], op0=ALU.mult, op1=ALU.add),
                                    reads=[BU_, GB, WCBS], writes=[GB])
                P.dma("sp", c_sgo[par], ret_s[l][s0:s0 + 2].rearrange("s h d e -> d s h e"), GS[par], reads=[XSB[par]])
                P.dma("sp", c_cgo[par], C_s[l][s0:s0 + 2].rearrange("s h d e -> d s h e"), GC[par], reads=[GCB[par]])
            P.op("act", lambda e: e.copy(out=oTs[:, 0, :], in_=bor[:, 0:64]), reads=[BOR], writes=[OTS])
            P.op("act", lambda e: e.copy(out=oTs[:, 1, :], in_=bom[:, 0:64]), reads=[BOM], writes=[OTS])
            btr_, BTR_ = nb()
            btm_, BTM_ = nb()
            for typ, (bt, BK) in enumerate(((btr_, BTR_), (btm_, BTM_))):
                for h in range(4):
                    P.op("pe", lambda e, bt=bt, typ=typ, h=h: e.transpose(bt[S16, h * 128:(h + 1) * 128], oTs[:, typ, h * NS:(h + 1) * NS], ident_f),
                         reads=[OTS, CST], writes=[BK])
            hs = tmp_u[S16, :, :]
            t2 = ypre[S16, :].rearrange("p (g d) -> p g d", g=8)
            inter_r = btr_[S16, :].rearrange("p (h d) -> p h d", h=4)
            inter_m = btm_[S16, :].rearrange("p (h d) -> p h d", h=4)
            P.op("dve", lambda e: e.tensor_tensor(out=hs[:, 0:4, :], in0=inter_r, in1=gamb.unsqueeze(2).broadcast_to([NS, 4, HD]), op=ALU.mult),
                 reads=[BTR_, CST, SG], writes=[TMPU])
            P.op("dve", lambda e: e.tensor_tensor(out=t2[:, 0:4, :], in0=v2s, in1=bc4(10), op=ALU.mult), reads=[HM, SG], writes=[YPRE])
            P.op("pool", lambda e: e.tensor_tensor(out=hs[:, 0:4, :], in0=hs[:, 0:4, :], in1=t2[:, 0:4, :], op=ALU.add), reads=[TMPU, YPRE], writes=[TMPU])
            P.op("dve", lambda e: e.tensor_tensor(out=hs[:, 4:8, :], in0=inter_m, in1=bc4(9), op=ALU.mult), reads=[BTM_, SG], writes=[TMPU])
            P.op("dve", lambda e: e.tensor_tensor(out=t2[:, 4:8, :], in0=v_s, in1=bc4(11), op=ALU.mult), reads=[GATED, SG], writes=[YPRE])
            P.op("pool", lambda e: e.tensor_tensor(out=hs[:, 4:8, :], in0=hs[:, 4:8, :], in1=t2[:, 4:8, :], op=ALU.add), reads=[TMPU, YPRE], writes=[TMPU])
            P.op("dve", lambda e: e.tensor_tensor(out=G_(12), in0=G_(13), in1=G_(9), op=ALU.mult), reads=[SG], writes=[SG])
            P.op("dve", lambda e: e.tensor_tensor(out=G_(12), in0=G_(12), in1=G_(11), op=ALU.add), reads=[SG], writes=[SG])
            P.op("act", lambda e: e.activation(out=G_(12), in_=G_(12), func=AF.Abs), reads=[SG], writes=[SG])
            P.op("dve", lambda e: e.tensor_tensor(out=G_(12), in0=G_(12), in1=G_(8), op=ALU.max), reads=[SG], writes=[SG])
            P.op("dve", lambda e: e.reciprocal(out=G_(12), in_=G_(12)), reads=[SG], writes=[SG])
            P.op("dve", lambda e: e.tensor_tensor(out=hs[:, 4:8, :], in0=hs[:, 4:8, :], in1=bc4(12), op=ALU.mult), reads=[TMPU, SG], writes=[TMPU])
            for g in range(8):
                P.op("dve", lambda e, g=g: e.bn_stats(out=sstats[:, g, :], in_=hs[:, g, :]), reads=[TMPU], writes=[SSTATS])
            for g in range(8):
                P.op("dve", lambda e, g=g: e.bn_aggr(out=smv[:, g, :], in_=sstats[:, g, :]), reads=[SSTATS], writes=[SMV])
            P.op("pool", lambda e: e.tensor_scalar(out=srs[:], in0=smv[:, :, 1], scalar1=GN_EPS, scalar2=None, op0=ALU.add), reads=[SMV], writes=[SRS])
            P.op("pool", lambda e: e.tensor_tensor(out=srs[:], in0=srs[:], in1=mhalf[S16, :], op=ALU.pow), reads=[SRS, MHALF], writes=[SRS])
            P.op("dve", lambda e: e.scalar_tensor_tensor(out=snb[:], in0=smv[:, :, 0], scalar=-1.0, in1=srs[:], op0=ALU.mult, op1=ALU.mult),
                 reads=[SMV, SRS], writes=[SNB])
            on2 = on[S16, :, :]
            for g in range(8):
                P.op("act", lambda e, g=g: e.activation(out=on2[:, g, :], in_=hs[:, g, :], func=AF.Identity, scale=srs[:, g:g + 1], bias=snb[:, g:g + 1]),
                     reads=[TMPU, SRS, SNB], writes=[ON])
            P.op("dve", lambda e: e.scalar_tensor_tensor(out=sz[S16, 512:1024], in0=th[S16, :], scalar=1.0, in1=sz[S16, 512:1024], op0=ALU.add, op1=ALU.mult),
                 reads=[TH, SZ], writes=[SZ])
            P.op("pool", lambda e: e.tensor_tensor(out=sz[S16, :], in0=sz[S16, :], in1=gbc[S16, :], op=ALU.mult), reads=[SZ, GBC], writes=[SZ])
            P.op("pool", lambda e: e.tensor_tensor(out=gated[S16, :], in0=on2.rearrange("p g d -> p (g d)"), in1=sz[S16, :], op=ALU.mult),
                 reads=[ON, SZ], writes=[GATED])
            bt, BK = nb(); vg = bfv(bt)
            for k in range(8):
                P.op("pe", lambda e, k=k: e.transpose(vg[:, k * NS:(k + 1) * NS], gated[S16, k * 128:(k + 1) * 128], ident_bf[S16, 0:NS]),
                     reads=[GATED, IDB], writes=[BK])
            P.op("act", lambda e: e.copy(out=gT[:, :, 0:NS], in_=vg[:, 0:8 * NS].rearrange("p (k s) -> p k s", k=8)), reads=[BK], writes=[GT])
            yps = ypre[S16, :]
            for n in range(2):
                bt, BK = nb()
                for k in range(8):
                    P.op("pe", lambda e, k=k, n=n, bt=bt: e.matmul(bt[S16, :], lhsT=gT[:, k, 0:NS], rhs=w_out_sb[:, k, n * 512:(n + 1) * 512],
                                                               start=(k == 0), stop=(k == 7)), reads=[GT, WOUT[n]], writes=[BK])
                P.op("dve", lambda e, n=n, bt=bt: e.scalar_tensor_tensor(out=yps[:, n * 512:(n + 1) * 512], in0=xs_sb[:, n * 512:(n + 1) * 512], scalar=ALPHA,
                                                                     in1=bt[S16, :], op0=ALU.mult, op1=ALU.add), reads=[BK, XSS], writes=[YPRE])
                P.op("dve", lambda e, n=n: e.bn_stats(out=slst[:, n, :], in_=yps[:, n * 512:(n + 1) * 512]), reads=[YPRE], writes=[SLST])
            P.op("dve", lambda e: e.bn_aggr(out=slmv[:], in_=slst[:].rearrange("p a b -> p (a b)")), reads=[SLST], writes=[SLMV])
            P.op("pool", lambda e: e.tensor_scalar(out=slrs[:, 0:1], in0=slmv[:, 1:2], scalar1=LN_EPS, scalar2=None, op0=ALU.add), reads=[SLMV], writes=[SLRS])
            P.op("pool", lambda e: e.tensor_tensor(out=slrs[:, 0:1], in0=slrs[:, 0:1], in1=mhalf[S16, 0:1], op=ALU.pow), reads=[SLRS, MHALF], writes=[SLRS])
            P.op("dve", lambda e: e.scalar_tensor_tensor(out=slrs[:, 1:2], in0=slmv[:, 0:1], scalar=-1.0, in1=slrs[:, 0:1], op0=ALU.mult, op1=ALU.mult),
                 reads=[SLMV, SLRS], writes=[SLRS])
            P.op("act", lambda e: e.activation(out=xs_sb[:], in_=yps, func=AF.Identity, scale=slrs[:, 0:1], bias=slrs[:, 1:2]),
                 reads=[YPRE, SLRS], writes=[XSS])
            P.op("pool", lambda e: e.tensor_tensor(out=xs_sb[:], in0=xs_sb[:], in1=lng[S16, :], op=ALU.mult), reads=[XSS, LNG], writes=[XSS])
            P.op("pool", lambda e: e.tensor_tensor(out=xs_sb[:], in0=xs_sb[:], in1=lnb[S16, :], op=ALU.add), reads=[XSS, LNB], writes=[XSS])
            if l == n_layers - 1:
                P.dma("sp", c_so[5], y_s, xs_sb[:], reads=[XSS])

        def finalize_prompt(l):
            P.dma("sp", c_fin[1], ret_p[l].rearrange("h d e -> d h e"), S_f[:], reads=[SF])
            P.dma("sp", c_fin[2], C_p[l].rearrange("h d e -> d h e"), C_f[:, :, 0:128], reads=[CF])
            P.dma("sp", c_fin[3], n_p[l].rearrange("h d -> d h"), C_f[:, :, 128], reads=[CF], allow_slow_non_contiguous=True)
            for half in range(2):
                bt, BK = nb()
                for c4 in range(4):
                    ch = half * 4 + c4
                    P.op("pe", lambda e, ch=ch, c4=c4, bt=bt: e.transpose(bt[0:3, c4 * 128:(c4 + 1) * 128], convp_sb[:, ch, :], ident_f),
                         reads=[CONVP, CST], writes=[BK])
                P.op("act", lambda e, half=half, bt=bt: e.copy(out=convp_T[:, half * 512:(half + 1) * 512], in_=bt[0:3, :]),
                     reads=[BK], writes=[CONVPT])
            P.dma("sp", c_fin[4], conv_p[l], convp_T[:], reads=[CONVPT])

        def stage(name):
            if stop_at == name:
                raise StopBuild()

        try:
            for l in range(n_layers):
                load_weights(l)
                stage("weights")
                load_params(l)
                stage("params")
                prepass(l)
                stage("prepass")
                NSEG = 25
                OFF = (NSEG + 1) // 2
                gens = [main_tile(l, t) for t in range(n_tiles)]
                done = [0] * n_tiles
                fin = [False] * n_tiles
                slots = {}
                for t in range(n_tiles):
                    for k in range(NSEG):
                        slots.setdefault(t * OFF + k, []).append((t, k))
                for sl_ in sorted(slots):
                    for (t, k) in sorted(slots[sl_]):
                        assert not fin[t], "main_tile has fewer steps than NSEG"
                        try:
                            next(gens[t])
                            done[t] += 1
                        except StopIteration:
                            fin[t] = True
                assert all(fin), "main_tile has more steps than NSEG: %s" % done
                stage("tiles")
                finalize_prompt(l)
                if do_sample:
                    P.barrier()
                    sample_path(l)
                    P.barrier()
                stage('sample')
        except StopBuild:
            pass

        for c in P.chans:
            if c.val:
                P.wait_chan("sp", c)
        with nc.Block() as block:
            P.finish(block)
    return nc


def make_consts():
    cst = np.zeros((128, NCST), np.float32)
    cst[:, K_ID:K_ID + 128] = np.eye(128, dtype=np.float32)
    idx = np.arange(128)
    cst[:, K_TRI:K_TRI + 128] = (idx[:, None] <= idx[None, :]).astype(np.float32)
    for h in range(H):
        lg = np.float32(LOGG[h])
        cst[:, K_KDEC + h] = np.exp(lg * (np.float32(L - 1) - idx.astype(np.float32))).astype(np.float32) * np.float32(ISQ)
        cst[:, K_QDEC + h * 128:K_QDEC + (h + 1) * 128] = np.exp(lg * (idx.astype(np.float32) + 1.0 - L)).astype(np.float32)[None, :]
    cst[:, K_ONE:K_ONE + 128] = 1.0
    for h in range(H):
        cst[:, K_GAM + h] = np.float32(GAM[h])
    half = HD // 2
    inv = (np.float32(10000.0) ** (-np.arange(half, dtype=np.float32) / np.float32(half))).astype(np.float32)

    def rope(pos):
        ang = (pos.astype(np.float32)[:, None] * inv[None, :]).astype(np.float32)
        c = np.cos(ang.astype(np.float64)).astype(np.float32)
        s = np.sin(ang.astype(np.float64)).astype(np.float32)
        out = np.zeros((len(pos), 2, HD), np.float32)
        out[:, 0, :half] = c
        out[:, 0, half:] = c
        out[:, 1, :half] = -s
        out[:, 1, half:] = s
        return out

    rope_p = rope(np.arange(T))
    rope_s = rope(np.array([PAST_LEN])).reshape(2 * HD)
    return cst, rope_p, rope_s


_CACHE = {}


def kernel(x_prompt, x_sample, state_ret, state_mlstm_C, state_mlstm_n, state_mlstm_m, state_conv,
           w_in, conv_w, conv_b, b_i, b_f, g_ret, g_m, w_out, ln_g, ln_b):
    n = 8
    if "nc" not in _CACHE:
        _CACHE["nc"] = build_program()
    nc = _CACHE["nc"]
    cst, rope_p, rope_s = make_consts()
    f = lambda a: np.ascontiguousarray(np.asarray(a, dtype=np.float32))
    shared = dict(w_in=f(w_in), conv_w=f(conv_w), conv_b=f(conv_b), b_i=f(b_i), b_f=f(b_f), g_ret=f(g_ret), g_m=f(g_m),
                  w_out=f(w_out), ln_g=f(ln_g), ln_b=f(ln_b), cst=cst, rope_p=rope_p, rope_s=rope_s)
    in_maps = []
    for c in range(n):
        s0, s1 = c * NS, (c + 1) * NS
        m = dict(shared)
        m["xp"] = f(x_prompt[c])
        m["xs"] = f(x_sample[s0:s1, 0, :])
        m["sret"] = f(state_ret[:, s0:s1])
        m["sC"] = f(state_mlstm_C[:, s0:s1])
        m["sn"] = f(state_mlstm_n[:, s0:s1])
        m["sm"] = f(state_mlstm_m[:, s0:s1])
        m["sconv"] = f(state_conv[:, s0:s1])
        in_maps.append(m)
    res = run_bass_kernel_spmd(nc, in_maps, core_ids=list(range(n)))
    R = res.results
    y_p = np.stack([R[c]["y_p"] for c in range(n)], 0)
    y_s = np.concatenate([R[c]["y_s"] for c in range(n)], 0)[:, None, :]
    ret_p = np.stack([R[c]["ret_p"] for c in range(n)], 1)
    C_p = np.stack([R[c]["C_p"] for c in range(n)], 1)
    n_p = np.stack([R[c]["n_p"] for c in range(n)], 1)
    m_p = np.stack([R[c]["m_p"] for c in range(n)], 1)
    conv_p = np.stack([R[c]["conv_p"] for c in range(n)], 1)
    ret_s = np.concatenate([R[c]["ret_s"] for c in range(n)], 1)
    C_s = np.concatenate([R[c]["C_s"] for c in range(n)], 1)
    n_s = np.concatenate([R[c]["n_s"] for c in range(n)], 1)
    m_s = np.concatenate([R[c]["m_s"] for c in range(n)], 1)
    conv_s = np.concatenate([R[c]["conv_s"] for c in range(n)], 1)
    return (y_p, y_s, ret_p, C_p, n_p, m_p, conv_p, ret_s, C_s, n_s, m_s, conv_s)
```

```python
from contextlib import ExitStack
import numpy as np
import concourse.bass as bass
import concourse.mybir as mybir
from concourse.bass_utils import run_bass_kernel_spmd

F32 = mybir.dt.float32
BF16 = mybir.dt.bfloat16
AF = mybir.ActivationFunctionType
ALU = mybir.AluOpType
AX = mybir.AxisListType

D = 1024
T = 2048
NT = 16
L = 128
NS = 16
H = 4
HD = 128
N_IN = 4616
PAST_LEN = 16384
ALPHA = (2 * 2) ** 0.25
GN_EPS = 1e-5
LN_EPS = 1e-5
ISQ = float(HD ** -0.5)
LNISQ = float(np.log(HD ** -0.5))
GAM = [float(np.float32(1.0) - np.float32(2.0) ** np.float32(-5.0 - h)) for h in range(H)]
LOGG = [float(np.log(np.float32(g))) for g in GAM]
GAML = [float(np.exp(np.float32(lg) * L)) for lg in LOGG]

C_RQ, C_RK, C_RV, C_RZ, C_MQK, C_MV, C_MO, C_MZ, C_G = 0, 512, 1024, 1536, 2048, 3072, 3584, 4096, 4608

K_ID = 0
K_TRI = 128
K_KDEC = 256
K_QDEC = 260
K_ONE = 772
K_GAM = 900
NCST = 904


class Buf:
    __slots__ = ("name", "w", "r", "psum")

    def __init__(self, name="", psum=False):
        self.name = name
        self.w = None
        self.r = []
        self.psum = psum


class Chan:
    def __init__(self, prog, name):
        self.sem = prog.new_sem(name)
        self.val = 0


class Prog:
    ENG = ("pe", "act", "dve", "pool", "sp")

    def __init__(self, nc, stack):
        self.nc = nc
        self.stack = stack
        self.items = {e: [] for e in self.ENG}
        self.cnt = {e: 0 for e in self.ENG}
        self.esem = {e: self.new_sem("prog_" + e) for e in self.ENG}
        self.waited = {e: {} for e in self.ENG}
        self.chans = []

    def new_sem(self, name):
        return self.stack.enter_context(self.nc.semaphore(name))

    def chan(self, name):
        c = Chan(self, name)
        self.chans.append(c)
        return c

    def _need(self, eng, reads, writes):
        need = {}

        def add(ev):
            if ev is None:
                return
            k = (ev[0], id(ev[1]) if ev[0] == 'c' else ev[1])
            if k not in need or need[k][2] < ev[2]:
                need[k] = ev

        for b in reads:
            add(b.w)
            if b.psum:
                for r in b.r:
                    if not (r[0] == 'e' and r[1] == eng):
                        add(r)
        for b in writes:
            if b.w is not None and not (b.w[0] == 'e' and b.w[1] == eng):
                add(b.w)
            for r in b.r:
                if not (r[0] == 'e' and r[1] == eng):
                    add(r)
        wd = self.waited[eng]
        for k, ev in need.items():
            if wd.get(k, 0) >= ev[2]:
                continue
            wd[k] = ev[2]
            if ev[0] == 'e':
                self.items[eng].append(('we', ev[1], ev[2]))
            else:
                self.items[eng].append(('wc', ev[1].sem, ev[2]))

    def op(self, eng, fn, reads=(), writes=()):
        self._need(eng, reads, writes)
        self.cnt[eng] += 1
        ev = ('e', eng, self.cnt[eng])
        self.items[eng].append(('op', fn, self.cnt[eng]))
        for b in reads:
            b.r.append(ev)
        for b in writes:
            b.w = ev
            b.r = []
        return ev

    def dma(self, eng, chan, out, in_, reads=(), writes=(), **kw):
        self._need(eng, reads, writes)
        chan.val += 16
        ev = ('c', chan, chan.val)
        self.items[eng].append(('dma', lambda e, out=out, in_=in_, kw=kw, sem=chan.sem:
                                e.dma_start(out=out, in_=in_, **kw).then_inc(sem, 16)))
        for b in reads:
            b.r.append(ev)
        for b in writes:
            b.w = ev
            b.r = []
        return ev

    def barrier(self):
        for eng in self.ENG:
            wd = self.waited[eng]
            for e2 in self.ENG:
                if e2 == eng or self.cnt[e2] == 0:
                    continue
                k = ('e', e2)
                if wd.get(k, 0) >= self.cnt[e2]:
                    continue
                wd[k] = self.cnt[e2]
                self.items[eng].append(('we', e2, self.cnt[e2]))
            for c in self.chans:
                if c.val == 0:
                    continue
                k = ('c', id(c))
                if wd.get(k, 0) >= c.val:
                    continue
                wd[k] = c.val
                self.items[eng].append(('wc', c.sem, c.val))

    def wait_chan(self, eng, chan):
        self.items[eng].append(('wc', chan.sem, chan.val))

    def finish(self, block):
        targets = {e: set() for e in self.ENG}
        for e in self.ENG:
            for it in self.items[e]:
                if it[0] == 'we':
                    targets[it[1]].add(it[2])
        rank = {}
        for e in self.ENG:
            rank[e] = {s_: i + 1 for i, s_ in enumerate(sorted(targets[e]))}
        self.n_signals = {e: len(rank[e]) for e in self.ENG}

        def run(eng, e):
            sem = self.esem[eng]
            rk = rank[eng]
            for it in self.items[eng]:
                if it[0] == 'we':
                    e.wait_ge(self.esem[it[1]], rank[it[1]][it[2]])
                elif it[0] == 'wc':
                    e.wait_ge(it[1], it[2])
                elif it[0] == 'op':
                    ins = it[1](e)
                    if it[2] in rk:
                        ins.then_inc(sem, 1)
                else:
                    it[1](e)

        @block.tensor
        def _(e):
            run("pe", e)

        @block.scalar
        def _(e):
            run("act", e)

        @block.vector
        def _(e):
            run("dve", e)

        @block.gpsimd
        def _(e):
            run("pool", e)

        @block.sync
        def _(e):
            run("sp", e)


class StopBuild(Exception):
    pass


def build_program(n_layers=2, n_tiles=NT, do_sample=True, dbg=None, stop_at=None):
    nc = bass.Bass("TRN2", target_bir_lowering=False)
    dt_in = lambda name, shape: nc.dram_tensor(name, shape, F32, kind="ExternalInput").ap()
    dt_out = lambda name, shape: nc.dram_tensor(name, shape, F32, kind="ExternalOutput").ap()
    xp = dt_in("xp", [T, D])
    xs = dt_in("xs", [NS, D])
    sret = dt_in("sret", [2, NS, H, HD, HD])
    sC = dt_in("sC", [2, NS, H, HD, HD])
    sn = dt_in("sn", [2, NS, H, HD])
    sm = dt_in("sm", [2, NS, H])
    sconv = dt_in("sconv", [2, NS, 3, D])
    w_in = dt_in("w_in", [2, D, N_IN])
    conv_w = dt_in("conv_w", [2, 4, D])
    conv_b = dt_in("conv_b", [2, D])
    b_i = dt_in("b_i", [2, H])
    b_f = dt_in("b_f", [2, H])
    g_ret = dt_in("g_ret", [2, 512])
    g_m = dt_in("g_m", [2, 512])
    w_out = dt_in("w_out", [2, D, D])
    ln_g = dt_in("ln_g", [2, D])
    ln_b = dt_in("ln_b", [2, D])
    cst_d = dt_in("cst", [128, NCST])
    rope_p = dt_in("rope_p", [T, 2, HD])
    rope_s = dt_in("rope_s", [2 * HD])

    y_p = dt_out("y_p", [T, D])
    y_s = dt_out("y_s", [NS, D])
    ret_p = dt_out("ret_p", [2, H, HD, HD])
    C_p = dt_out("C_p", [2, H, HD, HD])
    n_p = dt_out("n_p", [2, H, HD])
    m_p = dt_out("m_p", [2, H])
    conv_p = dt_out("conv_p", [2, 3, D])
    ret_s = dt_out("ret_s", [2, NS, H, HD, HD])
    C_s = dt_out("C_s", [2, NS, H, HD, HD])
    n_s = dt_out("n_s", [2, NS, H, HD])
    m_s = dt_out("m_s", [2, NS, H])
    conv_s = dt_out("conv_s", [2, NS, 3, D])
    y0 = nc.dram_tensor("y0_scratch", [T, D], F32, kind="Internal").ap()
    dbg_out = {}
    if dbg:
        for name, shape in dbg.items():
            dbg_out[name] = dt_out("dbg_" + name, shape)

    with ExitStack() as st:
        P = Prog(nc, st)
        sb = lambda name, shape, dt=F32: st.enter_context(nc.sbuf_tensor("s_" + name, shape, dt))
        out_chans = []

        def ochan(name):
            c = P.chan(name)
            out_chans.append(c)
            return c

        banks = []
        for i in range(8):
            t_ = st.enter_context(nc.psum_tensor("bank%d" % i, [128, 512], F32))
            banks.append((t_, Buf("bank%d" % i, psum=True)))
        bank_ctr = [0]

        def nb():
            i = bank_ctr[0] % 8
            bank_ctr[0] += 1
            return banks[i]

        def bfv(bank_t):
            return bank_t[:].bitcast(BF16)

        cst = sb("cst", [128, NCST]); CST = Buf("cst")
        ident_bf = sb("ident_bf", [128, 128], BF16); IDB = Buf()
        mask_bf = sb("mask_bf", [128, 128], BF16); MSK = Buf()
        c_ld = P.chan("c_ld")
        P.dma("sp", c_ld, cst[:], cst_d, writes=[CST])
        ident_f = cst[:, K_ID:K_ID + 128]
        tri_f = cst[:, K_TRI:K_TRI + 128]
        ones_f = cst[:, K_ONE:K_ONE + 128]
        P.op("dve", lambda e: e.tensor_copy(out=ident_bf[:], in_=ident_f), reads=[CST], writes=[IDB])
        P.op("dve", lambda e: e.tensor_copy(out=mask_bf[:], in_=tri_f), reads=[CST], writes=[MSK])
        kdec = lambda h: cst[:, K_KDEC + h:K_KDEC + h + 1]
        qdecT = cst[:, K_QDEC:K_QDEC + 512].rearrange("p (h l) -> p h l", h=4)

        w_in_sb = sb("w_in_sb", [128, 8, N_IN], BF16)
        WIN = [Buf("win%d" % i) for i in range(10)]
        w_out_sb = sb("w_out_sb", [128, 8, D], BF16)
        WOUT = [Buf("wout0"), Buf("wout1")]
        c_win = [P.chan("c_win%d" % i) for i in range(10)]
        c_wout = [P.chan("c_wout%d" % i) for i in range(2)]
        gbc = sb("gbc", [128, 1024]); GBC = Buf()
        lng = sb("lng", [128, 1024]); LNG = Buf()
        lnb = sb("lnb", [128, 1024]); LNB = Buf()
        bias8 = sb("bias8", [128, 8]); BIAS8 = Buf()
        cwb_in = sb("cwb_in", [40, 128]); CWBIN = Buf()
        cwT = sb("cwT", [128, 40]); CWT = Buf()
        diag = sb("diag", [128, 32, 128], BF16); DIAG = Buf()
        c_par = [P.chan("c_par%d" % i) for i in range(8)]

        def wblk(i):
            return (i * 512, min((i + 1) * 512, N_IN))

        def load_weights(l):
            wv = w_in[l].rearrange("(k p) n -> p k n", p=128)
            for i in [9] + list(range(9)):
                a, b_ = wblk(i)
                P.dma("pool", c_win[i], w_in_sb[:, :, a:b_], wv[:, :, a:b_], writes=[WIN[i]])
            wo = w_out[l].rearrange("(k p) n -> p k n", p=128)
            for i in range(2):
                P.dma("pool", c_wout[i], w_out_sb[:, :, i * 512:(i + 1) * 512], wo[:, :, i * 512:(i + 1) * 512],
                      writes=[WOUT[i]])

        def load_params(l):
            P.dma("sp", c_par[0], gbc[:, 0:512], g_ret[l].partition_broadcast(128), writes=[GBC])
            P.dma("sp", c_par[1], gbc[:, 512:1024], g_m[l].partition_broadcast(128), writes=[GBC])
            P.dma("sp", c_par[2], lng[:], ln_g[l].partition_broadcast(128), writes=[LNG])
            P.dma("sp", c_par[3], lnb[:], ln_b[l].partition_broadcast(128), writes=[LNB])
            P.dma("sp", c_par[4], bias8[:, 0:4], b_i[l].partition_broadcast(128), writes=[BIAS8])
            P.dma("sp", c_par[5], bias8[:, 4:8], b_f[l].partition_broadcast(128), writes=[BIAS8])
            P.dma("sp", c_par[6], cwb_in[0:32, :], conv_w[l].rearrange("j (ch c) -> (j ch) c", c=128), writes=[CWBIN])
            P.dma("sp", c_par[7], cwb_in[32:40, :], conv_b[l].rearrange("(ch c) -> ch c", c=128), writes=[CWBIN])
            P.op("pool", lambda e: e.tensor_scalar(out=gbc[:, 512:1024], in0=gbc[:, 512:1024], scalar1=0.5, scalar2=None,
                                                   op0=ALU.mult), reads=[GBC], writes=[GBC])
            bt, BK = nb()
            P.op("pe", lambda e: e.transpose(bt[:, 0:40], cwb_in[0:40, :], ident_f[0:40, 0:40]), reads=[CWBIN, CST], writes=[BK])
            P.op("act", lambda e: e.copy(out=cwT[:], in_=bt[:, 0:40]), reads=[BK], writes=[CWT])
            for ch in range(8):
                for j in range(4):
                    idx = j * 8 + ch
                    P.op("pool", lambda e, ch=ch, j=j, idx=idx: e.tensor_scalar(
                        out=diag[:, ch * 4 + j, :], in0=ident_f, scalar1=cwT[:, idx:idx + 1], scalar2=None, op0=ALU.mult),
                        reads=[CST, CWT], writes=[DIAG])

        gates = sb("gates", [128, NT, 8]); GATES = Buf()
        lneg = sb("lneg", [128, NT, 4]); LNEG = Buf()
        bneg = sb("bneg", [128, NT, 4]); BNEG = Buf()
        u_sb = sb("u_sb", [128, NT, 4]); USB = Buf()
        uT_sb = sb("uT_sb", [64, 128]); UTS = Buf()
        umaxc = sb("umaxc", [64, 1]); UMX = Buf()
        row = sb("row", [1, 6, 64]); ROW = Buf()
        cw_b = sb("cw_b", [128, 2, NT + 1, 4]); CWB = Buf()
        pk = sb("pk", [128, NT, 4]); PK = Buf()
        thr = sb("thr", [128, NT, 4]); THR = Buf()
        tmp64 = sb("tmp64", [128, NT, 4]); TMP64 = Buf()

        x_sb = [sb("x_sb%d" % i, [128, D]) for i in range(3)]; XSB = [Buf(), Buf(), Buf()]
        rope_sb = [sb("rope_sb%d" % i, [128, 2, HD]) for i in range(2)]; ROPE = [Buf(), Buf()]
        c_x = [P.chan("c_x0"), P.chan("c_x1"), P.chan("c_x2")]
        c_rope = [P.chan("c_rope0"), P.chan("c_rope1")]
        x_bf2 = [sb("x_bf%d" % i, [128, D], BF16) for i in range(2)]; XBF2 = [Buf(), Buf()]
        x_bf = x_bf2[0]; XBF = XBF2[0]
        c_xb = [P.chan("c_xb0"), P.chan("c_xb1")]
        xT2 = [sb("xT%d" % i, [128, 8, 128], BF16) for i in range(2)]; XT2 = [Buf(), Buf()]
        xT = xT2[0]; XT = XT2[0]
        tmp_t = sb("tmp_t", [128, 8, HD]); TMPT = Buf(); TMPT_K = Buf()
        tmp_u = sb("tmp_u", [128, 8, HD]); TMPU = Buf(); TMPU_K = Buf()
        qk_rot = sb("qk_rot", [128, 8, HD], BF16); QKR = Buf(); QKR_K = Buf()
        v2 = sb("v2", [128, 4, HD], BF16); V2 = Buf()
        sz2 = [sb("sz%d" % i, [128, 1024]) for i in range(2)]; SZ2 = [Buf(), Buf()]
        sz = sz2[0]; SZ = SZ2[0]
        SZA2 = [Buf(), Buf()]; SZA = SZA2[0]
        qT2 = sb("qT2", [128, 4, 128], BF16); QT2 = Buf()
        kT = sb("kT", [128, 4, 128], BF16); KT = Buf()
        s2 = sb("s2", [128, 4, 128], BF16); S2 = Buf()
        S_f = sb("S_f", [128, 4, HD]); SF = Buf()
        S_bf = sb("S_bf", [128, 4, HD], BF16); SBF = Buf()
        hist = sb("hist", [128, 8, 131], BF16); HIST = Buf()
        qkm = sb("qkm", [128, 8, 128], BF16); QKM = Buf()
        kp = sb("kp", [128, 4, 128], BF16); KP = Buf()
        v1 = sb("v1", [128, 4, 130], BF16); V1 = Buf()
        th = sb("th", [128, 512]); TH = Buf()
        s2m = sb("s2m", [128, 4, 128], BF16); S2M = Buf()
        C_f = sb("C_f", [128, 4, 130]); CF = Buf()
        C_bf = sb("C_bf", [128, 4, 130], BF16); CBF = Buf()
        dn = sb("dn", [128, 4]); DN = Buf()
        hm = sb("hm", [128, 4, HD]); HM = Buf()
        o_sb = sb("o_sb", [128, 4, HD]); OSB = Buf()
        stats = sb("stats", [128, 8, 6]); STATS = Buf()
        mv = sb("mv", [128, 8, 2]); MV = Buf()
        rstd = sb("rstd", [128, 8]); RSTD = Buf()
        nbias = sb("nbias", [128, 8]); NBIAS = Buf()
        mhalf = sb("mhalf", [128, 8]); MHALF = Buf()
        on = sb("on", [128, 8, HD]); ON = Buf(); ON_M = Buf()
        gated = sb("gated", [128, 1024], BF16); GATED = Buf(); GATED_M = Buf()
        gT = sb("gT", [128, 8, 128], BF16); GT = Buf()
        ypre = sb("ypre", [128, D]); YPRE = Buf(); YPRE_B = Buf()
        lstats = sb("lstats", [128, 2, 6]); LSTATS = Buf()
        lmv = sb("lmv", [128, 2]); LMV = Buf()
        lrs = sb("lrs", [128, 2]); LRS = Buf()
        c_y = [ochan("c_y0"), ochan("c_y1")]
        convp_sb = sb("convp_sb", [128, 8, 3]); CONVP = Buf()
        convp_T = sb("convp_T", [3, D]); CONVPT = Buf()
        c_fin = [ochan("c_fin%d" % i) for i in range(5)]

        P.op("pool", lambda e: e.memset(mhalf[:], -0.5), writes=[MHALF])

        def dbg_dump(name, src_ap, bufs):
            if name in dbg_out:
                c = ochan("c_dbg_" + name)
                P.dma("sp", c, dbg_out[name], src_ap, reads=bufs)

        def load_x(l, t, par):
            src = xp if l == 0 else y0
            P.dma("sp", c_x[par], x_sb[par][:], src[t * 128:(t + 1) * 128, :], writes=[XSB[par]],
                  reads=([Y0B[t]] if l > 0 else []))

        def load_xbf(l, t, par):
            src = xp if l == 0 else y0
            P.dma("pool", c_xb[par], x_bf2[par][:], src[t * 128:(t + 1) * 128, :], writes=[XBF2[par]],
                  reads=([Y0B[t]] if l > 0 else []))

        def load_main(l, t):
            par = t % 2
            load_x(l, t, t % 3)
            P.dma("sp", c_rope[par], rope_sb[par][:], rope_p[t * 128:(t + 1) * 128], writes=[ROPE[par]])

        def make_xT(par, xT=None, XT=None, xi=None):
            if xT is None:
                xT, XT = xT2[0], XT2[0]
            if xi is None:
                xi = par
            xb = x_bf2[par]; XB = XBF2[par]
            P.op("dve", lambda e: e.tensor_copy(out=xb[:], in_=x_sb[xi][:]), reads=[XSB[xi]], writes=[XB])
            bt, BK = nb()
            v = bfv(bt)
            for k in range(8):
                P.op("pe", lambda e, k=k: e.transpose(v[:, k * 128:(k + 1) * 128], xb[:, k * 128:(k + 1) * 128], ident_bf[:]),
                     reads=[XB, IDB], writes=[BK])
            P.op("act", lambda e: e.copy(out=xT[:].rearrange("p k c -> p (k c)"), in_=v), reads=[BK], writes=[XT])

        Y0B = [Buf("y0_%d" % t) for t in range(NT)]

        def prepass(l):
            load_x(l, 0, 0)
            for t in range(n_tiles):
                par = t % 2
                if t + 1 < n_tiles:
                    load_x(l, t + 1, (t + 1) % 2)
                xTp = xT2[par]; XTp = XT2[par]
                make_xT(par, xTp, XTp)
                bt, BK = nb()
                for k in range(8):
                    P.op("pe", lambda e, k=k, xTp=xTp: e.matmul(bt[:, 0:8], lhsT=xTp[:, k, :], rhs=w_in_sb[:, k, C_G:C_G + 8],
                                                                start=(k == 0), stop=(k == 7)), reads=[XTp, WIN[9]], writes=[BK])
                P.op("dve", lambda e, t=t: e.tensor_tensor(out=gates[:, t, :], in0=bt[:, 0:8], in1=bias8[:], op=ALU.add),
                     reads=[BK, BIAS8], writes=[GATES])
            nt = n_tiles
            stage('pp_loop')
            P.op("act", lambda e: e.activation(out=lneg[:, 0:nt, :], in_=gates[:, 0:nt, 4:8], func=AF.Exp, scale=-1.0),
                 reads=[GATES], writes=[LNEG])
            P.op("act", lambda e: e.activation(out=lneg[:, 0:nt, :], in_=lneg[:, 0:nt, :], func=AF.Ln, bias=1.0),
                 reads=[LNEG], writes=[LNEG])
            dbg_dump('gates', gates[:], [GATES])
            dbg_dump('lneg', lneg[:], [LNEG])
            stage('pp_a')
            bt, BK = nb()
            ln2 = lneg[:].rearrange("p t h -> p (t h)")
            P.op("pe", lambda e: e.matmul(bt[:, 0:nt * 4], lhsT=tri_f, rhs=ln2[:, 0:nt * 4], start=True, stop=True),
                 reads=[LNEG, CST], writes=[BK])
            stage('pp_b')
            bt2, BK2 = nb()
            P.op("pe", lambda e: e.matmul(bt2[0:1, 0:nt * 4], lhsT=ones_f[:, 0:1], rhs=ln2[:, 0:nt * 4], start=True, stop=True),
                 reads=[LNEG, CST], writes=[BK2])
            stage('pp_c')
            P.op("act", lambda e: e.copy(out=bneg[:].rearrange("p t h -> p (t h)")[:, 0:nt * 4], in_=bt[:, 0:nt * 4]),
                 reads=[BK], writes=[BNEG])
            P.op("dve", lambda e: e.tensor_tensor(out=u_sb[:, 0:nt, :], in0=gates[:, 0:nt, 0:4], in1=bneg[:, 0:nt, :], op=ALU.add),
                 reads=[GATES, BNEG], writes=[USB])
            P.op("act", lambda e: e.copy(out=row[0:1, 1, 0:nt * 4], in_=bt2[0:1, 0:nt * 4]), reads=[BK2], writes=[ROW])
            stage('pp_cum')
            bt3, BK3 = nb()
            u2 = u_sb[:].rearrange("p t h -> p (t h)")
            P.op("pe", lambda e: e.transpose(bt3[0:nt * 4, 0:128], u2[:, 0:nt * 4], ident_f), reads=[USB, CST], writes=[BK3])
            P.op("dve", lambda e: e.tensor_reduce(out=umaxc[0:nt * 4, :], in_=bt3[0:nt * 4, 0:128], axis=AX.X, op=ALU.max),
                 reads=[BK3], writes=[UMX])
            bt4, BK4 = nb()
            P.op("pe", lambda e: e.transpose(bt4[0:1, 0:nt * 4], umaxc[0:nt * 4, 0:1], ident_f[0:nt * 4, 0:nt * 4]),
                 reads=[UMX, CST], writes=[BK4])
            P.op("act", lambda e: e.copy(out=row[0:1, 0, 0:nt * 4], in_=bt4[0:1, 0:nt * 4]), reads=[BK4], writes=[ROW])
            stage('pp_umax')
            rv_ = lambda i: row[0:1, i, 0:nt * 4].rearrange("p (t h) -> p t h", h=4)
            P.op("dve", lambda e: e.tensor_scalar(out=row[0:1, 3, 0:nt * 4], in0=row[0:1, 1, 0:nt * 4], scalar1=-1.0, scalar2=None,
                                                  op0=ALU.mult), reads=[ROW], writes=[ROW])
            for h in range(4):
                P.op("dve", lambda e, h=h: e.tensor_tensor_scan(out=rv_(2)[:, :, h], data0=rv_(0)[:, :, h], data1=rv_(3)[:, :, h],
                                                                initial=0.0, op0=ALU.max, op1=ALU.add), reads=[ROW], writes=[ROW])
            P.op("dve", lambda e: e.tensor_tensor(out=row[0:1, 3, 0:nt * 4], in0=row[0:1, 2, 0:nt * 4], in1=row[0:1, 1, 0:nt * 4],
                                                  op=ALU.add), reads=[ROW], writes=[ROW])
            P.op("dve", lambda e: e.memset(row[0:1, 5, 0:4], 0.0), reads=[ROW], writes=[ROW])
            if nt > 1:
                P.op("dve", lambda e: e.tensor_copy(out=row[0:1, 5, 4:nt * 4], in_=row[0:1, 2, 0:(nt - 1) * 4]), reads=[ROW], writes=[ROW])
            P.op("dve", lambda e: e.tensor_tensor(out=row[0:1, 4, 0:nt * 4], in0=row[0:1, 5, 0:nt * 4], in1=row[0:1, 3, 0:nt * 4],
                                                  op=ALU.subtract), reads=[ROW], writes=[ROW])
            P.op("act", lambda e: e.activation(out=row[0:1, 4, 0:nt * 4], in_=row[0:1, 4, 0:nt * 4], func=AF.Exp),
                 reads=[ROW], writes=[ROW])
            stage('pp_scan')
            bt5, BK5 = nb()
            P.op("pe", lambda e: e.matmul(bt5[:, 0:128], lhsT=ones_f[0:1, :], rhs=row[0:1, 3:5, :].rearrange("p a b -> p (a b)"),
                                          start=True, stop=True), reads=[ROW, CST], writes=[BK5])
            P.op("act", lambda e: e.copy(out=cw_b[:, :, 0:NT, :], in_=bt5[:, 0:128].rearrange("p (a t h) -> p a t h", a=2, h=4)),
                 reads=[BK5], writes=[CWB])
            P.op("pool", lambda e: e.memset(cw_b[:, :, NT, :], 1.0), reads=[CWB], writes=[CWB])
            P.op("dve", lambda e: e.tensor_tensor(out=tmp64[:, 0:nt, :], in0=u_sb[:, 0:nt, :], in1=cw_b[:, 0, 0:nt, :], op=ALU.subtract),
                 reads=[USB, CWB], writes=[TMP64])
            P.op("dve", lambda e: e.tensor_scalar(out=tmp64[:, 0:nt, :], in0=tmp64[:, 0:nt, :], scalar1=LNISQ, scalar2=None, op0=ALU.add),
                 reads=[TMP64], writes=[TMP64])
            P.op("act", lambda e: e.activation(out=pk[:, 0:nt, :], in_=tmp64[:, 0:nt, :], func=AF.Exp),
                 reads=[TMP64], writes=[PK])
            P.op("dve", lambda e: e.tensor_tensor(out=tmp64[:, 0:nt, :], in0=bneg[:, 0:nt, :], in1=cw_b[:, 0, 0:nt, :], op=ALU.subtract),
                 reads=[BNEG, CWB, PK], writes=[TMP64])
            P.op("act", lambda e: e.activation(out=thr[:, 0:nt, :], in_=tmp64[:, 0:nt, :], func=AF.Exp), reads=[TMP64], writes=[THR])
            P.dma("sp", c_fin[0], m_p[l:l + 1, :], row[0:1, 2, (nt - 1) * 4:nt * 4], reads=[ROW])

        def main_tile(l, t):
            par = t % 2
            last = (t == n_tiles - 1)
            xT = xT2[par]; XT = XT2[par]
            xi = t % 3
            sz = sz2[par]; SZ = SZ2[par]; SZA = SZA2[par]
            if t == 0:
                load_main(l, 0)
            if not last:
                load_main(l, t + 1)
            make_xT(par, xT, XT, xi)
            yield

            def proj(bt, BK, c0, n, wb):
                for k in range(8):
                    P.op("pe", lambda e, k=k: e.matmul(bt[:, 0:n], lhsT=xT[:, k, :], rhs=w_in_sb[:, k, c0:c0 + n],
                                                       start=(k == 0), stop=(k == 7)), reads=[XT, WIN[wb]], writes=[BK])

            cos2 = rope_sb[par][:, 0, :]
            sin2 = rope_sb[par][:, 1, :]
            for i in range(2):
                bt, BK = nb(); proj(bt, BK, C_RQ if i == 0 else C_RK, 512, i)
                src = bt[:].rearrange("p (h d) -> p h d", h=4)
                dst_t = tmp_t[:, i * 4:(i + 1) * 4, :]
                dst_u = tmp_u[:, i * 4:(i + 1) * 4, :]
                TT_ = TMPT if i == 0 else TMPT_K
                TU_ = TMPU if i == 0 else TMPU_K
                P.op("dve", lambda e, src=src, dst_t=dst_t: e.tensor_tensor(
                    out=dst_t, in0=src, in1=cos2.unsqueeze(1).broadcast_to([128, 4, HD]), op=ALU.mult),
                    reads=[BK, ROPE[par]], writes=[TT_])
                P.op("dve", lambda e, src=src, dst_u=dst_u: e.tensor_tensor(
                    out=dst_u[:, :, 0:64], in0=src[:, :, 64:128], in1=sin2[:, 0:64].unsqueeze(1).broadcast_to([128, 4, 64]), op=ALU.mult),
                    reads=[BK, ROPE[par]], writes=[TU_])
                P.op("dve", lambda e, src=src, dst_u=dst_u: e.tensor_tensor(
                    out=dst_u[:, :, 64:128], in0=src[:, :, 0:64], in1=sin2[:, 64:128].unsqueeze(1).broadcast_to([128, 4, 64]), op=ALU.mult),
                    reads=[BK, ROPE[par]], writes=[TU_])
                if i == 0:
                    P.op("pool", lambda e: e.tensor_tensor(out=qk_rot[:, 0:4, :], in0=tmp_t[:, 0:4, :], in1=tmp_u[:, 0:4, :], op=ALU.add),
                         reads=[TMPT, TMPU], writes=[QKR])
                    yield
            P.op("dve", lambda e: e.tensor_tensor(out=qk_rot[:, 4:8, :], in0=tmp_t[:, 4:8, :], in1=tmp_u[:, 4:8, :], op=ALU.add),
                 reads=[TMPT_K, TMPU_K], writes=[QKR_K])
            yield
            bv, BV = nb(); proj(bv, BV, C_RV, 512, 2)
            for h in range(4):
                P.op("act", lambda e, h=h: e.activation(out=v2[:, h, :], in_=bv[:, h * 128:(h + 1) * 128], func=AF.Copy, scale=kdec(h)),
                     reads=[BV, CST], writes=[V2])
            yield
            bz, BZ = nb(); proj(bz, BZ, C_RZ, 512, 3)
            P.op("act", lambda e: e.activation(out=sz[:, 0:512], in_=bz[:], func=AF.Silu), reads=[BZ], writes=[SZA])
            P.op("pool", lambda e: e.tensor_tensor(out=sz[:, 0:512], in0=sz[:, 0:512], in1=gbc[:, 0:512], op=ALU.mult), reads=[SZA, GBC], writes=[SZA])
            yield
            btq, BTQ = nb(); vtq = bfv(btq)
            for g in range(4):
                P.op("pe", lambda e, g=g: e.transpose(vtq[:, g * 128:(g + 1) * 128], qk_rot[:, g, :], ident_bf[:]),
                     reads=[QKR, IDB], writes=[BTQ])
            P.op("dve", lambda e: e.tensor_tensor(out=qT2[:], in0=vtq[:, 0:512].rearrange("p (h l) -> p h l", h=4), in1=qdecT, op=ALU.mult),
                 reads=[BTQ, CST], writes=[QT2])
            yield
            btk, BTK = nb(); vtk = bfv(btk)
            for g in range(4):
                P.op("pe", lambda e, g=g: e.transpose(vtk[:, g * 128:(g + 1) * 128], qk_rot[:, 4 + g, :], ident_bf[:]),
                     reads=[QKR_K, IDB], writes=[BTK])
            P.op("act", lambda e: e.copy(out=kT[:].rearrange("p h l -> p (h l)"), in_=vtk[:, 0:512]), reads=[BTK], writes=[KT])
            yield
            bs, BS = nb()
            for h in range(4):
                P.op("pe", lambda e, h=h: e.matmul(bs[:, h * 128:(h + 1) * 128], lhsT=kT[:, h, :], rhs=qT2[:, h, :], start=True, stop=True),
                     reads=[KT, QT2], writes=[BS])
            P.op("dve", lambda e: e.tensor_tensor(out=s2[:], in0=bs[:].rearrange("p (h l) -> p h l", h=4),
                                                  in1=mask_bf[:].unsqueeze(1).broadcast_to([128, 4, 128]), op=ALU.mult),
                 reads=[BS, MSK], writes=[S2])
            yield
            bo, BO = nb()
            for h in range(4):
                first = (t == 0)
                P.op("pe", lambda e, h=h, first=first: e.matmul(bo[:, h * 128:(h + 1) * 128], lhsT=s2[:, h, :], rhs=v2[:, h, :],
                                                                start=True, stop=first), reads=[S2, V2], writes=[BO])
                if not first:
                    P.op("pe", lambda e, h=h: e.matmul(bo[:, h * 128:(h + 1) * 128], lhsT=qT2[:, h, :], rhs=S_bf[:, h, :],
                                                       start=False, stop=True), reads=[QT2, SBF], writes=[BO])
            P.op("act", lambda e: e.copy(out=o_sb[:].rearrange("p h d -> p (h d)"), in_=bo[:]), reads=[BO], writes=[OSB])
            for h in range(4):
                P.op("dve", lambda e, h=h: e.bn_stats(out=stats[:, h, :], in_=o_sb[:, h, :]), reads=[OSB], writes=[STATS])
            yield
            bu, BU = nb()
            for h in range(4):
                P.op("pe", lambda e, h=h: e.matmul(bu[:, h * 128:(h + 1) * 128], lhsT=qk_rot[:, 4 + h, :], rhs=v2[:, h, :],
                                                   start=True, stop=True), reads=[QKR_K, V2], writes=[BU])
            for h in range(4):
                if t == 0:
                    P.op("dve", lambda e, h=h: e.tensor_copy(out=S_f[:, h, :], in_=bu[:, h * 128:(h + 1) * 128]), reads=[BU], writes=[SF])
                else:
                    P.op("dve", lambda e, h=h: e.scalar_tensor_tensor(out=S_f[:, h, :], in0=S_f[:, h, :], scalar=GAML[h],
                                                                     in1=bu[:, h * 128:(h + 1) * 128], op0=ALU.mult, op1=ALU.add),
                         reads=[BU, SF], writes=[SF])
            if not last:
                for h in range(4):
                    P.op("act", lambda e, h=h: e.activation(out=S_bf[:, h, :], in_=S_f[:, h, :], func=AF.Copy, scale=GAML[h]),
                         reads=[SF], writes=[SBF])
            yield
            if t == 0:
                P.op("pool", lambda e: e.memset(hist[:, :, 0:3], 0.0), writes=[HIST])
            else:
                P.op("pool", lambda e: e.tensor_copy(out=hist[:, :, 0:3], in_=hist[:, :, 128:131]), reads=[HIST], writes=[HIST])
            for i in range(2):
                bt, BK = nb()
                for c4 in range(4):
                    ch = i * 4 + c4
                    c0 = C_MQK + ch * 128
                    wb = 4 + ch // 4
                    for k in range(8):
                        P.op("pe", lambda e, k=k, bt=bt, c4=c4, c0=c0: e.matmul(bt[:, c4 * 128:(c4 + 1) * 128],
                                                                             lhsT=w_in_sb[:, k, c0:c0 + 128], rhs=xT[:, k, :],
                                                                             start=(k == 0), stop=(k == 7)),
                             reads=[XT, WIN[wb]], writes=[BK])
                P.op("act", lambda e, i=i, bt=bt: e.copy(out=hist[:, i * 4:(i + 1) * 4, 3:131], in_=bt[:].rearrange("p (c t) -> p c t", c=4)),
                     reads=[BK], writes=[HIST])
                if last:
                    P.op("dve", lambda e, i=i, bt=bt: e.tensor_copy(out=convp_sb[:, i * 4:(i + 1) * 4, :],
                                                                    in_=bt[:].rearrange("p (c t) -> p c t", c=4)[:, :, 125:128]),
                         reads=[BK], writes=[CONVP])
                yield
            for i in range(2):
                bt, BK = nb()
                for c4 in range(4):
                    ch = i * 4 + c4
                    for j in range(4):
                        P.op("pe", lambda e, j=j, bt=bt, ch=ch, c4=c4: e.matmul(bt[:, c4 * 128:(c4 + 1) * 128], lhsT=diag[:, ch * 4 + j, :],
                                                                            rhs=hist[:, ch, j:j + 128], start=(j == 0), stop=(j == 3)),
                             reads=[DIAG, HIST], writes=[BK])
                for c4 in range(4):
                    ch = i * 4 + c4
                    P.op("act", lambda e, bt=bt, ch=ch, c4=c4: e.activation(out=qkm[:, ch, :], in_=bt[:, c4 * 128:(c4 + 1) * 128],
                                                                        func=AF.Silu, bias=cwT[:, 32 + ch:33 + ch]),
                         reads=[BK, CWT], writes=[QKM])
                yield
            bkt, BKT = nb(); vkt = bfv(bkt)
            for h in range(4):
                P.op("pe", lambda e, h=h: e.transpose(vkt[:, h * 128:(h + 1) * 128], qkm[:, 4 + h, :], ident_bf[:]),
                     reads=[QKM, IDB], writes=[BKT])
            for h in range(4):
                P.op("act", lambda e, h=h: e.activation(out=kp[:, h, :], in_=vkt[:, h * 128:(h + 1) * 128], func=AF.Copy,
                                                        scale=pk[:, t, h:h + 1]), reads=[BKT, PK], writes=[KP])
            yield
            bmv, BMV = nb(); proj(bmv, BMV, C_MV, 512, 6)
            if l == 0 and t == 0:
                P.op("pool", lambda e: e.memset(v1[:, :, 128:130], 1.0), writes=[V1])
            P.op("act", lambda e: e.copy(out=v1[:, :, 0:128], in_=bmv[:].rearrange("p (h d) -> p h d", h=4)), reads=[BMV], writes=[V1])
            yield
            bmo, BMO = nb(); proj(bmo, BMO, C_MO, 512, 7)
            P.op("act", lambda e: e.activation(out=th[:], in_=bmo[:], func=AF.Tanh, scale=0.5), reads=[BMO], writes=[TH])
            yield
            bmz, BMZ = nb(); proj(bmz, BMZ, C_MZ, 512, 8)
            P.op("act", lambda e: e.activation(out=sz[:, 512:1024], in_=bmz[:], func=AF.Silu), reads=[BMZ], writes=[SZ])
            P.op("dve", lambda e: e.scalar_tensor_tensor(out=sz[:, 512:1024], in0=th[:], scalar=1.0, in1=sz[:, 512:1024], op0=ALU.add, op1=ALU.mult),
                 reads=[TH, SZ], writes=[SZ])
            P.op("pool", lambda e: e.tensor_tensor(out=sz[:, 512:1024], in0=sz[:, 512:1024], in1=gbc[:, 512:1024], op=ALU.mult), reads=[SZ, GBC], writes=[SZ])
            yield
            bsm, BSM = nb()
            for h in range(4):
                P.op("pe", lambda e, h=h: e.matmul(bsm[:, h * 128:(h + 1) * 128], lhsT=qkm[:, 4 + h, :], rhs=qkm[:, h, :], start=True, stop=True),
                     reads=[QKM], writes=[BSM])
            for h in range(4):
                P.op("dve", lambda e, h=h: e.scalar_tensor_tensor(out=s2m[:, h, :], in0=bsm[:, h * 128:(h + 1) * 128], scalar=pk[:, t, h:h + 1],
                                                                 in1=mask_bf[:], op0=ALU.mult, op1=ALU.mult),
                     reads=[BSM, PK, MSK], writes=[S2M])
            yield
            bn0, BN0 = nb(); bn1, BN1 = nb()
            for h in range(4):
                bt, BK = (bn0, BN0) if h < 2 else (bn1, BN1)
                o_ = (h % 2) * 130
                first = (t == 0)
                P.op("pe", lambda e, h=h, bt=bt, o_=o_, first=first: e.matmul(bt[:, o_:o_ + 130], lhsT=s2m[:, h, :], rhs=v1[:, h, :],
                                                                            start=True, stop=first), reads=[S2M, V1], writes=[BK])
                if not first:
                    P.op("pe", lambda e, h=h, bt=bt, o_=o_: e.matmul(bt[:, o_:o_ + 130], lhsT=qkm[:, h, :], rhs=C_bf[:, h, :],
                                                                   start=False, stop=True), reads=[QKM, CBF], writes=[BK])
            for i, (bt, BK) in enumerate(((bn0, BN0), (bn1, BN1))):
                P.op("act", lambda e, i=i, bt=bt: e.activation(out=dn[:, 2 * i:2 * i + 2], in_=bt[:, 128:259:130], func=AF.Abs),
                     reads=[BK], writes=[DN])
            P.op("dve", lambda e: e.tensor_tensor(out=dn[:], in0=dn[:], in1=thr[:, t, :], op=ALU.max), reads=[DN, THR], writes=[DN])
            P.op("dve", lambda e: e.reciprocal(out=dn[:], in_=dn[:]), reads=[DN], writes=[DN])
            for h in range(4):
                bt, BK = (bn0, BN0) if h < 2 else (bn1, BN1)
                o_ = (h % 2) * 130
                P.op("act", lambda e, h=h, bt=bt, o_=o_: e.activation(out=hm[:, h, :], in_=bt[:, o_:o_ + 128], func=AF.Copy, scale=dn[:, h:h + 1]),
                     reads=[BK, DN], writes=[HM])
            for h in range(4):
                P.op("dve", lambda e, h=h: e.bn_stats(out=stats[:, 4 + h, :], in_=hm[:, h, :]), reads=[HM], writes=[STATS])
            yield
            bu0, BU0 = nb(); bu1, BU1 = nb()
            for h in range(4):
                bt, BK = (bu0, BU0) if h < 2 else (bu1, BU1)
                o_ = (h % 2) * 130
                P.op("pe", lambda e, h=h, bt=bt, o_=o_: e.matmul(bt[:, o_:o_ + 130], lhsT=kp[:, h, :], rhs=v1[:, h, :], start=True, stop=True),
                     reads=[KP, V1], writes=[BK])
            for h in range(4):
                bt, BK = (bu0, BU0) if h < 2 else (bu1, BU1)
                o_ = (h % 2) * 130
                if t == 0:
                    P.op("dve", lambda e, h=h, bt=bt, o_=o_: e.tensor_copy(out=C_f[:, h, :], in_=bt[:, o_:o_ + 130]), reads=[BK], writes=[CF])
                else:
                    P.op("dve", lambda e, h=h, bt=bt, o_=o_: e.scalar_tensor_tensor(out=C_f[:, h, :], in0=C_f[:, h, :], scalar=cw_b[:, 1, t, h:h + 1],
                                                                                  in1=bt[:, o_:o_ + 130], op0=ALU.mult, op1=ALU.add),
                         reads=[BK, CF, CWB], writes=[CF])
            if not last:
                for h in range(4):
                    P.op("act", lambda e, h=h: e.activation(out=C_bf[:, h, :], in_=C_f[:, h, :], func=AF.Copy, scale=cw_b[:, 1, t + 1, h:h + 1]),
                         reads=[CF, CWB], writes=[CBF])
            yield
            for g in range(8):
                P.op("dve", lambda e, g=g: e.bn_aggr(out=mv[:, g, :], in_=stats[:, g, :]), reads=[STATS], writes=[MV])
            P.op("pool", lambda e: e.tensor_scalar(out=rstd[:], in0=mv[:, :, 1], scalar1=GN_EPS, scalar2=None, op0=ALU.add),
                 reads=[MV], writes=[RSTD])
            P.op("pool", lambda e: e.tensor_tensor(out=rstd[:], in0=rstd[:], in1=mhalf[:], op=ALU.pow), reads=[RSTD, MHALF], writes=[RSTD])
            P.op("dve", lambda e: e.scalar_tensor_tensor(out=nbias[:], in0=mv[:, :, 0], scalar=-1.0, in1=rstd[:], op0=ALU.mult, op1=ALU.mult),
                 reads=[MV, RSTD], writes=[NBIAS])
            for g in range(8):
                if g < 4:
                    src = o_sb[:, g, :]; SB_ = OSB
                else:
                    src = hm[:, g - 4, :]; SB_ = HM
                P.op("act", lambda e, g=g, src=src: e.activation(out=on[:, g, :], in_=src, func=AF.Identity, scale=rstd[:, g:g + 1],
                                                               bias=nbias[:, g:g + 1]), reads=[SB_, RSTD, NBIAS], writes=[ON if g < 4 else ON_M])
            onf = on[:].rearrange("p g d -> p (g d)")
            P.op("pool", lambda e: e.tensor_tensor(out=gated[:, 0:512], in0=onf[:, 0:512], in1=sz[:, 0:512], op=ALU.mult),
                 reads=[ON, SZA], writes=[GATED])
            P.op("dve", lambda e: e.tensor_tensor(out=gated[:, 512:1024], in0=onf[:, 512:1024], in1=sz[:, 512:1024], op=ALU.mult),
                 reads=[ON_M, SZ], writes=[GATED_M])
            yield
            bg, BG = nb(); vg = bfv(bg)
            for k in range(8):
                P.op("pe", lambda e, k=k: e.transpose(vg[:, k * 128:(k + 1) * 128], gated[:, k * 128:(k + 1) * 128], ident_bf[:]),
                     reads=[GATED if k < 4 else GATED_M, IDB], writes=[BG])
            P.op("act", lambda e: e.copy(out=gT[:].rearrange("p k c -> p (k c)"), in_=vg), reads=[BG], writes=[GT])
            for n in range(2):
                yield
                bt, BK = nb()
                for k in range(8):
                    P.op("pe", lambda e, k=k, n=n, bt=bt: e.matmul(bt[:], lhsT=gT[:, k, :], rhs=w_out_sb[:, k, n * 512:(n + 1) * 512],
                                                               start=(k == 0), stop=(k == 7)), reads=[GT, WOUT[n]], writes=[BK])
                P.op("dve", lambda e, n=n, bt=bt: e.scalar_tensor_tensor(out=ypre[:, n * 512:(n + 1) * 512], in0=x_sb[xi][:, n * 512:(n + 1) * 512],
                                                                     scalar=ALPHA, in1=bt[:], op0=ALU.mult, op1=ALU.add),
                     reads=[BK, XSB[xi]], writes=[YPRE if n == 0 else YPRE_B])
                P.op("dve", lambda e, n=n: e.bn_stats(out=lstats[:, n, :], in_=ypre[:, n * 512:(n + 1) * 512]), reads=[YPRE if n == 0 else YPRE_B], writes=[LSTATS])
            P.op("dve", lambda e: e.bn_aggr(out=lmv[:], in_=lstats[:].rearrange("p a b -> p (a b)")), reads=[LSTATS], writes=[LMV])
            P.op("pool", lambda e: e.tensor_scalar(out=lrs[:, 0:1], in0=lmv[:, 1:2], scalar1=LN_EPS, scalar2=None, op0=ALU.add),
                 reads=[LMV], writes=[LRS])
            P.op("pool", lambda e: e.tensor_tensor(out=lrs[:, 0:1], in0=lrs[:, 0:1], in1=mhalf[:, 0:1], op=ALU.pow), reads=[LRS, MHALF], writes=[LRS])
            P.op("dve", lambda e: e.scalar_tensor_tensor(out=lrs[:, 1:2], in0=lmv[:, 0:1], scalar=-1.0, in1=lrs[:, 0:1], op0=ALU.mult, op1=ALU.mult),
                 reads=[LMV, LRS], writes=[LRS])
            P.op("act", lambda e: e.activation(out=ypre[:], in_=ypre[:], func=AF.Identity, scale=lrs[:, 0:1], bias=lrs[:, 1:2]),
                 reads=[YPRE, YPRE_B, LRS], writes=[YPRE, YPRE_B])
            P.op("pool", lambda e: e.tensor_tensor(out=ypre[:, 0:512], in0=ypre[:, 0:512], in1=lng[:, 0:512], op=ALU.mult), reads=[YPRE, LNG], writes=[YPRE])
            P.op("dve", lambda e: e.tensor_tensor(out=ypre[:, 512:1024], in0=ypre[:, 512:1024], in1=lng[:, 512:1024], op=ALU.mult), reads=[YPRE_B, LNG], writes=[YPRE_B])
            P.op("pool", lambda e: e.tensor_tensor(out=ypre[:, 0:512], in0=ypre[:, 0:512], in1=lnb[:, 0:512], op=ALU.add), reads=[YPRE, LNB], writes=[YPRE])
            P.op("dve", lambda e: e.tensor_tensor(out=ypre[:, 512:1024], in0=ypre[:, 512:1024], in1=lnb[:, 512:1024], op=ALU.add), reads=[YPRE_B, LNB], writes=[YPRE_B])
            if l == n_layers - 1:
                P.dma("sp", c_y[par], y_p[t * 128:(t + 1) * 128, :], ypre[:], reads=[YPRE, YPRE_B])
            else:
                P.dma("sp", c_y[par], y0[t * 128:(t + 1) * 128, :], ypre[:], reads=[YPRE, YPRE_B], writes=[Y0B[t]])


        xs_sb = sb("xs_sb", [NS, D]); XSS = Buf("xs")
        ropes = sb("ropes", [NS, 2, HD]); ROPES = Buf()
        sg = sb("sg", [NS, 16, 4]); SG = Buf()
        wexp = sb("wexp", [NS, NS, 4]); WEXP = Buf()
        wcbs = sb("wcbs", [128, NS * 4]); WCBS = Buf()
        qTs = sb("qTs", [128, 8, NS], BF16); QTS = Buf()
        oTs = sb("oTs", [128, 2, 64]); OTS = Buf()
        sstats = sb("sstats", [NS, 8, 6]); SSTATS = Buf()
        smv = sb("smv", [NS, 8, 2]); SMV = Buf()
        srs = sb("srs", [NS, 8]); SRS = Buf()
        snb = sb("snb", [NS, 8]); SNB = Buf()
        slst = sb("slst", [NS, 2, 6]); SLST = Buf()
        slmv = sb("slmv", [NS, 2]); SLMV = Buf()
        slrs = sb("slrs", [NS, 2]); SLRS = Buf()
        c_s = [P.chan("c_s%d" % i) for i in range(8)]
        c_sg = [P.chan("c_sg0"), P.chan("c_sg1")]
        c_cg = [P.chan("c_cg0"), P.chan("c_cg1")]
        c_so = [ochan("c_so%d" % i) for i in range(8)]
        c_sgo = [ochan("c_sgo0"), ochan("c_sgo1")]
        c_cgo = [ochan("c_cgo0"), ochan("c_cgo1")]
        id16 = ident_f[0:NS, 0:NS]
        gamb = cst[0:NS, K_GAM:K_GAM + 4]
        sbank_ctr = [0]

        def snb_():
            i = sbank_ctr[0] % 6
            sbank_ctr[0] += 1
            return banks[i]

        def sample_path(l):
            S16 = slice(0, NS)
            if l == 0:
                P.dma("sp", c_s[0], xs_sb[:], xs, writes=[XSS])
                P.dma("sp", c_s[7], ropes[:].rearrange("p a d -> p (a d)"), rope_s.partition_broadcast(NS), writes=[ROPES])
            sc = [x_sb[0][S16, :], x_sb[1][S16, :], x_sb[2][S16, :]]
            SCB = [XSB[0], XSB[1], XSB[2]]
            szs = sz2[0]; SZ = SZ2[0]; sz = sz2[0]
            for j in range(3):
                P.dma("sp", c_s[1 + j], sc[j], sconv[l][:, j, :], writes=[SCB[j]])
            P.dma("sp", c_s[4], sg[:, 0, :], sm[l], writes=[SG])
            n0v = o_sb[S16, :, :]
            P.dma("sp", c_s[5], n0v, sn[l], writes=[OSB])
            P.op("pool", lambda e: e.tensor_copy(out=x_bf[S16, :], in_=xs_sb[:]), reads=[XSS], writes=[XBF])
            bt, BK = nb(); v = bfv(bt)
            for k in range(8):
                P.op("pe", lambda e, k=k: e.transpose(v[:, k * NS:(k + 1) * NS], x_bf[S16, k * 128:(k + 1) * 128], ident_bf[S16, 0:NS]),
                     reads=[XBF, IDB], writes=[BK])
            P.op("act", lambda e: e.copy(out=xT[:, :, 0:NS], in_=v[:, 0:8 * NS].rearrange("p (k s) -> p k s", k=8)), reads=[BK], writes=[XT])

            def sproj(c0, n, wb):
                bt, BK = nb()
                for k in range(8):
                    P.op("pe", lambda e, k=k: e.matmul(bt[S16, 0:n], lhsT=xT[:, k, 0:NS], rhs=w_in_sb[:, k, c0:c0 + n],
                                                       start=(k == 0), stop=(k == 7)), reads=[XT, WIN[wb]], writes=[BK])
                return bt, BK

            cos2 = ropes[:, 0, :]
            sin2 = ropes[:, 1, :]
            for i, c0 in enumerate((C_RQ, C_RK)):
                bt, BK = sproj(c0, 512, i)
                src = bt[S16, :].rearrange("p (h d) -> p h d", h=4)
                dst_t = tmp_t[S16, i * 4:(i + 1) * 4, :]
                dst_u = tmp_u[S16, i * 4:(i + 1) * 4, :]
                P.op("dve", lambda e, src=src, dst_t=dst_t: e.tensor_tensor(out=dst_t, in0=src, in1=cos2.unsqueeze(1).broadcast_to([NS, 4, HD]), op=ALU.mult),
                     reads=[BK, ROPES], writes=[TMPT])
                P.op("dve", lambda e, src=src, dst_u=dst_u: e.tensor_tensor(out=dst_u[:, :, 0:64], in0=src[:, :, 64:128],
                                                                          in1=sin2[:, 0:64].unsqueeze(1).broadcast_to([NS, 4, 64]), op=ALU.mult),
                     reads=[BK, ROPES], writes=[TMPU])
                P.op("dve", lambda e, src=src, dst_u=dst_u: e.tensor_tensor(out=dst_u[:, :, 64:128], in0=src[:, :, 0:64],
                                                                          in1=sin2[:, 64:128].unsqueeze(1).broadcast_to([NS, 4, 64]), op=ALU.mult),
                     reads=[BK, ROPES], writes=[TMPU])
            qk_s = on[S16, :, :]
            P.op("pool", lambda e: e.tensor_tensor(out=qk_s, in0=tmp_t[S16, :, :], in1=tmp_u[S16, :, :], op=ALU.add), reads=[TMPT, TMPU], writes=[ON])
            v2s = hm[S16, :, :]
            bt, BK = sproj(C_RV, 512, 2)
            P.op("act", lambda e, bt=bt: e.mul(out=v2s.rearrange("p h d -> p (h d)"), in_=bt[S16, :], mul=ISQ), reads=[BK], writes=[HM])
            bt, BK = sproj(C_RZ, 512, 3)
            P.op("act", lambda e, bt=bt: e.activation(out=sz[S16, 0:512], in_=bt[S16, :], func=AF.Silu), reads=[BK], writes=[SZ])
            mqk_s = ypre[S16, :]
            for i in range(2):
                bt, BK = sproj(C_MQK + i * 512, 512, 4 + i)
                P.op("act", lambda e, bt=bt, i=i: e.copy(out=mqk_s[:, i * 512:(i + 1) * 512], in_=bt[S16, :]), reads=[BK], writes=[YPRE])
            P.dma("sp", c_so[0], conv_s[l][:, 0, :], sc[1], reads=[SCB[1]])
            P.dma("sp", c_so[1], conv_s[l][:, 1, :], sc[2], reads=[SCB[2]])
            P.dma("sp", c_so[2], conv_s[l][:, 2, :], mqk_s, reads=[YPRE])
            Wb = sz2[1][S16, :]; WB = SZ2[1]
            acc = tmp_t[S16, :, :].rearrange("p g d -> p (g d)")
            tmpc = tmp_u[S16, :, :].rearrange("p g d -> p (g d)")
            full = [sc[0], sc[1], sc[2], mqk_s]
            FB = [SCB[0], SCB[1], SCB[2], YPRE]
            for n_, j in enumerate((3, 0, 1, 2)):
                P.dma("sp", c_s[6], Wb, conv_w[l][j].partition_broadcast(NS), writes=[WB])
                if n_ == 0:
                    P.op("dve", lambda e, j=j: e.tensor_tensor(out=acc, in0=full[j], in1=Wb, op=ALU.mult), reads=[FB[j], WB, ON], writes=[TMPT])
                else:
                    P.op("dve", lambda e, j=j: e.tensor_tensor(out=tmpc, in0=full[j], in1=Wb, op=ALU.mult), reads=[FB[j], WB, ON], writes=[TMPU])
                    P.op("pool", lambda e: e.tensor_tensor(out=acc, in0=acc, in1=tmpc, op=ALU.add), reads=[TMPU, TMPT], writes=[TMPT])
            P.dma("sp", c_s[6], Wb, conv_b[l].partition_broadcast(NS), writes=[WB])
            P.op("pool", lambda e: e.tensor_tensor(out=acc, in0=acc, in1=Wb, op=ALU.add), reads=[WB, TMPT], writes=[TMPT])
            P.op("act", lambda e: e.activation(out=acc, in_=acc, func=AF.Silu), reads=[TMPT], writes=[TMPT])
            qkm_s = tmp_t[S16, :, :]
            v_s = gated[S16, :].bitcast(F32).rearrange("p (h d) -> p h d", h=4)
            bt, BK = sproj(C_MV, 512, 6)
            P.op("act", lambda e, bt=bt: e.copy(out=v_s.rearrange("p h d -> p (h d)"), in_=bt[S16, :]), reads=[BK], writes=[GATED])
            bt, BK = sproj(C_MO, 512, 7)
            P.op("act", lambda e, bt=bt: e.activation(out=th[S16, :], in_=bt[S16, :], func=AF.Tanh, scale=0.5), reads=[BK], writes=[TH])
            bt, BK = sproj(C_MZ, 512, 8)
            P.op("act", lambda e, bt=bt: e.activation(out=sz[S16, 512:1024], in_=bt[S16, :], func=AF.Silu), reads=[BK], writes=[SZ])
            bt, BK = sproj(C_G, 8, 9)
            G_ = lambda i: sg[:, i, :]
            P.op("dve", lambda e, bt=bt: e.tensor_tensor(out=sg[:, 1:3, :].rearrange("p a h -> p (a h)"), in0=bt[S16, 0:8], in1=bias8[S16, :], op=ALU.add),
                 reads=[BK, BIAS8], writes=[SG])
            P.op("act", lambda e: e.activation(out=G_(3), in_=G_(2), func=AF.Exp, scale=-1.0), reads=[SG], writes=[SG])
            P.op("act", lambda e: e.activation(out=G_(3), in_=G_(3), func=AF.Ln, bias=1.0), reads=[SG], writes=[SG])
            P.op("dve", lambda e: e.tensor_tensor(out=G_(4), in0=G_(1), in1=G_(3), op=ALU.add), reads=[SG], writes=[SG])
            P.op("dve", lambda e: e.tensor_tensor(out=G_(5), in0=G_(4), in1=G_(0), op=ALU.max), reads=[SG], writes=[SG])
            P.op("dve", lambda e: e.tensor_tensor(out=G_(6), in0=G_(5), in1=G_(3), op=ALU.subtract), reads=[SG], writes=[SG])
            P.dma("sp", c_so[3], m_s[l], G_(6), reads=[SG])
            P.op("dve", lambda e: e.tensor_tensor(out=G_(14), in0=G_(4), in1=G_(5), op=ALU.subtract), reads=[SG], writes=[SG])
            P.op("dve", lambda e: e.tensor_scalar(out=G_(14), in0=G_(14), scalar1=LNISQ, scalar2=None, op0=ALU.add), reads=[SG], writes=[SG])
            P.op("act", lambda e: e.activation(out=G_(7), in_=G_(14), func=AF.Exp), reads=[SG], writes=[SG])
            P.op("dve", lambda e: e.tensor_tensor(out=G_(14), in0=G_(3), in1=G_(5), op=ALU.subtract), reads=[SG], writes=[SG])
            P.op("act", lambda e: e.activation(out=G_(8), in_=G_(14), func=AF.Exp), reads=[SG], writes=[SG])
            P.op("dve", lambda e: e.tensor_tensor(out=G_(14), in0=G_(0), in1=G_(5), op=ALU.subtract), reads=[SG], writes=[SG])
            P.op("act", lambda e: e.activation(out=G_(9), in_=G_(14), func=AF.Exp), reads=[SG], writes=[SG])
            bt, BK = nb()
            for g in range(4):
                P.op("pe", lambda e, g=g, bt=bt: e.transpose(bt[:, g * NS:(g + 1) * NS], qk_s[:, g, :], id16), reads=[ON, CST], writes=[BK])
            for g in range(4):
                P.op("pe", lambda e, g=g, bt=bt: e.transpose(bt[:, (4 + g) * NS:(5 + g) * NS], qkm_s[:, g, :], id16), reads=[TMPT, CST], writes=[BK])
            P.op("act", lambda e, bt=bt: e.copy(out=qTs[:].rearrange("p g s -> p (g s)"), in_=bt[:, 0:8 * NS]), reads=[BK], writes=[QTS])
            bc4 = lambda i: sg[:, i, :].unsqueeze(2).broadcast_to([NS, 4, HD])
            P.op("dve", lambda e: e.tensor_tensor(out=qkm_s[:, 4:8, :], in0=qkm_s[:, 4:8, :], in1=bc4(7), op=ALU.mult), reads=[TMPT, SG], writes=[TMPT])
            kbf = gT[S16, :, :]
            P.op("act", lambda e: e.copy(out=kbf[:, 0:4, :], in_=qk_s[:, 4:8, :]), reads=[ON], writes=[GT])
            P.op("act", lambda e: e.copy(out=kbf[:, 4:8, :], in_=qkm_s[:, 4:8, :]), reads=[TMPT], writes=[GT])
            prod = tmp_u[S16, :, :]
            P.op("dve", lambda e: e.tensor_tensor(out=prod[:, 0:4, :], in0=qk_s[:, 0:4, :], in1=qk_s[:, 4:8, :], op=ALU.mult), reads=[ON], writes=[TMPU])
            P.op("dve", lambda e: e.tensor_reduce(out=G_(10), in_=prod[:, 0:4, :], axis=AX.X, op=ALU.add), reads=[TMPU], writes=[SG])
            P.op("dve", lambda e: e.tensor_tensor(out=prod[:, 4:8, :], in0=qkm_s[:, 0:4, :], in1=qkm_s[:, 4:8, :], op=ALU.mult), reads=[TMPT], writes=[TMPU])
            P.op("dve", lambda e: e.tensor_reduce(out=G_(11), in_=prod[:, 4:8, :], axis=AX.X, op=ALU.add), reads=[TMPU], writes=[SG])
            P.op("dve", lambda e: e.tensor_tensor(out=prod[:, 0:4, :], in0=qkm_s[:, 0:4, :], in1=n0v, op=ALU.mult), reads=[TMPT, OSB, SG], writes=[TMPU])
            P.op("dve", lambda e: e.tensor_reduce(out=G_(13), in_=prod[:, 0:4, :], axis=AX.X, op=ALU.add), reads=[TMPU], writes=[SG])
            P.op("dve", lambda e: e.tensor_tensor(out=n0v, in0=n0v, in1=bc4(9), op=ALU.mult), reads=[OSB, SG], writes=[OSB])
            P.op("pool", lambda e: e.tensor_tensor(out=n0v, in0=n0v, in1=qkm_s[:, 4:8, :], op=ALU.add), reads=[OSB, TMPT], writes=[OSB])
            P.dma("sp", c_so[4], n_s[l], n0v, reads=[OSB])
            P.op("dve", lambda e: e.tensor_tensor(out=wexp[:], in0=sg[:, 9, :].unsqueeze(1).broadcast_to([NS, NS, 4]),
                                                  in1=id16.unsqueeze(2).broadcast_to([NS, NS, 4]), op=ALU.mult), reads=[SG, CST], writes=[WEXP])
            bt, BK = nb()
            P.op("pe", lambda e, bt=bt: e.matmul(bt[:, 0:NS * 4], lhsT=ones_f[0:NS, :], rhs=wexp[:].rearrange("p s h -> p (s h)"), start=True, stop=True),
                 reads=[WEXP, CST], writes=[BK])
            P.op("act", lambda e, bt=bt: e.copy(out=wcbs[:], in_=bt[:, 0:NS * 4]), reads=[BK], writes=[WCBS])
            GS = [x_sb[0][:].rearrange("p (s h e) -> p s h e", s=2, h=4), x_sb[1][:].rearrange("p (s h e) -> p s h e", s=2, h=4)]
            GC = [x_sb[2][:].rearrange("p (s h e) -> p s h e", s=2, h=4), sz2[1][:].rearrange("p (s h e) -> p s h e", s=2, h=4)]
            GCB = [XSB[2], SZ2[1]]
            vexp = hist[S16, :, :].rearrange("p c t -> p (c t)")[:, 0:1024].rearrange("p (s h e) -> p s h e", s=2, h=4)
            Gbf = [qk_rot[:].rearrange("p (s h) e -> p s h e", s=2), qkm[:].rearrange("p (s h) e -> p s h e", s=2)]
            GBFB = [[QKR, QKR_K], [QKM]]
            bor, BOR = banks[6]
            bom, BOM = banks[7]
            for g in range(NS // 2):
                par = g % 2
                s0 = g * 2
                P.dma("sp", c_sg[par], GS[par], sret[l][s0:s0 + 2].rearrange("s h d e -> d s h e"), writes=[XSB[par]])
                P.dma("sp", c_cg[par], GC[par], sC[l][s0:s0 + 2].rearrange("s h d e -> d s h e"), writes=[GCB[par]])
                for typ in range(2):
                    Gt = GS[par] if typ == 0 else GC[par]
                    GB = XSB[par] if typ == 0 else GCB[par]
                    bo_, BO_ = (bor, BOR) if typ == 0 else (bom, BOM)
                    gbf = Gbf[typ]
                    P.op("act", lambda e, Gt=Gt, gbf=gbf: e.copy(out=gbf.rearrange("p s h e -> p (s h e)"), in_=Gt.rearrange("p s h e -> p (s h e)")),
                         reads=[GB], writes=GBFB[typ])
                    for sl in range(2):
                        for h in range(4):
                            col = h * NS + s0 + sl
                            P.op("pe", lambda e, gbf=gbf, bo_=bo_, sl=sl, h=h, col=col, typ=typ, s0=s0: e.matmul(
                                bo_[:, col:col + 1], lhsT=gbf[:, sl, h, :], rhs=qTs[:, typ * 4 + h, s0 + sl:s0 + sl + 1], start=True, stop=True),
                                reads=GBFB[typ] + [QTS], writes=[BO_])
                    vsrc = v2s if typ == 0 else v_s
                    VB = HM if typ == 0 else GATED
                    P.op("dve", lambda e, vsrc=vsrc, s0=s0: e.tensor_tensor(
                        out=vexp, in0=vsrc.unsqueeze(1).broadcast_to([NS, 2, 4, HD]),
                        in1=id16[:, s0:s0 + 2].unsqueeze(2).unsqueeze(3).broadcast_to([NS, 2, 4, HD]), op=ALU.mult),
                        reads=[VB, CST], writes=[HIST])
                    ksrc = kbf[:, 0:4, :] if typ == 0 else kbf[:, 4:8, :]
                    KB = GT
                    for sl in range(2):
                        bu_, BU_ = snb_()
                        for h in range(4):
                            P.op("pe", lambda e, bu_=bu_, h=h, sl=sl, ksrc=ksrc: e.matmul(bu_[:, h * 128:(h + 1) * 128], lhsT=ksrc[:, h, :], rhs=vexp[:, sl, h, :],
                                                                                    start=True, stop=True), reads=[KB, HIST], writes=[BU_])
                        for h in range(4):
                            if typ == 0:
                                P.op("dve", lambda e, bu_=bu_, h=h, sl=sl, Gt=Gt: e.scalar_tensor_tensor(
                                    out=Gt[:, sl, h, :], in0=Gt[:, sl, h, :], scalar=GAM[h], in1=bu_[:, h * 128:(h + 1) * 128], op0=ALU.mult, op1=ALU.add),
                                    reads=[BU_, GB], writes=[GB])
                            else:
                                ci = (s0 + sl) * 4 + h
                                P.op("dve", lambda e, bu_=bu_, h=h, sl=sl, Gt=Gt, ci=ci: e.scalar_tensor_tensor(
                                    out=Gt[:, sl, h, :], in0=Gt[:, sl, h, :], scalar=wcbs[:, ci:ci + 1], in1=bu_[:, h * 128:(h + 1) * 128], op0=ALU.mult, op1=ALU.add),
                                    reads=[BU_, GB, WCBS], writes=[GB])
                P.dma("sp", c_sgo[par], ret_s[l][s0:s0 + 2].rearrange("s h d e -> d s h e"), GS[par], reads=[XSB[par]])
                P.dma("sp", c_cgo[par], C_s[l][s0:s0 + 2].rearrange("s h d e -> d s h e"), GC[par], reads=[GCB[par]])
            P.op("act", lambda e: e.copy(out=oTs[:, 0, :], in_=bor[:, 0:64]), reads=[BOR], writes=[OTS])
            P.op("act", lambda e: e.copy(out=oTs[:, 1, :], in_=bom[:, 0:64]), reads=[BOM], writes=[OTS])
            btr_, BTR_ = nb()
            btm_, BTM_ = nb()
            for typ, (bt, BK) in enumerate(((btr_, BTR_), (btm_, BTM_))):
                for h in range(4):
                    P.op("pe", lambda e, bt=bt, typ=typ, h=h: e.transpose(bt[S16, h * 128:(h + 1) * 128], oTs[:, typ, h * NS:(h + 1) * NS], ident_f),
                         reads=[OTS, CST], writes=[BK])
            hs = tmp_u[S16, :, :]
            t2 = ypre[S16, :].rearrange("p (g d) -> p g d", g=8)
            inter_r = btr_[S16, :].rearrange("p (h d) -> p h d", h=4)
            inter_m = btm_[S16, :].rearrange("p (h d) -> p h d", h=4)
            P.op("dve", lambda e: e.tensor_tensor(out=hs[:, 0:4, :], in0=inter_r, in1=gamb.unsqueeze(2).broadcast_to([NS, 4, HD]), op=ALU.mult),
                 reads=[BTR_, CST, SG], writes=[TMPU])
            P.op("dve", lambda e: e.tensor_tensor(out=t2[:, 0:4, :], in0=v2s, in1=bc4(10), op=ALU.mult), reads=[HM, SG], writes=[YPRE])
            P.op("pool", lambda e: e.tensor_tensor(out=hs[:, 0:4, :], in0=hs[:, 0:4, :], in1=t2[:, 0:4, :], op=ALU.add), reads=[TMPU, YPRE], writes=[TMPU])
            P.op("dve", lambda e: e.tensor_tensor(out=hs[:, 4:8, :], in0=inter_m, in1=bc4(9), op=ALU.mult), reads=[BTM_, SG], writes=[TMPU])
            P.op("dve", lambda e: e.tensor_tensor(out=t2[:, 4:8, :], in0=v_s, in1=bc4(11), op=ALU.mult), reads=[GATED, SG], writes=[YPRE])
            P.op("pool", lambda e: e.tensor_tensor(out=hs[:, 4:8, :], in0=hs[:, 4:8, :], in1=t2[:, 4:8, :], op=ALU.add), reads=[TMPU, YPRE], writes=[TMPU])
            P.op("dve", lambda e: e.tensor_tensor(out=G_(12), in0=G_(13), in1=G_(9), op=ALU.mult), reads=[SG], writes=[SG])
            P.op("dve", lambda e: e.tensor_tensor(out=G_(12), in0=G_(12), in1=G_(11), op=ALU.add), reads=[SG], writes=[SG])
            P.op("act", lambda e: e.activation(out=G_(12), in_=G_(12), func=AF.Abs), reads=[SG], writes=[SG])
            P.op("dve", lambda e: e.tensor_tensor(out=G_(12), in0=G_(12), in1=G_(8), op=ALU.max), reads=[SG], writes=[SG])
            P.op("dve", lambda e: e.reciprocal(out=G_(12), in_=G_(12)), reads=[SG], writes=[SG])
            P.op("dve", lambda e: e.tensor_tensor(out=hs[:, 4:8, :], in0=hs[:, 4:8, :], in1=bc4(12), op=ALU.mult), reads=[TMPU, SG], writes=[TMPU])
            for g in range(8):
                P.op("dve", lambda e, g=g: e.bn_stats(out=sstats[:, g, :], in_=hs[:, g, :]), reads=[TMPU], writes=[SSTATS])
            for g in range(8):
                P.op("dve", lambda e, g=g: e.bn_aggr(out=smv[:, g, :], in_=sstats[:, g, :]), reads=[SSTATS], writes=[SMV])
            P.op("pool", lambda e: e.tensor_scalar(out=srs[:], in0=smv[:, :, 1], scalar1=GN_EPS, scalar2=None, op0=ALU.add), reads=[SMV], writes=[SRS])
            P.op("pool", lambda e: e.tensor_tensor(out=srs[:], in0=srs[:], in1=mhalf[S16, :], op=ALU.pow), reads=[SRS, MHALF], writes=[SRS])
            P.op("dve", lambda e: e.scalar_tensor_tensor(out=snb[:], in0=smv[:, :, 0], scalar=-1.0, in1=srs[:], op0=ALU.mult, op1=ALU.mult),
                 reads=[SMV, SRS], writes=[SNB])
            on2 = on[S16, :, :]
            for g in range(8):
                P.op("act", lambda e, g=g: e.activation(out=on2[:, g, :], in_=hs[:, g, :], func=AF.Identity, scale=srs[:, g:g + 1], bias=snb[:, g:g + 1]),
                     reads=[TMPU, SRS, SNB], writes=[ON])
            P.op("dve", lambda e: e.scalar_tensor_tensor(out=sz[S16, 512:1024], in0=th[S16, :], scalar=1.0, in1=sz[S16, 512:1024], op0=ALU.add, op1=ALU.mult),
                 reads=[TH, SZ], writes=[SZ])
            P.op("pool", lambda e: e.tensor_tensor(out=sz[S16, :], in0=sz[S16, :], in1=gbc[S16, :], op=ALU.mult), reads=[SZ, GBC], writes=[SZ])
            P.op("pool", lambda e: e.tensor_tensor(out=gated[S16, :], in0=on2.rearrange("p g d -> p (g d)"), in1=sz[S16, :], op=ALU.mult),
                 reads=[ON, SZ], writes=[GATED])
            bt, BK = nb(); vg = bfv(bt)
            for k in range(8):
                P.op("pe", lambda e, k=k: e.transpose(vg[:, k * NS:(k + 1) * NS], gated[S16, k * 128:(k + 1) * 128], ident_bf[S16, 0:NS]),
                     reads=[GATED, IDB], writes=[BK])
            P.op("act", lambda e: e.copy(out=gT[:, :, 0:NS], in_=vg[:, 0:8 * NS].rearrange("p (k s) -> p k s", k=8)), reads=[BK], writes=[GT])
            yps = ypre[S16, :]
            for n in range(2):
                bt, BK = nb()
                for k in range(8):
                    P.op("pe", lambda e, k=k, n=n, bt=bt: e.matmul(bt[S16, :], lhsT=gT[:, k, 0:NS], rhs=w_out_sb[:, k, n * 512:(n + 1) * 512],
                                                               start=(k == 0), stop=(k == 7)), reads=[GT, WOUT[n]], writes=[BK])
                P.op("dve", lambda e, n=n, bt=bt: e.scalar_tensor_tensor(out=yps[:, n * 512:(n + 1) * 512], in0=xs_sb[:, n * 512:(n + 1) * 512], scalar=ALPHA,
                                                                     in1=bt[S16, :], op0=ALU.mult, op1=ALU.add), reads=[BK, XSS], writes=[YPRE])
                P.op("dve", lambda e, n=n: e.bn_stats(out=slst[:, n, :], in_=yps[:, n * 512:(n + 1) * 512]), reads=[YPRE], writes=[SLST])
            P.op("dve", lambda e: e.bn_aggr(out=slmv[:], in_=slst[:].rearrange("p a b -> p (a b)")), reads=[SLST], writes=[SLMV])
            P.op("pool", lambda e: e.tensor_scalar(out=slrs[:, 0:1], in0=slmv[:, 1:2], scalar1=LN_EPS, scalar2=None, op0=ALU.add), reads=[SLMV], writes=[SLRS])
            P.op("pool", lambda e: e.tensor_tensor(out=slrs[:, 0:1], in0=slrs[:, 0:1], in1=mhalf[S16, 0:1], op=ALU.pow), reads=[SLRS, MHALF], writes=[SLRS])
            P.op("dve", lambda e: e.scalar_tensor_tensor(out=slrs[:, 1:2], in0=slmv[:, 0:1], scalar=-1.0, in1=slrs[:, 0:1], op0=ALU.mult, op1=ALU.mult),
                 reads=[SLMV, SLRS], writes=[SLRS])
            P.op("act", lambda e: e.activation(out=xs_sb[:], in_=yps, func=AF.Identity, scale=slrs[:, 0:1], bias=slrs[:, 1:2]),
                 reads=[YPRE, SLRS], writes=[XSS])
            P.op("pool", lambda e: e.tensor_tensor(out=xs_sb[:], in0=xs_sb[:], in1=lng[S16, :], op=ALU.mult), reads=[XSS, LNG], writes=[XSS])
            P.op("pool", lambda e: e.tensor_tensor(out=xs_sb[:], in0=xs_sb[:], in1=lnb[S16, :], op=ALU.add), reads=[XSS, LNB], writes=[XSS])
            if l == n_layers - 1:
                P.dma("sp", c_so[5], y_s, xs_sb[:], reads=[XSS])

        def finalize_prompt(l):
            P.dma("sp", c_fin[1], ret_p[l].rearrange("h d e -> d h e"), S_f[:], reads=[SF])
            P.dma("sp", c_fin[2], C_p[l].rearrange("h d e -> d h e"), C_f[:, :, 0:128], reads=[CF])
            P.dma("sp", c_fin[3], n_p[l].rearrange("h d -> d h"), C_f[:, :, 128], reads=[CF], allow_slow_non_contiguous=True)
            for half in range(2):
                bt, BK = nb()
                for c4 in range(4):
                    ch = half * 4 + c4
                    P.op("pe", lambda e, ch=ch, c4=c4, bt=bt: e.transpose(bt[0:3, c4 * 128:(c4 + 1) * 128], convp_sb[:, ch, :], ident_f),
                         reads=[CONVP, CST], writes=[BK])
                P.op("act", lambda e, half=half, bt=bt: e.copy(out=convp_T[:, half * 512:(half + 1) * 512], in_=bt[0:3, :]),
                     reads=[BK], writes=[CONVPT])
            P.dma("sp", c_fin[4], conv_p[l], convp_T[:], reads=[CONVPT])

        def stage(name):
            if stop_at == name:
                raise StopBuild()

        try:
            for l in range(n_layers):
                load_weights(l)
                stage("weights")
                load_params(l)
                stage("params")
                prepass(l)
                stage("prepass")
                NSEG = 25
                OFF = (NSEG + 1) // 2
                gens = [main_tile(l, t) for t in range(n_tiles)]
                done = [0] * n_tiles
                fin = [False] * n_tiles
                slots = {}
                for t in range(n_tiles):
                    for k in range(NSEG):
                        slots.setdefault(t * OFF + k, []).append((t, k))
                for sl_ in sorted(slots):
                    for (t, k) in sorted(slots[sl_]):
                        assert not fin[t], "main_tile has fewer steps than NSEG"
                        try:
                            next(gens[t])
                            done[t] += 1
                        except StopIteration:
                            fin[t] = True
                assert all(fin), "main_tile has more steps than NSEG: %s" % done
                stage("tiles")
                finalize_prompt(l)
                if do_sample:
                    P.barrier()
                    sample_path(l)
                    P.barrier()
                stage('sample')
        except StopBuild:
            pass

        for c in P.chans:
            if c.val:
                P.wait_chan("sp", c)
        with nc.Block() as block:
            P.finish(block)
    return nc


def make_consts():
    cst = np.zeros((128, NCST), np.float32)
    cst[:, K_ID:K_ID + 128] = np.eye(128, dtype=np.float32)
    idx = np.arange(128)
    cst[:, K_TRI:K_TRI + 128] = (idx[:, None] <= idx[None, :]).astype(np.float32)
    for h in range(H):
        lg = np.float32(LOGG[h])
        cst[:, K_KDEC + h] = np.exp(lg * (np.float32(L - 1) - idx.astype(np.float32))).astype(np.float32) * np.float32(ISQ)
        cst[:, K_QDEC + h * 128:K_QDEC + (h + 1) * 128] = np.exp(lg * (idx.astype(np.float32) + 1.0 - L)).astype(np.float32)[None, :]
    cst[:, K_ONE:K_ONE + 128] = 1.0
    for h in range(H):
        cst[:, K_GAM + h] = np.float32(GAM[h])
    half = HD // 2
    inv = (np.float32(10000.0) ** (-np.arange(half, dtype=np.float32) / np.float32(half))).astype(np.float32)

    def rope(pos):
        ang = (pos.astype(np.float32)[:, None] * inv[None, :]).astype(np.float32)
        c = np.cos(ang.astype(np.float64)).astype(np.float32)
        s = np.sin(ang.astype(np.float64)).astype(np.float32)
        out = np.zeros((len(pos), 2, HD), np.float32)
        out[:, 0, :half] = c
        out[:, 0, half:] = c
        out[:, 1, :half] = -s
        out[:, 1, half:] = s
        return out

    rope_p = rope(np.arange(T))
    rope_s = rope(np.array([PAST_LEN])).reshape(2 * HD)
    return cst, rope_p, rope_s


_CACHE = {}


def kernel(x_prompt, x_sample, state_ret, state_mlstm_C, state_mlstm_n, state_mlstm_m, state_conv,
           w_in, conv_w, conv_b, b_i, b_f, g_ret, g_m, w_out, ln_g, ln_b):
    n = 8
    if "nc" not in _CACHE:
        _CACHE["nc"] = build_program()
    nc = _CACHE["nc"]
    cst, rope_p, rope_s = make_consts()
    f = lambda a: np.ascontiguousarray(np.asarray(a, dtype=np.float32))
    shared = dict(w_in=f(w_in), conv_w=f(conv_w), conv_b=f(conv_b), b_i=f(b_i), b_f=f(b_f), g_ret=f(g_ret), g_m=f(g_m),
                  w_out=f(w_out), ln_g=f(ln_g), ln_b=f(ln_b), cst=cst, rope_p=rope_p, rope_s=rope_s)
    in_maps = []
    for c in range(n):
        s0, s1 = c * NS, (c + 1) * NS
        m = dict(shared)
        m["xp"] = f(x_prompt[c])
        m["xs"] = f(x_sample[s0:s1, 0, :])
        m["sret"] = f(state_ret[:, s0:s1])
        m["sC"] = f(state_mlstm_C[:, s0:s1])
        m["sn"] = f(state_mlstm_n[:, s0:s1])
        m["sm"] = f(state_mlstm_m[:, s0:s1])
        m["sconv"] = f(state_conv[:, s0:s1])
        in_maps.append(m)
    res = run_bass_kernel_spmd(nc, in_maps, core_ids=list(range(n)))
    R = res.results
    y_p = np.stack([R[c]["y_p"] for c in range(n)], 0)
    y_s = np.concatenate([R[c]["y_s"] for c in range(n)], 0)[:, None, :]
    ret_p = np.stack([R[c]["ret_p"] for c in range(n)], 1)
    C_p = np.stack([R[c]["C_p"] for c in range(n)], 1)
    n_p = np.stack([R[c]["n_p"] for c in range(n)], 1)
    m_p = np.stack([R[c]["m_p"] for c in range(n)], 1)
    conv_p = np.stack([R[c]["conv_p"] for c in range(n)], 1)
    ret_s = np.concatenate([R[c]["ret_s"] for c in range(n)], 1)
    C_s = np.concatenate([R[c]["C_s"] for c in range(n)], 1)
    n_s = np.concatenate([R[c]["n_s"] for c in range(n)], 1)
    m_s = np.concatenate([R[c]["m_s"] for c in range(n)], 1)
    conv_s = np.concatenate([R[c]["conv_s"] for c in range(n)], 1)
    return (y_p, y_s, ret_p, C_p, n_p, m_p, conv_p, ret_s, C_s, n_s, m_s, conv_s)
```

```python
import os
from contextlib import ExitStack
import numpy as np
import concourse.bass as bass
import concourse.mybir as mybir
from concourse.bass_utils import run_bass_kernel_spmd

F32 = mybir.dt.float32
BF16 = mybir.dt.bfloat16
AF = mybir.ActivationFunctionType
ALU = mybir.AluOpType
AX = mybir.AxisListType

D = 1024
T = 2048
NT = 16
L = 128
NS = 16
H = 4
HD = 128
N_IN = 4616
PAST_LEN = 16384
ALPHA = (2 * 2) ** 0.25
GN_EPS = 1e-5
LN_EPS = 1e-5
ISQ = float(HD ** -0.5)
LNISQ = float(np.log(HD ** -0.5))
GAM = [float(np.float32(1.0) - np.float32(2.0) ** np.float32(-5.0 - h)) for h in range(H)]
LOGG = [float(np.log(np.float32(g))) for g in GAM]
GAML = [float(np.exp(np.float32(lg) * L)) for lg in LOGG]

C_RQ, C_RK, C_RV, C_RZ, C_MQK, C_MV, C_MO, C_MZ, C_G = 0, 512, 1024, 1536, 2048, 3072, 3584, 4096, 4608

K_ID = 0
K_TRI = 128
K_KDEC = 256
K_QDEC = 260
K_ONE = 772
K_GAM = 900
NCST = 904


class Buf:
    __slots__ = ("name", "w", "r", "psum")

    def __init__(self, name="", psum=False):
        self.name = name
        self.w = None
        self.r = []
        self.psum = psum


class Chan:
    def __init__(self, prog, name):
        self.sem = prog.new_sem(name)
        self.val = 0


class Prog:
    ENG = ("pe", "act", "dve", "pool", "sp")

    def __init__(self, nc, stack):
        self.nc = nc
        self.stack = stack
        self.items = {e: [] for e in self.ENG}
        self.cnt = {e: 0 for e in self.ENG}
        self.esem = {e: self.new_sem("prog_" + e) for e in self.ENG}
        self.waited = {e: {} for e in self.ENG}
        self.chans = []

    def new_sem(self, name):
        return self.stack.enter_context(self.nc.semaphore(name))

    def chan(self, name):
        c = Chan(self, name)
        self.chans.append(c)
        return c

    def _need(self, eng, reads, writes):
        need = {}

        def add(ev):
            if ev is None:
                return
            k = (ev[0], id(ev[1]) if ev[0] == 'c' else ev[1])
            if k not in need or need[k][2] < ev[2]:
                need[k] = ev

        for b in reads:
            add(b.w)
            if b.psum:
                for r in b.r:
                    if not (r[0] == 'e' and r[1] == eng):
                        add(r)
        for b in writes:
            if b.w is not None and not (b.w[0] == 'e' and b.w[1] == eng):
                add(b.w)
            for r in b.r:
                if not (r[0] == 'e' and r[1] == eng):
                    add(r)
        wd = self.waited[eng]
        for k, ev in need.items():
            if wd.get(k, 0) >= ev[2]:
                continue
            wd[k] = ev[2]
            if ev[0] == 'e':
                self.items[eng].append(('we', ev[1], ev[2]))
            else:
                self.items[eng].append(('wc', ev[1].sem, ev[2]))

    def op(self, eng, fn, reads=(), writes=()):
        self._need(eng, reads, writes)
        self.cnt[eng] += 1
        ev = ('e', eng, self.cnt[eng])
        self.items[eng].append(('op', fn, self.cnt[eng]))
        for b in reads:
            b.r.append(ev)
        for b in writes:
            b.w = ev
            b.r = []
        return ev

    def dma(self, eng, chan, out, in_, reads=(), writes=(), **kw):
        self._need(eng, reads, writes)
        chan.val += 16
        ev = ('c', chan, chan.val)
        self.items[eng].append(('dma', lambda e, out=out, in_=in_, kw=kw, sem=chan.sem:
                                e.dma_start(out=out, in_=in_, **kw).then_inc(sem, 16)))
        for b in reads:
            b.r.append(ev)
        for b in writes:
            b.w = ev
            b.r = []
        return ev

    def barrier(self):
        for eng in self.ENG:
            wd = self.waited[eng]
            for e2 in self.ENG:
                if e2 == eng or self.cnt[e2] == 0:
                    continue
                k = ('e', e2)
                if wd.get(k, 0) >= self.cnt[e2]:
                    continue
                wd[k] = self.cnt[e2]
                self.items[eng].append(('we', e2, self.cnt[e2]))
            for c in self.chans:
                if c.val == 0:
                    continue
                k = ('c', id(c))
                if wd.get(k, 0) >= c.val:
                    continue
                wd[k] = c.val
                self.items[eng].append(('wc', c.sem, c.val))

    def wait_chan(self, eng, chan):
        self.items[eng].append(('wc', chan.sem, chan.val))

    def finish(self, block):
        targets = {e: set() for e in self.ENG}
        for e in self.ENG:
            for it in self.items[e]:
                if it[0] == 'we':
                    targets[it[1]].add(it[2])
        rank = {}
        for e in self.ENG:
            rank[e] = {s_: i + 1 for i, s_ in enumerate(sorted(targets[e]))}
        self.n_signals = {e: len(rank[e]) for e in self.ENG}

        def run(eng, e):
            sem = self.esem[eng]
            rk = rank[eng]
            for it in self.items[eng]:
                if it[0] == 'we':
                    e.wait_ge(self.esem[it[1]], rank[it[1]][it[2]])
                elif it[0] == 'wc':
                    e.wait_ge(it[1], it[2])
                elif it[0] == 'op':
                    ins = it[1](e)
                    if it[2] in rk:
                        ins.then_inc(sem, 1)
                else:
                    it[1](e)

        @block.tensor
        def _(e):
            run("pe", e)

        @block.scalar
        def _(e):
            run("act", e)

        @block.vector
        def _(e):
            run("dve", e)

        @block.gpsimd
        def _(e):
            run("pool", e)

        @block.sync
        def _(e):
            run("sp", e)


class StopBuild(Exception):
    pass


def build_program(n_layers=2, n_tiles=NT, do_sample=True, dbg=None, stop_at=None):
    nc = bass.Bass("TRN2", target_bir_lowering=False)
    dt_in = lambda name, shape: nc.dram_tensor(name, shape, F32, kind="ExternalInput").ap()
    dt_out = lambda name, shape: nc.dram_tensor(name, shape, F32, kind="ExternalOutput").ap()
    xp = dt_in("xp", [T, D])
    xs = dt_in("xs", [NS, D])
    sret = dt_in("sret", [2, NS, H, HD, HD])
    sC = dt_in("sC", [2, NS, H, HD, HD])
    sn = dt_in("sn", [2, NS, H, HD])
    sm = dt_in("sm", [2, NS, H])
    sconv = dt_in("sconv", [2, NS, 3, D])
    w_in = dt_in("w_in", [2, D, N_IN])
    conv_w = dt_in("conv_w", [2, 4, D])
    conv_b = dt_in("conv_b", [2, D])
    b_i = dt_in("b_i", [2, H])
    b_f = dt_in("b_f", [2, H])
    g_ret = dt_in("g_ret", [2, 512])
    g_m = dt_in("g_m", [2, 512])
    w_out = dt_in("w_out", [2, D, D])
    ln_g = dt_in("ln_g", [2, D])
    ln_b = dt_in("ln_b", [2, D])
    cst_d = dt_in("cst", [128, NCST])
    rope_p = dt_in("rope_p", [T, 2, HD])
    rope_s = dt_in("rope_s", [2 * HD])

    y_p = dt_out("y_p", [T, D])
    y_s = dt_out("y_s", [NS, D])
    ret_p = dt_out("ret_p", [2, H, HD, HD])
    C_p = dt_out("C_p", [2, H, HD, HD])
    n_p = dt_out("n_p", [2, H, HD])
    m_p = dt_out("m_p", [2, H])
    conv_p = dt_out("conv_p", [2, 3, D])
    ret_s = dt_out("ret_s", [2, NS, H, HD, HD])
    C_s = dt_out("C_s", [2, NS, H, HD, HD])
    n_s = dt_out("n_s", [2, NS, H, HD])
    m_s = dt_out("m_s", [2, NS, H])
    conv_s = dt_out("conv_s", [2, NS, 3, D])
    y0 = nc.dram_tensor("y0_scratch", [T, D], F32, kind="Internal").ap()
    dbg_out = {}
    if dbg:
        for name, shape in dbg.items():
            dbg_out[name] = dt_out("dbg_" + name, shape)

    with ExitStack() as st:
        P = Prog(nc, st)
        sb = lambda name, shape, dt=F32: st.enter_context(nc.sbuf_tensor("s_" + name, shape, dt))
        out_chans = []

        def ochan(name):
            c = P.chan(name)
            out_chans.append(c)
            return c

        banks = []
        for i in range(8):
            t_ = st.enter_context(nc.psum_tensor("bank%d" % i, [128, 512], F32))
            banks.append((t_, Buf("bank%d" % i, psum=True)))
        bank_ctr = [0]

        def nb():
            i = bank_ctr[0] % 8
            bank_ctr[0] += 1
            return banks[i]

        def bfv(bank_t):
            return bank_t[:].bitcast(BF16)

        cst = sb("cst", [128, NCST]); CST = Buf("cst")
        ident_bf = sb("ident_bf", [128, 128], BF16); IDB = Buf()
        mask_bf = sb("mask_bf", [128, 128], BF16); MSK = Buf()
        c_ld = P.chan("c_ld")
        P.dma("sp", c_ld, cst[:], cst_d, writes=[CST])
        ident_f = cst[:, K_ID:K_ID + 128]
        tri_f = cst[:, K_TRI:K_TRI + 128]
        ones_f = cst[:, K_ONE:K_ONE + 128]
        P.op("dve", lambda e: e.tensor_copy(out=ident_bf[:], in_=ident_f), reads=[CST], writes=[IDB])
        P.op("dve", lambda e: e.tensor_copy(out=mask_bf[:], in_=tri_f), reads=[CST], writes=[MSK])
        kdec = lambda h: cst[:, K_KDEC + h:K_KDEC + h + 1]
        qdecT = cst[:, K_QDEC:K_QDEC + 512].rearrange("p (h l) -> p h l", h=4)

        w_in_sb = sb("w_in_sb", [128, 8, N_IN], BF16)
        WIN = [Buf("win%d" % i) for i in range(10)]
        w_out_sb = sb("w_out_sb", [128, 8, D], BF16)
        WOUT = [Buf("wout0"), Buf("wout1")]
        c_win = [P.chan("c_win%d" % i) for i in range(10)]
        c_wout = [P.chan("c_wout%d" % i) for i in range(2)]
        gbc = sb("gbc", [128, 1024]); GBC = Buf()
        lng = sb("lng", [128, 1024]); LNG = Buf()
        lnb = sb("lnb", [128, 1024]); LNB = Buf()
        bias8 = sb("bias8", [128, 8]); BIAS8 = Buf()
        cwb_in = sb("cwb_in", [40, 128]); CWBIN = Buf()
        cwT = sb("cwT", [128, 40]); CWT = Buf()
        diag = sb("diag", [128, 32, 128], BF16); DIAG = Buf()
        c_par = [P.chan("c_par%d" % i) for i in range(8)]

        def wblk(i):
            return (i * 512, min((i + 1) * 512, N_IN))

        def load_weights_in(l):
            wv = w_in[l].rearrange("(k p) n -> p k n", p=128)
            for i in [9] + list(range(9)):
                a, b_ = wblk(i)
                P.dma("pool", c_win[i], w_in_sb[:, :, a:b_], wv[:, :, a:b_], writes=[WIN[i]])

        def load_weights(l):
            if l == 0 or not do_sample:
                load_weights_in(l)
            wo = w_out[l].rearrange("(k p) n -> p k n", p=128)
            for i in range(2):
                P.dma("pool", c_wout[i], w_out_sb[:, :, i * 512:(i + 1) * 512], wo[:, :, i * 512:(i + 1) * 512],
                      writes=[WOUT[i]])

        def load_params(l):
            P.dma("sp", c_par[0], gbc[:, 0:512], g_ret[l].partition_broadcast(128), writes=[GBC])
            P.dma("sp", c_par[1], gbc[:, 512:1024], g_m[l].partition_broadcast(128), writes=[GBC])
            P.dma("sp", c_par[2], lng[:], ln_g[l].partition_broadcast(128), writes=[LNG])
            P.dma("sp", c_par[3], lnb[:], ln_b[l].partition_broadcast(128), writes=[LNB])
            P.dma("sp", c_par[4], bias8[:, 0:4], b_i[l].partition_broadcast(128), writes=[BIAS8])
            P.dma("sp", c_par[5], bias8[:, 4:8], b_f[l].partition_broadcast(128), writes=[BIAS8])
            P.dma("sp", c_par[6], cwb_in[0:32, :], conv_w[l].rearrange("j (ch c) -> (j ch) c", c=128), writes=[CWBIN])
            P.dma("sp", c_par[7], cwb_in[32:40, :], conv_b[l].rearrange("(ch c) -> ch c", c=128), writes=[CWBIN])
            P.op("pool", lambda e: e.tensor_scalar(out=gbc[:, 512:1024], in0=gbc[:, 512:1024], scalar1=0.5, scalar2=None,
                                                   op0=ALU.mult), reads=[GBC], writes=[GBC])
            bt, BK = nb()
            P.op("pe", lambda e: e.transpose(bt[:, 0:40], cwb_in[0:40, :], ident_f[0:40, 0:40]), reads=[CWBIN, CST], writes=[BK])
            P.op("act", lambda e: e.copy(out=cwT[:], in_=bt[:, 0:40]), reads=[BK], writes=[CWT])
            for ch in range(8):
                for j in range(4):
                    idx = j * 8 + ch
                    P.op("pool", lambda e, ch=ch, j=j, idx=idx: e.tensor_scalar(
                        out=diag[:, ch * 4 + j, :], in0=ident_f, scalar1=cwT[:, idx:idx + 1], scalar2=None, op0=ALU.mult),
                        reads=[CST, CWT], writes=[DIAG])

        gates = sb("gates", [128, NT, 8]); GATES = Buf()
        lneg = sb("lneg", [128, NT, 4]); LNEG = Buf()
        bneg = sb("bneg", [128, NT, 4]); BNEG = Buf()
        u_sb = sb("u_sb", [128, NT, 4]); USB = Buf()
        uT_sb = sb("uT_sb", [64, 128]); UTS = Buf()
        umaxc = sb("umaxc", [64, 1]); UMX = Buf()
        row = sb("row", [1, 6, 64]); ROW = Buf()
        cw_b = sb("cw_b", [128, 2, NT + 1, 4]); CWB = Buf()
        pk = sb("pk", [128, NT, 4]); PK = Buf()
        thr = sb("thr", [128, NT, 4]); THR = Buf()
        tmp64 = sb("tmp64", [128, NT, 4]); TMP64 = Buf()

        x_sb = [sb("x_sb%d" % i, [128, D]) for i in range(3)]; XSB = [Buf(), Buf(), Buf()]
        rope_sb = [sb("rope_sb%d" % i, [128, 2, HD]) for i in range(2)]; ROPE = [Buf(), Buf()]
        c_x = [P.chan("c_x0"), P.chan("c_x1"), P.chan("c_x2")]
        c_rope = [P.chan("c_rope0"), P.chan("c_rope1")]
        x_bf2 = [sb("x_bf%d" % i, [128, D], BF16) for i in range(2)]; XBF2 = [Buf(), Buf()]
        x_bf = x_bf2[0]; XBF = XBF2[0]
        c_xb = [P.chan("c_xb0"), P.chan("c_xb1")]
        xT2 = [sb("xT%d" % i, [128, 8, 128], BF16) for i in range(2)]; XT2 = [Buf(), Buf()]
        xT = xT2[0]; XT = XT2[0]
        tmp_t = sb("tmp_t", [128, 8, HD]); TMPT = Buf(); TMPT_K = Buf()
        tmp_u = sb("tmp_u", [128, 8, HD]); TMPU = Buf(); TMPU_K = Buf()
        qk_rot = sb("qk_rot", [128, 8, HD], BF16); QKR = Buf(); QKR_K = Buf()
        v2 = sb("v2", [128, 4, HD], BF16); V2 = Buf()
        sz2 = [sb("sz%d" % i, [128, 1024]) for i in range(2)]; SZ2 = [Buf(), Buf()]
        sz = sz2[0]; SZ = SZ2[0]
        SZA2 = [Buf(), Buf()]; SZA = SZA2[0]
        qT2 = sb("qT2", [128, 4, 128], BF16); QT2 = Buf()
        kT = sb("kT", [128, 4, 128], BF16); KT = Buf()
        s2 = sb("s2", [128, 4, 128], BF16); S2 = Buf()
        S_f = sb("S_f", [128, 4, HD]); SF = Buf()
        S_bf = sb("S_bf", [128, 4, HD], BF16); SBF = Buf()
        hist = sb("hist", [128, 8, 131], BF16); HIST = Buf()
        qkm = sb("qkm", [128, 8, 128], BF16); QKM = Buf()
        kp = sb("kp", [128, 4, 128], BF16); KP = Buf()
        v1 = sb("v1", [128, 4, 130], BF16); V1 = Buf()
        th = sb("th", [128, 512]); TH = Buf()
        s2m = sb("s2m", [128, 4, 128], BF16); S2M = Buf()
        C_f = sb("C_f", [128, 4, 130]); CF = Buf()
        C_bf = sb("C_bf", [128, 4, 130], BF16); CBF = Buf()
        dn = sb("dn", [128, 4]); DN = Buf()
        hm = sb("hm", [128, 4, HD]); HM = Buf()
        o_sb = sb("o_sb", [128, 4, HD]); OSB = Buf()
        stats = sb("stats", [128, 8, 6]); STATS = Buf()
        mv = sb("mv", [128, 8, 2]); MV = Buf()
        rstd = sb("rstd", [128, 8]); RSTD = Buf()
        nbias = sb("nbias", [128, 8]); NBIAS = Buf()
        mhalf = sb("mhalf", [128, 8]); MHALF = Buf()
        on = sb("on", [128, 8, HD]); ON = Buf(); ON_M = Buf()
        gated = sb("gated", [128, 1024], BF16); GATED = Buf(); GATED_M = Buf()
        gT = sb("gT", [128, 8, 128], BF16); GT = Buf()
        ypre = sb("ypre", [128, D]); YPRE = Buf(); YPRE_B = Buf()
        lstats = sb("lstats", [128, 2, 6]); LSTATS = Buf()
        lmv = sb("lmv", [128, 2]); LMV = Buf()
        lrs = sb("lrs", [128, 2]); LRS = Buf()
        c_y = [ochan("c_y0"), ochan("c_y1")]
        convp_sb = sb("convp_sb", [128, 8, 3]); CONVP = Buf()
        convp_T = sb("convp_T", [3, D]); CONVPT = Buf()
        c_fin = [ochan("c_fin%d" % i) for i in range(5)]

        P.op("pool", lambda e: e.memset(mhalf[:], -0.5), writes=[MHALF])

        def dbg_dump(name, src_ap, bufs):
            if name in dbg_out:
                c = ochan("c_dbg_" + name)
                P.dma("sp", c, dbg_out[name], src_ap, reads=bufs)

        def load_x(l, t, par):
            src = xp if l == 0 else y0
            P.dma("sp", c_x[par], x_sb[par][:], src[t * 128:(t + 1) * 128, :], writes=[XSB[par]],
                  reads=([Y0B[t]] if l > 0 else []))

        def load_xbf(l, t, par):
            src = xp if l == 0 else y0
            P.dma("pool", c_xb[par], x_bf2[par][:], src[t * 128:(t + 1) * 128, :], writes=[XBF2[par]],
                  reads=([Y0B[t]] if l > 0 else []))

        def load_main(l, t):
            par = t % 2
            load_x(l, t, t % 3)
            P.dma("sp", c_rope[par], rope_sb[par][:], rope_p[t * 128:(t + 1) * 128], writes=[ROPE[par]])

        def make_xT(par, xT=None, XT=None, xi=None):
            if xT is None:
                xT, XT = xT2[0], XT2[0]
            if xi is None:
                xi = par
            xb = x_bf2[par]; XB = XBF2[par]
            P.op("dve", lambda e: e.tensor_copy(out=xb[:], in_=x_sb[xi][:]), reads=[XSB[xi]], writes=[XB])
            bt, BK = nb()
            v = bfv(bt)
            for k in range(8):
                P.op("pe", lambda e, k=k: e.transpose(v[:, k * 128:(k + 1) * 128], xb[:, k * 128:(k + 1) * 128], ident_bf[:]),
                     reads=[XB, IDB], writes=[BK])
            P.op("act", lambda e: e.copy(out=xT[:].rearrange("p k c -> p (k c)"), in_=v), reads=[BK], writes=[XT])

        Y0B = [Buf("y0_%d" % t) for t in range(NT)]

        def prepass(l):
            load_x(l, 0, 0)
            for t in range(n_tiles):
                par = t % 2
                if t + 1 < n_tiles:
                    load_x(l, t + 1, (t + 1) % 2)
                xTp = xT2[par]; XTp = XT2[par]
                make_xT(par, xTp, XTp)
                bt, BK = nb()
                for k in range(8):
                    P.op("pe", lambda e, k=k, xTp=xTp: e.matmul(bt[:, 0:8], lhsT=xTp[:, k, :], rhs=w_in_sb[:, k, C_G:C_G + 8],
                                                                start=(k == 0), stop=(k == 7)), reads=[XTp, WIN[9]], writes=[BK])
                P.op("dve", lambda e, t=t: e.tensor_tensor(out=gates[:, t, :], in0=bt[:, 0:8], in1=bias8[:], op=ALU.add),
                     reads=[BK, BIAS8], writes=[GATES])
            nt = n_tiles
            stage('pp_loop')
            P.op("act", lambda e: e.activation(out=lneg[:, 0:nt, :], in_=gates[:, 0:nt, 4:8], func=AF.Exp, scale=-1.0),
                 reads=[GATES], writes=[LNEG])
            P.op("act", lambda e: e.activation(out=lneg[:, 0:nt, :], in_=lneg[:, 0:nt, :], func=AF.Ln, bias=1.0),
                 reads=[LNEG], writes=[LNEG])
            dbg_dump('gates', gates[:], [GATES])
            dbg_dump('lneg', lneg[:], [LNEG])
            stage('pp_a')
            bt, BK = nb()
            ln2 = lneg[:].rearrange("p t h -> p (t h)")
            P.op("pe", lambda e: e.matmul(bt[:, 0:nt * 4], lhsT=tri_f, rhs=ln2[:, 0:nt * 4], start=True, stop=True),
                 reads=[LNEG, CST], writes=[BK])
            stage('pp_b')
            bt2, BK2 = nb()
            P.op("pe", lambda e: e.matmul(bt2[0:1, 0:nt * 4], lhsT=ones_f[:, 0:1], rhs=ln2[:, 0:nt * 4], start=True, stop=True),
                 reads=[LNEG, CST], writes=[BK2])
            stage('pp_c')
            P.op("act", lambda e: e.copy(out=bneg[:].rearrange("p t h -> p (t h)")[:, 0:nt * 4], in_=bt[:, 0:nt * 4]),
                 reads=[BK], writes=[BNEG])
            P.op("dve", lambda e: e.tensor_tensor(out=u_sb[:, 0:nt, :], in0=gates[:, 0:nt, 0:4], in1=bneg[:, 0:nt, :], op=ALU.add),
                 reads=[GATES, BNEG], writes=[USB])
            P.op("act", lambda e: e.copy(out=row[0:1, 1, 0:nt * 4], in_=bt2[0:1, 0:nt * 4]), reads=[BK2], writes=[ROW])
            stage('pp_cum')
            bt3, BK3 = nb()
            u2 = u_sb[:].rearrange("p t h -> p (t h)")
            P.op("pe", lambda e: e.transpose(bt3[0:nt * 4, 0:128], u2[:, 0:nt * 4], ident_f), reads=[USB, CST], writes=[BK3])
            P.op("dve", lambda e: e.tensor_reduce(out=umaxc[0:nt * 4, :], in_=bt3[0:nt * 4, 0:128], axis=AX.X, op=ALU.max),
                 reads=[BK3], writes=[UMX])
            bt4, BK4 = nb()
            P.op("pe", lambda e: e.transpose(bt4[0:1, 0:nt * 4], umaxc[0:nt * 4, 0:1], ident_f[0:nt * 4, 0:nt * 4]),
                 reads=[UMX, CST], writes=[BK4])
            P.op("act", lambda e: e.copy(out=row[0:1, 0, 0:nt * 4], in_=bt4[0:1, 0:nt * 4]), reads=[BK4], writes=[ROW])
            stage('pp_umax')
            rv_ = lambda i: row[0:1, i, 0:nt * 4].rearrange("p (t h) -> p t h", h=4)
            P.op("dve", lambda e: e.tensor_scalar(out=row[0:1, 3, 0:nt * 4], in0=row[0:1, 1, 0:nt * 4], scalar1=-1.0, scalar2=None,
                                                  op0=ALU.mult), reads=[ROW], writes=[ROW])
            for h in range(4):
                P.op("dve", lambda e, h=h: e.tensor_tensor_scan(out=rv_(2)[:, :, h], data0=rv_(0)[:, :, h], data1=rv_(3)[:, :, h],
                                                                initial=0.0, op0=ALU.max, op1=ALU.add), reads=[ROW], writes=[ROW])
            P.op("dve", lambda e: e.tensor_tensor(out=row[0:1, 3, 0:nt * 4], in0=row[0:1, 2, 0:nt * 4], in1=row[0:1, 1, 0:nt * 4],
                                                  op=ALU.add), reads=[ROW], writes=[ROW])
            P.op("dve", lambda e: e.memset(row[0:1, 5, 0:4], 0.0), reads=[ROW], writes=[ROW])
            if nt > 1:
                P.op("dve", lambda e: e.tensor_copy(out=row[0:1, 5, 4:nt * 4], in_=row[0:1, 2, 0:(nt - 1) * 4]), reads=[ROW], writes=[ROW])
            P.op("dve", lambda e: e.tensor_tensor(out=row[0:1, 4, 0:nt * 4], in0=row[0:1, 5, 0:nt * 4], in1=row[0:1, 3, 0:nt * 4],
                                                  op=ALU.subtract), reads=[ROW], writes=[ROW])
            P.op("act", lambda e: e.activation(out=row[0:1, 4, 0:nt * 4], in_=row[0:1, 4, 0:nt * 4], func=AF.Exp),
                 reads=[ROW], writes=[ROW])
            stage('pp_scan')
            bt5, BK5 = nb()
            P.op("pe", lambda e: e.matmul(bt5[:, 0:128], lhsT=ones_f[0:1, :], rhs=row[0:1, 3:5, :].rearrange("p a b -> p (a b)"),
                                          start=True, stop=True), reads=[ROW, CST], writes=[BK5])
            P.op("act", lambda e: e.copy(out=cw_b[:, :, 0:NT, :], in_=bt5[:, 0:128].rearrange("p (a t h) -> p a t h", a=2, h=4)),
                 reads=[BK5], writes=[CWB])
            P.op("pool", lambda e: e.memset(cw_b[:, :, NT, :], 1.0), reads=[CWB], writes=[CWB])
            P.op("dve", lambda e: e.tensor_tensor(out=tmp64[:, 0:nt, :], in0=u_sb[:, 0:nt, :], in1=cw_b[:, 0, 0:nt, :], op=ALU.subtract),
                 reads=[USB, CWB], writes=[TMP64])
            P.op("dve", lambda e: e.tensor_scalar(out=tmp64[:, 0:nt, :], in0=tmp64[:, 0:nt, :], scalar1=LNISQ, scalar2=None, op0=ALU.add),
                 reads=[TMP64], writes=[TMP64])
            P.op("act", lambda e: e.activation(out=pk[:, 0:nt, :], in_=tmp64[:, 0:nt, :], func=AF.Exp),
                 reads=[TMP64], writes=[PK])
            P.op("dve", lambda e: e.tensor_tensor(out=tmp64[:, 0:nt, :], in0=bneg[:, 0:nt, :], in1=cw_b[:, 0, 0:nt, :], op=ALU.subtract),
                 reads=[BNEG, CWB, PK], writes=[TMP64])
            P.op("act", lambda e: e.activation(out=thr[:, 0:nt, :], in_=tmp64[:, 0:nt, :], func=AF.Exp), reads=[TMP64], writes=[THR])
            P.dma("sp", c_fin[0], m_p[l:l + 1, :], row[0:1, 2, (nt - 1) * 4:nt * 4], reads=[ROW])

        def main_tile(l, t):
            par = t % 2
            last = (t == n_tiles - 1)
            xT = xT2[par]; XT = XT2[par]
            xi = t % 3
            sz = sz2[par]; SZ = SZ2[par]; SZA = SZA2[par]
            if t == 0:
                load_main(l, 0)
            if not last:
                load_main(l, t + 1)
            make_xT(par, xT, XT, xi)
            yield

            def proj(bt, BK, c0, n, wb):
                for k in range(8):
                    P.op("pe", lambda e, k=k: e.matmul(bt[:, 0:n], lhsT=xT[:, k, :], rhs=w_in_sb[:, k, c0:c0 + n],
                                                       start=(k == 0), stop=(k == 7)), reads=[XT, WIN[wb]], writes=[BK])

            cos2 = rope_sb[par][:, 0, :]
            sin2 = rope_sb[par][:, 1, :]
            for i in range(2):
                bt, BK = nb(); proj(bt, BK, C_RQ if i == 0 else C_RK, 512, i)
                src = bt[:].rearrange("p (h d) -> p h d", h=4)
                dst_t = tmp_t[:, i * 4:(i + 1) * 4, :]
                dst_u = tmp_u[:, i * 4:(i + 1) * 4, :]
                TT_ = TMPT if i == 0 else TMPT_K
                TU_ = TMPU if i == 0 else TMPU_K
                P.op("dve", lambda e, src=src, dst_t=dst_t: e.tensor_tensor(
                    out=dst_t, in0=src, in1=cos2.unsqueeze(1).broadcast_to([128, 4, HD]), op=ALU.mult),
                    reads=[BK, ROPE[par]], writes=[TT_])
                P.op("dve", lambda e, src=src, dst_u=dst_u: e.tensor_tensor(
                    out=dst_u[:, :, 0:64], in0=src[:, :, 64:128], in1=sin2[:, 0:64].unsqueeze(1).broadcast_to([128, 4, 64]), op=ALU.mult),
                    reads=[BK, ROPE[par]], writes=[TU_])
                P.op("dve", lambda e, src=src, dst_u=dst_u: e.tensor_tensor(
                    out=dst_u[:, :, 64:128], in0=src[:, :, 0:64], in1=sin2[:, 64:128].unsqueeze(1).broadcast_to([128, 4, 64]), op=ALU.mult),
                    reads=[BK, ROPE[par]], writes=[TU_])
                if i == 0:
                    P.op("pool", lambda e: e.tensor_tensor(out=qk_rot[:, 0:4, :], in0=tmp_t[:, 0:4, :], in1=tmp_u[:, 0:4, :], op=ALU.add),
                         reads=[TMPT, TMPU], writes=[QKR])
                    yield
            P.op("dve", lambda e: e.tensor_tensor(out=qk_rot[:, 4:8, :], in0=tmp_t[:, 4:8, :], in1=tmp_u[:, 4:8, :], op=ALU.add),
                 reads=[TMPT_K, TMPU_K], writes=[QKR_K])
            yield
            bv, BV = nb(); proj(bv, BV, C_RV, 512, 2)
            for h in range(4):
                P.op("act", lambda e, h=h: e.activation(out=v2[:, h, :], in_=bv[:, h * 128:(h + 1) * 128], func=AF.Copy, scale=kdec(h)),
                     reads=[BV, CST], writes=[V2])
            yield
            bz, BZ = nb(); proj(bz, BZ, C_RZ, 512, 3)
            P.op("act", lambda e: e.activation(out=sz[:, 0:512], in_=bz[:], func=AF.Silu), reads=[BZ], writes=[SZA])
            P.op("pool", lambda e: e.tensor_tensor(out=sz[:, 0:512], in0=sz[:, 0:512], in1=gbc[:, 0:512], op=ALU.mult), reads=[SZA, GBC], writes=[SZA])
            yield
            btq, BTQ = nb(); vtq = bfv(btq)
            for g in range(4):
                P.op("pe", lambda e, g=g: e.transpose(vtq[:, g * 128:(g + 1) * 128], qk_rot[:, g, :], ident_bf[:]),
                     reads=[QKR, IDB], writes=[BTQ])
            P.op("dve", lambda e: e.tensor_tensor(out=qT2[:], in0=vtq[:, 0:512].rearrange("p (h l) -> p h l", h=4), in1=qdecT, op=ALU.mult),
                 reads=[BTQ, CST], writes=[QT2])
            yield
            btk, BTK = nb(); vtk = bfv(btk)
            for g in range(4):
                P.op("pe", lambda e, g=g: e.transpose(vtk[:, g * 128:(g + 1) * 128], qk_rot[:, 4 + g, :], ident_bf[:]),
                     reads=[QKR_K, IDB], writes=[BTK])
            P.op("act", lambda e: e.copy(out=kT[:].rearrange("p h l -> p (h l)"), in_=vtk[:, 0:512]), reads=[BTK], writes=[KT])
            yield
            bs, BS = nb()
            for h in range(4):
                P.op("pe", lambda e, h=h: e.matmul(bs[:, h * 128:(h + 1) * 128], lhsT=kT[:, h, :], rhs=qT2[:, h, :], start=True, stop=True),
                     reads=[KT, QT2], writes=[BS])
            P.op("dve", lambda e: e.tensor_tensor(out=s2[:], in0=bs[:].rearrange("p (h l) -> p h l", h=4),
                                                  in1=mask_bf[:].unsqueeze(1).broadcast_to([128, 4, 128]), op=ALU.mult),
                 reads=[BS, MSK], writes=[S2])
            yield
            bo, BO = nb()
            for h in range(4):
                first = (t == 0)
                P.op("pe", lambda e, h=h, first=first: e.matmul(bo[:, h * 128:(h + 1) * 128], lhsT=s2[:, h, :], rhs=v2[:, h, :],
                                                                start=True, stop=first), reads=[S2, V2], writes=[BO])
                if not first:
                    P.op("pe", lambda e, h=h: e.matmul(bo[:, h * 128:(h + 1) * 128], lhsT=qT2[:, h, :], rhs=S_bf[:, h, :],
                                                       start=False, stop=True), reads=[QT2, SBF], writes=[BO])
            P.op("act", lambda e: e.copy(out=o_sb[:].rearrange("p h d -> p (h d)"), in_=bo[:]), reads=[BO], writes=[OSB])
            for h in range(4):
                P.op("dve", lambda e, h=h: e.bn_stats(out=stats[:, h, :], in_=o_sb[:, h, :]), reads=[OSB], writes=[STATS])
            yield
            bu, BU = nb()
            for h in range(4):
                P.op("pe", lambda e, h=h: e.matmul(bu[:, h * 128:(h + 1) * 128], lhsT=qk_rot[:, 4 + h, :], rhs=v2[:, h, :],
                                                   start=True, stop=True), reads=[QKR_K, V2], writes=[BU])
            for h in range(4):
                if t == 0:
                    P.op("dve", lambda e, h=h: e.tensor_copy(out=S_f[:, h, :], in_=bu[:, h * 128:(h + 1) * 128]), reads=[BU], writes=[SF])
                else:
                    P.op("dve", lambda e, h=h: e.scalar_tensor_tensor(out=S_f[:, h, :], in0=S_f[:, h, :], scalar=GAML[h],
                                                                     in1=bu[:, h * 128:(h + 1) * 128], op0=ALU.mult, op1=ALU.add),
                         reads=[BU, SF], writes=[SF])
            if not last:
                for h in range(4):
                    P.op("act", lambda e, h=h: e.activation(out=S_bf[:, h, :], in_=S_f[:, h, :], func=AF.Copy, scale=GAML[h]),
                         reads=[SF], writes=[SBF])
            yield
            if t == 0:
                P.op("pool", lambda e: e.memset(hist[:, :, 0:3], 0.0), writes=[HIST])
            else:
                P.op("pool", lambda e: e.tensor_copy(out=hist[:, :, 0:3], in_=hist[:, :, 128:131]), reads=[HIST], writes=[HIST])
            for i in range(2):
                bt, BK = nb()
                for c4 in range(4):
                    ch = i * 4 + c4
                    c0 = C_MQK + ch * 128
                    wb = 4 + ch // 4
                    for k in range(8):
                        P.op("pe", lambda e, k=k, bt=bt, c4=c4, c0=c0: e.matmul(bt[:, c4 * 128:(c4 + 1) * 128],
                                                                             lhsT=w_in_sb[:, k, c0:c0 + 128], rhs=xT[:, k, :],
                                                                             start=(k == 0), stop=(k == 7)),
                             reads=[XT, WIN[wb]], writes=[BK])
                P.op("act", lambda e, i=i, bt=bt: e.copy(out=hist[:, i * 4:(i + 1) * 4, 3:131], in_=bt[:].rearrange("p (c t) -> p c t", c=4)),
                     reads=[BK], writes=[HIST])
                if last:
                    P.op("dve", lambda e, i=i, bt=bt: e.tensor_copy(out=convp_sb[:, i * 4:(i + 1) * 4, :],
                                                                    in_=bt[:].rearrange("p (c t) -> p c t", c=4)[:, :, 125:128]),
                         reads=[BK], writes=[CONVP])
                yield
            for i in range(2):
                bt, BK = nb()
                for c4 in range(4):
                    ch = i * 4 + c4
                    for j in range(4):
                        P.op("pe", lambda e, j=j, bt=bt, ch=ch, c4=c4: e.matmul(bt[:, c4 * 128:(c4 + 1) * 128], lhsT=diag[:, ch * 4 + j, :],
                                                                            rhs=hist[:, ch, j:j + 128], start=(j == 0), stop=(j == 3)),
                             reads=[DIAG, HIST], writes=[BK])
                for c4 in range(4):
                    ch = i * 4 + c4
                    P.op("act", lambda e, bt=bt, ch=ch, c4=c4: e.activation(out=qkm[:, ch, :], in_=bt[:, c4 * 128:(c4 + 1) * 128],
                                                                        func=AF.Silu, bias=cwT[:, 32 + ch:33 + ch]),
                         reads=[BK, CWT], writes=[QKM])
                yield
            bkt, BKT = nb(); vkt = bfv(bkt)
            for h in range(4):
                P.op("pe", lambda e, h=h: e.transpose(vkt[:, h * 128:(h + 1) * 128], qkm[:, 4 + h, :], ident_bf[:]),
                     reads=[QKM, IDB], writes=[BKT])
            for h in range(4):
                P.op("act", lambda e, h=h: e.activation(out=kp[:, h, :], in_=vkt[:, h * 128:(h + 1) * 128], func=AF.Copy,
                                                        scale=pk[:, t, h:h + 1]), reads=[BKT, PK], writes=[KP])
            yield
            bmv, BMV = nb(); proj(bmv, BMV, C_MV, 512, 6)
            if l == 0 and t == 0:
                P.op("pool", lambda e: e.memset(v1[:, :, 128:130], 1.0), writes=[V1])
            P.op("act", lambda e: e.copy(out=v1[:, :, 0:128], in_=bmv[:].rearrange("p (h d) -> p h d", h=4)), reads=[BMV], writes=[V1])
            yield
            bmo, BMO = nb(); proj(bmo, BMO, C_MO, 512, 7)
            P.op("act", lambda e: e.activation(out=th[:], in_=bmo[:], func=AF.Tanh, scale=0.5), reads=[BMO], writes=[TH])
            yield
            bmz, BMZ = nb(); proj(bmz, BMZ, C_MZ, 512, 8)
            P.op("act", lambda e: e.activation(out=sz[:, 512:1024], in_=bmz[:], func=AF.Silu), reads=[BMZ], writes=[SZ])
            P.op("dve", lambda e: e.scalar_tensor_tensor(out=sz[:, 512:1024], in0=th[:], scalar=1.0, in1=sz[:, 512:1024], op0=ALU.add, op1=ALU.mult),
                 reads=[TH, SZ], writes=[SZ])
            P.op("pool", lambda e: e.tensor_tensor(out=sz[:, 512:1024], in0=sz[:, 512:1024], in1=gbc[:, 512:1024], op=ALU.mult), reads=[SZ, GBC], writes=[SZ])
            yield
            bsm, BSM = nb()
            for h in range(4):
                P.op("pe", lambda e, h=h: e.matmul(bsm[:, h * 128:(h + 1) * 128], lhsT=qkm[:, 4 + h, :], rhs=qkm[:, h, :], start=True, stop=True),
                     reads=[QKM], writes=[BSM])
            for h in range(4):
                P.op("dve", lambda e, h=h: e.scalar_tensor_tensor(out=s2m[:, h, :], in0=bsm[:, h * 128:(h + 1) * 128], scalar=pk[:, t, h:h + 1],
                                                                 in1=mask_bf[:], op0=ALU.mult, op1=ALU.mult),
                     reads=[BSM, PK, MSK], writes=[S2M])
            yield
            bn0, BN0 = nb(); bn1, BN1 = nb()
            for h in range(4):
                bt, BK = (bn0, BN0) if h < 2 else (bn1, BN1)
                o_ = (h % 2) * 130
                first = (t == 0)
                P.op("pe", lambda e, h=h, bt=bt, o_=o_, first=first: e.matmul(bt[:, o_:o_ + 130], lhsT=s2m[:, h, :], rhs=v1[:, h, :],
                                                                            start=True, stop=first), reads=[S2M, V1], writes=[BK])
                if not first:
                    P.op("pe", lambda e, h=h, bt=bt, o_=o_: e.matmul(bt[:, o_:o_ + 130], lhsT=qkm[:, h, :], rhs=C_bf[:, h, :],
                                                                   start=False, stop=True), reads=[QKM, CBF], writes=[BK])
            for i, (bt, BK) in enumerate(((bn0, BN0), (bn1, BN1))):
                P.op("act", lambda e, i=i, bt=bt: e.activation(out=dn[:, 2 * i:2 * i + 2], in_=bt[:, 128:259:130], func=AF.Abs),
                     reads=[BK], writes=[DN])
            P.op("dve", lambda e: e.tensor_tensor(out=dn[:], in0=dn[:], in1=thr[:, t, :], op=ALU.max), reads=[DN, THR], writes=[DN])
            P.op("dve", lambda e: e.reciprocal(out=dn[:], in_=dn[:]), reads=[DN], writes=[DN])
            for h in range(4):
                bt, BK = (bn0, BN0) if h < 2 else (bn1, BN1)
                o_ = (h % 2) * 130
                P.op("act", lambda e, h=h, bt=bt, o_=o_: e.activation(out=hm[:, h, :], in_=bt[:, o_:o_ + 128], func=AF.Copy, scale=dn[:, h:h + 1]),
                     reads=[BK, DN], writes=[HM])
            for h in range(4):
                P.op("dve", lambda e, h=h: e.bn_stats(out=stats[:, 4 + h, :], in_=hm[:, h, :]), reads=[HM], writes=[STATS])
            yield
            bu0, BU0 = nb(); bu1, BU1 = nb()
            for h in range(4):
                bt, BK = (bu0, BU0) if h < 2 else (bu1, BU1)
                o_ = (h % 2) * 130
                P.op("pe", lambda e, h=h, bt=bt, o_=o_: e.matmul(bt[:, o_:o_ + 130], lhsT=kp[:, h, :], rhs=v1[:, h, :], start=True, stop=True),
                     reads=[KP, V1], writes=[BK])
            for h in range(4):
                bt, BK = (bu0, BU0) if h < 2 else (bu1, BU1)
                o_ = (h % 2) * 130
                if t == 0:
                    P.op("dve", lambda e, h=h, bt=bt, o_=o_: e.tensor_copy(out=C_f[:, h, :], in_=bt[:, o_:o_ + 130]), reads=[BK], writes=[CF])
                else:
                    P.op("dve", lambda e, h=h, bt=bt, o_=o_: e.scalar_tensor_tensor(out=C_f[:, h, :], in0=C_f[:, h, :], scalar=cw_b[:, 1, t, h:h + 1],
                                                                                  in1=bt[:, o_:o_ + 130], op0=ALU.mult, op1=ALU.add),
                         reads=[BK, CF, CWB], writes=[CF])
            if not last:
                for h in range(4):
                    P.op("act", lambda e, h=h: e.activation(out=C_bf[:, h, :], in_=C_f[:, h, :], func=AF.Copy, scale=cw_b[:, 1, t + 1, h:h + 1]),
                         reads=[CF, CWB], writes=[CBF])
            yield
            for g in range(8):
                P.op("dve", lambda e, g=g: e.bn_aggr(out=mv[:, g, :], in_=stats[:, g, :]), reads=[STATS], writes=[MV])
            P.op("pool", lambda e: e.tensor_scalar(out=rstd[:], in0=mv[:, :, 1], scalar1=GN_EPS, scalar2=None, op0=ALU.add),
                 reads=[MV], writes=[RSTD])
            P.op("pool", lambda e: e.tensor_tensor(out=rstd[:], in0=rstd[:], in1=mhalf[:], op=ALU.pow), reads=[RSTD, MHALF], writes=[RSTD])
            P.op("dve", lambda e: e.scalar_tensor_tensor(out=nbias[:], in0=mv[:, :, 0], scalar=-1.0, in1=rstd[:], op0=ALU.mult, op1=ALU.mult),
                 reads=[MV, RSTD], writes=[NBIAS])
            for g in range(8):
                if g < 4:
                    src = o_sb[:, g, :]; SB_ = OSB
                else:
                    src = hm[:, g - 4, :]; SB_ = HM
                P.op("act", lambda e, g=g, src=src: e.activation(out=on[:, g, :], in_=src, func=AF.Identity, scale=rstd[:, g:g + 1],
                                                               bias=nbias[:, g:g + 1]), reads=[SB_, RSTD, NBIAS], writes=[ON if g < 4 else ON_M])
            onf = on[:].rearrange("p g d -> p (g d)")
            P.op("pool", lambda e: e.tensor_tensor(out=gated[:, 0:512], in0=onf[:, 0:512], in1=sz[:, 0:512], op=ALU.mult),
                 reads=[ON, SZA], writes=[GATED])
            P.op("dve", lambda e: e.tensor_tensor(out=gated[:, 512:1024], in0=onf[:, 512:1024], in1=sz[:, 512:1024], op=ALU.mult),
                 reads=[ON_M, SZ], writes=[GATED_M])
            yield
            bg, BG = nb(); vg = bfv(bg)
            for k in range(8):
                P.op("pe", lambda e, k=k: e.transpose(vg[:, k * 128:(k + 1) * 128], gated[:, k * 128:(k + 1) * 128], ident_bf[:]),
                     reads=[GATED if k < 4 else GATED_M, IDB], writes=[BG])
            P.op("act", lambda e: e.copy(out=gT[:].rearrange("p k c -> p (k c)"), in_=vg), reads=[BG], writes=[GT])
            for n in range(2):
                yield
                bt, BK = nb()
                for k in range(8):
                    P.op("pe", lambda e, k=k, n=n, bt=bt: e.matmul(bt[:], lhsT=gT[:, k, :], rhs=w_out_sb[:, k, n * 512:(n + 1) * 512],
                                                               start=(k == 0), stop=(k == 7)), reads=[GT, WOUT[n]], writes=[BK])
                P.op("dve", lambda e, n=n, bt=bt: e.scalar_tensor_tensor(out=ypre[:, n * 512:(n + 1) * 512], in0=x_sb[xi][:, n * 512:(n + 1) * 512],
                                                                     scalar=ALPHA, in1=bt[:], op0=ALU.mult, op1=ALU.add),
                     reads=[BK, XSB[xi]], writes=[YPRE if n == 0 else YPRE_B])
                P.op("dve", lambda e, n=n: e.bn_stats(out=lstats[:, n, :], in_=ypre[:, n * 512:(n + 1) * 512]), reads=[YPRE if n == 0 else YPRE_B], writes=[LSTATS])
            P.op("dve", lambda e: e.bn_aggr(out=lmv[:], in_=lstats[:].rearrange("p a b -> p (a b)")), reads=[LSTATS], writes=[LMV])
            P.op("pool", lambda e: e.tensor_scalar(out=lrs[:, 0:1], in0=lmv[:, 1:2], scalar1=LN_EPS, scalar2=None, op0=ALU.add),
                 reads=[LMV], writes=[LRS])
            P.op("pool", lambda e: e.tensor_tensor(out=lrs[:, 0:1], in0=lrs[:, 0:1], in1=mhalf[:, 0:1], op=ALU.pow), reads=[LRS, MHALF], writes=[LRS])
            P.op("dve", lambda e: e.scalar_tensor_tensor(out=lrs[:, 1:2], in0=lmv[:, 0:1], scalar=-1.0, in1=lrs[:, 0:1], op0=ALU.mult, op1=ALU.mult),
                 reads=[LMV, LRS], writes=[LRS])
            P.op("act", lambda e: e.activation(out=ypre[:], in_=ypre[:], func=AF.Identity, scale=lrs[:, 0:1], bias=lrs[:, 1:2]),
                 reads=[YPRE, YPRE_B, LRS], writes=[YPRE, YPRE_B])
            P.op("pool", lambda e: e.tensor_tensor(out=ypre[:, 0:512], in0=ypre[:, 0:512], in1=lng[:, 0:512], op=ALU.mult), reads=[YPRE, LNG], writes=[YPRE])
            P.op("dve", lambda e: e.tensor_tensor(out=ypre[:, 512:1024], in0=ypre[:, 512:1024], in1=lng[:, 512:1024], op=ALU.mult), reads=[YPRE_B, LNG], writes=[YPRE_B])
            P.op("pool", lambda e: e.tensor_tensor(out=ypre[:, 0:512], in0=ypre[:, 0:512], in1=lnb[:, 0:512], op=ALU.add), reads=[YPRE, LNB], writes=[YPRE])
            P.op("dve", lambda e: e.tensor_tensor(out=ypre[:, 512:1024], in0=ypre[:, 512:1024], in1=lnb[:, 512:1024], op=ALU.add), reads=[YPRE_B, LNB], writes=[YPRE_B])
            if l == n_layers - 1:
                P.dma("sp", c_y[par], y_p[t * 128:(t + 1) * 128, :], ypre[:], reads=[YPRE, YPRE_B])
            else:
                P.dma("sp", c_y[par], y0[t * 128:(t + 1) * 128, :], ypre[:], reads=[YPRE, YPRE_B], writes=[Y0B[t]])


        xs_sb = sb("xs_sb", [NS, D]); XSS = Buf("xs")
        ropes = sb("ropes", [NS, 2, HD]); ROPES = Buf()
        sg = sb("sg", [NS, 16, 4]); SG = Buf()
        wexp = sb("wexp", [NS, NS, 4]); WEXP = Buf()
        wcbs = sb("wcbs", [128, NS * 4]); WCBS = Buf()
        qTs = sb("qTs", [128, 8, NS], BF16); QTS = Buf()
        oTs = sb("oTs", [128, 2, 64]); OTS = Buf()
        sstats = sb("sstats", [NS, 8, 6]); SSTATS = Buf()
        smv = sb("smv", [NS, 8, 2]); SMV = Buf()
        srs = sb("srs", [NS, 8]); SRS = Buf()
        snb = sb("snb", [NS, 8]); SNB = Buf()
        slst = sb("slst", [NS, 2, 6]); SLST = Buf()
        slmv = sb("slmv", [NS, 2]); SLMV = Buf()
        slrs = sb("slrs", [NS, 2]); SLRS = Buf()
        c_s = [P.chan("c_s%d" % i) for i in range(8)]
        c_sg = [P.chan("c_sg0"), P.chan("c_sg1")]
        c_cg = [P.chan("c_cg0"), P.chan("c_cg1")]
        c_so = [ochan("c_so%d" % i) for i in range(8)]
        c_sgo = [ochan("c_sgo0"), ochan("c_sgo1")]
        c_cgo = [ochan("c_cgo0"), ochan("c_cgo1")]
        id16 = ident_f[0:NS, 0:NS]
        gamb = cst[0:NS, K_GAM:K_GAM + 4]
        sbank_ctr = [0]

        def snb_():
            i = sbank_ctr[0] % 6
            sbank_ctr[0] += 1
            return banks[i]

        def sample_path(l):
            S16 = slice(0, NS)
            if l == 0:
                P.dma("sp", c_s[0], xs_sb[:], xs, writes=[XSS])
                P.dma("sp", c_s[7], ropes[:].rearrange("p a d -> p (a d)"), rope_s.partition_broadcast(NS), writes=[ROPES])
            sc = [x_sb[0][S16, :], x_sb[1][S16, :], x_sb[2][S16, :]]
            SCB = [XSB[0], XSB[1], XSB[2]]
            szs = sz2[0]; SZ = SZ2[0]; sz = sz2[0]
            for j in range(3):
                P.dma("sp", c_s[1 + j], sc[j], sconv[l][:, j, :], writes=[SCB[j]])
            P.dma("sp", c_s[4], sg[:, 0, :], sm[l], writes=[SG])
            n0v = o_sb[S16, :, :]
            P.dma("sp", c_s[5], n0v, sn[l], writes=[OSB])
            P.op("pool", lambda e: e.tensor_copy(out=x_bf[S16, :], in_=xs_sb[:]), reads=[XSS], writes=[XBF])
            bt, BK = nb(); v = bfv(bt)
            for k in range(8):
                P.op("pe", lambda e, k=k: e.transpose(v[:, k * NS:(k + 1) * NS], x_bf[S16, k * 128:(k + 1) * 128], ident_bf[S16, 0:NS]),
                     reads=[XBF, IDB], writes=[BK])
            P.op("act", lambda e: e.copy(out=xT[:, :, 0:NS], in_=v[:, 0:8 * NS].rearrange("p (k s) -> p k s", k=8)), reads=[BK], writes=[XT])

            def sproj(c0, n, wb):
                bt, BK = nb()
                for k in range(8):
                    P.op("pe", lambda e, k=k: e.matmul(bt[S16, 0:n], lhsT=xT[:, k, 0:NS], rhs=w_in_sb[:, k, c0:c0 + n],
                                                       start=(k == 0), stop=(k == 7)), reads=[XT, WIN[wb]], writes=[BK])
                return bt, BK

            cos2 = ropes[:, 0, :]
            sin2 = ropes[:, 1, :]
            for i, c0 in enumerate((C_RQ, C_RK)):
                bt, BK = sproj(c0, 512, i)
                src = bt[S16, :].rearrange("p (h d) -> p h d", h=4)
                dst_t = tmp_t[S16, i * 4:(i + 1) * 4, :]
                dst_u = tmp_u[S16, i * 4:(i + 1) * 4, :]
                P.op("dve", lambda e, src=src, dst_t=dst_t: e.tensor_tensor(out=dst_t, in0=src, in1=cos2.unsqueeze(1).broadcast_to([NS, 4, HD]), op=ALU.mult),
                     reads=[BK, ROPES], writes=[TMPT])
                P.op("dve", lambda e, src=src, dst_u=dst_u: e.tensor_tensor(out=dst_u[:, :, 0:64], in0=src[:, :, 64:128],
                                                                          in1=sin2[:, 0:64].unsqueeze(1).broadcast_to([NS, 4, 64]), op=ALU.mult),
                     reads=[BK, ROPES], writes=[TMPU])
                P.op("dve", lambda e, src=src, dst_u=dst_u: e.tensor_tensor(out=dst_u[:, :, 64:128], in0=src[:, :, 0:64],
                                                                          in1=sin2[:, 64:128].unsqueeze(1).broadcast_to([NS, 4, 64]), op=ALU.mult),
                     reads=[BK, ROPES], writes=[TMPU])
            qk_s = on[S16, :, :]
            P.op("pool", lambda e: e.tensor_tensor(out=qk_s, in0=tmp_t[S16, :, :], in1=tmp_u[S16, :, :], op=ALU.add), reads=[TMPT, TMPU], writes=[ON])
            v2s = hm[S16, :, :]
            bt, BK = sproj(C_RV, 512, 2)
            P.op("act", lambda e, bt=bt: e.mul(out=v2s.rearrange("p h d -> p (h d)"), in_=bt[S16, :], mul=ISQ), reads=[BK], writes=[HM])
            bt, BK = sproj(C_RZ, 512, 3)
            P.op("act", lambda e, bt=bt: e.activation(out=sz[S16, 0:512], in_=bt[S16, :], func=AF.Silu), reads=[BK], writes=[SZ])
            mqk_s = ypre[S16, :]
            for i in range(2):
                bt, BK = sproj(C_MQK + i * 512, 512, 4 + i)
                P.op("act", lambda e, bt=bt, i=i: e.copy(out=mqk_s[:, i * 512:(i + 1) * 512], in_=bt[S16, :]), reads=[BK], writes=[YPRE])
            P.dma("sp", c_so[0], conv_s[l][:, 0, :], sc[1], reads=[SCB[1]])
            P.dma("sp", c_so[1], conv_s[l][:, 1, :], sc[2], reads=[SCB[2]])
            P.dma("sp", c_so[2], conv_s[l][:, 2, :], mqk_s, reads=[YPRE])
            Wb = sz2[1][S16, :]; WB = SZ2[1]
            acc = tmp_t[S16, :, :].rearrange("p g d -> p (g d)")
            tmpc = tmp_u[S16, :, :].rearrange("p g d -> p (g d)")
            full = [sc[0], sc[1], sc[2], mqk_s]
            FB = [SCB[0], SCB[1], SCB[2], YPRE]
            for n_, j in enumerate((3, 0, 1, 2)):
                P.dma("sp", c_s[6], Wb, conv_w[l][j].partition_broadcast(NS), writes=[WB])
                if n_ == 0:
                    P.op("dve", lambda e, j=j: e.tensor_tensor(out=acc, in0=full[j], in1=Wb, op=ALU.mult), reads=[FB[j], WB, ON], writes=[TMPT])
                else:
                    P.op("dve", lambda e, j=j: e.tensor_tensor(out=tmpc, in0=full[j], in1=Wb, op=ALU.mult), reads=[FB[j], WB, ON], writes=[TMPU])
                    P.op("pool", lambda e: e.tensor_tensor(out=acc, in0=acc, in1=tmpc, op=ALU.add), reads=[TMPU, TMPT], writes=[TMPT])
            P.dma("sp", c_s[6], Wb, conv_b[l].partition_broadcast(NS), writes=[WB])
            P.op("pool", lambda e: e.tensor_tensor(out=acc, in0=acc, in1=Wb, op=ALU.add), reads=[WB, TMPT], writes=[TMPT])
            P.op("act", lambda e: e.activation(out=acc, in_=acc, func=AF.Silu), reads=[TMPT], writes=[TMPT])
            qkm_s = tmp_t[S16, :, :]
            v_s = gated[S16, :].bitcast(F32).rearrange("p (h d) -> p h d", h=4)
            bt, BK = sproj(C_MV, 512, 6)
            P.op("act", lambda e, bt=bt: e.copy(out=v_s.rearrange("p h d -> p (h d)"), in_=bt[S16, :]), reads=[BK], writes=[GATED])
            bt, BK = sproj(C_MO, 512, 7)
            P.op("act", lambda e, bt=bt: e.activation(out=th[S16, :], in_=bt[S16, :], func=AF.Tanh, scale=0.5), reads=[BK], writes=[TH])
            bt, BK = sproj(C_MZ, 512, 8)
            P.op("act", lambda e, bt=bt: e.activation(out=sz[S16, 512:1024], in_=bt[S16, :], func=AF.Silu), reads=[BK], writes=[SZ])
            bt, BK = sproj(C_G, 8, 9)
            G_ = lambda i: sg[:, i, :]
            P.op("dve", lambda e, bt=bt: e.tensor_tensor(out=sg[:, 1:3, :].rearrange("p a h -> p (a h)"), in0=bt[S16, 0:8], in1=bias8[S16, :], op=ALU.add),
                 reads=[BK, BIAS8], writes=[SG])
            P.op("act", lambda e: e.activation(out=G_(3), in_=G_(2), func=AF.Exp, scale=-1.0), reads=[SG], writes=[SG])
            P.op("act", lambda e: e.activation(out=G_(3), in_=G_(3), func=AF.Ln, bias=1.0), reads=[SG], writes=[SG])
            P.op("dve", lambda e: e.tensor_tensor(out=G_(4), in0=G_(1), in1=G_(3), op=ALU.add), reads=[SG], writes=[SG])
            P.op("dve", lambda e: e.tensor_tensor(out=G_(5), in0=G_(4), in1=G_(0), op=ALU.max), reads=[SG], writes=[SG])
            P.op("dve", lambda e: e.tensor_tensor(out=G_(6), in0=G_(5), in1=G_(3), op=ALU.subtract), reads=[SG], writes=[SG])
            P.dma("sp", c_so[3], m_s[l], G_(6), reads=[SG])
            P.op("dve", lambda e: e.tensor_tensor(out=G_(14), in0=G_(4), in1=G_(5), op=ALU.subtract), reads=[SG], writes=[SG])
            P.op("dve", lambda e: e.tensor_scalar(out=G_(14), in0=G_(14), scalar1=LNISQ, scalar2=None, op0=ALU.add), reads=[SG], writes=[SG])
            P.op("act", lambda e: e.activation(out=G_(7), in_=G_(14), func=AF.Exp), reads=[SG], writes=[SG])
            P.op("dve", lambda e: e.tensor_tensor(out=G_(14), in0=G_(3), in1=G_(5), op=ALU.subtract), reads=[SG], writes=[SG])
            P.op("act", lambda e: e.activation(out=G_(8), in_=G_(14), func=AF.Exp), reads=[SG], writes=[SG])
            P.op("dve", lambda e: e.tensor_tensor(out=G_(14), in0=G_(0), in1=G_(5), op=ALU.subtract), reads=[SG], writes=[SG])
            P.op("act", lambda e: e.activation(out=G_(9), in_=G_(14), func=AF.Exp), reads=[SG], writes=[SG])
            bt, BK = nb()
            for g in range(4):
                P.op("pe", lambda e, g=g, bt=bt: e.transpose(bt[:, g * NS:(g + 1) * NS], qk_s[:, g, :], id16), reads=[ON, CST], writes=[BK])
            for g in range(4):
                P.op("pe", lambda e, g=g, bt=bt: e.transpose(bt[:, (4 + g) * NS:(5 + g) * NS], qkm_s[:, g, :], id16), reads=[TMPT, CST], writes=[BK])
            P.op("act", lambda e, bt=bt: e.copy(out=qTs[:].rearrange("p g s -> p (g s)"), in_=bt[:, 0:8 * NS]), reads=[BK], writes=[QTS])
            bc4 = lambda i: sg[:, i, :].unsqueeze(2).broadcast_to([NS, 4, HD])
            P.op("dve", lambda e: e.tensor_tensor(out=qkm_s[:, 4:8, :], in0=qkm_s[:, 4:8, :], in1=bc4(7), op=ALU.mult), reads=[TMPT, SG], writes=[TMPT])
            kbf = gT[S16, :, :]
            P.op("act", lambda e: e.copy(out=kbf[:, 0:4, :], in_=qk_s[:, 4:8, :]), reads=[ON], writes=[GT])
            P.op("act", lambda e: e.copy(out=kbf[:, 4:8, :], in_=qkm_s[:, 4:8, :]), reads=[TMPT], writes=[GT])
            prod = tmp_u[S16, :, :]
            P.op("dve", lambda e: e.tensor_tensor(out=prod[:, 0:4, :], in0=qk_s[:, 0:4, :], in1=qk_s[:, 4:8, :], op=ALU.mult), reads=[ON], writes=[TMPU])
            P.op("dve", lambda e: e.tensor_reduce(out=G_(10), in_=prod[:, 0:4, :], axis=AX.X, op=ALU.add), reads=[TMPU], writes=[SG])
            P.op("dve", lambda e: e.tensor_tensor(out=prod[:, 4:8, :], in0=qkm_s[:, 0:4, :], in1=qkm_s[:, 4:8, :], op=ALU.mult), reads=[TMPT], writes=[TMPU])
            P.op("dve", lambda e: e.tensor_reduce(out=G_(11), in_=prod[:, 4:8, :], axis=AX.X, op=ALU.add), reads=[TMPU], writes=[SG])
            P.op("dve", lambda e: e.tensor_tensor(out=prod[:, 0:4, :], in0=qkm_s[:, 0:4, :], in1=n0v, op=ALU.mult), reads=[TMPT, OSB, SG], writes=[TMPU])
            P.op("dve", lambda e: e.tensor_reduce(out=G_(13), in_=prod[:, 0:4, :], axis=AX.X, op=ALU.add), reads=[TMPU], writes=[SG])
            P.op("dve", lambda e: e.tensor_tensor(out=n0v, in0=n0v, in1=bc4(9), op=ALU.mult), reads=[OSB, SG], writes=[OSB])
            P.op("pool", lambda e: e.tensor_tensor(out=n0v, in0=n0v, in1=qkm_s[:, 4:8, :], op=ALU.add), reads=[OSB, TMPT], writes=[OSB])
            P.dma("sp", c_so[4], n_s[l], n0v, reads=[OSB])
            P.op("dve", lambda e: e.tensor_tensor(out=wexp[:], in0=sg[:, 9, :].unsqueeze(1).broadcast_to([NS, NS, 4]),
                                                  in1=id16.unsqueeze(2).broadcast_to([NS, NS, 4]), op=ALU.mult), reads=[SG, CST], writes=[WEXP])
            bt, BK = nb()
            P.op("pe", lambda e, bt=bt: e.matmul(bt[:, 0:NS * 4], lhsT=ones_f[0:NS, :], rhs=wexp[:].rearrange("p s h -> p (s h)"), start=True, stop=True),
                 reads=[WEXP, CST], writes=[BK])
            P.op("act", lambda e, bt=bt: e.copy(out=wcbs[:], in_=bt[:, 0:NS * 4]), reads=[BK], writes=[WCBS])
            if l + 1 < n_layers:
                load_weights_in(l + 1)
            GS = [x_sb[0][:].rearrange("p (s h e) -> p s h e", s=2, h=4), x_sb[1][:].rearrange("p (s h e) -> p s h e", s=2, h=4)]
            GC = [x_sb[2][:].rearrange("p (s h e) -> p s h e", s=2, h=4), sz2[1][:].rearrange("p (s h e) -> p s h e", s=2, h=4)]
            GCB = [XSB[2], SZ2[1]]
            vexp = hist[S16, :, :].rearrange("p c t -> p (c t)")[:, 0:1024].rearrange("p (s h e) -> p s h e", s=2, h=4)
            Gbf = [qk_rot[:].rearrange("p (s h) e -> p s h e", s=2), qkm[:].rearrange("p (s h) e -> p s h e", s=2)]
            GBFB = [[QKR, QKR_K], [QKM]]
            bor, BOR = banks[6]
            bom, BOM = banks[7]
            def load_group(g):
                par = g % 2
                s0 = g * 2
                P.dma("sp", c_sg[par], GS[par], sret[l][s0:s0 + 2].rearrange("s h d e -> d s h e"), writes=[XSB[par]])
                P.dma("sp", c_cg[par], GC[par], sC[l][s0:s0 + 2].rearrange("s h d e -> d s h e"), writes=[GCB[par]])

            load_group(0)
            for g in range(NS // 2):
                par = g % 2
                s0 = g * 2
                if g + 1 < NS // 2:
                    load_group(g + 1)
                for typ in range(2):
                    Gt = GS[par] if typ == 0 else GC[par]
                    GB = XSB[par] if typ == 0 else GCB[par]
                    bo_, BO_ = (bor, BOR) if typ == 0 else (bom, BOM)
                    gbf = Gbf[typ]
                    P.op("act", lambda e, Gt=Gt, gbf=gbf: e.copy(out=gbf.rearrange("p s h e -> p (s h e)"), in_=Gt.rearrange("p s h e -> p (s h e)")),
                         reads=[GB], writes=GBFB[typ])
                    for sl in range(2):
                        for h in range(4):
                            col = h * NS + s0 + sl
                            P.op("pe", lambda e, gbf=gbf, bo_=bo_, sl=sl, h=h, col=col, typ=typ, s0=s0: e.matmul(
                                bo_[:, col:col + 1], lhsT=gbf[:, sl, h, :], rhs=qTs[:, typ * 4 + h, s0 + sl:s0 + sl + 1], start=True, stop=True),
                                reads=GBFB[typ] + [QTS], writes=[BO_])
                    vsrc = v2s if typ == 0 else v_s
                    VB = HM if typ == 0 else GATED
                    P.op("dve", lambda e, vsrc=vsrc, s0=s0: e.tensor_tensor(
                        out=vexp, in0=vsrc.unsqueeze(1).broadcast_to([NS, 2, 4, HD]),
                        in1=id16[:, s0:s0 + 2].unsqueeze(2).unsqueeze(3).broadcast_to([NS, 2, 4, HD]), op=ALU.mult),
                        reads=[VB, CST], writes=[HIST])
                    ksrc = kbf[:, 0:4, :] if typ == 0 else kbf[:, 4:8, :]
                    KB = GT
                    for sl in range(2):
                        bu_, BU_ = snb_()
                        for h in range(4):
                            P.op("pe", lambda e, bu_=bu_, h=h, sl=sl, ksrc=ksrc: e.matmul(bu_[:, h * 128:(h + 1) * 128], lhsT=ksrc[:, h, :], rhs=vexp[:, sl, h, :],
                                                                                    start=True, stop=True), reads=[KB, HIST], writes=[BU_])
                        for h in range(4):
                            if typ == 0:
                                P.op("dve", lambda e, bu_=bu_, h=h, sl=sl, Gt=Gt: e.scalar_tensor_tensor(
                                    out=Gt[:, sl, h, :], in0=Gt[:, sl, h, :], scalar=GAM[h], in1=bu_[:, h * 128:(h + 1) * 128], op0=ALU.mult, op1=ALU.add),
                                    reads=[BU_, GB], writes=[GB])
                            else:
                                ci = (s0 + sl) * 4 + h
                                P.op("dve", lambda e, bu_=bu_, h=h, sl=sl, Gt=Gt, ci=ci: e.scalar_tensor_tensor(
                                    out=Gt[:, sl, h, :], in0=Gt[:, sl, h, :], scalar=wcbs[:, ci:ci + 1], in1=bu_[:, h * 128:(h + 1) * 128], op0=ALU.mult, op1=ALU.add),
                                    reads=[BU_, GB, WCBS], writes=[GB])
                P.dma("sp", c_sgo[par], ret_s[l][s0:s0 + 2].rearrange("s h d e -> d s h e"), GS[par], reads=[XSB[par]])
                P.dma("sp", c_cgo[par], C_s[l][s0:s0 + 2].rearrange("s h d e -> d s h e"), GC[par], reads=[GCB[par]])
            P.op("act", lambda e: e.copy(out=oTs[:, 0, :], in_=bor[:, 0:64]), reads=[BOR], writes=[OTS])
            P.op("act", lambda e: e.copy(out=oTs[:, 1, :], in_=bom[:, 0:64]), reads=[BOM], writes=[OTS])
            btr_, BTR_ = nb()
            btm_, BTM_ = nb()
            for typ, (bt, BK) in enumerate(((btr_, BTR_), (btm_, BTM_))):
                for h in range(4):
                    P.op("pe", lambda e, bt=bt, typ=typ, h=h: e.transpose(bt[S16, h * 128:(h + 1) * 128], oTs[:, typ, h * NS:(h + 1) * NS], ident_f),
                         reads=[OTS, CST], writes=[BK])
            hs = tmp_u[S16, :, :]
            t2 = ypre[S16, :].rearrange("p (g d) -> p g d", g=8)
            inter_r = btr_[S16, :].rearrange("p (h d) -> p h d", h=4)
            inter_m = btm_[S16, :].rearrange("p (h d) -> p h d", h=4)
            P.op("dve", lambda e: e.tensor_tensor(out=hs[:, 0:4, :], in0=inter_r, in1=gamb.unsqueeze(2).broadcast_to([NS, 4, HD]), op=ALU.mult),
                 reads=[BTR_, CST, SG], writes=[TMPU])
            P.op("dve", lambda e: e.tensor_tensor(out=t2[:, 0:4, :], in0=v2s, in1=bc4(10), op=ALU.mult), reads=[HM, SG], writes=[YPRE])
            P.op("pool", lambda e: e.tensor_tensor(out=hs[:, 0:4, :], in0=hs[:, 0:4, :], in1=t2[:, 0:4, :], op=ALU.add), reads=[TMPU, YPRE], writes=[TMPU])
            P.op("dve", lambda e: e.tensor_tensor(out=hs[:, 4:8, :], in0=inter_m, in1=bc4(9), op=ALU.mult), reads=[BTM_, SG], writes=[TMPU])
            P.op("dve", lambda e: e.tensor_tensor(out=t2[:, 4:8, :], in0=v_s, in1=bc4(11), op=ALU.mult), reads=[GATED, SG], writes=[YPRE])
            P.op("pool", lambda e: e.tensor_tensor(out=hs[:, 4:8, :], in0=hs[:, 4:8, :], in1=t2[:, 4:8, :], op=ALU.add), reads=[TMPU, YPRE], writes=[TMPU])
            P.op("dve", lambda e: e.tensor_tensor(out=G_(12), in0=G_(13), in1=G_(9), op=ALU.mult), reads=[SG], writes=[SG])
            P.op("dve", lambda e: e.tensor_tensor(out=G_(12), in0=G_(12), in1=G_(11), op=ALU.add), reads=[SG], writes=[SG])
            P.op("act", lambda e: e.activation(out=G_(12), in_=G_(12), func=AF.Abs), reads=[SG], writes=[SG])
            P.op("dve", lambda e: e.tensor_tensor(out=G_(12), in0=G_(12), in1=G_(8), op=ALU.max), reads=[SG], writes=[SG])
            P.op("dve", lambda e: e.reciprocal(out=G_(12), in_=G_(12)), reads=[SG], writes=[SG])
            P.op("dve", lambda e: e.tensor_tensor(out=hs[:, 4:8, :], in0=hs[:, 4:8, :], in1=bc4(12), op=ALU.mult), reads=[TMPU, SG], writes=[TMPU])
            for g in range(8):
                P.op("dve", lambda e, g=g: e.bn_stats(out=sstats[:, g, :], in_=hs[:, g, :]), reads=[TMPU], writes=[SSTATS])
            for g in range(8):
                P.op("dve", lambda e, g=g: e.bn_aggr(out=smv[:, g, :], in_=sstats[:, g, :]), reads=[SSTATS], writes=[SMV])
            P.op("pool", lambda e: e.tensor_scalar(out=srs[:], in0=smv[:, :, 1], scalar1=GN_EPS, scalar2=None, op0=ALU.add), reads=[SMV], writes=[SRS])
            P.op("pool", lambda e: e.tensor_tensor(out=srs[:], in0=srs[:], in1=mhalf[S16, :], op=ALU.pow), reads=[SRS, MHALF], writes=[SRS])
            P.op("dve", lambda e: e.scalar_tensor_tensor(out=snb[:], in0=smv[:, :, 0], scalar=-1.0, in1=srs[:], op0=ALU.mult, op1=ALU.mult),
                 reads=[SMV, SRS], writes=[SNB])
            on2 = on[S16, :, :]
            for g in range(8):
                P.op("act", lambda e, g=g: e.activation(out=on2[:, g, :], in_=hs[:, g, :], func=AF.Identity, scale=srs[:, g:g + 1], bias=snb[:, g:g + 1]),
                     reads=[TMPU, SRS, SNB], writes=[ON])
            P.op("dve", lambda e: e.scalar_tensor_tensor(out=sz[S16, 512:1024], in0=th[S16, :], scalar=1.0, in1=sz[S16, 512:1024], op0=ALU.add, op1=ALU.mult),
                 reads=[TH, SZ], writes=[SZ])
            P.op("pool", lambda e: e.tensor_tensor(out=sz[S16, :], in0=sz[S16, :], in1=gbc[S16, :], op=ALU.mult), reads=[SZ, GBC], writes=[SZ])
            P.op("pool", lambda e: e.tensor_tensor(out=gated[S16, :], in0=on2.rearrange("p g d -> p (g d)"), in1=sz[S16, :], op=ALU.mult),
                 reads=[ON, SZ], writes=[GATED])
            bt, BK = nb(); vg = bfv(bt)
            for k in range(8):
                P.op("pe", lambda e, k=k: e.transpose(vg[:, k * NS:(k + 1) * NS], gated[S16, k * 128:(k + 1) * 128], ident_bf[S16, 0:NS]),
                     reads=[GATED, IDB], writes=[BK])
            P.op("act", lambda e: e.copy(out=gT[:, :, 0:NS], in_=vg[:, 0:8 * NS].rearrange("p (k s) -> p k s", k=8)), reads=[BK], writes=[GT])
            yps = ypre[S16, :]
            for n in range(2):
                bt, BK = nb()
                for k in range(8):
                    P.op("pe", lambda e, k=k, n=n, bt=bt: e.matmul(bt[S16, :], lhsT=gT[:, k, 0:NS], rhs=w_out_sb[:, k, n * 512:(n + 1) * 512],
                                                               start=(k == 0), stop=(k == 7)), reads=[GT, WOUT[n]], writes=[BK])
                P.op("dve", lambda e, n=n, bt=bt: e.scalar_tensor_tensor(out=yps[:, n * 512:(n + 1) * 512], in0=xs_sb[:, n * 512:(n + 1) * 512], scalar=ALPHA,
                                                                     in1=bt[S16, :], op0=ALU.mult, op1=ALU.add), reads=[BK, XSS], writes=[YPRE])
                P.op("dve", lambda e, n=n: e.bn_stats(out=slst[:, n, :], in_=yps[:, n * 512:(n + 1) * 512]), reads=[YPRE], writes=[SLST])
            P.op("dve", lambda e: e.bn_aggr(out=slmv[:], in_=slst[:].rearrange("p a b -> p (a b)")), reads=[SLST], writes=[SLMV])
            P.op("pool", lambda e: e.tensor_scalar(out=slrs[:, 0:1], in0=slmv[:, 1:2], scalar1=LN_EPS, scalar2=None, op0=ALU.add), reads=[SLMV], writes=[SLRS])
            P.op("pool", lambda e: e.tensor_tensor(out=slrs[:, 0:1], in0=slrs[:, 0:1], in1=mhalf[S16, 0:1], op=ALU.pow), reads=[SLRS, MHALF], writes=[SLRS])
            P.op("dve", lambda e: e.scalar_tensor_tensor(out=slrs[:, 1:2], in0=slmv[:, 0:1], scalar=-1.0, in1=slrs[:, 0:1], op0=ALU.mult, op1=ALU.mult),
                 reads=[SLMV, SLRS], writes=[SLRS])
            P.op("act", lambda e: e.activation(out=xs_sb[:], in_=yps, func=AF.Identity, scale=slrs[:, 0:1], bias=slrs[:, 1:2]),
                 reads=[YPRE, SLRS], writes=[XSS])
            P.op("pool", lambda e: e.tensor_tensor(out=xs_sb[:], in0=xs_sb[:], in1=lng[S16, :], op=ALU.mult), reads=[XSS, LNG], writes=[XSS])
            P.op("pool", lambda e: e.tensor_tensor(out=xs_sb[:], in0=xs_sb[:], in1=lnb[S16, :], op=ALU.add), reads=[XSS, LNB], writes=[XSS])
            if l == n_layers - 1:
                P.dma("sp", c_so[5], y_s, xs_sb[:], reads=[XSS])

        def finalize_prompt(l):
            P.dma("sp", c_fin[1], ret_p[l].rearrange("h d e -> d h e"), S_f[:], reads=[SF])
            P.dma("sp", c_fin[2], C_p[l].rearrange("h d e -> d h e"), C_f[:, :, 0:128], reads=[CF])
            P.dma("sp", c_fin[3], n_p[l].rearrange("h d -> d h"), C_f[:, :, 128], reads=[CF], allow_slow_non_contiguous=True)
            for half in range(2):
                bt, BK = nb()
                for c4 in range(4):
                    ch = half * 4 + c4
                    P.op("pe", lambda e, ch=ch, c4=c4, bt=bt: e.transpose(bt[0:3, c4 * 128:(c4 + 1) * 128], convp_sb[:, ch, :], ident_f),
                         reads=[CONVP, CST], writes=[BK])
                P.op("act", lambda e, half=half, bt=bt: e.copy(out=convp_T[:, half * 512:(half + 1) * 512], in_=bt[0:3, :]),
                     reads=[BK], writes=[CONVPT])
            P.dma("sp", c_fin[4], conv_p[l], convp_T[:], reads=[CONVPT])

        def stage(name):
            if stop_at == name:
                raise StopBuild()

        try:
            for l in range(n_layers):
                load_weights(l)
                stage("weights")
                load_params(l)
                stage("params")
                prepass(l)
                stage("prepass")
                NSEG = 25
                OFF = int(os.environ.get("MK_OFF", (NSEG + 1) // 2))
                gens = [main_tile(l, t) for t in range(n_tiles)]
                done = [0] * n_tiles
                fin = [False] * n_tiles
                slots = {}
                for t in range(n_tiles):
                    for k in range(NSEG):
                        slots.setdefault(t * OFF + k, []).append((t, k))
                for sl_ in sorted(slots):
                    for (t, k) in sorted(slots[sl_]):
                        assert not fin[t], "main_tile has fewer steps than NSEG"
                        try:
                            next(gens[t])
                            done[t] += 1
                        except StopIteration:
                            fin[t] = True
                assert all(fin), "main_tile has more steps than NSEG: %s" % done
                stage("tiles")
                finalize_prompt(l)
                if do_sample:
                    P.barrier()
                    sample_path(l)
                    P.barrier()
                stage('sample')
        except StopBuild:
            pass

        for c in P.chans:
            if c.val:
                P.wait_chan("sp", c)
        with nc.Block() as block:
            P.finish(block)
    return nc


def make_consts():
    cst = np.zeros((128, NCST), np.float32)
    cst[:, K_ID:K_ID + 128] = np.eye(128, dtype=np.float32)
    idx = np.arange(128)
    cst[:, K_TRI:K_TRI + 128] = (idx[:, None] <= idx[None, :]).astype(np.float32)
    for h in range(H):
        lg = np.float32(LOGG[h])
        cst[:, K_KDEC + h] = np.exp(lg * (np.float32(L - 1) - idx.astype(np.float32))).astype(np.float32) * np.float32(ISQ)
        cst[:, K_QDEC + h * 128:K_QDEC + (h + 1) * 128] = np.exp(lg * (idx.astype(np.float32) + 1.0 - L)).astype(np.float32)[None, :]
    cst[:, K_ONE:K_ONE + 128] = 1.0
    for h in range(H):
        cst[:, K_GAM + h] = np.float32(GAM[h])
    half = HD // 2
    inv = (np.float32(10000.0) ** (-np.arange(half, dtype=np.float32) / np.float32(half))).astype(np.float32)

    def rope(pos):
        ang = (pos.astype(np.float32)[:, None] * inv[None, :]).astype(np.float32)
        c = np.cos(ang.astype(np.float64)).astype(np.float32)
        s = np.sin(ang.astype(np.float64)).astype(np.float32)
        out = np.zeros((len(pos), 2, HD), np.float32)
        out[:, 0, :half] = c
        out[:, 0, half:] = c
        out[:, 1, :half] = -s
        out[:, 1, half:] = s
        return out

    rope_p = rope(np.arange(T))
    rope_s = rope(np.array([PAST_LEN])).reshape(2 * HD)
    return cst, rope_p, rope_s


_CACHE = {}


def kernel(x_prompt, x_sample, state_ret, state_mlstm_C, state_mlstm_n, state_mlstm_m, state_conv,
           w_in, conv_w, conv_b, b_i, b_f, g_ret, g_m, w_out, ln_g, ln_b):
    n = 8
    if "nc" not in _CACHE:
        _CACHE["nc"] = build_program()
    nc = _CACHE["nc"]
    cst, rope_p, rope_s = make_consts()
    f = lambda a: np.ascontiguousarray(np.asarray(a, dtype=np.float32))
    shared = dict(w_in=f(w_in), conv_w=f(conv_w), conv_b=f(conv_b), b_i=f(b_i), b_f=f(b_f), g_ret=f(g_ret), g_m=f(g_m),
                  w_out=f(w_out), ln_g=f(ln_g), ln_b=f(ln_b), cst=cst, rope_p=rope_p, rope_s=rope_s)
    in_maps = []
    for c in range(n):
        s0, s1 = c * NS, (c + 1) * NS
        m = dict(shared)
        m["xp"] = f(x_prompt[c])
        m["xs"] = f(x_sample[s0:s1, 0, :])
        m["sret"] = f(state_ret[:, s0:s1])
        m["sC"] = f(state_mlstm_C[:, s0:s1])
        m["sn"] = f(state_mlstm_n[:, s0:s1])
        m["sm"] = f(state_mlstm_m[:, s0:s1])
        m["sconv"] = f(state_conv[:, s0:s1])
        in_maps.append(m)
    res = run_bass_kernel_spmd(nc, in_maps, core_ids=list(range(n)))
    R = res.results
    y_p = np.stack([R[c]["y_p"] for c in range(n)], 0)
    y_s = np.concatenate([R[c]["y_s"] for c in range(n)], 0)[:, None, :]
    ret_p = np.stack([R[c]["ret_p"] for c in range(n)], 1)
    C_p = np.stack([R[c]["C_p"] for c in range(n)], 1)
    n_p = np.stack([R[c]["n_p"] for c in range(n)], 1)
    m_p = np.stack([R[c]["m_p"] for c in range(n)], 1)
    conv_p = np.stack([R[c]["conv_p"] for c in range(n)], 1)
    ret_s = np.concatenate([R[c]["ret_s"] for c in range(n)], 1)
    C_s = np.concatenate([R[c]["C_s"] for c in range(n)], 1)
    n_s = np.concatenate([R[c]["n_s"] for c in range(n)], 1)
    m_s = np.concatenate([R[c]["m_s"] for c in range(n)], 1)
    conv_s = np.concatenate([R[c]["conv_s"] for c in range(n)], 1)
    return (y_p, y_s, ret_p, C_p, n_p, m_p, conv_p, ret_s, C_s, n_s, m_s, conv_s)
```
